# Optimizing a Trainium2 kernel written in Bass

```python
import math
import jax, jax.numpy as jnp
from jax import lax
import numpy as np


D_MODEL = 4096
BATCH = 4
SEQ = 2048
DEPTH = 4

N_MIXERS = 3
N_NSA = (DEPTH + 2) // 3
N_RG = (DEPTH + 1) // 3
N_HG = DEPTH // 3
NORM_EPS = 1e-6

NSA_HEADS = 32
NSA_KV_GROUPS = 4
NSA_Q_PER_GROUP = NSA_HEADS // NSA_KV_GROUPS
NSA_HEAD_DIM = D_MODEL // NSA_HEADS
NSA_INNER = NSA_HEADS * NSA_HEAD_DIM
NSA_KV = NSA_KV_GROUPS * NSA_HEAD_DIM
NSA_IN = NSA_INNER + 6 * NSA_KV + 3 * NSA_HEADS + NSA_INNER
CMP_LEN = 32
CMP_STRIDE = 16
SEL_LEN = 64
SEL_TOPK = 16
WINDOW = 512
SEL_QCHUNK = 16
WIN_QBLOCK = 128
ALIBI_MAX_EXP = 8.0
NEG_INF = -1e30
FORCE_SCORE = 1e6

RG_WIDTH = D_MODEL
RG_BLOCKS = 16
RG_BLOCK = RG_WIDTH // RG_BLOCKS
RG_CONV = 4
RG_C = 8.0

HG_HEADS = 32
HG_KEY_DIM = 128
HG_VAL_DIM = D_MODEL // HG_HEADS
HG_KEY = HG_HEADS * HG_KEY_DIM
HG_VAL = HG_HEADS * HG_VAL_DIM
HG_IN = 2 * HG_KEY + 2 * HG_VAL
HG_CHUNK = 64

kernel_name = 'hybrid_nsa_rglru_hgrn2_sandwich'


def _rmsnorm(x, gain):
    xf = x.astype(jnp.float32)
    y = xf * lax.rsqrt(jnp.mean(xf * xf, axis=-1, keepdims=True) + NORM_EPS)
    return (y * gain.astype(jnp.float32)).astype(x.dtype)


def _split(t, sizes):
    cuts = [int(c) for c in np.cumsum(sizes)[:-1]]
    return jnp.split(t, cuts, axis=-1)


def _alibi_slopes():
    return 2.0 ** (-ALIBI_MAX_EXP * jnp.arange(1, NSA_HEADS + 1, dtype=jnp.float32) / NSA_HEADS)


def _compress(t, pe, w1, w2):
    bsz, seq, g, d = t.shape
    ratio = CMP_LEN // CMP_STRIDE
    n_chunk = seq // CMP_STRIDE
    n_cmp = n_chunk - ratio + 1
    chunks = t.reshape(bsz, n_chunk, CMP_STRIDE, g, d)
    blocks = jnp.concatenate([chunks[:, r:r + n_cmp] for r in range(ratio)], axis=2)
    blocks = blocks + pe[None, None, :, None, :].astype(t.dtype)
    flat = blocks.transpose(0, 1, 3, 2, 4).reshape(bsz, n_cmp, g, CMP_LEN * d)
    return jax.nn.gelu(flat @ w1) @ w2


def _nsa_select(q, ks, vs, p_cmp, slopes, pos, scale):
    bsz, seq, g, hg, d = q.shape
    c_ratio = CMP_LEN // CMP_STRIDE
    s_ratio = SEL_LEN // CMP_STRIDE
    n_sel = seq // SEL_LEN
    pg = p_cmp.sum(axis=2)
    padded = jnp.pad(pg, ((0, 0), (0, 0), (0, 0), (c_ratio - 1, c_ratio - 1)))
    span = s_ratio * (n_sel - 1) + 1
    p_slc = sum(padded[..., m + n:m + n + span:s_ratio] for m in range(s_ratio) for n in range(c_ratio))
    blk = jnp.arange(n_sel)[None, :]
    cur = (jnp.arange(seq) // SEL_LEN)[:, None]
    forced = (blk == 0) | (blk == cur) | (blk == cur - 1)
    future = blk > cur
    score = jnp.where(forced, FORCE_SCORE, jnp.where(future, -1.0, p_slc))
    n_top = min(SEL_TOPK, n_sel)
    _, idx = lax.top_k(score, n_top)

    k_blocks = ks.reshape(bsz, n_sel, SEL_LEN, g, d).transpose(0, 3, 1, 2, 4)
    v_blocks = vs.reshape(bsz, n_sel, SEL_LEN, g, d).transpose(0, 3, 1, 2, 4)
    n_q = seq // SEL_QCHUNK
    q_ch = q.transpose(0, 2, 1, 3, 4).reshape(bsz, g, n_q, SEL_QCHUNK, hg, d).transpose(2, 0, 1, 3, 4, 5)
    idx_ch = idx.reshape(bsz, g, n_q, SEL_QCHUNK, n_top).transpose(2, 0, 1, 3, 4)
    t_ch = pos.reshape(n_q, SEL_QCHUNK)
    gather = jax.vmap(jax.vmap(lambda blocks, ix: blocks[ix]))
    offs = jnp.arange(SEL_LEN)
    n_keys = n_top * SEL_LEN

    def sel_chunk(args):
        qq, ii, tt = args
        kk = gather(k_blocks, ii).reshape(bsz, g, SEL_QCHUNK, n_keys, d)
        vv = gather(v_blocks, ii).reshape(bsz, g, SEL_QCHUNK, n_keys, d)
        kpos = (ii[..., None] * SEL_LEN + offs).reshape(bsz, g, SEL_QCHUNK, n_keys).astype(jnp.float32)
        dist = (tt[None, None, :, None] - kpos)[:, :, :, None, :]
        s = jnp.einsum('bgqhd,bgqkd->bgqhk', qq, kk).astype(jnp.float32) * scale
        s = s - slopes[None, :, None, :, None] * dist
        s = jnp.where(dist >= 0, s, NEG_INF)
        pr = jax.nn.softmax(s, axis=-1)
        return jnp.einsum('bgqhk,bgqkd->bgqhd', pr.astype(vv.dtype), vv)

    o = lax.map(sel_chunk, (q_ch, idx_ch, t_ch))
    return o.transpose(1, 0, 3, 2, 4, 5).reshape(bsz, seq, g, hg, d)


def _nsa_window(q, kw, vw, slopes, scale):
    bsz, seq, g, hg, d = q.shape
    n_blk = seq // WIN_QBLOCK
    kw_pad = jnp.pad(kw, ((0, 0), (WINDOW, 0), (0, 0), (0, 0)))
    vw_pad = jnp.pad(vw, ((0, 0), (WINDOW, 0), (0, 0), (0, 0)))
    q_blk = q.reshape(bsz, n_blk, WIN_QBLOCK, g, hg, d).transpose(1, 0, 2, 3, 4, 5)
    span = WIN_QBLOCK + WINDOW

    def win_block(args):
        c, qq = args
        start = c * WIN_QBLOCK
        kk = lax.dynamic_slice_in_dim(kw_pad, start, span, axis=1)
        vv = lax.dynamic_slice_in_dim(vw_pad, start, span, axis=1)
        tq = (start + jnp.arange(WIN_QBLOCK)).astype(jnp.float32)
        tk = (start - WINDOW + jnp.arange(span)).astype(jnp.float32)
        dist = tq[:, None] - tk[None, :]
        valid = (dist >= 0) & (dist < WINDOW) & (tk[None, :] >= 0)
        s = jnp.einsum('bqghd,bkgd->bghqk', qq, kk).astype(jnp.float32) * scale
        s = jnp.where(valid, s - slopes[None, :, :, None, None] * dist, NEG_INF)
        pr = jax.nn.softmax(s, axis=-1)
        return jnp.einsum('bghqk,bkgd->bqghd', pr.astype(vv.dtype), vv)

    o = lax.map(win_block, (jnp.arange(n_blk), q_blk))
    return o.transpose(1, 0, 2, 3, 4, 5).reshape(bsz, seq, g, hg, d)


def _nsa_mixer(h, w_in, cmp_pe, cmp_w1, cmp_w2, w_out):
    bsz, seq, _ = h.shape
    f32 = jnp.float32
    scale = NSA_HEAD_DIM ** -0.5
    q, kc, vc, ks, vs, kw, vw, gl, z = _split(h @ w_in, [NSA_INNER] + [NSA_KV] * 6 + [3 * NSA_HEADS, NSA_INNER])
    q = q.reshape(bsz, seq, NSA_KV_GROUPS, NSA_Q_PER_GROUP, NSA_HEAD_DIM)
    kc, vc, ks, vs, kw, vw = [t.reshape(bsz, seq, NSA_KV_GROUPS, NSA_HEAD_DIM) for t in (kc, vc, ks, vs, kw, vw)]
    slopes = _alibi_slopes().reshape(NSA_KV_GROUPS, NSA_Q_PER_GROUP)
    pos = jnp.arange(seq, dtype=f32)

    k_cmp = _compress(kc, cmp_pe[0], cmp_w1[0], cmp_w2[0])
    v_cmp = _compress(vc, cmp_pe[1], cmp_w1[1], cmp_w2[1])
    n_cmp = k_cmp.shape[1]
    cmp_end = jnp.arange(n_cmp, dtype=f32) * CMP_STRIDE + (CMP_LEN - 1)
    dist = pos[:, None] - cmp_end[None, :]
    valid = dist >= 0
    s = jnp.einsum('bsgjd,bngd->bgjsn', q, k_cmp).astype(f32) * scale - slopes[None, :, :, None, None] * dist
    s = jnp.where(valid, s, NEG_INF)
    p_cmp = jax.nn.softmax(s, axis=-1) * jnp.any(valid, axis=-1)[:, None].astype(f32)
    o_cmp = jnp.einsum('bgjsn,bngd->bsgjd', p_cmp.astype(v_cmp.dtype), v_cmp)

    o_slc = _nsa_select(q, ks, vs, p_cmp, slopes, pos, scale)
    o_win = _nsa_window(q, kw, vw, slopes, scale)

    g = jax.nn.sigmoid(gl.astype(f32)).reshape(bsz, seq, 3, NSA_KV_GROUPS, NSA_Q_PER_GROUP, 1)
    o = g[:, :, 0] * o_cmp + g[:, :, 1] * o_slc + g[:, :, 2] * o_win
    y = o.reshape(bsz, seq, NSA_INNER).astype(h.dtype) * jax.nn.silu(z)
    return y @ w_out


def _lin_rec_combine(left, right):
    a_l, b_l = left
    a_r, b_r = right
    return (a_l * a_r, a_r * b_l + b_r)


def _rglru_mixer(h, w_in, conv_w, conv_b, gate_w, gate_b, lam, w_out):
    bsz, seq, _ = h.shape
    f32 = jnp.float32
    xb, z = _split(h @ w_in, [RG_WIDTH, RG_WIDTH])
    xb = lax.conv_general_dilated(xb, conv_w[:, None, :].astype(xb.dtype), window_strides=(1,), padding=[(RG_CONV - 1, 0)], dimension_numbers=('NWC', 'WIO', 'NWC'), feature_group_count=RG_WIDTH) + conv_b.astype(xb.dtype)
    xg = xb.reshape(bsz, seq, RG_BLOCKS, RG_BLOCK)
    gates = jax.nn.sigmoid((jnp.einsum('bsnc,knce->kbsne', xg, gate_w) + gate_b[:, None, None]).astype(f32)).reshape(2, bsz, seq, RG_WIDTH)
    i_gate, r_gate = gates[0], gates[1]
    log_a = -RG_C * r_gate * jax.nn.softplus(-lam.astype(f32))
    a = jnp.exp(log_a)
    mult = jnp.sqrt(-jnp.expm1(2.0 * log_a))
    mult = jnp.where((jnp.arange(seq) == 0)[None, :, None], 1.0, mult)
    u = mult * i_gate * xb.astype(f32)
    _, hs = lax.associative_scan(_lin_rec_combine, (a, u), axis=1)
    y = hs.astype(h.dtype) * jax.nn.silu(z)
    return y @ w_out


def _hgrn2_chunk_scan(q, k, v, log_f):
    bsz, nh, seq, dk = q.shape
    dv = v.shape[-1]
    n_c = seq // HG_CHUNK

    def to_chunks(t):
        return t.reshape(bsz, nh, n_c, HG_CHUNK, t.shape[-1]).transpose(2, 0, 1, 3, 4)

    causal = jnp.tril(jnp.ones((HG_CHUNK, HG_CHUNK), dtype=bool))[:, :, None]

    def step(state, inp):
        qc, kc, vc, gc = inp
        b = jnp.cumsum(gc, axis=2)
        o_inter = jnp.einsum('bhtd,bhdv->bhtv', qc * jnp.exp(b), state)
        rel = jnp.where(causal, b[:, :, :, None, :] - b[:, :, None, :, :], -jnp.inf)
        att = jnp.einsum('bhtd,bhsd,bhtsd->bhts', qc, kc, jnp.exp(rel))
        o = o_inter + jnp.einsum('bhts,bhsv->bhtv', att, vc)
        b_last = b[:, :, -1:, :]
        new_state = jnp.exp(b_last[:, :, 0, :])[..., None] * state + jnp.einsum('bhsd,bhsv->bhdv', kc * jnp.exp(b_last - b), vc)
        return new_state, o

    init = jnp.zeros((bsz, nh, dk, dv), jnp.float32)
    _, o = lax.scan(step, init, (to_chunks(q), to_chunks(k), to_chunks(v), to_chunks(log_f)))
    return o.transpose(1, 0, 3, 2, 4).reshape(bsz, seq, nh, dv)


def _hgrn2_mixer(h, w_in, lb, norm_gain, w_out):
    bsz, seq, _ = h.shape
    f32 = jnp.float32
    q, f, v, g = _split(h @ w_in, [HG_KEY, HG_KEY, HG_VAL, HG_VAL])
    q = jax.nn.silu(q.astype(f32))
    f = lb + (1.0 - lb) * jax.nn.sigmoid(f.astype(f32))
    log_f = jnp.log(f)
    k = 1.0 - f
    heads_k = lambda t: t.reshape(bsz, seq, HG_HEADS, HG_KEY_DIM).transpose(0, 2, 1, 3)
    vh = v.astype(f32).reshape(bsz, seq, HG_HEADS, HG_VAL_DIM).transpose(0, 2, 1, 3)
    o = _hgrn2_chunk_scan(heads_k(q), heads_k(k), vh, heads_k(log_f))
    o = _rmsnorm(o, norm_gain) * jax.nn.silu(g.astype(f32).reshape(bsz, seq, HG_HEADS, HG_VAL_DIM))
    return o.reshape(bsz, seq, HG_VAL).astype(h.dtype) @ w_out


def setup_inputs(seed: int = 0) -> dict:
    key = jax.random.key(seed)
    ks = jax.random.split(key, 20)
    f32 = jnp.float32

    def nrm(k, shape, scale):
        return jax.random.normal(k, shape, f32) * scale

    x = nrm(ks[0], (BATCH, SEQ, D_MODEL), 1.0)
    pre_norm_gain = 1.0 + nrm(ks[1], (DEPTH, D_MODEL), 0.02)
    post_norm_gain = 1.0 + nrm(ks[2], (DEPTH, D_MODEL), 0.02)
    nsa_w_in = nrm(ks[3], (N_NSA, D_MODEL, NSA_IN), D_MODEL ** -0.5)
    nsa_cmp_pe = nrm(ks[4], (N_NSA, 2, CMP_LEN, NSA_HEAD_DIM), 0.1)
    nsa_cmp_w1 = nrm(ks[5], (N_NSA, 2, CMP_LEN * NSA_HEAD_DIM, NSA_HEAD_DIM), (CMP_LEN * NSA_HEAD_DIM) ** -0.5)
    nsa_cmp_w2 = nrm(ks[6], (N_NSA, 2, NSA_HEAD_DIM, NSA_HEAD_DIM), NSA_HEAD_DIM ** -0.5)
    nsa_w_out = nrm(ks[7], (N_NSA, NSA_INNER, D_MODEL), NSA_INNER ** -0.5)
    rg_w_in = nrm(ks[8], (N_RG, D_MODEL, 2 * RG_WIDTH), D_MODEL ** -0.5)
    rg_conv_w = nrm(ks[9], (N_RG, RG_CONV, RG_WIDTH), RG_CONV ** -0.5)
    rg_conv_b = nrm(ks[10], (N_RG, RG_WIDTH), 0.01)
    rg_gate_w = nrm(ks[11], (N_RG, 2, RG_BLOCKS, RG_BLOCK, RG_BLOCK), RG_BLOCK ** -0.5)
    rg_gate_b = nrm(ks[12], (N_RG, 2, RG_BLOCKS, RG_BLOCK), 0.01)
    u = jax.random.uniform(ks[13], (N_RG, RG_WIDTH), f32, 0.9, 0.999)
    log_a = jnp.log(u) / RG_C
    rg_lambda = log_a - jnp.log(-jnp.expm1(log_a))
    rg_w_out = nrm(ks[14], (N_RG, RG_WIDTH, D_MODEL), RG_WIDTH ** -0.5)
    hg_w_in = nrm(ks[15], (N_HG, D_MODEL, HG_IN), D_MODEL ** -0.5)
    hg_lb_logits = nrm(ks[16], (DEPTH, HG_KEY), 0.1)
    hg_norm_gain = 1.0 + nrm(ks[17], (N_HG, HG_VAL_DIM), 0.02)
    hg_w_out = nrm(ks[18], (N_HG, HG_VAL, D_MODEL), HG_VAL ** -0.5)
    return {'x': x, 'pre_norm_gain': pre_norm_gain, 'post_norm_gain': post_norm_gain,
            'nsa_w_in': nsa_w_in, 'nsa_cmp_pe': nsa_cmp_pe, 'nsa_cmp_w1': nsa_cmp_w1, 'nsa_cmp_w2': nsa_cmp_w2, 'nsa_w_out': nsa_w_out,
            'rg_w_in': rg_w_in, 'rg_conv_w': rg_conv_w, 'rg_conv_b': rg_conv_b, 'rg_gate_w': rg_gate_w, 'rg_gate_b': rg_gate_b,
            'rg_lambda': rg_lambda, 'rg_w_out': rg_w_out,
            'hg_w_in': hg_w_in, 'hg_lb_logits': hg_lb_logits, 'hg_norm_gain': hg_norm_gain, 'hg_w_out': hg_w_out}


def reference(x, pre_norm_gain, post_norm_gain, nsa_w_in, nsa_cmp_pe, nsa_cmp_w1, nsa_cmp_w2, nsa_w_out,
              rg_w_in, rg_conv_w, rg_conv_b, rg_gate_w, rg_gate_b, rg_lambda, rg_w_out,
              hg_w_in, hg_lb_logits, hg_norm_gain, hg_w_out):
    lb_all = jax.nn.softmax(hg_lb_logits.astype(jnp.float32), axis=0)
    lb_all = jnp.cumsum(lb_all, axis=0) - lb_all[0]
    for i in range(DEPTH):
        h = _rmsnorm(x, pre_norm_gain[i])
        kind, j = i % N_MIXERS, i // N_MIXERS
        if kind == 0:
            y = _nsa_mixer(h, nsa_w_in[j], nsa_cmp_pe[j], nsa_cmp_w1[j], nsa_cmp_w2[j], nsa_w_out[j])
        elif kind == 1:
            y = _rglru_mixer(h, rg_w_in[j], rg_conv_w[j], rg_conv_b[j], rg_gate_w[j], rg_gate_b[j], rg_lambda[j], rg_w_out[j])
        else:
            y = _hgrn2_mixer(h, hg_w_in[j], lb_all[i], hg_norm_gain[j], hg_w_out[j])
        x = x + _rmsnorm(y, post_norm_gain[i])
    return x
```

```python
import contextlib
import numpy as np
import ml_dtypes
import concourse.bass as bass
import concourse.mybir as mybir
from concourse.bass_utils import run_bass_kernel_spmd

F32 = mybir.dt.float32
BF16 = mybir.dt.bfloat16
AF = mybir.ActivationFunctionType
ALU = mybir.AluOpType
AX = mybir.AxisListType

D = 4096
B = 4
S = 2048
EPS = 1e-6
NCORES = 8


class Prog:
    ENGS = ("pe", "act", "dve", "pool", "sp")
    NDS = 24

    def __init__(self, nc, es, self_sync=True):
        self.nc = nc
        self.es = es
        self.self_sync = self_sync
        self.eng = dict(pe=nc.tensor, act=nc.scalar, dve=nc.vector, pool=nc.gpsimd, sp=nc.sync)
        self.sem = {e: es.enter_context(nc.semaphore("s_" + e)) for e in self.ENGS}
        self.cnt = {e: 0 for e in self.ENGS}
        self.seen = {e: {} for e in self.ENGS}
        self.dsem = [es.enter_context(nc.semaphore("d%d" % i)) for i in range(self.NDS)]
        self.dcnt = [0] * self.NDS
        self.drr = 0
        self.state = {}
        self.ninst = 0
        self.pre = ""
        self.es_cur = None
        self.ccsem = es.enter_context(nc.semaphore("ccsem"))
        self.cccnt = 0

    def sb(self, name, shape, dt):
        es = self.es_cur if self.es_cur is not None else self.es
        return es.enter_context(self.nc.sbuf_tensor(self.pre + name, list(shape), dt))

    def ps(self, name, shape, dt=F32):
        return self.es.enter_context(self.nc.psum_tensor(name, list(shape), dt))

    def _deps(self, reads, writes):
        deps = []
        for k in reads:
            st = self.state.get(k)
            if st is not None and st[0] is not None:
                deps.append(st[0])
        for k in writes:
            st = self.state.get(k)
            if st is not None:
                if st[0] is not None:
                    deps.append(st[0])
                deps.extend(st[1].values())
        return deps

    def _wait(self, e, deps):
        best = {}
        for (sid, sem, val) in deps:
            if sid not in best or best[sid][1] < val:
                best[sid] = (sem, val)
        for sid, (sem, val) in best.items():
            if self.seen[e].get(sid, 0) >= val:
                continue
            if sid == e and (e == "pe" or not self.self_sync):
                continue
            self.eng[e].wait_ge(sem, val)
            self.ninst += 1
            self.seen[e][sid] = val

    def _record(self, tok, reads, writes):
        for k in reads:
            st = self.state.get(k)
            if st is None:
                st = [None, {}]
                self.state[k] = st
            st[1][tok[0]] = tok
        for k in writes:
            self.state[k] = [tok, {}]

    def op(self, e, fn, reads=(), writes=(), sig=True):
        self._wait(e, self._deps(reads, writes))
        ins = fn(self.eng[e])
        self.ninst += 1
        if sig:
            self.cnt[e] += 1
            ins.then_inc(self.sem[e], 1)
            self._record((e, self.sem[e], self.cnt[e]), reads, writes)
        else:
            self._record((e, self.sem[e], self.cnt[e] + 1), reads, writes)

    def dma(self, q, out, in_, reads=(), writes=()):
        i = self.drr
        self.drr = (i + 1) % self.NDS
        sem = self.dsem[i]
        sid = "d%d" % i
        deps = self._deps(reads, writes)
        if self.dcnt[i] > 0:
            deps.append((sid, sem, self.dcnt[i]))
        self._wait(q, deps)
        self.eng[q].dma_start(out=out, in_=in_).then_inc(sem, 16)
        self.ninst += 1
        self.dcnt[i] += 16
        self._record((sid, sem, self.dcnt[i]), reads, writes)

    def cc(self, kind, groups, in_, out, reads=(), writes=()):
        deps = self._deps(reads, writes)
        self._wait("pool", deps)
        self.eng["pool"].collective_compute(kind, ALU.bypass, replica_groups=groups, ins=[in_.opt()], outs=[out.opt()]).then_inc(self.ccsem)
        self.ninst += 1
        self.cccnt += 1
        self._record(("cc", self.ccsem, self.cccnt), reads, writes)

    def finish(self):
        for i in range(self.NDS):
            if self.dcnt[i] > 0:
                self.eng["sp"].wait_ge(self.dsem[i], self.dcnt[i])
        for e in self.ENGS:
            if e != "sp" and self.cnt[e] > 0:
                self.eng["sp"].wait_ge(self.sem[e], self.cnt[e])
        if self.cccnt > 0:
            self.eng["sp"].wait_ge(self.ccsem, self.cccnt)


class PsumRing:
    def __init__(self, P, n=8):
        self.P = P
        self.t = [P.ps("psr%d" % i, [128, 512]) for i in range(n)]
        self.i = 0

    def next(self):
        i = self.i % len(self.t)
        self.i += 1
        return self.t[i], ("psr", i)


def barrier(P):
    toks = []
    for e in P.ENGS:
        if P.cnt[e] > 0:
            toks.append((e, P.sem[e], P.cnt[e]))
    for i in range(P.NDS):
        if P.dcnt[i] > 0:
            toks.append(("d%d" % i, P.dsem[i], P.dcnt[i]))
    if P.cccnt > 0:
        toks.append(("cc", P.ccsem, P.cccnt))
    for e in P.ENGS:
        P._wait(e, [t for t in toks if t[0] != e])


def phase_hT(P, nc, PR, x, gT_sb, ident, hT, ntok=S, gcol=0, xrow=lambda tt: tt * 128):
    KC = D // 128
    with contextlib.ExitStack() as es2:
        xs = [es2.enter_context(nc.sbuf_tensor(P.pre + "hx%d" % i, [128, D], F32)) for i in range(2)]
        junk = es2.enter_context(nc.sbuf_tensor(P.pre + "hjunk", [128, D], BF16))
        st = es2.enter_context(nc.sbuf_tensor(P.pre + "hst", [128, 8], F32))
        for tt in range(ntok // 128):
            xb = xs[tt % 2]
            xk = ("hx", tt % 2)
            P.dma("sp", xb[:], x[xrow(tt):xrow(tt) + 128, :], reads=["x_g"], writes=[xk])
            P.op("act", lambda e: e.activation(out=junk[:], in_=xb[:], func=AF.Square, accum_out=st[:, 0:1]),
                 reads=[xk], writes=["hjunk", "hst0"])
            P.op("dve", lambda e: e.tensor_scalar(out=st[:, 1:2], in0=st[:, 0:1], scalar1=1.0 / D, scalar2=EPS,
                                                   op0=ALU.mult, op1=ALU.add), reads=["hst0"], writes=["hst1"])
            P.op("act", lambda e: e.activation(out=st[:, 2:3], in_=st[:, 1:2], func=AF.Sqrt),
                 reads=["hst1"], writes=["hst2"])
            P.op("dve", lambda e: e.reciprocal(out=st[:, 3:4], in_=st[:, 2:3]), reads=["hst2"], writes=["hst3"])
            P.op("dve", lambda e: e.tensor_scalar(out=xb[:], in0=xb[:], scalar1=st[:, 3:4], scalar2=None,
                                                   op0=ALU.mult), reads=[xk, "hst3"], writes=[xk])
            for c0 in range(0, KC, 4):
                ps, pk = PR.next()
                for k in range(4):
                    c = c0 + k
                    P.op("pe", lambda e, c=c, k=k, ps=ps: e.transpose(
                        out=ps[:, k * 128:(k + 1) * 128], in_=xb[:, c * 128:(c + 1) * 128], identity=ident[:]),
                        reads=[xk, "ident"], writes=[pk])
                for k in range(4):
                    c = c0 + k
                    P.op("act", lambda e, c=c, k=k, ps=ps, tt=tt: e.activation(
                        out=hT[:, c, tt * 128:(tt + 1) * 128], in_=ps[:, k * 128:(k + 1) * 128],
                        func=AF.Copy, scale=gT_sb[:, gcol + c:gcol + c + 1]),
                        reads=[pk, "gT"], writes=[("hT", tt // 4)])
        barrier(P)


def proj_fm(P, PR, hT, w_chunk, wkey, tb, evac):
    KC = D // 128
    ps, pk = PR.next()
    for c in range(KC):
        P.op("pe", lambda e, c=c, ps=ps: e.matmul(ps[:], w_chunk[:, c, :], hT[:, c, tb * 512:(tb + 1) * 512],
                                                  start=(c == 0), stop=(c == KC - 1)),
             reads=[wkey, ("hT", tb)], writes=[pk], sig=(c == KC - 1))
    evac(ps, pk)


def emit_outproj(P, nc, PR, yT, x, w, gain, outs, hm, NT=1024):
    KC = D // 128
    NB = 256
    TH = 512
    if True:
        yT_alt = P.sb("yT_alt", [128, KC, TH], BF16)
        yT_sb = P.sb("yT_sb", [128, KC, TH], BF16)
        w_sb = [P.sb("w_sb%d" % i, [128, KC, NB], BF16) for i in range(2)]
        o_sb = [P.sb("o_sb%d" % i, [128, D], F32) for i in range(TH // 128)]
        x_sb = P.sb("x_sb", [128, D], F32)
        g_sb = P.sb("g_sb", [128, D], F32)
        junk = P.sb("junk", [128, D], BF16)
        ss = P.sb("ss", [128, 8], F32)
        yT_v = yT.rearrange("(c p) t -> p c t", p=128)
        w_v = w.rearrange("(c p) n -> p c n", p=128)
        P.dma("sp", g_sb[:], gain, writes=["g"])
        wi = 0
        pi = 0
        for half in range(NT // TH):
            for r_ in range(2):
                for k_ in range(4):
                    c0_, s0_ = r_ * 16 + k_ * 4, k_ * 8 + r_ * 4
                    P.dma("sp", yT_sb[:, c0_:c0_ + 4, :], yT_v[:, s0_:s0_ + 4, half * TH:(half + 1) * TH], reads=["yT_g"], writes=["yT"])
                    P.dma("sp", yT_alt[:, c0_:c0_ + 4, :], yT_v[:, s0_:s0_ + 4, 1024 + half * TH:1024 + (half + 1) * TH],
                          reads=["yT_g"], writes=["yT_alt"])
            P.op("dve", lambda e: e.tensor_scalar(out=yT_sb[:], in0=yT_sb[:], scalar1=hm[:, 0:1], scalar2=None, op0=ALU.mult),
                 reads=["yT", "hm"], writes=["yT"])
            P.op("dve", lambda e: e.scalar_tensor_tensor(out=yT_sb[:], in0=yT_alt[:], scalar=hm[:, 1:2], in1=yT_sb[:],
                                                          op0=ALU.mult, op1=ALU.add), reads=["yT", "yT_alt", "hm"], writes=["yT"])
            for nb in range(D // NB):
                wb = wi % 2
                wi += 1
                for c0 in range(0, KC, 8):
                    P.dma("pool", w_sb[wb][:, c0:c0 + 8, :], w_v[:, c0:c0 + 8, nb * NB:(nb + 1) * NB],
                          writes=[("w", wb, c0)])
                for tt in range(TH // 128):
                    ps, pk = PR.next()
                    for c in range(KC):
                        P.op("pe", lambda e, c=c, tt=tt, ps=ps, wb=wb: e.matmul(
                            ps[:, 0:NB], yT_sb[:, c, tt * 128:(tt + 1) * 128], w_sb[wb][:, c, :],
                            start=(c == 0), stop=(c == KC - 1)),
                            reads=["yT", ("w", wb, (c // 8) * 8)], writes=[pk], sig=(c == KC - 1))
                    P.op("act", lambda e, tt=tt, ps=ps, nb=nb: e.copy(
                        out=o_sb[tt][:, nb * NB:(nb + 1) * NB], in_=ps[:, 0:NB]),
                        reads=[pk], writes=[("o", tt)])
            for tt in range(TH // 128):
                t0 = half * TH + tt * 128
                P.dma("sp", x_sb[:], x[t0:t0 + 128, :], reads=["x_my"], writes=["x"])
                P.op("act", lambda e, tt=tt: e.activation(out=junk[:], in_=o_sb[tt][:], func=AF.Square,
                                                         accum_out=ss[:, 0:1]),
                     reads=[("o", tt)], writes=["junk", "ss"])
                P.op("dve", lambda e: e.tensor_scalar(out=ss[:, 1:2], in0=ss[:, 0:1], scalar1=1.0 / D, scalar2=EPS,
                                                       op0=ALU.mult, op1=ALU.add), reads=["ss"], writes=["ss1"])
                P.op("act", lambda e: e.activation(out=ss[:, 2:3], in_=ss[:, 1:2], func=AF.Sqrt),
                     reads=["ss1"], writes=["ss2"])
                P.op("dve", lambda e: e.reciprocal(out=ss[:, 3:4], in_=ss[:, 2:3]), reads=["ss2"], writes=["ss3"])
                P.op("dve", lambda e, tt=tt: e.scalar_tensor_tensor(
                    out=o_sb[tt][:], in0=o_sb[tt][:], scalar=ss[:, 3:4], in1=g_sb[:], op0=ALU.mult, op1=ALU.mult),
                    reads=[("o", tt), "ss3", "g"], writes=[("o", tt)])
                P.op("dve", lambda e, tt=tt: e.tensor_tensor(out=o_sb[tt][:], in0=o_sb[tt][:], in1=x_sb[:], op=ALU.add),
                     reads=[("o", tt), "x"], writes=[("o", tt)])
                for oi, o_ in enumerate(outs):
                    P.dma("sp", o_[t0:t0 + 128, :], o_sb[tt][:], reads=[("o", tt)], writes=["x_out%d" % oi])


def emit_rg(P, nc, PR, gT_sb, gcol, ident, io):
    x = io["x"]
    gT = io["gT"]
    ident_d = io["ident"]
    wx = io["wx"]
    wz = io["wz"]
    cw_d = io["cw"]
    cb_d = io["cb"]
    lam_d = io["lam"]
    gb_d = io["gb"]
    gw_d = io["gw"]
    yT = io["yT"]
    KC = D // 128
    if True:
        hT = P.sb("hT", [128, KC, S], BF16)
        cw = P.sb("cw_sb", [128, 64], F32)
        cb = P.sb("cb_sb", [128, 16], F32)
        lam = P.sb("lam_sb", [128, 16], F32)
        c1 = P.sb("c1_sb", [128, 16], F32)
        gb = P.sb("gb_sb", [128, 32], F32)
        P.dma("sp", cw[:], cw_d, writes=["cw"])
        P.dma("sp", cb[:], cb_d, writes=["cb"])
        P.dma("sp", lam[:], lam_d, writes=["lam"])
        P.dma("sp", gb[:], gb_d, writes=["gb"])
        phase_hT(P, nc, PR, x, gT_sb, ident, hT, gcol=gcol, xrow=io["xrow"])
        P.op("act", lambda e: e.activation(out=c1[:], in_=lam[:], func=AF.Exp, scale=-1.0), reads=["lam"], writes=["c1"])
        P.op("act", lambda e: e.activation(out=c1[:], in_=c1[:], func=AF.Ln, bias=1.0), reads=["c1"], writes=["c1"])
        P.op("dve", lambda e: e.tensor_scalar(out=c1[:], in0=c1[:], scalar1=-8.0, scalar2=None, op0=ALU.mult),
             reads=["c1"], writes=["c1"])
        wxs = [P.sb("wxs%d" % i, [128, KC, 128], BF16) for i in range(2)]
        wzs = [P.sb("wzs%d" % i, [128, KC, 128], BF16) for i in range(2)]
        gws = P.sb("gws", [128, 2, 2, 256], BF16)
        xraw = [P.sb("xraw%d" % i, [128, 3 + 512], F32) for i in range(2)]
        xc = [P.sb("xc%d" % i, [128, 512], F32) for i in range(2)]
        xcb = [P.sb("xcb%d" % i, [128, 512], BF16) for i in range(2)]
        gi = [P.sb("gi%d" % i, [128, 512], F32) for i in range(2)]
        ga = [P.sb("ga%d" % i, [128, 512], F32) for i in range(2)]
        gm = [P.sb("gm%d" % i, [128, 512], F32) for i in range(2)]
        hs = [[P.sb("hs%d_%d" % (i, k), [128, 512], F32) for k in range(2)] for i in range(2)]
        zs = [P.sb("zs%d" % i, [128, 512], F32) for i in range(2)]
        ys = [P.sb("ys%d" % i, [128, 512], BF16) for i in range(2)]
        wx_v = wx.rearrange("(c p) n -> p c n", p=128)
        wz_v = wz.rearrange("(c p) n -> p c n", p=128)
        for blk in range(8):
            for j in range(2):
                col0 = blk * 256 + j * 128
                for c0 in range(0, KC, 8):
                    P.dma("pool", wxs[j][:, c0:c0 + 8, :], wx_v[:, c0:c0 + 8, col0:col0 + 128], writes=[("wx", j)])
                for c0 in range(0, KC, 8):
                    P.dma("pool", wzs[j][:, c0:c0 + 8, :], wz_v[:, c0:c0 + 8, col0:col0 + 128], writes=[("wz", j)])
            for k in range(2):
                P.dma("pool", gws[:, k], gw_d[k, blk].rearrange("(jc p) e -> p jc e", p=128), writes=["gw"])
            for j in range(2):
                P.op("dve", lambda e, j=j: e.memset(xraw[j][:, 0:3], 0.0), writes=[("xraw", j)])
            for tb in range(4):
                for j in range(2):
                    ch = blk * 2 + j

                    def ev_x(ps, pk, j=j):
                        P.op("act", lambda e: e.copy(out=xraw[j][:, 3:515], in_=ps[:]), reads=[pk], writes=[("xraw", j)])
                    proj_fm(P, PR, hT, wxs[j], ("wx", j), tb, ev_x)
                    P.op("dve", lambda e, j=j, ch=ch: e.tensor_scalar(
                        out=xc[j][:], in0=xraw[j][:, 0:512], scalar1=cw[:, ch * 4:ch * 4 + 1], scalar2=cb[:, ch:ch + 1],
                        op0=ALU.mult, op1=ALU.add), reads=[("xraw", j), "cw", "cb"], writes=[("xc", j)])
                    for k in range(1, 4):
                        P.op("dve", lambda e, j=j, ch=ch, k=k: e.scalar_tensor_tensor(
                            out=xc[j][:], in0=xraw[j][:, k:k + 512], scalar=cw[:, ch * 4 + k:ch * 4 + k + 1],
                            in1=xc[j][:], op0=ALU.mult, op1=ALU.add), reads=[("xraw", j), ("xc", j), "cw"],
                            writes=[("xc", j)])
                    P.op("act", lambda e, j=j: e.copy(out=xcb[j][:], in_=xc[j][:]), reads=[("xc", j)], writes=[("xcb", j)])
                    P.op("dve", lambda e, j=j: e.tensor_copy(out=xraw[j][:, 0:3], in_=xraw[j][:, 512:515]),
                         reads=[("xraw", j)], writes=[("xraw", j)])
                for je in range(2):
                    ch = blk * 2 + je
                    for k in range(2):
                        ps, pk = PR.next()
                        for jc in range(2):
                            P.op("pe", lambda e, k=k, jc=jc, je=je, ps=ps: e.matmul(
                                ps[:], gws[:, k, jc, je * 128:(je + 1) * 128], xcb[jc][:], start=(jc == 0), stop=(jc == 1)),
                                reads=["gw", ("xcb", jc)], writes=[pk])
                        dst = gi[je] if k == 0 else ga[je]
                        dk = ("gi", je) if k == 0 else ("ga", je)
                        P.op("act", lambda e, ps=ps, dst=dst, k=k, ch=ch: e.activation(
                            out=dst[:], in_=ps[:], func=AF.Sigmoid, bias=gb[:, k * 16 + ch:k * 16 + ch + 1]),
                            reads=[pk, "gb"], writes=[dk])
                    P.op("act", lambda e, je=je, ch=ch: e.activation(out=ga[je][:], in_=ga[je][:], func=AF.Exp,
                                                                   scale=c1[:, ch:ch + 1]),
                         reads=[("ga", je), "c1"], writes=[("ga", je)])
                    P.op("dve", lambda e, je=je: e.tensor_tensor(out=gm[je][:], in0=ga[je][:], in1=ga[je][:], op=ALU.mult),
                         reads=[("ga", je)], writes=[("gm", je)])
                    P.op("act", lambda e, je=je: e.activation(out=gm[je][:], in_=gm[je][:], func=AF.Sqrt, scale=-1.0, bias=1.0),
                         reads=[("gm", je)], writes=[("gm", je)])
                    if tb == 0:
                        P.op("dve", lambda e, je=je: e.memset(gm[je][:, 0:1], 1.0), writes=[("gm", je)])
                    P.op("dve", lambda e, je=je: e.tensor_tensor(out=gi[je][:], in0=gi[je][:], in1=gm[je][:], op=ALU.mult),
                         reads=[("gi", je), ("gm", je)], writes=[("gi", je)])
                    P.op("dve", lambda e, je=je: e.tensor_tensor(out=gi[je][:], in0=gi[je][:], in1=xc[je][:], op=ALU.mult),
                         reads=[("gi", je), ("xc", je)], writes=[("gi", je)])
                    cur = hs[je][tb % 2]
                    prev = hs[je][(tb + 1) % 2]
                    init = 0.0 if tb == 0 else prev[:, 511:512]
                    P.op("dve", lambda e, je=je, cur=cur, init=init: e.tensor_tensor_scan(
                        out=cur[:], data0=ga[je][:], data1=gi[je][:], initial=init, op0=ALU.mult, op1=ALU.add),
                        reads=[("ga", je), ("gi", je), ("hs", je, (tb + 1) % 2)], writes=[("hs", je, tb % 2)])

                    def ev_z(ps, pk, je=je):
                        P.op("act", lambda e: e.activation(out=zs[je][:], in_=ps[:], func=AF.Silu), reads=[pk],
                             writes=[("zs", je)])
                    proj_fm(P, PR, hT, wzs[je], ("wz", je), tb, ev_z)
                    P.op("dve", lambda e, je=je, cur=cur: e.tensor_tensor(out=ys[je][:], in0=cur[:], in1=zs[je][:], op=ALU.mult),
                         reads=[("hs", je, tb % 2), ("zs", je)], writes=[("ys", je)])
                    P.dma("sp", yT[ch * 128:(ch + 1) * 128, tb * 512:(tb + 1) * 512], ys[je][:], reads=[("ys", je)], writes=["ysrc"])


def emit_hg(P, nc, PR, gT_sb, gcol, ident, io, layer=2):
    x = io["x"]
    gT = io["gT"]
    ident_d = io["ident"]
    ones_d = io["ones"]
    rmask_d = io["rmask"]
    bdmask_d = io["bdmask"]
    lbl_d = io["lbl"]
    ng_d = io["ng"]
    yT = io["yT"]
    wd = [io[n] for n in ("wq", "wf", "wv", "wg")]
    KC = D // 128
    if True:
        hT = P.sb("hT", [128, KC, S], BF16)
        ones = P.sb("ones_sb", [128, 128], F32)
        rmask = P.sb("rmask_sb", [128, 512], F32)
        bdmask = P.sb("bdmask_sb", [128, 128], F32)
        lbl = P.sb("lbl_sb", [128, 64], F32)
        lbe = P.sb("lbe_sb", [128, 64], F32)
        lb = P.sb("lb_sb", [128, 16], F32)
        oml = P.sb("oml_sb", [128, 16], F32)
        lsum = P.sb("lsum_sb", [128, 16], F32)
        ng = P.sb("ng_sb", [128, 1], F32)
        for t, d, k in ((ones, ones_d, "ones"), (rmask, rmask_d, "rmask"),
                        (bdmask, bdmask_d, "bdmask"), (lbl, lbl_d, "lbl"), (ng, ng_d, "ng")):
            P.dma("sp", t[:], d, writes=[k])
        phase_hT(P, nc, PR, x, gT_sb, ident, hT, gcol=gcol, xrow=io["xrow"])
        P.op("act", lambda e: e.activation(out=lbe[:], in_=lbl[:], func=AF.Exp), reads=["lbl"], writes=["lbe"])
        P.op("dve", lambda e: e.tensor_tensor(out=lsum[:], in0=lbe[:, 0:16], in1=lbe[:, 16:32], op=ALU.add),
             reads=["lbe"], writes=["lsum"])
        P.op("dve", lambda e: e.tensor_tensor(out=lsum[:], in0=lsum[:], in1=lbe[:, 32:48], op=ALU.add),
             reads=["lbe", "lsum"], writes=["lsum"])
        P.op("dve", lambda e: e.tensor_tensor(out=lsum[:], in0=lsum[:], in1=lbe[:, 48:64], op=ALU.add),
             reads=["lbe", "lsum"], writes=["lsum"])
        P.op("dve", lambda e: e.reciprocal(out=lsum[:], in_=lsum[:]), reads=["lsum"], writes=["lsum"])
        P.op("dve", lambda e: e.memset(lb[:], 0.0), writes=["lb"])
        for l in range(1, layer + 1):
            P.op("dve", lambda e, l=l: e.tensor_tensor(out=lb[:], in0=lb[:], in1=lbe[:, l * 16:(l + 1) * 16], op=ALU.add),
                 reads=["lb", "lbe"], writes=["lb"])
        P.op("dve", lambda e: e.tensor_tensor(out=lb[:], in0=lb[:], in1=lsum[:], op=ALU.mult), reads=["lb", "lsum"], writes=["lb"])
        P.op("dve", lambda e: e.tensor_scalar(out=oml[:], in0=lb[:], scalar1=-1.0, scalar2=1.0, op0=ALU.mult, op1=ALU.add),
             reads=["lb"], writes=["oml"])
        ws = [P.sb("hw%d" % i, [128, KC, 128], BF16) for i in range(4)]
        wv = [w.rearrange("(c p) n -> p c n", p=128) for w in wd]
        f32b = lambda n: P.sb(n, [128, 512], F32)
        qs, ff, logf, kk, bb, eb, enb, kef, gs, osb, sq, rstd, tmpo = [f32b("hgb%d" % i) for i in range(13)]
        qe = P.sb("qe", [128, 512], BF16)
        kebf = P.sb("kebf", [128, 512], BF16)
        ys = P.sb("hys", [128, 512], BF16)
        vtm = [P.sb("vtm%d" % i, [128, 128], BF16) for i in range(4)]
        ketm = [P.sb("ketm%d" % i, [128, 128], BF16) for i in range(2)]
        attT = [P.sb("attT%d" % i, [128, 128], BF16) for i in range(2)]
        Sf = P.sb("Sf", [128, 128], F32)
        Stmp = P.sb("Stmp", [128, 128], F32)
        Sb = P.sb("Sb", [128, 128], BF16)
        for hh in range(16):
            for i in range(4):
                for c0 in range(0, KC, 8):
                    P.dma("pool", ws[i][:, c0:c0 + 8, :], wv[i][:, c0:c0 + 8, hh * 128:(hh + 1) * 128], writes=[("hw", i)])
            P.op("dve", lambda e: e.memset(Sf[:], 0.0), writes=["Sf"])
            P.op("dve", lambda e: e.memset(Sb[:], 0.0), writes=["Sb"])
            for tb in range(4):
                def ev_q(ps, pk):
                    P.op("act", lambda e: e.activation(out=qs[:], in_=ps[:], func=AF.Silu), reads=[pk], writes=["qs"])
                proj_fm(P, PR, hT, ws[0], ("hw", 0), tb, ev_q)

                def ev_f(ps, pk):
                    P.op("act", lambda e: e.activation(out=ff[:], in_=ps[:], func=AF.Sigmoid), reads=[pk], writes=["ff"])
                proj_fm(P, PR, hT, ws[1], ("hw", 1), tb, ev_f)

                def ev_g(ps, pk):
                    P.op("act", lambda e: e.activation(out=gs[:], in_=ps[:], func=AF.Silu), reads=[pk], writes=["gs"])
                proj_fm(P, PR, hT, ws[3], ("hw", 3), tb, ev_g)
                for tt in range(4):
                    ps, pk = PR.next()
                    for c in range(KC):
                        P.op("pe", lambda e, c=c, ps=ps, tt=tt: e.matmul(
                            ps[:, 0:128], hT[:, c, tb * 512 + tt * 128:tb * 512 + (tt + 1) * 128], ws[2][:, c, :],
                            start=(c == 0), stop=(c == KC - 1)), reads=[("hw", 2), ("hT", tb)], writes=[pk], sig=(c == KC - 1))
                    P.op("act", lambda e, ps=ps, tt=tt: e.copy(out=vtm[tt][:], in_=ps[:, 0:128]), reads=[pk], writes=[("vtm", tt)])
                P.op("dve", lambda e: e.tensor_scalar(out=ff[:], in0=ff[:], scalar1=oml[:, hh:hh + 1], scalar2=lb[:, hh:hh + 1],
                                                       op0=ALU.mult, op1=ALU.add), reads=["ff", "oml", "lb"], writes=["ff"])
                P.op("act", lambda e: e.activation(out=logf[:], in_=ff[:], func=AF.Ln), reads=["ff"], writes=["logf"])
                P.op("dve", lambda e: e.tensor_scalar(out=kk[:], in0=ff[:], scalar1=-1.0, scalar2=1.0, op0=ALU.mult, op1=ALU.add),
                     reads=["ff"], writes=["kk"])
                P.op("dve", lambda e: e.tensor_tensor_scan(out=bb[:], data0=rmask[:], data1=logf[:], initial=0.0,
                                                            op0=ALU.mult, op1=ALU.add), reads=["rmask", "logf"], writes=["bb"])
                P.op("act", lambda e: e.activation(out=eb[:], in_=bb[:], func=AF.Exp), reads=["bb"], writes=["eb"])
                P.op("act", lambda e: e.activation(out=enb[:], in_=bb[:], func=AF.Exp, scale=-1.0), reads=["bb"], writes=["enb"])
                P.op("dve", lambda e: e.tensor_tensor(out=qe[:], in0=qs[:], in1=eb[:], op=ALU.mult), reads=["qs", "eb"], writes=["qe"])
                P.op("dve", lambda e: e.tensor_tensor(out=kef[:], in0=kk[:], in1=enb[:], op=ALU.mult), reads=["kk", "enb"], writes=["kef"])
                P.op("act", lambda e: e.copy(out=kebf[:], in_=kef[:]), reads=["kef"], writes=["kebf"])
                for tt in range(4):
                    sl = slice(tt * 128, (tt + 1) * 128)
                    kt = ketm[tt % 2]
                    at = attT[tt % 2]
                    ktk = ("ketm", tt % 2)
                    atk = ("attT", tt % 2)
                    ps, pk = PR.next()
                    P.op("pe", lambda e, ps=ps, sl=sl: e.transpose(out=ps[:, 0:128], in_=kef[:, sl], identity=ident[:]),
                         reads=["kef", "ident"], writes=[pk])
                    P.op("act", lambda e, ps=ps, kt=kt: e.copy(out=kt[:], in_=ps[:, 0:128]), reads=[pk], writes=[ktk])
                    ps2, pk2 = PR.next()
                    P.op("pe", lambda e, ps2=ps2, sl=sl: e.matmul(ps2[:, 0:128], kebf[:, sl], qe[:, sl], start=True, stop=True),
                         reads=["kebf", "qe"], writes=[pk2])
                    P.op("dve", lambda e, ps2=ps2, at=at: e.tensor_tensor(out=at[:], in0=ps2[:, 0:128], in1=bdmask[:], op=ALU.mult),
                         reads=[pk2, "bdmask"], writes=[atk])
                    pso, pko = PR.next()
                    P.op("pe", lambda e, pso=pso, at=at, tt=tt: e.matmul(pso[:, 0:128], vtm[tt][:], at[:], start=True, stop=False),
                         reads=[("vtm", tt), atk], writes=[pko])
                    for cc in range(2):
                        csl = slice(tt * 128 + cc * 64, tt * 128 + (cc + 1) * 64)
                        rows = slice(cc * 64, (cc + 1) * 64)
                        P.op("pe", lambda e, pso=pso, cc=cc, csl=csl: e.matmul(
                            pso[:, cc * 64:(cc + 1) * 64], Sb[:], qe[:, csl], start=False, stop=(cc == 1)),
                            reads=["Sb", "qe"], writes=[pko])
                        pss_, pks = PR.next()
                        P.op("pe", lambda e, pss_=pss_, kt=kt, rows=rows, tt=tt: e.matmul(
                            pss_[:, 0:128], kt[rows, :], vtm[tt][rows, :], start=True, stop=True),
                            reads=[ktk, ("vtm", tt)], writes=[pks])
                        ecol = tt * 128 + (cc + 1) * 64 - 1
                        P.op("dve", lambda e, ecol=ecol: e.tensor_scalar(out=Stmp[:], in0=Sf[:], scalar1=eb[:, ecol:ecol + 1],
                                                                        scalar2=None, op0=ALU.mult),
                             reads=["Sf", "eb"], writes=["Stmp"])
                        P.op("dve", lambda e, ecol=ecol, pss_=pss_: e.scalar_tensor_tensor(
                            out=Sf[:], in0=pss_[:, 0:128], scalar=eb[:, ecol:ecol + 1], in1=Stmp[:], op0=ALU.mult, op1=ALU.add),
                            reads=[pks, "eb", "Stmp"], writes=["Sf"])
                        P.op("act", lambda e: e.copy(out=Sb[:], in_=Sf[:]), reads=["Sf"], writes=["Sb"])
                    P.op("act", lambda e, pso=pso, sl=sl: e.copy(out=osb[:, sl], in_=pso[:, 0:128]), reads=[pko], writes=["osb"])
                P.op("act", lambda e: e.activation(out=sq[:], in_=osb[:], func=AF.Square), reads=["osb"], writes=["sq"])
                psn, pkn = PR.next()
                P.op("pe", lambda e, psn=psn: e.matmul(psn[:], ones[:], sq[:], start=True, stop=True), reads=["ones", "sq"], writes=[pkn])
                P.op("act", lambda e, psn=psn: e.activation(out=rstd[:], in_=psn[:], func=AF.Ln, scale=1.0 / 128, bias=EPS),
                     reads=[pkn], writes=["rstd"])
                P.op("act", lambda e: e.activation(out=rstd[:], in_=rstd[:], func=AF.Exp, scale=-0.5), reads=["rstd"], writes=["rstd"])
                P.op("dve", lambda e: e.scalar_tensor_tensor(out=tmpo[:], in0=osb[:], scalar=ng[:, 0:1], in1=rstd[:],
                                                              op0=ALU.mult, op1=ALU.mult), reads=["osb", "ng", "rstd"], writes=["tmpo"])
                P.op("dve", lambda e: e.tensor_tensor(out=ys[:], in0=tmpo[:], in1=gs[:], op=ALU.mult), reads=["tmpo", "gs"], writes=["ys"])
                P.dma("sp", yT[hh * 128:(hh + 1) * 128, tb * 512:(tb + 1) * 512], ys[:], reads=["ys"], writes=["ysrc"])


NSA_SCALE = 128 ** -0.5
NEGM = -1.0e7
NREL = 19


def emit_nsa(P, nc, PR, psUl, psDl, gT_sb, gcol, ident, io):
    x = io["x"]
    gT = io["gT"]
    ident_d = io["ident"]
    wq = io["wq"]
    wz = io["wz"]
    wkv = io["wkv"]
    wgl = io["wgl"]
    peT_d = io["peT"]
    w1_d = io["w1"]
    w2_d = io["w2"]
    slope_d = io["slope"]
    cst_d = io["cst"]
    R_d = io["Rt"]
    E_d = io["E"]
    selb0_d = io["selb0"]
    Mmat_d = io["Mmat"]
    A_d = io["At"]
    Badd_d = io["Bt"]
    SelR_d = io["SelR"]
    yT = io["yT"]
    qT_s = io["qT_s"]
    zT_s = io["zT_s"]
    kvT_s = io["kvT_s"]
    vtm_s = io["vtm_s"]
    tcmp_s = io["tcmp_s"]
    KC = D // 128
    if True:
        GT = P.sb("GT", [48, S], F32)
        with contextlib.ExitStack() as es1:
            hT = es1.enter_context(nc.sbuf_tensor(P.pre + "hT", [128, KC, S], BF16))
            wsb = [es1.enter_context(nc.sbuf_tensor(P.pre + "nw%d" % i, [128, KC, 128], BF16)) for i in range(2)]
            wglsb = es1.enter_context(nc.sbuf_tensor(P.pre + "wglsb", [128, KC, 48], BF16))
            stg = [es1.enter_context(nc.sbuf_tensor(P.pre + "stg%d" % i, [128, 512], BF16)) for i in range(2)]
            phase_hT(P, nc, PR, x, gT_sb, ident, hT, gcol=gcol, xrow=io["xrow"])
            si = [0]
            wi = [0]

            def load_w(src_v, col0):
                b = wi[0] % 2
                wi[0] += 1
                for c0 in range(0, KC, 8):
                    P.dma("pool", wsb[b][:, c0:c0 + 8, :], src_v[:, c0:c0 + 8, col0:col0 + 128], writes=[("nw", b)])
                return wsb[b], ("nw", b)

            def fm_chunk(src_v, col0, dst, func, scale):
                w, wk = load_w(src_v, col0)
                for tb in range(4):
                    def ev(ps, pk, tb=tb):
                        b = si[0] % 2
                        si[0] += 1
                        P.op("act", lambda e: e.activation(out=stg[b][:], in_=ps[:], func=func, scale=scale),
                             reads=[pk], writes=[("stg", b)])
                        P.dma("sp", dst[:, tb * 512:(tb + 1) * 512], stg[b][:], reads=[("stg", b)], writes=["scratch"])
                    proj_fm(P, PR, hT, w, wk, tb, ev)

            wq_v = wq.rearrange("(c p) n -> p c n", p=128)
            wz_v = wz.rearrange("(c p) n -> p c n", p=128)
            wkv_v = wkv.rearrange("(c p) n -> p c n", p=128)
            for hl in range(16):
                fm_chunk(wq_v, hl * 128, qT_s[hl], AF.Copy, NSA_SCALE)
                fm_chunk(wz_v, hl * 128, zT_s[hl], AF.Silu, 1.0)
            for gl in range(2):
                for n, i in enumerate((0, 1, 2, 4)):
                    fm_chunk(wkv_v, i * 256 + gl * 128, kvT_s[gl, n], AF.Copy, 1.0)
                for n, i in enumerate((3, 5)):
                    w, wk = load_w(wkv_v, i * 256 + gl * 128)
                    for tt in range(16):
                        ps, pk = PR.next()
                        for c in range(KC):
                            P.op("pe", lambda e, c=c, ps=ps, tt=tt, w=w: e.matmul(
                                ps[:, 0:128], hT[:, c, tt * 128:(tt + 1) * 128], w[:, c, :],
                                start=(c == 0), stop=(c == KC - 1)), reads=[wk, ("hT", tt // 4)], writes=[pk], sig=(c == KC - 1))
                        b = si[0] % 2
                        si[0] += 1
                        P.op("act", lambda e, ps=ps, b=b: e.copy(out=stg[b][:, 0:128], in_=ps[:, 0:128]), reads=[pk],
                             writes=[("stg", b)])
                        P.dma("sp", vtm_s[gl, n, tt * 128:(tt + 1) * 128, :], stg[b][:, 0:128], reads=[("stg", b)],
                              writes=["scratch"])
            for c0 in range(0, KC, 8):
                P.dma("pool", wglsb[:, c0:c0 + 8, :], wgl.rearrange("(c p) n -> p c n", p=128)[:, c0:c0 + 8, :], writes=["wgl"])
            for tb in range(4):
                ps, pk = PR.next()
                for c in range(KC):
                    P.op("pe", lambda e, c=c, ps=ps, tb=tb: e.matmul(ps[0:48, :], wglsb[:, c, :], hT[:, c, tb * 512:(tb + 1) * 512],
                                                                    start=(c == 0), stop=(c == KC - 1)),
                         reads=["wgl", ("hT", tb)], writes=[pk])
                P.op("act", lambda e, ps=ps, tb=tb: e.activation(out=GT[:, tb * 512:(tb + 1) * 512], in_=ps[0:48, :], func=AF.Sigmoid),
                     reads=[pk], writes=["GT"])
            barrier(P)
        Rt = P.sb("Rt_sb", [128, 13, 512], F32)
        SelR = P.sb("SelR_sb", [48, 48 * 128], F32)
        slope = P.sb("slope_sb", [128, 16], F32)
        cst = P.sb("cst_sb", [128, 16 * NREL], F32)
        E = P.sb("E_sb", [32, 2048], BF16)
        selbT = P.sb("selbT", [32, 2048], BF16)
        Mmat = P.sb("Mmat_sb", [128, 32], F32)
        At = P.sb("At_sb", [128, 256], F32)
        Bt = P.sb("Bt_sb", [128, 256], F32)
        onesb = P.sb("onesb", [128, 128], BF16)
        for k in range(13):
            P.dma("sp", Rt[:, k, :], R_d[k], writes=["Rt"])
        for t, d, k in ((SelR, SelR_d, "SelR"), (slope, slope_d, "slope"), (cst, cst_d, "cst"), (E, E_d, "E"),
                        (Mmat, Mmat_d, "Mmat"), (At, A_d, "At"), (Bt, Badd_d, "Bt")):
            P.dma("sp", t[:], d, writes=[k])
        P.op("dve", lambda e: e.memset(onesb[:], 1.0), writes=["onesb"])
        kvT = P.sb("kvT", [128, 4, S], BF16)
        vtm = P.sb("vtm", [128, 2, 16, 128], BF16)
        w1sb = P.sb("w1sb", [128, 32, 128], BF16)
        w2sb = P.sb("w2sb", [128, 128], BF16)
        peT = P.sb("peT_sb", [128, 32], F32)
        peTb = P.sb("peTb", [128, 32], BF16)
        cb_ = P.sb("cmpb", [128, 1], F32)
        xg = P.sb("xg", [128, 128], F32)
        xg2 = P.sb("xg2", [128, 128], F32)
        hidT = P.sb("hidT", [128, 128], BF16)
        KcT = P.sb("KcT", [128, 128], BF16)
        Vc = P.sb("Vc", [128, 128], BF16)
        qTb = P.sb("qTb", [128, S], BF16)
        zTb = P.sb("zTb", [128, S], BF16)
        tmp = [P.sb("atmp%d" % i, [128, 512], F32) for i in range(4)]
        Pf = P.sb("Pf", [128, 512], F32)
        Pb = [P.sb("Pb%d" % i, [128, 512], BF16) for i in range(4)]
        rden = P.sb("rden", [128, 512], F32)
        Ff = P.sb("Ff", [128, 512], F32)
        Tt = P.sb("Tt", [128, 512], F32)
        acc = P.sb("acc", [128, 512], F32)
        pn = P.sb("pn", [128, 512], F32)
        ys = P.sb("nys", [128, 512], BF16)
        pslc = P.sb("pslc", [128, 256], F32)
        sc = P.sb("sc", [128, 32], F32)
        sc2 = P.sb("sc2", [128, 32], F32)
        m8 = P.sb("m8", [128, 16], F32)
        selb = P.sb("selb", [128, 32], F32)
        ti_ = [0]
        bi_ = [0]
        pend = []

        def drain(keep=0):
            while len(pend) > keep:
                pend.pop(0)()

        def branch_step(kT_ap, kkey, v_ap, vkey, qsl, hl, rvar, cidx, nk, extra=None, first=False, last=False, want_f32=False):
            ps, pk = PR.next()
            P.op("pe", lambda e: e.matmul(ps[0:nk, :], kT_ap, qTb[:, qsl], start=True, stop=(extra is None)),
                 reads=[kkey, "qTb"], writes=[pk])
            if extra is not None:
                P.op("pe", lambda e: e.matmul(ps[0:nk, :], extra, selbT[:, qsl], start=False, stop=True),
                     reads=["E", "selbT"], writes=[pk])
            i = ti_[0] % 4
            ti_[0] += 1
            psU, psD = psUl[bi_[0] % 2], psDl[bi_[0] % 2]
            uk, dk = ("psU", bi_[0] % 2), ("psD", bi_[0] % 2)
            P.op("dve", lambda e: e.scalar_tensor_tensor(out=tmp[i][0:nk, :], in0=Rt[0:nk, rvar, :], scalar=slope[0:nk, hl:hl + 1],
                                                          in1=ps[0:nk, :], op0=ALU.mult, op1=ALU.add),
                 reads=["Rt", "slope", pk], writes=[("atmp", i)])
            if want_f32:
                P.op("act", lambda e: e.activation(out=Pf[0:nk, :], in_=tmp[i][0:nk, :], func=AF.Exp), reads=[("atmp", i)], writes=["Pf"])
                P.op("act", lambda e: e.copy(out=Pb[i][0:nk, :], in_=Pf[0:nk, :]), reads=["Pf"], writes=[("Pb", i)])
            elif cidx is None:
                P.op("act", lambda e: e.activation(out=Pb[i][0:nk, :], in_=tmp[i][0:nk, :], func=AF.Exp), reads=[("atmp", i)],
                     writes=[("Pb", i)])
            else:
                P.op("act", lambda e: e.activation(out=Pb[i][0:nk, :], in_=tmp[i][0:nk, :], func=AF.Exp,
                                                   bias=cst[0:nk, cidx:cidx + 1]), reads=[("atmp", i), "cst"], writes=[("Pb", i)])

            def stage_b():
                P.op("pe", lambda e: e.matmul(psU[:], v_ap, Pb[i][0:nk, :], start=first, stop=last), reads=[vkey, ("Pb", i)], writes=[uk])
                P.op("pe", lambda e: e.matmul(psD[:], onesb[0:nk, :], Pb[i][0:nk, :], start=first, stop=last),
                     reads=["onesb", ("Pb", i)], writes=[dk])
            pend.append(stage_b)

        def finish_branch(br, hl, qsl, mode):
            psU, psD = psUl[bi_[0] % 2], psDl[bi_[0] % 2]
            uk, dk = ("psU", bi_[0] % 2), ("psD", bi_[0] % 2)
            bi_[0] += 1

            def fin():
                P.op("dve", lambda e: e.tensor_scalar(out=rden[:], in0=psD[:], scalar1=1e-18, scalar2=None, op0=ALU.max),
                     reads=[dk], writes=["rden"])
                P.op("act", lambda e: e.activation(out=rden[:], in_=rden[:], func=AF.Ln), reads=["rden"], writes=["rden"])
                P.op("act", lambda e: e.activation(out=rden[:], in_=rden[:], func=AF.Exp, scale=-1.0), reads=["rden"], writes=["rden"])
                r = br * 16 + hl
                psG, gk = PR.next()
                P.op("pe", lambda e: e.matmul(psG[:], SelR[:, r * 128:(r + 1) * 128], GT[:, qsl], start=True, stop=True),
                     reads=["SelR", "GT"], writes=[gk])
                P.op("dve", lambda e: e.tensor_tensor(out=Ff[:], in0=rden[:], in1=psG[:], op=ALU.mult), reads=["rden", gk], writes=["Ff"])
                if mode == "ret":
                    P.op("dve", lambda e: e.tensor_tensor(out=Tt[:], in0=psU[:], in1=Ff[:], op=ALU.mult), reads=[uk, "Ff"], writes=["Tt"])
                elif mode == "set":
                    P.op("dve", lambda e: e.tensor_tensor(out=acc[:], in0=psU[:], in1=Ff[:], op=ALU.mult), reads=[uk, "Ff"], writes=["acc"])
                else:
                    P.op("dve", lambda e: e.tensor_tensor(out=Tt[:], in0=psU[:], in1=Ff[:], op=ALU.mult), reads=[uk, "Ff"], writes=["Tt"])
                    P.op("dve", lambda e: e.tensor_tensor(out=acc[:], in0=acc[:], in1=Tt[:], op=ALU.add), reads=["acc", "Tt"], writes=["acc"])
            pend.append(fin)

        def cidx_of(hl, m):
            return hl * NREL + (m + 15)

        for gl in range(2):
            for n in range(4):
                P.dma("sp", kvT[:, n, :], kvT_s[gl, n], reads=["scratch"], writes=["kvT"])
            for n in range(2):
                P.dma("sp", vtm[:, n], vtm_s[gl, n].rearrange("(t p) d -> p t d", p=128), reads=["scratch"], writes=["vtm"])
            P.dma("sp", selbT[:, 0:1024], selb0_d, writes=["selbT"])
            for which in range(2):
                for c0 in range(0, 32, 8):
                    P.dma("pool", w1sb[:, c0:c0 + 8, :], w1_d[which].rearrange("(l d) e -> d l e", d=128)[:, c0:c0 + 8, :], writes=["w1sb"])
                P.dma("pool", w2sb[:], w2_d[which], writes=["w2sb"])
                P.dma("sp", peT[:], peT_d[which], writes=["peT"])
                P.op("act", lambda e: e.copy(out=peTb[:], in_=peT[:]), reads=["peT"], writes=["peTb"])
                ps, pk = PR.next()
                for l in range(32):
                    P.op("pe", lambda e, l=l, ps=ps: e.matmul(ps[:, 0:127], w1sb[:, l, :], kvT[:, which, l:l + 16 * 126 + 1:16],
                                                             start=(l == 0), stop=(l == 31)), reads=["w1sb", "kvT"], writes=[pk])
                ps2, pk2 = PR.next()
                for l in range(32):
                    P.op("pe", lambda e, l=l, ps2=ps2: e.matmul(ps2[:, 0:1], w1sb[:, l, :], peTb[:, l:l + 1],
                                                               start=(l == 0), stop=(l == 31)), reads=["w1sb", "peTb"], writes=[pk2])
                P.op("act", lambda e, ps2=ps2: e.copy(out=cb_[:], in_=ps2[:, 0:1]), reads=[pk2], writes=["cmpb"])
                P.op("act", lambda e, ps=ps: e.activation(out=xg[:, 0:127], in_=ps[:, 0:127], func=AF.Identity, bias=cb_[:, 0:1]),
                     reads=[pk, "cmpb"], writes=["xg"])
                P.op("dve", lambda e: e.tensor_tensor(out=xg2[:, 0:127], in0=xg[:, 0:127], in1=xg[:, 0:127], op=ALU.mult), reads=["xg"], writes=["xg2"])
                P.op("dve", lambda e: e.tensor_scalar(out=xg2[:, 0:127], in0=xg2[:, 0:127], scalar1=0.044715, scalar2=1.0,
                                                       op0=ALU.mult, op1=ALU.add), reads=["xg2"], writes=["xg2"])
                P.op("dve", lambda e: e.tensor_tensor(out=xg2[:, 0:127], in0=xg2[:, 0:127], in1=xg[:, 0:127], op=ALU.mult),
                     reads=["xg", "xg2"], writes=["xg2"])
                P.op("act", lambda e: e.activation(out=xg2[:, 0:127], in_=xg2[:, 0:127], func=AF.Sigmoid, scale=1.5957691216),
                     reads=["xg2"], writes=["xg2"])
                P.op("dve", lambda e: e.tensor_tensor(out=hidT[:, 0:127], in0=xg[:, 0:127], in1=xg2[:, 0:127], op=ALU.mult),
                     reads=["xg", "xg2"], writes=["hidT"])
                ps3, pk3 = PR.next()
                if which == 0:
                    P.op("pe", lambda e, ps3=ps3: e.matmul(ps3[:, 0:127], w2sb[:], hidT[:, 0:127], start=True, stop=True),
                         reads=["w2sb", "hidT"], writes=[pk3])
                    P.op("act", lambda e, ps3=ps3: e.copy(out=KcT[:, 0:127], in_=ps3[:, 0:127]), reads=[pk3], writes=["KcT"])
                else:
                    P.op("pe", lambda e, ps3=ps3: e.matmul(ps3[0:127, 0:128], hidT[:, 0:127], w2sb[:], start=True, stop=True),
                         reads=["w2sb", "hidT"], writes=[pk3])
                    P.op("act", lambda e, ps3=ps3: e.copy(out=Vc[0:127, :], in_=ps3[0:127, 0:128]), reads=[pk3], writes=["Vc"])
            for j in range(8):
                hl = gl * 8 + j
                P.dma("sp", qTb[:], qT_s[hl], reads=["scratch"], writes=["qTb"])
                for qb in range(4):
                    qsl = slice(qb * 512, (qb + 1) * 512)
                    branch_step(KcT[:, 0:127], "KcT", Vc[0:127, :], "Vc", qsl, hl, 9 + qb, None, 127, first=True, last=True,
                                want_f32=(qb >= 2))
                    finish_branch(0, hl, qsl, "ret")
                    drain(0)
                    P.dma("sp", tcmp_s[j, :, qsl], Tt[:], reads=["Tt"], writes=["tcmp"])
                    if qb >= 2:
                        P.op("dve", lambda e: e.tensor_tensor(out=pn[0:127, :], in0=Pf[0:127, :], in1=rden[0:127, :], op=ALU.mult),
                             reads=["Pf", "rden"], writes=["pn"])
                        ps, pk = PR.next()
                        for k in range(4):
                            P.op("pe", lambda e, k=k, ps=ps: e.matmul(ps[:, k * 32:(k + 1) * 32], pn[0:127, k * 128:(k + 1) * 128],
                                                                     Mmat[0:127, :], start=True, stop=True),
                                 reads=["pn", "Mmat"], writes=[pk])
                        dst = pslc[:, (qb - 2) * 128:(qb - 1) * 128]
                        if j == 0:
                            P.op("dve", lambda e, ps=ps, dst=dst: e.tensor_copy(out=dst, in_=ps[:, 0:128]), reads=[pk], writes=["pslc"])
                        else:
                            P.op("dve", lambda e, ps=ps, dst=dst: e.tensor_tensor(out=dst, in0=dst, in1=ps[:, 0:128], op=ALU.add),
                                 reads=[pk, "pslc"], writes=["pslc"])
            for ti in range(8):
                csl = slice(ti * 32, (ti + 1) * 32)
                P.op("dve", lambda e, csl=csl: e.tensor_tensor(out=sc[:], in0=pslc[:, csl], in1=At[:, csl], op=ALU.mult),
                     reads=["pslc", "At"], writes=["sc"])
                P.op("dve", lambda e, csl=csl: e.tensor_tensor(out=sc[:], in0=sc[:], in1=Bt[:, csl], op=ALU.add), reads=["sc", "Bt"], writes=["sc"])
                P.op("dve", lambda e: e.max(out=m8[:, 0:8], in_=sc[:]), reads=["sc"], writes=["m8"])
                P.op("dve", lambda e: e.match_replace(out=sc2[:], in_to_replace=m8[:, 0:8], in_values=sc[:], imm_value=-1e30),
                     reads=["sc", "m8"], writes=["sc2"])
                P.op("dve", lambda e: e.max(out=m8[:, 8:16], in_=sc2[:]), reads=["sc2"], writes=["m8"])
                P.op("dve", lambda e: e.tensor_scalar(out=selb[:], in0=sc[:], scalar1=m8[:, 15:16], scalar2=None, op0=ALU.is_ge),
                     reads=["sc", "m8"], writes=["selb"])
                P.op("dve", lambda e: e.tensor_scalar(out=selb[:], in0=selb[:], scalar1=30000.0, scalar2=-30000.0, op0=ALU.mult, op1=ALU.add),
                     reads=["selb"], writes=["selb"])
                ps, pk = PR.next()
                P.op("pe", lambda e, ps=ps: e.transpose(out=ps[0:32, 0:128], in_=selb[:], identity=ident[:]), reads=["selb", "ident"], writes=[pk])
                P.op("act", lambda e, ps=ps, ti=ti: e.copy(out=selbT[:, 1024 + ti * 128:1024 + (ti + 1) * 128], in_=ps[0:32, 0:128]),
                     reads=[pk], writes=["selbT"])
            for j in range(8):
                hl = gl * 8 + j
                drain(0)
                P.dma("sp", qTb[:], qT_s[hl], reads=["scratch"], writes=["qTb"])
                P.dma("sp", zTb[:], zT_s[hl], reads=["scratch"], writes=["zTb"])
                for qb in range(4):
                    qsl = slice(qb * 512, (qb + 1) * 512)
                    drain(0)
                    P.dma("sp", pn[:], tcmp_s[j, :, qsl], reads=["tcmp"], writes=["pn"])
                    kts = list(range(0, 4 * qb + 4))
                    for kt in kts:
                        m = 4 * qb - kt
                        rvar, ci = (0, cidx_of(hl, -m)) if m >= 1 else (1 + (-m), cidx_of(hl, -m))
                        branch_step(kvT[:, 2, kt * 128:(kt + 1) * 128], "kvT", vtm[:, 0, kt, :], "vtm", qsl, hl, rvar, ci, 128,
                                    extra=(E[:, kt * 128:(kt + 1) * 128] if qb >= 2 else None), first=(kt == kts[0]),
                                    last=(kt == kts[-1]))
                        drain(2)
                    finish_branch(1, hl, qsl, "set")
                    kts = list(range(max(0, 4 * qb - 4), 4 * qb + 4))
                    for kt in kts:
                        m = 4 * qb - kt
                        if m >= 1:
                            rvar = 5 + (4 - m)
                        else:
                            rvar = 1 + (-m)
                        branch_step(kvT[:, 3, kt * 128:(kt + 1) * 128], "kvT", vtm[:, 1, kt, :], "vtm", qsl, hl, rvar, cidx_of(hl, -m), 128,
                                    first=(kt == kts[0]), last=(kt == kts[-1]))
                        drain(2)
                    finish_branch(2, hl, qsl, "add")
                    drain(0)
                    P.op("dve", lambda e: e.tensor_tensor(out=acc[:], in0=acc[:], in1=pn[:], op=ALU.add), reads=["acc", "pn"], writes=["acc"])
                    P.op("dve", lambda e, qsl=qsl: e.tensor_tensor(out=ys[:], in0=acc[:], in1=zTb[:, qsl], op=ALU.mult),
                         reads=["acc", "zTb"], writes=["nys"])
                    P.dma("sp", yT[hl * 128:(hl + 1) * 128, qsl], ys[:], reads=["nys"], writes=["ysrc"])


IDENT = np.eye(128, dtype=np.float32)


def gainT(g):
    return np.ascontiguousarray(g.reshape(32, 128).T).astype(np.float32)


def colT(v, h, n=16):
    return np.ascontiguousarray(v[h * n * 128:(h + 1) * n * 128].reshape(n, 128).T).astype(np.float32)

def nsa_consts(h):
    heads = np.arange(16) + 16 * h
    sl = (2.0 ** (-8.0 * (heads + 1) / 32)).astype(np.float64)
    slope = np.broadcast_to(sl[None, :], (128, 16)).astype(np.float32)
    cst = np.zeros((128, 16, NREL), np.float64)
    for m in range(-15, 4):
        cst[:, :, m + 15] = sl[None, :] * 128.0 * m
    kj = np.arange(128)[:, None].astype(np.float64)
    qi = np.arange(512)[None, :].astype(np.float64)
    base = kj - qi
    Rt = np.zeros((13, 128, 512), np.float64)
    Rt[0] = base
    for v in range(4):
        Rt[1 + v] = np.where(-128 * v + qi - kj >= 0, base, NEGM)
    for n, rel in enumerate((512, 384, 256, 128)):
        Rt[5 + n] = np.where(rel + qi - kj < 512, base, NEGM)
    for qb in range(4):
        dist = 512 * qb + qi - 16 * kj - 31
        Rt[9 + qb] = np.where(dist >= 0, -dist, NEGM)
    key = np.arange(2048)
    E = (key[None, :] // 64 == np.arange(32)[:, None]).astype(np.float32)
    t = np.arange(1024)
    selb0 = np.where(np.arange(32)[:, None] <= (t[None, :] // 64), 0.0, -30000.0)
    Mmat = np.zeros((128, 32), np.float32)
    for i in range(32):
        for off, wgt in ((-1, 1), (0, 2), (1, 2), (2, 2), (3, 1)):
            n = 4 * i + off
            if 0 <= n < 127:
                Mmat[n, i] += wgt
    At = np.zeros((128, 8, 32), np.float32)
    Bt = np.zeros((128, 8, 32), np.float32)
    for ti in range(8):
        tt = 1024 + ti * 128 + np.arange(128)
        cur = (tt // 64)[:, None]
        blk = np.arange(32)[None, :]
        forced = (blk == 0) | (blk == cur) | (blk == cur - 1)
        future = blk > cur
        At[:, ti] = (~forced & ~future)
        Bt[:, ti] = np.where(forced, 1e6, np.where(future, -1.0, 0.0))
    SelR = np.zeros((48, 48, 128), np.float32)
    for r in range(48):
        SelR[r, r, :] = 1.0
    return {"slope": slope, "cst": np.ascontiguousarray(cst.reshape(128, 16 * NREL)).astype(np.float32),
            "Rt": Rt.astype(np.float32), "E": E.astype(ml_dtypes.bfloat16), "selb0": selb0.astype(ml_dtypes.bfloat16),
            "Mmat": Mmat, "At": np.ascontiguousarray(At.reshape(128, 256)), "Bt": np.ascontiguousarray(Bt.reshape(128, 256)),
            "SelR": np.ascontiguousarray(SelR.reshape(48, 48 * 128))}


PAIRS = [[0, 1], [2, 3], [4, 5], [6, 7]]
NSA_IN = ("wq", [D, 2048]), ("wz", [D, 2048]), ("wkv", [D, 1536]), ("wgl", [D, 48]), ("peT", [2, 128, 32]), \
    ("w1", [2, 4096, 128]), ("w2", [2, 128, 128])
NSA_CONST = ("slope", [128, 16], F32), ("cst", [128, 16 * NREL], F32), ("Rt", [13, 128, 512], F32), ("E", [32, 2048], BF16), \
    ("selb0", [32, 1024], BF16), ("Mmat", [128, 32], F32), ("At", [128, 256], F32), ("Bt", [128, 256], F32), \
    ("SelR", [48, 48 * 128], F32)
RG_IN = ("wx", [D, 2048]), ("wz", [D, 2048]), ("cw", [128, 64]), ("cb", [128, 16]), ("lam", [128, 16]), ("gb", [128, 32]), \
    ("gw", [2, 8, 256, 256])
HG_IN = ("wq", [D, 2048]), ("wf", [D, 2048]), ("wv", [D, 2048]), ("wg", [D, 2048]), ("lbl", [128, 64]), ("ng", [128, 1]), \
    ("ones", [128, 128]), ("rmask", [128, 512]), ("bdmask", [128, 128])


def build_fused():
    nc = bass.Bass("TRN2", target_bir_lowering=False)
    I = lambda n, s, dt=F32: nc.dram_tensor(n, list(s), dt, kind="ExternalInput").ap()
    T = lambda n, s, dt: nc.dram_tensor(n, list(s), dt, kind="Internal").ap()
    x = I("x", [S, D])
    x_my = I("x_my", [1024, D])
    gT_d = I("gT", [128, 128])
    post_d = I("post", [4, 128, D])
    ident_d = I("ident", [128, 128])
    hm_d = I("hm", [128, 2])
    consts = {n: I(n, s, dt) for n, s, dt in NSA_CONST}
    lio = {}
    for i in range(4):
        spec = (NSA_IN, RG_IN, HG_IN)[i % 3]
        lio[i] = {n: I("L%d_%s" % (i, n), s) for n, s in spec}
        lio[i]["wo"] = I("L%d_wo" % i, [D, D])
    out = nc.dram_tensor("out", [1024, D], F32, kind="ExternalOutput").ap()
    ysrc = T("ysrc", [2048, S], BF16)
    yT_g = T("yT_g", [D, S], BF16)
    xsrc = [T("xsrc0", [1024, D], F32), T("xsrc1", [1024, D], F32)]
    x_g = T("x_g", [S, D], F32)
    scratch = {"qT_s": T("qT_s", [16, 128, S], BF16), "zT_s": T("zT_s", [16, 128, S], BF16),
               "kvT_s": T("kvT_s", [2, 4, 128, S], BF16), "vtm_s": T("vtm_s", [2, 2, S, 128], BF16),
               "tcmp_s": T("tcmp_s", [8, 128, S], F32)}
    with contextlib.ExitStack() as es:
        P = Prog(nc, es)
        PR = PsumRing(P, 4)
        psUl = [P.ps("psU%d" % i, [128, 512]) for i in range(2)]
        psDl = [P.ps("psD%d" % i, [128, 512]) for i in range(2)]
        gT_sb = P.sb("gT_sb", [128, 128], F32)
        ident = P.sb("ident_sb", [128, 128], F32)
        hm = P.sb("hm_sb", [128, 2], F32)
        P.dma("sp", gT_sb[:], gT_d, writes=["gT"])
        P.dma("sp", ident[:], ident_d, writes=["ident"])
        P.dma("sp", hm[:], hm_d, writes=["hm"])
        for i in range(4):
            kind = i % 3
            io = dict(lio[i])
            io.update(gT=None, ident=None, x=(x if i == 0 else x_g), yT=ysrc)
            io["xrow"] = (lambda tt: tt * 128) if i == 0 else (lambda tt: (tt % 8) * 256 + (tt // 8) * 128)
            with contextlib.ExitStack() as esl:
                P.es_cur = esl
                P.pre = "L%dm_" % i
                if kind == 0:
                    io.update(consts)
                    io.update(scratch)
                    emit_nsa(P, nc, PR, psUl, psDl, gT_sb, i * 32, ident, io)
                elif kind == 1:
                    emit_rg(P, nc, PR, gT_sb, i * 32, ident, io)
                else:
                    emit_hg(P, nc, PR, gT_sb, i * 32, ident, io, layer=i)
                barrier(P)
            for k in range(4):
                P.cc("AllGather", PAIRS, ysrc[k * 512:(k + 1) * 512, :], yT_g[k * 1024:(k + 1) * 1024, :], reads=["ysrc"], writes=["yT_g"])
            barrier(P)
            with contextlib.ExitStack() as esl:
                P.es_cur = esl
                P.pre = "L%do_" % i
                x_res = x_my if i == 0 else xsrc[(i - 1) % 2]
                outs = [out] if i == 3 else [xsrc[i % 2]]
                emit_outproj(P, nc, PR, yT_g, x_res, io["wo"], post_d[i], outs, hm)
                barrier(P)
            P.es_cur = None
            if i < 3:
                for k in range(8):
                    P.cc("AllGather", PAIRS, xsrc[i % 2][k * 128:(k + 1) * 128, :], x_g[k * 256:(k + 1) * 256, :],
                         reads=["x_out0"], writes=["x_g"])
                barrier(P)
        P.finish()
    return nc


def kernel(x, pre_norm_gain, post_norm_gain, nsa_w_in, nsa_cmp_pe, nsa_cmp_w1, nsa_cmp_w2, nsa_w_out,
           rg_w_in, rg_conv_w, rg_conv_b, rg_gate_w, rg_gate_b, rg_lambda, rg_w_out,
           hg_w_in, hg_lb_logits, hg_norm_gain, hg_w_out):
    f = lambda a: np.asarray(a, dtype=np.float32)
    ca = np.ascontiguousarray
    x = f(x)
    pre, post = f(pre_norm_gain), f(post_norm_gain)
    common = {
        "gT": ca(np.concatenate([gainT(pre[i]) for i in range(4)], axis=1)),
        "post": ca(np.broadcast_to(post[:, None, :], (4, 128, D))).astype(np.float32),
        "ident": IDENT,
    }
    rmask = np.ones((128, 512), np.float32)
    rmask[:, ::64] = 0.0
    si = np.arange(128)[:, None]
    ti = np.arange(128)[None, :]
    bdmask = ((si // 64 == ti // 64) & (ti >= si)).astype(np.float32)
    in_maps = []
    for c in range(NCORES):
        b, h = c // 2, c % 2
        m = dict(common)
        m["x"] = ca(x[b])
        m["x_my"] = ca(x[b, h * 1024:(h + 1) * 1024])
        m["hm"] = ca(np.broadcast_to(np.array([[1.0 - h, float(h)]], np.float32), (128, 2)))
        m.update(nsa_consts(h))
        for i, j in ((0, 0), (3, 1)):
            w_in = f(nsa_w_in[j])
            pfx = "L%d_" % i
            m[pfx + "wq"] = ca(w_in[:, h * 2048:(h + 1) * 2048])
            m[pfx + "wz"] = ca(w_in[:, 7264 + h * 2048:7264 + (h + 1) * 2048])
            m[pfx + "wkv"] = ca(np.concatenate(
                [w_in[:, 4096 + k * 512 + h * 256:4096 + k * 512 + (h + 1) * 256] for k in range(6)], axis=1))
            m[pfx + "wgl"] = ca(np.concatenate(
                [w_in[:, 7168 + br * 32 + h * 16:7168 + br * 32 + (h + 1) * 16] for br in range(3)], axis=1))
            m[pfx + "peT"] = ca(f(nsa_cmp_pe[j]).transpose(0, 2, 1))
            m[pfx + "w1"] = ca(f(nsa_cmp_w1[j]))
            m[pfx + "w2"] = ca(f(nsa_cmp_w2[j]))
            m[pfx + "wo"] = ca(f(nsa_w_out[j]))
        w_in = f(rg_w_in[0])
        m["L1_wx"] = ca(w_in[:, h * 2048:(h + 1) * 2048])
        m["L1_wz"] = ca(w_in[:, 4096 + h * 2048:4096 + (h + 1) * 2048])
        cw = f(rg_conv_w[0])
        m["L1_cw"] = ca(np.stack([colT(cw[k], h) for k in range(4)], axis=-1).reshape(128, 64))
        m["L1_cb"] = colT(f(rg_conv_b[0]), h)
        m["L1_lam"] = colT(f(rg_lambda[0]), h)
        gbv = f(rg_gate_b[0])
        m["L1_gb"] = ca(np.stack([colT(gbv[k].reshape(-1), h) for k in range(2)], axis=1).reshape(128, 32))
        m["L1_gw"] = ca(f(rg_gate_w[0])[:, h * 8:(h + 1) * 8])
        m["L1_wo"] = ca(f(rg_w_out[0]))
        w_in = f(hg_w_in[0])
        for k, n in enumerate(("wq", "wf", "wv", "wg")):
            m["L2_" + n] = ca(w_in[:, k * 4096 + h * 2048:k * 4096 + (h + 1) * 2048])
        lbl = f(hg_lb_logits)
        m["L2_lbl"] = ca(np.concatenate([colT(lbl[l], h) for l in range(4)], axis=1))
        m["L2_ng"] = ca(f(hg_norm_gain[0]).reshape(128, 1))
        m["L2_ones"] = np.ones((128, 128), np.float32)
        m["L2_rmask"] = rmask
        m["L2_bdmask"] = bdmask
        m["L2_wo"] = ca(f(hg_w_out[0]))
        in_maps.append(m)
    nc = build_fused()
    res = run_bass_kernel_spmd(nc, in_maps, core_ids=list(range(NCORES)))
    out = np.empty((B, S, D), np.float32)
    for c in range(NCORES):
        b, h = c // 2, c % 2
        out[b, h * 1024:(h + 1) * 1024] = res.results[c]["out"]
    return out
```

```python
import contextlib
import numpy as np
import ml_dtypes
import concourse.bass as bass
import concourse.mybir as mybir
from concourse.bass_utils import run_bass_kernel_spmd

F32 = mybir.dt.float32
BF16 = mybir.dt.bfloat16
AF = mybir.ActivationFunctionType
ALU = mybir.AluOpType
AX = mybir.AxisListType

D = 4096
B = 4
S = 2048
EPS = 1e-6
NCORES = 8


class Prog:
    ENGS = ("pe", "act", "dve", "pool", "sp")
    NDS = 24

    def __init__(self, nc, es, self_sync=True):
        self.nc = nc
        self.es = es
        self.self_sync = self_sync
        self.eng = dict(pe=nc.tensor, act=nc.scalar, dve=nc.vector, pool=nc.gpsimd, sp=nc.sync)
        self.sem = {e: es.enter_context(nc.semaphore("s_" + e)) for e in self.ENGS}
        self.cnt = {e: 0 for e in self.ENGS}
        self.seen = {e: {} for e in self.ENGS}
        self.dsem = [es.enter_context(nc.semaphore("d%d" % i)) for i in range(self.NDS)]
        self.dcnt = [0] * self.NDS
        self.drr = 0
        self.state = {}
        self.ninst = 0
        self.pre = ""
        self.es_cur = None
        self.ccsem = es.enter_context(nc.semaphore("ccsem"))
        self.cccnt = 0

    def sb(self, name, shape, dt):
        es = self.es_cur if self.es_cur is not None else self.es
        return es.enter_context(self.nc.sbuf_tensor(self.pre + name, list(shape), dt))

    def ps(self, name, shape, dt=F32):
        return self.es.enter_context(self.nc.psum_tensor(name, list(shape), dt))

    def _deps(self, reads, writes):
        deps = []
        for k in reads:
            st = self.state.get(k)
            if st is not None and st[0] is not None:
                deps.append(st[0])
        for k in writes:
            st = self.state.get(k)
            if st is not None:
                if st[0] is not None:
                    deps.append(st[0])
                deps.extend(st[1].values())
        return deps

    def _wait(self, e, deps):
        best = {}
        for (sid, sem, val) in deps:
            if sid not in best or best[sid][1] < val:
                best[sid] = (sem, val)
        for sid, (sem, val) in best.items():
            if self.seen[e].get(sid, 0) >= val:
                continue
            if sid == e and (e == "pe" or not self.self_sync):
                continue
            self.eng[e].wait_ge(sem, val)
            self.ninst += 1
            self.seen[e][sid] = val

    def _record(self, tok, reads, writes):
        for k in reads:
            st = self.state.get(k)
            if st is None:
                st = [None, {}]
                self.state[k] = st
            st[1][tok[0]] = tok
        for k in writes:
            self.state[k] = [tok, {}]

    def op(self, e, fn, reads=(), writes=(), sig=True):
        self._wait(e, self._deps(reads, writes))
        ins = fn(self.eng[e])
        self.ninst += 1
        if sig:
            self.cnt[e] += 1
            ins.then_inc(self.sem[e], 1)
            self._record((e, self.sem[e], self.cnt[e]), reads, writes)
        else:
            self._record((e, self.sem[e], self.cnt[e] + 1), reads, writes)

    def dma(self, q, out, in_, reads=(), writes=()):
        i = self.drr
        self.drr = (i + 1) % self.NDS
        sem = self.dsem[i]
        sid = "d%d" % i
        deps = self._deps(reads, writes)
        if self.dcnt[i] > 0:
            deps.append((sid, sem, self.dcnt[i]))
        self._wait(q, deps)
        self.eng[q].dma_start(out=out, in_=in_).then_inc(sem, 16)
        self.ninst += 1
        self.dcnt[i] += 16
        self._record((sid, sem, self.dcnt[i]), reads, writes)

    def cc(self, kind, groups, in_, out, reads=(), writes=()):
        deps = self._deps(reads, writes)
        self._wait("pool", deps)
        self.eng["pool"].collective_compute(kind, ALU.bypass, replica_groups=groups, ins=[in_.opt()], outs=[out.opt()]).then_inc(self.ccsem)
        self.ninst += 1
        self.cccnt += 1
        self._record(("cc", self.ccsem, self.cccnt), reads, writes)

    def finish(self):
        for i in range(self.NDS):
            if self.dcnt[i] > 0:
                self.eng["sp"].wait_ge(self.dsem[i], self.dcnt[i])
        for e in self.ENGS:
            if e != "sp" and self.cnt[e] > 0:
                self.eng["sp"].wait_ge(self.sem[e], self.cnt[e])
        if self.cccnt > 0:
            self.eng["sp"].wait_ge(self.ccsem, self.cccnt)


class PsumRing:
    def __init__(self, P, n=8):
        self.P = P
        self.t = [P.ps("psr%d" % i, [128, 512]) for i in range(n)]
        self.i = 0

    def next(self):
        i = self.i % len(self.t)
        self.i += 1
        return self.t[i], ("psr", i)


def barrier(P):
    toks = []
    for e in P.ENGS:
        if P.cnt[e] > 0:
            toks.append((e, P.sem[e], P.cnt[e]))
    for i in range(P.NDS):
        if P.dcnt[i] > 0:
            toks.append(("d%d" % i, P.dsem[i], P.dcnt[i]))
    if P.cccnt > 0:
        toks.append(("cc", P.ccsem, P.cccnt))
    for e in P.ENGS:
        P._wait(e, [t for t in toks if t[0] != e])


def phase_hT(P, nc, PR, x, gT_sb, ident, hT, ntok=S, gcol=0, xsplit=False):
    KC = D // 128
    with contextlib.ExitStack() as es2:
        xs = [es2.enter_context(nc.sbuf_tensor(P.pre + "hx%d" % i, [128, D], F32)) for i in range(2)]
        junk = es2.enter_context(nc.sbuf_tensor(P.pre + "hjunk", [128, D], BF16))
        st = es2.enter_context(nc.sbuf_tensor(P.pre + "hst", [128, 8], F32))
        for tt in range(ntok // 128):
            xb = xs[tt % 2]
            xk = ("hx", tt % 2)
            if xsplit:
                for r_ in range(2):
                    row = (tt // 2) * 512 + r_ * 256 + (tt % 2) * 128
                    P.dma("sp", xb[:, r_ * 2048:(r_ + 1) * 2048], x[row:row + 128, :], reads=["x_g"], writes=[xk])
            else:
                P.dma("sp", xb[:], x[tt * 128:(tt + 1) * 128, :], reads=["x_g"], writes=[xk])
            P.op("act", lambda e: e.activation(out=junk[:], in_=xb[:], func=AF.Square, accum_out=st[:, 0:1]),
                 reads=[xk], writes=["hjunk", "hst0"])
            P.op("dve", lambda e: e.tensor_scalar(out=st[:, 1:2], in0=st[:, 0:1], scalar1=1.0 / D, scalar2=EPS,
                                                   op0=ALU.mult, op1=ALU.add), reads=["hst0"], writes=["hst1"])
            P.op("act", lambda e: e.activation(out=st[:, 2:3], in_=st[:, 1:2], func=AF.Sqrt),
                 reads=["hst1"], writes=["hst2"])
            P.op("dve", lambda e: e.reciprocal(out=st[:, 3:4], in_=st[:, 2:3]), reads=["hst2"], writes=["hst3"])
            P.op("dve", lambda e: e.tensor_scalar(out=xb[:], in0=xb[:], scalar1=st[:, 3:4], scalar2=None,
                                                   op0=ALU.mult), reads=[xk, "hst3"], writes=[xk])
            for c0 in range(0, KC, 4):
                ps, pk = PR.next()
                for k in range(4):
                    c = c0 + k
                    P.op("pe", lambda e, c=c, k=k, ps=ps: e.transpose(
                        out=ps[:, k * 128:(k + 1) * 128], in_=xb[:, c * 128:(c + 1) * 128], identity=ident[:]),
                        reads=[xk, "ident"], writes=[pk])
                for k in range(4):
                    c = c0 + k
                    P.op("act", lambda e, c=c, k=k, ps=ps, tt=tt: e.activation(
                        out=hT[:, c, tt * 128:(tt + 1) * 128], in_=ps[:, k * 128:(k + 1) * 128],
                        func=AF.Copy, scale=gT_sb[:, gcol + c:gcol + c + 1]),
                        reads=[pk, "gT"], writes=[("hT", tt // 4)])
        barrier(P)


def proj_fm(P, PR, hT, w_chunk, wkey, tb, evac):
    KC = D // 128
    ps, pk = PR.next()
    for c in range(KC):
        P.op("pe", lambda e, c=c, ps=ps: e.matmul(ps[:], w_chunk[:, c, :], hT[:, c, tb * 512:(tb + 1) * 512],
                                                  start=(c == 0), stop=(c == KC - 1)),
             reads=[wkey, ("hT", tb)], writes=[pk], sig=(c == KC - 1))
    evac(ps, pk)


def emit_outproj(P, nc, PR, yT, x, w, gain, out, ss_src, ss_g, mid_cc=None):
    KC = D // 128
    NB = 256
    TP = 1024
    NCL = 2048
    NTT = TP // 128
    if True:
        yT_sb = P.sb("yT_sb", [128, KC, TP], BF16)
        w_sb = [P.sb("w_sb%d" % i, [128, KC, NB], BF16) for i in range(2)]
        o_sb = [P.sb("o_sb%d" % i, [128, NCL], F32) for i in range(NTT)]
        x_sb = [P.sb("x_sb%d" % i, [128, NCL], F32) for i in range(2)]
        g_sb = P.sb("g_sb", [128, NCL], F32)
        junk = P.sb("junk", [128, NCL], BF16)
        ssq = P.sb("ssq", [128, NTT], F32)
        ssg = P.sb("ssg", [128, 2, NTT], F32)
        st = P.sb("ost", [128, 4 * NTT], F32)
        yT_v = yT.rearrange("(c p) t -> p c t", p=128)
        w_v = w.rearrange("(c p) n -> p c n", p=128)
        P.dma("sp", g_sb[:], gain, writes=["g"])
        wi = 0
        for p_ in range(S // TP):
            for r_ in range(2):
                for k_ in range(4):
                    c0_, s0_ = r_ * 16 + k_ * 4, k_ * 8 + r_ * 4
                    P.dma("sp", yT_sb[:, c0_:c0_ + 4, :], yT_v[:, s0_:s0_ + 4, p_ * TP:(p_ + 1) * TP], reads=["yT_g"], writes=["yT"])
            for nb in range(NCL // NB):
                wb = wi % 2
                wi += 1
                for c0 in range(0, KC, 8):
                    P.dma("pool", w_sb[wb][:, c0:c0 + 8, :], w_v[:, c0:c0 + 8, nb * NB:(nb + 1) * NB],
                          writes=[("w", wb, c0)])
                for tt in range(NTT):
                    ps, pk = PR.next()
                    for c in range(KC):
                        P.op("pe", lambda e, c=c, tt=tt, ps=ps, wb=wb: e.matmul(
                            ps[:, 0:NB], yT_sb[:, c, tt * 128:(tt + 1) * 128], w_sb[wb][:, c, :],
                            start=(c == 0), stop=(c == KC - 1)),
                            reads=["yT", ("w", wb, (c // 8) * 8)], writes=[pk], sig=(c == KC - 1))
                    P.op("act", lambda e, tt=tt, ps=ps, nb=nb: e.copy(
                        out=o_sb[tt][:, nb * NB:(nb + 1) * NB], in_=ps[:, 0:NB]),
                        reads=[pk], writes=[("o", tt)])
            if p_ == 1 and mid_cc is not None:
                mid_cc()
            for tt in range(NTT):
                P.op("act", lambda e, tt=tt: e.activation(out=junk[:], in_=o_sb[tt][:], func=AF.Square,
                                                         accum_out=ssq[:, tt:tt + 1]),
                     reads=[("o", tt)], writes=["junk", "ssq"])
            P.dma("sp", ss_src, ssq[:], reads=["ssq"], writes=["ss_src"])
            P.cc("AllGather", PAIRS, ss_src, ss_g, reads=["ss_src"], writes=["ss_g"])
            P.dma("sp", ssg[:], ss_g.rearrange("(r p) c -> p r c", p=128), reads=["ss_g"], writes=["ssg"])
            P.op("dve", lambda e: e.tensor_tensor(out=st[:, 0:NTT], in0=ssg[:, 0, :], in1=ssg[:, 1, :], op=ALU.add),
                 reads=["ssg"], writes=["st0"])
            P.op("dve", lambda e: e.tensor_scalar(out=st[:, NTT:2 * NTT], in0=st[:, 0:NTT], scalar1=1.0 / D, scalar2=EPS,
                                                   op0=ALU.mult, op1=ALU.add), reads=["st0"], writes=["st1"])
            P.op("act", lambda e: e.activation(out=st[:, 2 * NTT:3 * NTT], in_=st[:, NTT:2 * NTT], func=AF.Sqrt),
                 reads=["st1"], writes=["st2"])
            P.op("dve", lambda e: e.reciprocal(out=st[:, 3 * NTT:4 * NTT], in_=st[:, 2 * NTT:3 * NTT]), reads=["st2"], writes=["st3"])
            for tt in range(NTT):
                t0 = p_ * TP + tt * 128
                xb, xk = x_sb[tt % 2], ("x", tt % 2)
                P.dma("sp", xb[:], x[t0:t0 + 128, :], reads=["x_my"], writes=[xk])
                P.op("dve", lambda e, tt=tt: e.scalar_tensor_tensor(
                    out=o_sb[tt][:], in0=o_sb[tt][:], scalar=st[:, 3 * NTT + tt:3 * NTT + tt + 1], in1=g_sb[:],
                    op0=ALU.mult, op1=ALU.mult), reads=[("o", tt), "st3", "g"], writes=[("o", tt)])
                P.op("dve", lambda e, tt=tt, xb=xb: e.tensor_tensor(out=o_sb[tt][:], in0=o_sb[tt][:], in1=xb[:], op=ALU.add),
                     reads=[("o", tt), xk], writes=[("o", tt)])
                P.dma("sp", out[t0:t0 + 128, :], o_sb[tt][:], reads=[("o", tt)], writes=["x_out0"])


def emit_rg(P, nc, PR, gT_sb, gcol, ident, io):
    x = io["x"]
    gT = io["gT"]
    ident_d = io["ident"]
    wx = io["wx"]
    wz = io["wz"]
    cw_d = io["cw"]
    cb_d = io["cb"]
    lam_d = io["lam"]
    gb_d = io["gb"]
    gw_d = io["gw"]
    yT = io["yT"]
    KC = D // 128
    if True:
        hT = P.sb("hT", [128, KC, S], BF16)
        cw = P.sb("cw_sb", [128, 64], F32)
        cb = P.sb("cb_sb", [128, 16], F32)
        lam = P.sb("lam_sb", [128, 16], F32)
        c1 = P.sb("c1_sb", [128, 16], F32)
        gb = P.sb("gb_sb", [128, 32], F32)
        P.dma("sp", cw[:], cw_d, writes=["cw"])
        P.dma("sp", cb[:], cb_d, writes=["cb"])
        P.dma("sp", lam[:], lam_d, writes=["lam"])
        P.dma("sp", gb[:], gb_d, writes=["gb"])
        phase_hT(P, nc, PR, x, gT_sb, ident, hT, gcol=gcol, xsplit=io["xsplit"])
        P.op("act", lambda e: e.activation(out=c1[:], in_=lam[:], func=AF.Exp, scale=-1.0), reads=["lam"], writes=["c1"])
        P.op("act", lambda e: e.activation(out=c1[:], in_=c1[:], func=AF.Ln, bias=1.0), reads=["c1"], writes=["c1"])
        P.op("dve", lambda e: e.tensor_scalar(out=c1[:], in0=c1[:], scalar1=-8.0, scalar2=None, op0=ALU.mult),
             reads=["c1"], writes=["c1"])
        wxs = [P.sb("wxs%d" % i, [128, KC, 128], BF16) for i in range(2)]
        wzs = [P.sb("wzs%d" % i, [128, KC, 128], BF16) for i in range(2)]
        gws = P.sb("gws", [128, 2, 2, 256], BF16)
        xraw = [P.sb("xraw%d" % i, [128, 3 + 512], F32) for i in range(2)]
        xc = [P.sb("xc%d" % i, [128, 512], F32) for i in range(2)]
        xcb = [P.sb("xcb%d" % i, [128, 512], BF16) for i in range(2)]
        gi = [P.sb("gi%d" % i, [128, 512], F32) for i in range(2)]
        ga = [P.sb("ga%d" % i, [128, 512], F32) for i in range(2)]
        gm = [P.sb("gm%d" % i, [128, 512], F32) for i in range(2)]
        hs = [[P.sb("hs%d_%d" % (i, k), [128, 512], F32) for k in range(2)] for i in range(2)]
        zs = [P.sb("zs%d" % i, [128, 512], F32) for i in range(2)]
        ys = [P.sb("ys%d" % i, [128, 512], BF16) for i in range(2)]
        wx_v = wx.rearrange("(c p) n -> p c n", p=128)
        wz_v = wz.rearrange("(c p) n -> p c n", p=128)
        for blk in range(8):
            for j in range(2):
                col0 = blk * 256 + j * 128
                for c0 in range(0, KC, 8):
                    P.dma("pool", wxs[j][:, c0:c0 + 8, :], wx_v[:, c0:c0 + 8, col0:col0 + 128], writes=[("wx", j)])
                for c0 in range(0, KC, 8):
                    P.dma("pool", wzs[j][:, c0:c0 + 8, :], wz_v[:, c0:c0 + 8, col0:col0 + 128], writes=[("wz", j)])
            for k in range(2):
                P.dma("pool", gws[:, k], gw_d[k, blk].rearrange("(jc p) e -> p jc e", p=128), writes=["gw"])
            for j in range(2):
                P.op("dve", lambda e, j=j: e.memset(xraw[j][:, 0:3], 0.0), writes=[("xraw", j)])
            for tb in range(4):
                for j in range(2):
                    ch = blk * 2 + j

                    def ev_x(ps, pk, j=j):
                        P.op("act", lambda e: e.copy(out=xraw[j][:, 3:515], in_=ps[:]), reads=[pk], writes=[("xraw", j)])
                    proj_fm(P, PR, hT, wxs[j], ("wx", j), tb, ev_x)
                    P.op("dve", lambda e, j=j, ch=ch: e.tensor_scalar(
                        out=xc[j][:], in0=xraw[j][:, 0:512], scalar1=cw[:, ch * 4:ch * 4 + 1], scalar2=cb[:, ch:ch + 1],
                        op0=ALU.mult, op1=ALU.add), reads=[("xraw", j), "cw", "cb"], writes=[("xc", j)])
                    for k in range(1, 4):
                        P.op("dve", lambda e, j=j, ch=ch, k=k: e.scalar_tensor_tensor(
                            out=xc[j][:], in0=xraw[j][:, k:k + 512], scalar=cw[:, ch * 4 + k:ch * 4 + k + 1],
                            in1=xc[j][:], op0=ALU.mult, op1=ALU.add), reads=[("xraw", j), ("xc", j), "cw"],
                            writes=[("xc", j)])
                    P.op("act", lambda e, j=j: e.copy(out=xcb[j][:], in_=xc[j][:]), reads=[("xc", j)], writes=[("xcb", j)])
                    P.op("dve", lambda e, j=j: e.tensor_copy(out=xraw[j][:, 0:3], in_=xraw[j][:, 512:515]),
                         reads=[("xraw", j)], writes=[("xraw", j)])
                for je in range(2):
                    ch = blk * 2 + je
                    for k in range(2):
                        ps, pk = PR.next()
                        for jc in range(2):
                            P.op("pe", lambda e, k=k, jc=jc, je=je, ps=ps: e.matmul(
                                ps[:], gws[:, k, jc, je * 128:(je + 1) * 128], xcb[jc][:], start=(jc == 0), stop=(jc == 1)),
                                reads=["gw", ("xcb", jc)], writes=[pk])
                        dst = gi[je] if k == 0 else ga[je]
                        dk = ("gi", je) if k == 0 else ("ga", je)
                        P.op("act", lambda e, ps=ps, dst=dst, k=k, ch=ch: e.activation(
                            out=dst[:], in_=ps[:], func=AF.Sigmoid, bias=gb[:, k * 16 + ch:k * 16 + ch + 1]),
                            reads=[pk, "gb"], writes=[dk])
                    P.op("act", lambda e, je=je, ch=ch: e.activation(out=ga[je][:], in_=ga[je][:], func=AF.Exp,
                                                                   scale=c1[:, ch:ch + 1]),
                         reads=[("ga", je), "c1"], writes=[("ga", je)])
                    P.op("dve", lambda e, je=je: e.tensor_tensor(out=gm[je][:], in0=ga[je][:], in1=ga[je][:], op=ALU.mult),
                         reads=[("ga", je)], writes=[("gm", je)])
                    P.op("act", lambda e, je=je: e.activation(out=gm[je][:], in_=gm[je][:], func=AF.Sqrt, scale=-1.0, bias=1.0),
                         reads=[("gm", je)], writes=[("gm", je)])
                    if tb == 0:
                        P.op("dve", lambda e, je=je: e.memset(gm[je][:, 0:1], 1.0), writes=[("gm", je)])
                    P.op("dve", lambda e, je=je: e.tensor_tensor(out=gi[je][:], in0=gi[je][:], in1=gm[je][:], op=ALU.mult),
                         reads=[("gi", je), ("gm", je)], writes=[("gi", je)])
                    P.op("dve", lambda e, je=je: e.tensor_tensor(out=gi[je][:], in0=gi[je][:], in1=xc[je][:], op=ALU.mult),
                         reads=[("gi", je), ("xc", je)], writes=[("gi", je)])
                    cur = hs[je][tb % 2]
                    prev = hs[je][(tb + 1) % 2]
                    init = 0.0 if tb == 0 else prev[:, 511:512]
                    P.op("dve", lambda e, je=je, cur=cur, init=init: e.tensor_tensor_scan(
                        out=cur[:], data0=ga[je][:], data1=gi[je][:], initial=init, op0=ALU.mult, op1=ALU.add),
                        reads=[("ga", je), ("gi", je), ("hs", je, (tb + 1) % 2)], writes=[("hs", je, tb % 2)])

                    def ev_z(ps, pk, je=je):
                        P.op("act", lambda e: e.activation(out=zs[je][:], in_=ps[:], func=AF.Silu), reads=[pk],
                             writes=[("zs", je)])
                    proj_fm(P, PR, hT, wzs[je], ("wz", je), tb, ev_z)
                    P.op("dve", lambda e, je=je, cur=cur: e.tensor_tensor(out=ys[je][:], in0=cur[:], in1=zs[je][:], op=ALU.mult),
                         reads=[("hs", je, tb % 2), ("zs", je)], writes=[("ys", je)])
                    P.dma("sp", yT[ch * 128:(ch + 1) * 128, tb * 512:(tb + 1) * 512], ys[je][:], reads=[("ys", je)], writes=["ysrc"])


def emit_hg(P, nc, PR, gT_sb, gcol, ident, io, layer=2):
    x = io["x"]
    gT = io["gT"]
    ident_d = io["ident"]
    ones_d = io["ones"]
    rmask_d = io["rmask"]
    bdmask_d = io["bdmask"]
    lbl_d = io["lbl"]
    ng_d = io["ng"]
    yT = io["yT"]
    wd = [io[n] for n in ("wq", "wf", "wv", "wg")]
    KC = D // 128
    if True:
        hT = P.sb("hT", [128, KC, S], BF16)
        ones = P.sb("ones_sb", [128, 128], F32)
        rmask = P.sb("rmask_sb", [128, 512], F32)
        bdmask = P.sb("bdmask_sb", [128, 128], F32)
        lbl = P.sb("lbl_sb", [128, 64], F32)
        lbe = P.sb("lbe_sb", [128, 64], F32)
        lb = P.sb("lb_sb", [128, 16], F32)
        oml = P.sb("oml_sb", [128, 16], F32)
        lsum = P.sb("lsum_sb", [128, 16], F32)
        ng = P.sb("ng_sb", [128, 1], F32)
        for t, d, k in ((ones, ones_d, "ones"), (rmask, rmask_d, "rmask"),
                        (bdmask, bdmask_d, "bdmask"), (lbl, lbl_d, "lbl"), (ng, ng_d, "ng")):
            P.dma("sp", t[:], d, writes=[k])
        phase_hT(P, nc, PR, x, gT_sb, ident, hT, gcol=gcol, xsplit=io["xsplit"])
        P.op("act", lambda e: e.activation(out=lbe[:], in_=lbl[:], func=AF.Exp), reads=["lbl"], writes=["lbe"])
        P.op("dve", lambda e: e.tensor_tensor(out=lsum[:], in0=lbe[:, 0:16], in1=lbe[:, 16:32], op=ALU.add),
             reads=["lbe"], writes=["lsum"])
        P.op("dve", lambda e: e.tensor_tensor(out=lsum[:], in0=lsum[:], in1=lbe[:, 32:48], op=ALU.add),
             reads=["lbe", "lsum"], writes=["lsum"])
        P.op("dve", lambda e: e.tensor_tensor(out=lsum[:], in0=lsum[:], in1=lbe[:, 48:64], op=ALU.add),
             reads=["lbe", "lsum"], writes=["lsum"])
        P.op("dve", lambda e: e.reciprocal(out=lsum[:], in_=lsum[:]), reads=["lsum"], writes=["lsum"])
        P.op("dve", lambda e: e.memset(lb[:], 0.0), writes=["lb"])
        for l in range(1, layer + 1):
            P.op("dve", lambda e, l=l: e.tensor_tensor(out=lb[:], in0=lb[:], in1=lbe[:, l * 16:(l + 1) * 16], op=ALU.add),
                 reads=["lb", "lbe"], writes=["lb"])
        P.op("dve", lambda e: e.tensor_tensor(out=lb[:], in0=lb[:], in1=lsum[:], op=ALU.mult), reads=["lb", "lsum"], writes=["lb"])
        P.op("dve", lambda e: e.tensor_scalar(out=oml[:], in0=lb[:], scalar1=-1.0, scalar2=1.0, op0=ALU.mult, op1=ALU.add),
             reads=["lb"], writes=["oml"])
        ws = [P.sb("hw%d" % i, [128, KC, 128], BF16) for i in range(4)]
        wv = [w.rearrange("(c p) n -> p c n", p=128) for w in wd]
        f32b = lambda n: P.sb(n, [128, 512], F32)
        qs, ff, logf, kk, bb, eb, enb, kef, gs, osb, sq, rstd, tmpo = [f32b("hgb%d" % i) for i in range(13)]
        qe = P.sb("qe", [128, 512], BF16)
        kebf = P.sb("kebf", [128, 512], BF16)
        ys = P.sb("hys", [128, 512], BF16)
        vtm = [P.sb("vtm%d" % i, [128, 128], BF16) for i in range(4)]
        ketm = [P.sb("ketm%d" % i, [128, 128], BF16) for i in range(2)]
        attT = [P.sb("attT%d" % i, [128, 128], BF16) for i in range(2)]
        Sf = P.sb("Sf", [128, 128], F32)
        Stmp = P.sb("Stmp", [128, 128], F32)
        Sb = P.sb("Sb", [128, 128], BF16)
        for hh in range(16):
            for i in range(4):
                for c0 in range(0, KC, 8):
                    P.dma("pool", ws[i][:, c0:c0 + 8, :], wv[i][:, c0:c0 + 8, hh * 128:(hh + 1) * 128], writes=[("hw", i)])
            P.op("dve", lambda e: e.memset(Sf[:], 0.0), writes=["Sf"])
            P.op("dve", lambda e: e.memset(Sb[:], 0.0), writes=["Sb"])
            for tb in range(4):
                def ev_q(ps, pk):
                    P.op("act", lambda e: e.activation(out=qs[:], in_=ps[:], func=AF.Silu), reads=[pk], writes=["qs"])
                proj_fm(P, PR, hT, ws[0], ("hw", 0), tb, ev_q)

                def ev_f(ps, pk):
                    P.op("act", lambda e: e.activation(out=ff[:], in_=ps[:], func=AF.Sigmoid), reads=[pk], writes=["ff"])
                proj_fm(P, PR, hT, ws[1], ("hw", 1), tb, ev_f)

                def ev_g(ps, pk):
                    P.op("act", lambda e: e.activation(out=gs[:], in_=ps[:], func=AF.Silu), reads=[pk], writes=["gs"])
                proj_fm(P, PR, hT, ws[3], ("hw", 3), tb, ev_g)
                for tt in range(4):
                    ps, pk = PR.next()
                    for c in range(KC):
                        P.op("pe", lambda e, c=c, ps=ps, tt=tt: e.matmul(
                            ps[:, 0:128], hT[:, c, tb * 512 + tt * 128:tb * 512 + (tt + 1) * 128], ws[2][:, c, :],
                            start=(c == 0), stop=(c == KC - 1)), reads=[("hw", 2), ("hT", tb)], writes=[pk], sig=(c == KC - 1))
                    P.op("act", lambda e, ps=ps, tt=tt: e.copy(out=vtm[tt][:], in_=ps[:, 0:128]), reads=[pk], writes=[("vtm", tt)])
                P.op("dve", lambda e: e.tensor_scalar(out=ff[:], in0=ff[:], scalar1=oml[:, hh:hh + 1], scalar2=lb[:, hh:hh + 1],
                                                       op0=ALU.mult, op1=ALU.add), reads=["ff", "oml", "lb"], writes=["ff"])
                P.op("act", lambda e: e.activation(out=logf[:], in_=ff[:], func=AF.Ln), reads=["ff"], writes=["logf"])
                P.op("dve", lambda e: e.tensor_scalar(out=kk[:], in0=ff[:], scalar1=-1.0, scalar2=1.0, op0=ALU.mult, op1=ALU.add),
                     reads=["ff"], writes=["kk"])
                P.op("dve", lambda e: e.tensor_tensor_scan(out=bb[:], data0=rmask[:], data1=logf[:], initial=0.0,
                                                            op0=ALU.mult, op1=ALU.add), reads=["rmask", "logf"], writes=["bb"])
                P.op("act", lambda e: e.activation(out=eb[:], in_=bb[:], func=AF.Exp), reads=["bb"], writes=["eb"])
                P.op("act", lambda e: e.activation(out=enb[:], in_=bb[:], func=AF.Exp, scale=-1.0), reads=["bb"], writes=["enb"])
                P.op("dve", lambda e: e.tensor_tensor(out=qe[:], in0=qs[:], in1=eb[:], op=ALU.mult), reads=["qs", "eb"], writes=["qe"])
                P.op("dve", lambda e: e.tensor_tensor(out=kef[:], in0=kk[:], in1=enb[:], op=ALU.mult), reads=["kk", "enb"], writes=["kef"])
                P.op("act", lambda e: e.copy(out=kebf[:], in_=kef[:]), reads=["kef"], writes=["kebf"])
                for tt in range(4):
                    sl = slice(tt * 128, (tt + 1) * 128)
                    kt = ketm[tt % 2]
                    at = attT[tt % 2]
                    ktk = ("ketm", tt % 2)
                    atk = ("attT", tt % 2)
                    ps, pk = PR.next()
                    P.op("pe", lambda e, ps=ps, sl=sl: e.transpose(out=ps[:, 0:128], in_=kef[:, sl], identity=ident[:]),
                         reads=["kef", "ident"], writes=[pk])
                    P.op("act", lambda e, ps=ps, kt=kt: e.copy(out=kt[:], in_=ps[:, 0:128]), reads=[pk], writes=[ktk])
                    ps2, pk2 = PR.next()
                    P.op("pe", lambda e, ps2=ps2, sl=sl: e.matmul(ps2[:, 0:128], kebf[:, sl], qe[:, sl], start=True, stop=True),
                         reads=["kebf", "qe"], writes=[pk2])
                    P.op("dve", lambda e, ps2=ps2, at=at: e.tensor_tensor(out=at[:], in0=ps2[:, 0:128], in1=bdmask[:], op=ALU.mult),
                         reads=[pk2, "bdmask"], writes=[atk])
                    pso, pko = PR.next()
                    P.op("pe", lambda e, pso=pso, at=at, tt=tt: e.matmul(pso[:, 0:128], vtm[tt][:], at[:], start=True, stop=False),
                         reads=[("vtm", tt), atk], writes=[pko])
                    for cc in range(2):
                        csl = slice(tt * 128 + cc * 64, tt * 128 + (cc + 1) * 64)
                        rows = slice(cc * 64, (cc + 1) * 64)
                        P.op("pe", lambda e, pso=pso, cc=cc, csl=csl: e.matmul(
                            pso[:, cc * 64:(cc + 1) * 64], Sb[:], qe[:, csl], start=False, stop=(cc == 1)),
                            reads=["Sb", "qe"], writes=[pko])
                        pss_, pks = PR.next()
                        P.op("pe", lambda e, pss_=pss_, kt=kt, rows=rows, tt=tt: e.matmul(
                            pss_[:, 0:128], kt[rows, :], vtm[tt][rows, :], start=True, stop=True),
                            reads=[ktk, ("vtm", tt)], writes=[pks])
                        ecol = tt * 128 + (cc + 1) * 64 - 1
                        P.op("dve", lambda e, ecol=ecol: e.tensor_scalar(out=Stmp[:], in0=Sf[:], scalar1=eb[:, ecol:ecol + 1],
                                                                        scalar2=None, op0=ALU.mult),
                             reads=["Sf", "eb"], writes=["Stmp"])
                        P.op("dve", lambda e, ecol=ecol, pss_=pss_: e.scalar_tensor_tensor(
                            out=Sf[:], in0=pss_[:, 0:128], scalar=eb[:, ecol:ecol + 1], in1=Stmp[:], op0=ALU.mult, op1=ALU.add),
                            reads=[pks, "eb", "Stmp"], writes=["Sf"])
                        P.op("act", lambda e: e.copy(out=Sb[:], in_=Sf[:]), reads=["Sf"], writes=["Sb"])
                    P.op("act", lambda e, pso=pso, sl=sl: e.copy(out=osb[:, sl], in_=pso[:, 0:128]), reads=[pko], writes=["osb"])
                P.op("act", lambda e: e.activation(out=sq[:], in_=osb[:], func=AF.Square), reads=["osb"], writes=["sq"])
                psn, pkn = PR.next()
                P.op("pe", lambda e, psn=psn: e.matmul(psn[:], ones[:], sq[:], start=True, stop=True), reads=["ones", "sq"], writes=[pkn])
                P.op("act", lambda e, psn=psn: e.activation(out=rstd[:], in_=psn[:], func=AF.Ln, scale=1.0 / 128, bias=EPS),
                     reads=[pkn], writes=["rstd"])
                P.op("act", lambda e: e.activation(out=rstd[:], in_=rstd[:], func=AF.Exp, scale=-0.5), reads=["rstd"], writes=["rstd"])
                P.op("dve", lambda e: e.scalar_tensor_tensor(out=tmpo[:], in0=osb[:], scalar=ng[:, 0:1], in1=rstd[:],
                                                              op0=ALU.mult, op1=ALU.mult), reads=["osb", "ng", "rstd"], writes=["tmpo"])
                P.op("dve", lambda e: e.tensor_tensor(out=ys[:], in0=tmpo[:], in1=gs[:], op=ALU.mult), reads=["tmpo", "gs"], writes=["ys"])
                P.dma("sp", yT[hh * 128:(hh + 1) * 128, tb * 512:(tb + 1) * 512], ys[:], reads=["ys"], writes=["ysrc"])


NSA_SCALE = 128 ** -0.5
NEGM = -1.0e7
NREL = 19


def emit_nsa(P, nc, PR, psUl, psDl, gT_sb, gcol, ident, io):
    x = io["x"]
    gT = io["gT"]
    ident_d = io["ident"]
    wq = io["wq"]
    wz = io["wz"]
    wkv = io["wkv"]
    wgl = io["wgl"]
    peT_d = io["peT"]
    w1_d = io["w1"]
    w2_d = io["w2"]
    slope_d = io["slope"]
    cst_d = io["cst"]
    R_d = io["Rt"]
    E_d = io["E"]
    selb0_d = io["selb0"]
    Mmat_d = io["Mmat"]
    A_d = io["At"]
    Badd_d = io["Bt"]
    SelR_d = io["SelR"]
    yT = io["yT"]
    qT_s = io["qT_s"]
    zT_s = io["zT_s"]
    kvT_s = io["kvT_s"]
    vtm_s = io["vtm_s"]
    tcmp_s = io["tcmp_s"]
    KC = D // 128
    if True:
        GT = P.sb("GT", [48, S], F32)
        with contextlib.ExitStack() as es1:
            hT = es1.enter_context(nc.sbuf_tensor(P.pre + "hT", [128, KC, S], BF16))
            wsb = [es1.enter_context(nc.sbuf_tensor(P.pre + "nw%d" % i, [128, KC, 128], BF16)) for i in range(2)]
            wglsb = es1.enter_context(nc.sbuf_tensor(P.pre + "wglsb", [128, KC, 48], BF16))
            stg = [es1.enter_context(nc.sbuf_tensor(P.pre + "stg%d" % i, [128, 512], BF16)) for i in range(2)]
            phase_hT(P, nc, PR, x, gT_sb, ident, hT, gcol=gcol, xsplit=io["xsplit"])
            si = [0]
            wi = [0]

            def load_w(src_v, col0):
                b = wi[0] % 2
                wi[0] += 1
                for c0 in range(0, KC, 8):
                    P.dma("pool", wsb[b][:, c0:c0 + 8, :], src_v[:, c0:c0 + 8, col0:col0 + 128], writes=[("nw", b)])
                return wsb[b], ("nw", b)

            def fm_chunk(src_v, col0, dst, func, scale):
                w, wk = load_w(src_v, col0)
                for tb in range(4):
                    def ev(ps, pk, tb=tb):
                        b = si[0] % 2
                        si[0] += 1
                        P.op("act", lambda e: e.activation(out=stg[b][:], in_=ps[:], func=func, scale=scale),
                             reads=[pk], writes=[("stg", b)])
                        P.dma("sp", dst[:, tb * 512:(tb + 1) * 512], stg[b][:], reads=[("stg", b)], writes=["scratch"])
                    proj_fm(P, PR, hT, w, wk, tb, ev)

            wq_v = wq.rearrange("(c p) n -> p c n", p=128)
            wz_v = wz.rearrange("(c p) n -> p c n", p=128)
            wkv_v = wkv.rearrange("(c p) n -> p c n", p=128)
            for hl in range(16):
                fm_chunk(wq_v, hl * 128, qT_s[hl], AF.Copy, NSA_SCALE)
                fm_chunk(wz_v, hl * 128, zT_s[hl], AF.Silu, 1.0)
            for gl in range(2):
                for n, i in enumerate((0, 1, 2, 4)):
                    fm_chunk(wkv_v, i * 256 + gl * 128, kvT_s[gl, n], AF.Copy, 1.0)
                for n, i in enumerate((3, 5)):
                    w, wk = load_w(wkv_v, i * 256 + gl * 128)
                    for tt in range(16):
                        ps, pk = PR.next()
                        for c in range(KC):
                            P.op("pe", lambda e, c=c, ps=ps, tt=tt, w=w: e.matmul(
                                ps[:, 0:128], hT[:, c, tt * 128:(tt + 1) * 128], w[:, c, :],
                                start=(c == 0), stop=(c == KC - 1)), reads=[wk, ("hT", tt // 4)], writes=[pk], sig=(c == KC - 1))
                        b = si[0] % 2
                        si[0] += 1
                        P.op("act", lambda e, ps=ps, b=b: e.copy(out=stg[b][:, 0:128], in_=ps[:, 0:128]), reads=[pk],
                             writes=[("stg", b)])
                        P.dma("sp", vtm_s[gl, n, tt * 128:(tt + 1) * 128, :], stg[b][:, 0:128], reads=[("stg", b)],
                              writes=["scratch"])
            for c0 in range(0, KC, 8):
                P.dma("pool", wglsb[:, c0:c0 + 8, :], wgl.rearrange("(c p) n -> p c n", p=128)[:, c0:c0 + 8, :], writes=["wgl"])
            for tb in range(4):
                ps, pk = PR.next()
                for c in range(KC):
                    P.op("pe", lambda e, c=c, ps=ps, tb=tb: e.matmul(ps[0:48, :], wglsb[:, c, :], hT[:, c, tb * 512:(tb + 1) * 512],
                                                                    start=(c == 0), stop=(c == KC - 1)),
                         reads=["wgl", ("hT", tb)], writes=[pk])
                P.op("act", lambda e, ps=ps, tb=tb: e.activation(out=GT[:, tb * 512:(tb + 1) * 512], in_=ps[0:48, :], func=AF.Sigmoid),
                     reads=[pk], writes=["GT"])
            barrier(P)
        Rt = P.sb("Rt_sb", [128, 13, 512], F32)
        SelR = P.sb("SelR_sb", [48, 48 * 128], F32)
        slope = P.sb("slope_sb", [128, 16], F32)
        cst = P.sb("cst_sb", [128, 16 * NREL], F32)
        E = P.sb("E_sb", [32, 2048], BF16)
        selbT = P.sb("selbT", [32, 2048], BF16)
        Mmat = P.sb("Mmat_sb", [128, 32], F32)
        At = P.sb("At_sb", [128, 256], F32)
        Bt = P.sb("Bt_sb", [128, 256], F32)
        onesb = P.sb("onesb", [128, 128], BF16)
        for k in range(13):
            P.dma("sp", Rt[:, k, :], R_d[k], writes=["Rt"])
        for t, d, k in ((SelR, SelR_d, "SelR"), (slope, slope_d, "slope"), (cst, cst_d, "cst"), (E, E_d, "E"),
                        (Mmat, Mmat_d, "Mmat"), (At, A_d, "At"), (Bt, Badd_d, "Bt")):
            P.dma("sp", t[:], d, writes=[k])
        P.op("dve", lambda e: e.memset(onesb[:], 1.0), writes=["onesb"])
        kvT = P.sb("kvT", [128, 4, S], BF16)
        vtm = P.sb("vtm", [128, 2, 16, 128], BF16)
        w1sb = P.sb("w1sb", [128, 32, 128], BF16)
        w2sb = P.sb("w2sb", [128, 128], BF16)
        peT = P.sb("peT_sb", [128, 32], F32)
        peTb = P.sb("peTb", [128, 32], BF16)
        cb_ = P.sb("cmpb", [128, 1], F32)
        xg = P.sb("xg", [128, 128], F32)
        xg2 = P.sb("xg2", [128, 128], F32)
        hidT = P.sb("hidT", [128, 128], BF16)
        KcT = P.sb("KcT", [128, 128], BF16)
        Vc = P.sb("Vc", [128, 128], BF16)
        qTb = P.sb("qTb", [128, S], BF16)
        zTb = P.sb("zTb", [128, S], BF16)
        tmp = [P.sb("atmp%d" % i, [128, 512], F32) for i in range(4)]
        Pf = P.sb("Pf", [128, 512], F32)
        Pb = [P.sb("Pb%d" % i, [128, 512], BF16) for i in range(4)]
        rden = P.sb("rden", [128, 512], F32)
        Ff = P.sb("Ff", [128, 512], F32)
        Tt = P.sb("Tt", [128, 512], F32)
        acc = P.sb("acc", [128, 512], F32)
        pn = P.sb("pn", [128, 512], F32)
        ys = P.sb("nys", [128, 512], BF16)
        pslc = P.sb("pslc", [128, 256], F32)
        sc = P.sb("sc", [128, 32], F32)
        sc2 = P.sb("sc2", [128, 32], F32)
        m8 = P.sb("m8", [128, 16], F32)
        selb = P.sb("selb", [128, 32], F32)
        ti_ = [0]
        bi_ = [0]
        pend = []

        def drain(keep=0):
            while len(pend) > keep:
                pend.pop(0)()

        def branch_step(kT_ap, kkey, v_ap, vkey, qsl, hl, rvar, cidx, nk, extra=None, first=False, last=False, want_f32=False):
            ps, pk = PR.next()
            P.op("pe", lambda e: e.matmul(ps[0:nk, :], kT_ap, qTb[:, qsl], start=True, stop=(extra is None)),
                 reads=[kkey, "qTb"], writes=[pk])
            if extra is not None:
                P.op("pe", lambda e: e.matmul(ps[0:nk, :], extra, selbT[:, qsl], start=False, stop=True),
                     reads=["E", "selbT"], writes=[pk])
            i = ti_[0] % 4
            ti_[0] += 1
            psU, psD = psUl[bi_[0] % 2], psDl[bi_[0] % 2]
            uk, dk = ("psU", bi_[0] % 2), ("psD", bi_[0] % 2)
            P.op("dve", lambda e: e.scalar_tensor_tensor(out=tmp[i][0:nk, :], in0=Rt[0:nk, rvar, :], scalar=slope[0:nk, hl:hl + 1],
                                                          in1=ps[0:nk, :], op0=ALU.mult, op1=ALU.add),
                 reads=["Rt", "slope", pk], writes=[("atmp", i)])
            if want_f32:
                P.op("act", lambda e: e.activation(out=Pf[0:nk, :], in_=tmp[i][0:nk, :], func=AF.Exp), reads=[("atmp", i)], writes=["Pf"])
                P.op("act", lambda e: e.copy(out=Pb[i][0:nk, :], in_=Pf[0:nk, :]), reads=["Pf"], writes=[("Pb", i)])
            elif cidx is None:
                P.op("act", lambda e: e.activation(out=Pb[i][0:nk, :], in_=tmp[i][0:nk, :], func=AF.Exp), reads=[("atmp", i)],
                     writes=[("Pb", i)])
            else:
                P.op("act", lambda e: e.activation(out=Pb[i][0:nk, :], in_=tmp[i][0:nk, :], func=AF.Exp,
                                                   bias=cst[0:nk, cidx:cidx + 1]), reads=[("atmp", i), "cst"], writes=[("Pb", i)])

            def stage_b():
                P.op("pe", lambda e: e.matmul(psU[:], v_ap, Pb[i][0:nk, :], start=first, stop=last), reads=[vkey, ("Pb", i)], writes=[uk])
                P.op("pe", lambda e: e.matmul(psD[:], onesb[0:nk, :], Pb[i][0:nk, :], start=first, stop=last),
                     reads=["onesb", ("Pb", i)], writes=[dk])
            pend.append(stage_b)

        def finish_branch(br, hl, qsl, mode):
            psU, psD = psUl[bi_[0] % 2], psDl[bi_[0] % 2]
            uk, dk = ("psU", bi_[0] % 2), ("psD", bi_[0] % 2)
            bi_[0] += 1

            def fin():
                P.op("dve", lambda e: e.tensor_scalar(out=rden[:], in0=psD[:], scalar1=1e-18, scalar2=None, op0=ALU.max),
                     reads=[dk], writes=["rden"])
                P.op("act", lambda e: e.activation(out=rden[:], in_=rden[:], func=AF.Ln), reads=["rden"], writes=["rden"])
                P.op("act", lambda e: e.activation(out=rden[:], in_=rden[:], func=AF.Exp, scale=-1.0), reads=["rden"], writes=["rden"])
                r = br * 16 + hl
                psG, gk = PR.next()
                P.op("pe", lambda e: e.matmul(psG[:], SelR[:, r * 128:(r + 1) * 128], GT[:, qsl], start=True, stop=True),
                     reads=["SelR", "GT"], writes=[gk])
                P.op("dve", lambda e: e.tensor_tensor(out=Ff[:], in0=rden[:], in1=psG[:], op=ALU.mult), reads=["rden", gk], writes=["Ff"])
                if mode == "ret":
                    P.op("dve", lambda e: e.tensor_tensor(out=Tt[:], in0=psU[:], in1=Ff[:], op=ALU.mult), reads=[uk, "Ff"], writes=["Tt"])
                elif mode == "set":
                    P.op("dve", lambda e: e.tensor_tensor(out=acc[:], in0=psU[:], in1=Ff[:], op=ALU.mult), reads=[uk, "Ff"], writes=["acc"])
                else:
                    P.op("dve", lambda e: e.tensor_tensor(out=Tt[:], in0=psU[:], in1=Ff[:], op=ALU.mult), reads=[uk, "Ff"], writes=["Tt"])
                    P.op("dve", lambda e: e.tensor_tensor(out=acc[:], in0=acc[:], in1=Tt[:], op=ALU.add), reads=["acc", "Tt"], writes=["acc"])
            pend.append(fin)

        def cidx_of(hl, m):
            return hl * NREL + (m + 15)

        for gl in range(2):
            for n in range(4):
                P.dma("sp", kvT[:, n, :], kvT_s[gl, n], reads=["scratch"], writes=["kvT"])
            for n in range(2):
                P.dma("sp", vtm[:, n], vtm_s[gl, n].rearrange("(t p) d -> p t d", p=128), reads=["scratch"], writes=["vtm"])
            P.dma("sp", selbT[:, 0:1024], selb0_d, writes=["selbT"])
            for which in range(2):
                for c0 in range(0, 32, 8):
                    P.dma("pool", w1sb[:, c0:c0 + 8, :], w1_d[which].rearrange("(l d) e -> d l e", d=128)[:, c0:c0 + 8, :], writes=["w1sb"])
                P.dma("pool", w2sb[:], w2_d[which], writes=["w2sb"])
                P.dma("sp", peT[:], peT_d[which], writes=["peT"])
                P.op("act", lambda e: e.copy(out=peTb[:], in_=peT[:]), reads=["peT"], writes=["peTb"])
                ps, pk = PR.next()
                for l in range(32):
                    P.op("pe", lambda e, l=l, ps=ps: e.matmul(ps[:, 0:127], w1sb[:, l, :], kvT[:, which, l:l + 16 * 126 + 1:16],
                                                             start=(l == 0), stop=(l == 31)), reads=["w1sb", "kvT"], writes=[pk])
                ps2, pk2 = PR.next()
                for l in range(32):
                    P.op("pe", lambda e, l=l, ps2=ps2: e.matmul(ps2[:, 0:1], w1sb[:, l, :], peTb[:, l:l + 1],
                                                               start=(l == 0), stop=(l == 31)), reads=["w1sb", "peTb"], writes=[pk2])
                P.op("act", lambda e, ps2=ps2: e.copy(out=cb_[:], in_=ps2[:, 0:1]), reads=[pk2], writes=["cmpb"])
                P.op("act", lambda e, ps=ps: e.activation(out=xg[:, 0:127], in_=ps[:, 0:127], func=AF.Identity, bias=cb_[:, 0:1]),
                     reads=[pk, "cmpb"], writes=["xg"])
                P.op("dve", lambda e: e.tensor_tensor(out=xg2[:, 0:127], in0=xg[:, 0:127], in1=xg[:, 0:127], op=ALU.mult), reads=["xg"], writes=["xg2"])
                P.op("dve", lambda e: e.tensor_scalar(out=xg2[:, 0:127], in0=xg2[:, 0:127], scalar1=0.044715, scalar2=1.0,
                                                       op0=ALU.mult, op1=ALU.add), reads=["xg2"], writes=["xg2"])
                P.op("dve", lambda e: e.tensor_tensor(out=xg2[:, 0:127], in0=xg2[:, 0:127], in1=xg[:, 0:127], op=ALU.mult),
                     reads=["xg", "xg2"], writes=["xg2"])
                P.op("act", lambda e: e.activation(out=xg2[:, 0:127], in_=xg2[:, 0:127], func=AF.Sigmoid, scale=1.5957691216),
                     reads=["xg2"], writes=["xg2"])
                P.op("dve", lambda e: e.tensor_tensor(out=hidT[:, 0:127], in0=xg[:, 0:127], in1=xg2[:, 0:127], op=ALU.mult),
                     reads=["xg", "xg2"], writes=["hidT"])
                ps3, pk3 = PR.next()
                if which == 0:
                    P.op("pe", lambda e, ps3=ps3: e.matmul(ps3[:, 0:127], w2sb[:], hidT[:, 0:127], start=True, stop=True),
                         reads=["w2sb", "hidT"], writes=[pk3])
                    P.op("act", lambda e, ps3=ps3: e.copy(out=KcT[:, 0:127], in_=ps3[:, 0:127]), reads=[pk3], writes=["KcT"])
                else:
                    P.op("pe", lambda e, ps3=ps3: e.matmul(ps3[0:127, 0:128], hidT[:, 0:127], w2sb[:], start=True, stop=True),
                         reads=["w2sb", "hidT"], writes=[pk3])
                    P.op("act", lambda e, ps3=ps3: e.copy(out=Vc[0:127, :], in_=ps3[0:127, 0:128]), reads=[pk3], writes=["Vc"])
            for j in range(8):
                hl = gl * 8 + j
                P.dma("sp", qTb[:], qT_s[hl], reads=["scratch"], writes=["qTb"])
                for qb in range(4):
                    qsl = slice(qb * 512, (qb + 1) * 512)
                    branch_step(KcT[:, 0:127], "KcT", Vc[0:127, :], "Vc", qsl, hl, 9 + qb, None, 127, first=True, last=True,
                                want_f32=(qb >= 2))
                    finish_branch(0, hl, qsl, "ret")
                    drain(0)
                    P.dma("sp", tcmp_s[j, :, qsl], Tt[:], reads=["Tt"], writes=["tcmp"])
                    if qb >= 2:
                        P.op("dve", lambda e: e.tensor_tensor(out=pn[0:127, :], in0=Pf[0:127, :], in1=rden[0:127, :], op=ALU.mult),
                             reads=["Pf", "rden"], writes=["pn"])
                        ps, pk = PR.next()
                        for k in range(4):
                            P.op("pe", lambda e, k=k, ps=ps: e.matmul(ps[:, k * 32:(k + 1) * 32], pn[0:127, k * 128:(k + 1) * 128],
                                                                     Mmat[0:127, :], start=True, stop=True),
                                 reads=["pn", "Mmat"], writes=[pk])
                        dst = pslc[:, (qb - 2) * 128:(qb - 1) * 128]
                        if j == 0:
                            P.op("dve", lambda e, ps=ps, dst=dst: e.tensor_copy(out=dst, in_=ps[:, 0:128]), reads=[pk], writes=["pslc"])
                        else:
                            P.op("dve", lambda e, ps=ps, dst=dst: e.tensor_tensor(out=dst, in0=dst, in1=ps[:, 0:128], op=ALU.add),
                                 reads=[pk, "pslc"], writes=["pslc"])
            for ti in range(8):
                csl = slice(ti * 32, (ti + 1) * 32)
                P.op("dve", lambda e, csl=csl: e.tensor_tensor(out=sc[:], in0=pslc[:, csl], in1=At[:, csl], op=ALU.mult),
                     reads=["pslc", "At"], writes=["sc"])
                P.op("dve", lambda e, csl=csl: e.tensor_tensor(out=sc[:], in0=sc[:], in1=Bt[:, csl], op=ALU.add), reads=["sc", "Bt"], writes=["sc"])
                P.op("dve", lambda e: e.max(out=m8[:, 0:8], in_=sc[:]), reads=["sc"], writes=["m8"])
                P.op("dve", lambda e: e.match_replace(out=sc2[:], in_to_replace=m8[:, 0:8], in_values=sc[:], imm_value=-1e30),
                     reads=["sc", "m8"], writes=["sc2"])
                P.op("dve", lambda e: e.max(out=m8[:, 8:16], in_=sc2[:]), reads=["sc2"], writes=["m8"])
                P.op("dve", lambda e: e.tensor_scalar(out=selb[:], in0=sc[:], scalar1=m8[:, 15:16], scalar2=None, op0=ALU.is_ge),
                     reads=["sc", "m8"], writes=["selb"])
                P.op("dve", lambda e: e.tensor_scalar(out=selb[:], in0=selb[:], scalar1=30000.0, scalar2=-30000.0, op0=ALU.mult, op1=ALU.add),
                     reads=["selb"], writes=["selb"])
                ps, pk = PR.next()
                P.op("pe", lambda e, ps=ps: e.transpose(out=ps[0:32, 0:128], in_=selb[:], identity=ident[:]), reads=["selb", "ident"], writes=[pk])
                P.op("act", lambda e, ps=ps, ti=ti: e.copy(out=selbT[:, 1024 + ti * 128:1024 + (ti + 1) * 128], in_=ps[0:32, 0:128]),
                     reads=[pk], writes=["selbT"])
            for j in range(8):
                hl = gl * 8 + j
                drain(0)
                P.dma("sp", qTb[:], qT_s[hl], reads=["scratch"], writes=["qTb"])
                P.dma("sp", zTb[:], zT_s[hl], reads=["scratch"], writes=["zTb"])
                for qb in range(4):
                    qsl = slice(qb * 512, (qb + 1) * 512)
                    drain(0)
                    P.dma("sp", pn[:], tcmp_s[j, :, qsl], reads=["tcmp"], writes=["pn"])
                    kts = list(range(0, 4 * qb + 4))
                    for kt in kts:
                        m = 4 * qb - kt
                        rvar, ci = (0, cidx_of(hl, -m)) if m >= 1 else (1 + (-m), cidx_of(hl, -m))
                        branch_step(kvT[:, 2, kt * 128:(kt + 1) * 128], "kvT", vtm[:, 0, kt, :], "vtm", qsl, hl, rvar, ci, 128,
                                    extra=(E[:, kt * 128:(kt + 1) * 128] if qb >= 2 else None), first=(kt == kts[0]),
                                    last=(kt == kts[-1]))
                        drain(2)
                    finish_branch(1, hl, qsl, "set")
                    kts = list(range(max(0, 4 * qb - 4), 4 * qb + 4))
                    for kt in kts:
                        m = 4 * qb - kt
                        if m >= 1:
                            rvar = 5 + (4 - m)
                        else:
                            rvar = 1 + (-m)
                        branch_step(kvT[:, 3, kt * 128:(kt + 1) * 128], "kvT", vtm[:, 1, kt, :], "vtm", qsl, hl, rvar, cidx_of(hl, -m), 128,
                                    first=(kt == kts[0]), last=(kt == kts[-1]))
                        drain(2)
                    finish_branch(2, hl, qsl, "add")
                    drain(0)
                    P.op("dve", lambda e: e.tensor_tensor(out=acc[:], in0=acc[:], in1=pn[:], op=ALU.add), reads=["acc", "pn"], writes=["acc"])
                    P.op("dve", lambda e, qsl=qsl: e.tensor_tensor(out=ys[:], in0=acc[:], in1=zTb[:, qsl], op=ALU.mult),
                         reads=["acc", "zTb"], writes=["nys"])
                    P.dma("sp", yT[hl * 128:(hl + 1) * 128, qsl], ys[:], reads=["nys"], writes=["ysrc"])


IDENT = np.eye(128, dtype=np.float32)


def gainT(g):
    return np.ascontiguousarray(g.reshape(32, 128).T).astype(np.float32)


def colT(v, h, n=16):
    return np.ascontiguousarray(v[h * n * 128:(h + 1) * n * 128].reshape(n, 128).T).astype(np.float32)

def nsa_consts(h):
    heads = np.arange(16) + 16 * h
    sl = (2.0 ** (-8.0 * (heads + 1) / 32)).astype(np.float64)
    slope = np.broadcast_to(sl[None, :], (128, 16)).astype(np.float32)
    cst = np.zeros((128, 16, NREL), np.float64)
    for m in range(-15, 4):
        cst[:, :, m + 15] = sl[None, :] * 128.0 * m
    kj = np.arange(128)[:, None].astype(np.float64)
    qi = np.arange(512)[None, :].astype(np.float64)
    base = kj - qi
    Rt = np.zeros((13, 128, 512), np.float64)
    Rt[0] = base
    for v in range(4):
        Rt[1 + v] = np.where(-128 * v + qi - kj >= 0, base, NEGM)
    for n, rel in enumerate((512, 384, 256, 128)):
        Rt[5 + n] = np.where(rel + qi - kj < 512, base, NEGM)
    for qb in range(4):
        dist = 512 * qb + qi - 16 * kj - 31
        Rt[9 + qb] = np.where(dist >= 0, -dist, NEGM)
    key = np.arange(2048)
    E = (key[None, :] // 64 == np.arange(32)[:, None]).astype(np.float32)
    t = np.arange(1024)
    selb0 = np.where(np.arange(32)[:, None] <= (t[None, :] // 64), 0.0, -30000.0)
    Mmat = np.zeros((128, 32), np.float32)
    for i in range(32):
        for off, wgt in ((-1, 1), (0, 2), (1, 2), (2, 2), (3, 1)):
            n = 4 * i + off
            if 0 <= n < 127:
                Mmat[n, i] += wgt
    At = np.zeros((128, 8, 32), np.float32)
    Bt = np.zeros((128, 8, 32), np.float32)
    for ti in range(8):
        tt = 1024 + ti * 128 + np.arange(128)
        cur = (tt // 64)[:, None]
        blk = np.arange(32)[None, :]
        forced = (blk == 0) | (blk == cur) | (blk == cur - 1)
        future = blk > cur
        At[:, ti] = (~forced & ~future)
        Bt[:, ti] = np.where(forced, 1e6, np.where(future, -1.0, 0.0))
    SelR = np.zeros((48, 48, 128), np.float32)
    for r in range(48):
        SelR[r, r, :] = 1.0
    return {"slope": slope, "cst": np.ascontiguousarray(cst.reshape(128, 16 * NREL)).astype(np.float32),
            "Rt": Rt.astype(np.float32), "E": E.astype(ml_dtypes.bfloat16), "selb0": selb0.astype(ml_dtypes.bfloat16),
            "Mmat": Mmat, "At": np.ascontiguousarray(At.reshape(128, 256)), "Bt": np.ascontiguousarray(Bt.reshape(128, 256)),
            "SelR": np.ascontiguousarray(SelR.reshape(48, 48 * 128))}


PAIRS = [[0, 1], [2, 3], [4, 5], [6, 7]]
NSA_IN = ("wq", [D, 2048]), ("wz", [D, 2048]), ("wkv", [D, 1536]), ("wgl", [D, 48]), ("peT", [2, 128, 32]), \
    ("w1", [2, 4096, 128]), ("w2", [2, 128, 128])
NSA_CONST = ("slope", [128, 16], F32), ("cst", [128, 16 * NREL], F32), ("Rt", [13, 128, 512], F32), ("E", [32, 2048], BF16), \
    ("selb0", [32, 1024], BF16), ("Mmat", [128, 32], F32), ("At", [128, 256], F32), ("Bt", [128, 256], F32), \
    ("SelR", [48, 48 * 128], F32)
RG_IN = ("wx", [D, 2048]), ("wz", [D, 2048]), ("cw", [128, 64]), ("cb", [128, 16]), ("lam", [128, 16]), ("gb", [128, 32]), \
    ("gw", [2, 8, 256, 256])
HG_IN = ("wq", [D, 2048]), ("wf", [D, 2048]), ("wv", [D, 2048]), ("wg", [D, 2048]), ("lbl", [128, 64]), ("ng", [128, 1]), \
    ("ones", [128, 128]), ("rmask", [128, 512]), ("bdmask", [128, 128])


def build_fused():
    nc = bass.Bass("TRN2", target_bir_lowering=False)
    I = lambda n, s, dt=F32: nc.dram_tensor(n, list(s), dt, kind="ExternalInput").ap()
    T = lambda n, s, dt: nc.dram_tensor(n, list(s), dt, kind="Internal").ap()
    x = I("x", [S, D])
    x_my = I("x_my", [S, 2048])
    gT_d = I("gT", [128, 128])
    post_d = I("post", [4, 128, 2048])
    ident_d = I("ident", [128, 128])
    consts = {n: I(n, s, dt) for n, s, dt in NSA_CONST}
    lio = {}
    for i in range(4):
        spec = (NSA_IN, RG_IN, HG_IN)[i % 3]
        lio[i] = {n: I("L%d_%s" % (i, n), s) for n, s in spec}
        lio[i]["wo"] = I("L%d_wo" % i, [D, 2048])
    out = nc.dram_tensor("out", [S, 2048], F32, kind="ExternalOutput").ap()
    ysrc = T("ysrc", [2048, S], BF16)
    yT_g = T("yT_g", [D, S], BF16)
    xsrc = [T("xsrc0", [S, 2048], F32), T("xsrc1", [S, 2048], F32)]
    x_g = T("x_g", [2 * S, 2048], F32)
    ss_src = T("ss_src", [128, 8], F32)
    ss_g = T("ss_g", [256, 8], F32)
    scratch = {"qT_s": T("qT_s", [16, 128, S], BF16), "zT_s": T("zT_s", [16, 128, S], BF16),
               "kvT_s": T("kvT_s", [2, 4, 128, S], BF16), "vtm_s": T("vtm_s", [2, 2, S, 128], BF16),
               "tcmp_s": T("tcmp_s", [8, 128, S], F32)}
    with contextlib.ExitStack() as es:
        P = Prog(nc, es)
        PR = PsumRing(P, 4)
        psUl = [P.ps("psU%d" % i, [128, 512]) for i in range(2)]
        psDl = [P.ps("psD%d" % i, [128, 512]) for i in range(2)]
        gT_sb = P.sb("gT_sb", [128, 128], F32)
        ident = P.sb("ident_sb", [128, 128], F32)
        P.dma("sp", gT_sb[:], gT_d, writes=["gT"])
        P.dma("sp", ident[:], ident_d, writes=["ident"])
        for i in range(4):
            kind = i % 3
            io = dict(lio[i])
            io.update(gT=None, ident=None, x=(x if i == 0 else x_g), yT=ysrc, xsplit=(i > 0))
            with contextlib.ExitStack() as esl:
                P.es_cur = esl
                P.pre = "L%dm_" % i
                if kind == 0:
                    io.update(consts)
                    io.update(scratch)
                    emit_nsa(P, nc, PR, psUl, psDl, gT_sb, i * 32, ident, io)
                elif kind == 1:
                    emit_rg(P, nc, PR, gT_sb, i * 32, ident, io)
                else:
                    emit_hg(P, nc, PR, gT_sb, i * 32, ident, io, layer=i)
                barrier(P)
            for k in range(4):
                P.cc("AllGather", PAIRS, ysrc[k * 512:(k + 1) * 512, :], yT_g[k * 1024:(k + 1) * 1024, :], reads=["ysrc"], writes=["yT_g"])
            barrier(P)
            xo = out if i == 3 else xsrc[i % 2]

            def gather_x(k0, k1, xo=xo):
                for k in range(k0, k1):
                    P.cc("AllGather", PAIRS, xo[k * 256:(k + 1) * 256, :], x_g[k * 512:(k + 1) * 512, :],
                         reads=["x_out0"], writes=["x_g"])
            with contextlib.ExitStack() as esl:
                P.es_cur = esl
                P.pre = "L%do_" % i
                x_res = x_my if i == 0 else xsrc[(i - 1) % 2]
                emit_outproj(P, nc, PR, yT_g, x_res, io["wo"], post_d[i], xo, ss_src, ss_g,
                             mid_cc=((lambda: gather_x(0, 4)) if i < 3 else None))
                barrier(P)
            P.es_cur = None
            if i < 3:
                gather_x(4, 8)
                barrier(P)
        P.finish()
    return nc


def kernel(x, pre_norm_gain, post_norm_gain, nsa_w_in, nsa_cmp_pe, nsa_cmp_w1, nsa_cmp_w2, nsa_w_out,
           rg_w_in, rg_conv_w, rg_conv_b, rg_gate_w, rg_gate_b, rg_lambda, rg_w_out,
           hg_w_in, hg_lb_logits, hg_norm_gain, hg_w_out):
    f = lambda a: np.asarray(a, dtype=np.float32)
    ca = np.ascontiguousarray
    x = f(x)
    pre, post = f(pre_norm_gain), f(post_norm_gain)
    common = {
        "gT": ca(np.concatenate([gainT(pre[i]) for i in range(4)], axis=1)),
        "ident": IDENT,
    }
    rmask = np.ones((128, 512), np.float32)
    rmask[:, ::64] = 0.0
    si = np.arange(128)[:, None]
    ti = np.arange(128)[None, :]
    bdmask = ((si // 64 == ti // 64) & (ti >= si)).astype(np.float32)
    in_maps = []
    for c in range(NCORES):
        b, h = c // 2, c % 2
        m = dict(common)
        m["x"] = ca(x[b])
        m["x_my"] = ca(x[b][:, h * 2048:(h + 1) * 2048])
        m["post"] = ca(np.broadcast_to(post[:, None, h * 2048:(h + 1) * 2048], (4, 128, 2048))).astype(np.float32)
        m.update(nsa_consts(h))
        for i, j in ((0, 0), (3, 1)):
            w_in = f(nsa_w_in[j])
            pfx = "L%d_" % i
            m[pfx + "wq"] = ca(w_in[:, h * 2048:(h + 1) * 2048])
            m[pfx + "wz"] = ca(w_in[:, 7264 + h * 2048:7264 + (h + 1) * 2048])
            m[pfx + "wkv"] = ca(np.concatenate(
                [w_in[:, 4096 + k * 512 + h * 256:4096 + k * 512 + (h + 1) * 256] for k in range(6)], axis=1))
            m[pfx + "wgl"] = ca(np.concatenate(
                [w_in[:, 7168 + br * 32 + h * 16:7168 + br * 32 + (h + 1) * 16] for br in range(3)], axis=1))
            m[pfx + "peT"] = ca(f(nsa_cmp_pe[j]).transpose(0, 2, 1))
            m[pfx + "w1"] = ca(f(nsa_cmp_w1[j]))
            m[pfx + "w2"] = ca(f(nsa_cmp_w2[j]))
            m[pfx + "wo"] = ca(f(nsa_w_out[j])[:, h * 2048:(h + 1) * 2048])
        w_in = f(rg_w_in[0])
        m["L1_wx"] = ca(w_in[:, h * 2048:(h + 1) * 2048])
        m["L1_wz"] = ca(w_in[:, 4096 + h * 2048:4096 + (h + 1) * 2048])
        cw = f(rg_conv_w[0])
        m["L1_cw"] = ca(np.stack([colT(cw[k], h) for k in range(4)], axis=-1).reshape(128, 64))
        m["L1_cb"] = colT(f(rg_conv_b[0]), h)
        m["L1_lam"] = colT(f(rg_lambda[0]), h)
        gbv = f(rg_gate_b[0])
        m["L1_gb"] = ca(np.stack([colT(gbv[k].reshape(-1), h) for k in range(2)], axis=1).reshape(128, 32))
        m["L1_gw"] = ca(f(rg_gate_w[0])[:, h * 8:(h + 1) * 8])
        m["L1_wo"] = ca(f(rg_w_out[0])[:, h * 2048:(h + 1) * 2048])
        w_in = f(hg_w_in[0])
        for k, n in enumerate(("wq", "wf", "wv", "wg")):
            m["L2_" + n] = ca(w_in[:, k * 4096 + h * 2048:k * 4096 + (h + 1) * 2048])
        lbl = f(hg_lb_logits)
        m["L2_lbl"] = ca(np.concatenate([colT(lbl[l], h) for l in range(4)], axis=1))
        m["L2_ng"] = ca(f(hg_norm_gain[0]).reshape(128, 1))
        m["L2_ones"] = np.ones((128, 128), np.float32)
        m["L2_rmask"] = rmask
        m["L2_bdmask"] = bdmask
        m["L2_wo"] = ca(f(hg_w_out[0])[:, h * 2048:(h + 1) * 2048])
        in_maps.append(m)
    nc = build_fused()
    res = run_bass_kernel_spmd(nc, in_maps, core_ids=list(range(NCORES)))
    out = np.empty((B, S, D), np.float32)
    for c in range(NCORES):
        b, h = c // 2, c % 2
        out[b][:, h * 2048:(h + 1) * 2048] = res.results[c]["out"]
    return out
```

```python
import contextlib
import numpy as np
import ml_dtypes
import concourse.bass as bass
import concourse.mybir as mybir
from concourse.bass_utils import run_bass_kernel_spmd

F32 = mybir.dt.float32
BF16 = mybir.dt.bfloat16
AF = mybir.ActivationFunctionType
ALU = mybir.AluOpType
AX = mybir.AxisListType

D = 4096
B = 4
S = 2048
EPS = 1e-6
NCORES = 8


class Prog:
    ENGS = ("pe", "act", "dve", "pool", "sp")
    NDS = 24

    def __init__(self, nc, es, self_sync=True):
        self.nc = nc
        self.es = es
        self.self_sync = self_sync
        self.eng = dict(pe=nc.tensor, act=nc.scalar, dve=nc.vector, pool=nc.gpsimd, sp=nc.sync)
        self.sem = {e: es.enter_context(nc.semaphore("s_" + e)) for e in self.ENGS}
        self.cnt = {e: 0 for e in self.ENGS}
        self.seen = {e: {} for e in self.ENGS}
        self.dsem = [es.enter_context(nc.semaphore("d%d" % i)) for i in range(self.NDS)]
        self.dcnt = [0] * self.NDS
        self.drr = 0
        self.state = {}
        self.ninst = 0
        self.pre = ""
        self.es_cur = None
        self.ccsem = es.enter_context(nc.semaphore("ccsem"))
        self.cccnt = 0

    def sb(self, name, shape, dt):
        es = self.es_cur if self.es_cur is not None else self.es
        return es.enter_context(self.nc.sbuf_tensor(self.pre + name, list(shape), dt))

    def ps(self, name, shape, dt=F32):
        return self.es.enter_context(self.nc.psum_tensor(name, list(shape), dt))

    def _deps(self, reads, writes):
        deps = []
        for k in reads:
            st = self.state.get(k)
            if st is not None and st[0] is not None:
                deps.append(st[0])
        for k in writes:
            st = self.state.get(k)
            if st is not None:
                if st[0] is not None:
                    deps.append(st[0])
                deps.extend(st[1].values())
        return deps

    def _wait(self, e, deps):
        best = {}
        for (sid, sem, val) in deps:
            if sid not in best or best[sid][1] < val:
                best[sid] = (sem, val)
        for sid, (sem, val) in best.items():
            if self.seen[e].get(sid, 0) >= val:
                continue
            if sid == e and (e == "pe" or not self.self_sync):
                continue
            self.eng[e].wait_ge(sem, val)
            self.ninst += 1
            self.seen[e][sid] = val

    def _record(self, tok, reads, writes):
        for k in reads:
            st = self.state.get(k)
            if st is None:
                st = [None, {}]
                self.state[k] = st
            st[1][tok[0]] = tok
        for k in writes:
            self.state[k] = [tok, {}]

    def op(self, e, fn, reads=(), writes=(), sig=True):
        self._wait(e, self._deps(reads, writes))
        ins = fn(self.eng[e])
        self.ninst += 1
        if sig:
            self.cnt[e] += 1
            ins.then_inc(self.sem[e], 1)
            self._record((e, self.sem[e], self.cnt[e]), reads, writes)
        else:
            self._record((e, self.sem[e], self.cnt[e] + 1), reads, writes)

    def dma(self, q, out, in_, reads=(), writes=()):
        i = self.drr
        self.drr = (i + 1) % self.NDS
        sem = self.dsem[i]
        sid = "d%d" % i
        deps = self._deps(reads, writes)
        if self.dcnt[i] > 0:
            deps.append((sid, sem, self.dcnt[i]))
        self._wait(q, deps)
        self.eng[q].dma_start(out=out, in_=in_).then_inc(sem, 16)
        self.ninst += 1
        self.dcnt[i] += 16
        self._record((sid, sem, self.dcnt[i]), reads, writes)

    def cc(self, kind, groups, in_, out, reads=(), writes=()):
        deps = self._deps(reads, writes)
        self._wait("pool", deps)
        self.eng["pool"].collective_compute(kind, ALU.bypass, replica_groups=groups, ins=[in_.opt()], outs=[out.opt()]).then_inc(self.ccsem)
        self.ninst += 1
        self.cccnt += 1
        self._record(("cc", self.ccsem, self.cccnt), reads, writes)

    def finish(self):
        for i in range(self.NDS):
            if self.dcnt[i] > 0:
                self.eng["sp"].wait_ge(self.dsem[i], self.dcnt[i])
        for e in self.ENGS:
            if e != "sp" and self.cnt[e] > 0:
                self.eng["sp"].wait_ge(self.sem[e], self.cnt[e])
        if self.cccnt > 0:
            self.eng["sp"].wait_ge(self.ccsem, self.cccnt)


class PsumRing:
    def __init__(self, P, n=8):
        self.P = P
        self.t = [P.ps("psr%d" % i, [128, 512]) for i in range(n)]
        self.i = 0

    def next(self):
        i = self.i % len(self.t)
        self.i += 1
        return self.t[i], ("psr", i)


def barrier(P):
    toks = []
    for e in P.ENGS:
        if P.cnt[e] > 0:
            toks.append((e, P.sem[e], P.cnt[e]))
    for i in range(P.NDS):
        if P.dcnt[i] > 0:
            toks.append(("d%d" % i, P.dsem[i], P.dcnt[i]))
    if P.cccnt > 0:
        toks.append(("cc", P.ccsem, P.cccnt))
    for e in P.ENGS:
        P._wait(e, [t for t in toks if t[0] != e])


def phase_hT(P, nc, PR, x, gT_sb, ident, hT, ntok=S, gcol=0, xsplit=False):
    KC = D // 128
    with contextlib.ExitStack() as es2:
        xs = [es2.enter_context(nc.sbuf_tensor(P.pre + "hx%d" % i, [128, D], F32)) for i in range(2)]
        junk = es2.enter_context(nc.sbuf_tensor(P.pre + "hjunk", [128, D], BF16))
        st = es2.enter_context(nc.sbuf_tensor(P.pre + "hst", [128, 8], F32))
        for tt in range(ntok // 128):
            xb = xs[tt % 2]
            xk = ("hx", tt % 2)
            if xsplit:
                for r_ in range(2):
                    row = (tt // 2) * 512 + r_ * 256 + (tt % 2) * 128
                    P.dma("sp", xb[:, r_ * 2048:(r_ + 1) * 2048], x[row:row + 128, :], reads=["x_g"], writes=[xk])
            else:
                P.dma("sp", xb[:], x[tt * 128:(tt + 1) * 128, :], reads=["x_g"], writes=[xk])
            P.op("act", lambda e: e.activation(out=junk[:], in_=xb[:], func=AF.Square, accum_out=st[:, 0:1]),
                 reads=[xk], writes=["hjunk", "hst0"])
            P.op("dve", lambda e: e.tensor_scalar(out=st[:, 1:2], in0=st[:, 0:1], scalar1=1.0 / D, scalar2=EPS,
                                                   op0=ALU.mult, op1=ALU.add), reads=["hst0"], writes=["hst1"])
            P.op("act", lambda e: e.activation(out=st[:, 2:3], in_=st[:, 1:2], func=AF.Sqrt),
                 reads=["hst1"], writes=["hst2"])
            P.op("dve", lambda e: e.reciprocal(out=st[:, 3:4], in_=st[:, 2:3]), reads=["hst2"], writes=["hst3"])
            P.op("dve", lambda e: e.tensor_scalar(out=xb[:], in0=xb[:], scalar1=st[:, 3:4], scalar2=None,
                                                   op0=ALU.mult), reads=[xk, "hst3"], writes=[xk])
            for c0 in range(0, KC, 4):
                ps, pk = PR.next()
                for k in range(4):
                    c = c0 + k
                    P.op("pe", lambda e, c=c, k=k, ps=ps: e.transpose(
                        out=ps[:, k * 128:(k + 1) * 128], in_=xb[:, c * 128:(c + 1) * 128], identity=ident[:]),
                        reads=[xk, "ident"], writes=[pk])
                for k in range(4):
                    c = c0 + k
                    P.op("act", lambda e, c=c, k=k, ps=ps, tt=tt: e.activation(
                        out=hT[:, c, tt * 128:(tt + 1) * 128], in_=ps[:, k * 128:(k + 1) * 128],
                        func=AF.Copy, scale=gT_sb[:, gcol + c:gcol + c + 1]),
                        reads=[pk, "gT"], writes=[("hT", tt // 4)])
        barrier(P)


def proj_fm(P, PR, hT, w_chunk, wkey, tb, evac):
    KC = D // 128
    ps, pk = PR.next()
    for c in range(KC):
        P.op("pe", lambda e, c=c, ps=ps: e.matmul(ps[:], w_chunk[:, c, :], hT[:, c, tb * 512:(tb + 1) * 512],
                                                  start=(c == 0), stop=(c == KC - 1)),
             reads=[wkey, ("hT", tb)], writes=[pk], sig=(c == KC - 1))
    evac(ps, pk)


def emit_outproj(P, nc, PR, yT, x, w, gain, out, ss_src, ss_g, mid_cc=None):
    KC = D // 128
    NB = 256
    TP = 1024
    NCL = 2048
    NTT = TP // 128
    if True:
        yT_sb = P.sb("yT_sb", [128, KC, TP], BF16)
        w_sb = [P.sb("w_sb%d" % i, [128, KC, NB], BF16) for i in range(2)]
        o_sb = [P.sb("o_sb%d" % i, [128, NCL], F32) for i in range(NTT)]
        x_sb = [P.sb("x_sb%d" % i, [128, NCL], F32) for i in range(2)]
        g_sb = P.sb("g_sb", [128, NCL], F32)
        junk = P.sb("junk", [128, NCL], BF16)
        ssq = P.sb("ssq", [128, NTT], F32)
        ssg = P.sb("ssg", [128, 2, NTT], F32)
        st = P.sb("ost", [128, 4 * NTT], F32)
        yT_v = yT.rearrange("(c p) t -> p c t", p=128)
        w_v = w.rearrange("(c p) n -> p c n", p=128)
        P.dma("sp", g_sb[:], gain, writes=["g"])
        wi = 0
        for p_ in range(S // TP):
            for r_ in range(2):
                for k_ in range(4):
                    c0_, s0_ = r_ * 16 + k_ * 4, k_ * 8 + r_ * 4
                    P.dma("sp", yT_sb[:, c0_:c0_ + 4, :], yT_v[:, s0_:s0_ + 4, p_ * TP:(p_ + 1) * TP], reads=["yT_g"], writes=["yT"])
            for nb in range(NCL // NB):
                wb = wi % 2
                wi += 1
                for c0 in range(0, KC, 8):
                    P.dma("pool", w_sb[wb][:, c0:c0 + 8, :], w_v[:, c0:c0 + 8, nb * NB:(nb + 1) * NB],
                          writes=[("w", wb, c0)])
                for tt in range(NTT):
                    ps, pk = PR.next()
                    for c in range(KC):
                        P.op("pe", lambda e, c=c, tt=tt, ps=ps, wb=wb: e.matmul(
                            ps[:, 0:NB], yT_sb[:, c, tt * 128:(tt + 1) * 128], w_sb[wb][:, c, :],
                            start=(c == 0), stop=(c == KC - 1)),
                            reads=["yT", ("w", wb, (c // 8) * 8)], writes=[pk], sig=(c == KC - 1))
                    P.op("act", lambda e, tt=tt, ps=ps, nb=nb: e.copy(
                        out=o_sb[tt][:, nb * NB:(nb + 1) * NB], in_=ps[:, 0:NB]),
                        reads=[pk], writes=[("o", tt)])
            if p_ == 1 and mid_cc is not None:
                mid_cc()
            for tt in range(NTT):
                P.op("act", lambda e, tt=tt: e.activation(out=junk[:], in_=o_sb[tt][:], func=AF.Square,
                                                         accum_out=ssq[:, tt:tt + 1]),
                     reads=[("o", tt)], writes=["junk", "ssq"])
            P.dma("sp", ss_src, ssq[:], reads=["ssq"], writes=["ss_src"])
            P.cc("AllGather", PAIRS, ss_src, ss_g, reads=["ss_src"], writes=["ss_g"])
            P.dma("sp", ssg[:], ss_g.rearrange("(r p) c -> p r c", p=128), reads=["ss_g"], writes=["ssg"])
            P.op("dve", lambda e: e.tensor_tensor(out=st[:, 0:NTT], in0=ssg[:, 0, :], in1=ssg[:, 1, :], op=ALU.add),
                 reads=["ssg"], writes=["st0"])
            P.op("dve", lambda e: e.tensor_scalar(out=st[:, NTT:2 * NTT], in0=st[:, 0:NTT], scalar1=1.0 / D, scalar2=EPS,
                                                   op0=ALU.mult, op1=ALU.add), reads=["st0"], writes=["st1"])
            P.op("act", lambda e: e.activation(out=st[:, 2 * NTT:3 * NTT], in_=st[:, NTT:2 * NTT], func=AF.Sqrt),
                 reads=["st1"], writes=["st2"])
            P.op("dve", lambda e: e.reciprocal(out=st[:, 3 * NTT:4 * NTT], in_=st[:, 2 * NTT:3 * NTT]), reads=["st2"], writes=["st3"])
            for tt in range(NTT):
                t0 = p_ * TP + tt * 128
                xb, xk = x_sb[tt % 2], ("x", tt % 2)
                P.dma("sp", xb[:], x[t0:t0 + 128, :], reads=["x_my"], writes=[xk])
                P.op("dve", lambda e, tt=tt: e.scalar_tensor_tensor(
                    out=o_sb[tt][:], in0=o_sb[tt][:], scalar=st[:, 3 * NTT + tt:3 * NTT + tt + 1], in1=g_sb[:],
                    op0=ALU.mult, op1=ALU.mult), reads=[("o", tt), "st3", "g"], writes=[("o", tt)])
                P.op("dve", lambda e, tt=tt, xb=xb: e.tensor_tensor(out=o_sb[tt][:], in0=o_sb[tt][:], in1=xb[:], op=ALU.add),
                     reads=[("o", tt), xk], writes=[("o", tt)])
                P.dma("sp", out[t0:t0 + 128, :], o_sb[tt][:], reads=[("o", tt)], writes=["x_out0"])


def emit_rg(P, nc, PR, gT_sb, gcol, ident, io):
    x = io["x"]
    gT = io["gT"]
    ident_d = io["ident"]
    wx = io["wx"]
    wz = io["wz"]
    cw_d = io["cw"]
    cb_d = io["cb"]
    lam_d = io["lam"]
    gb_d = io["gb"]
    gw_d = io["gw"]
    yT = io["yT"]
    KC = D // 128
    if True:
        hT = P.sb("hT", [128, KC, S], BF16)
        cw = P.sb("cw_sb", [128, 64], F32)
        cb = P.sb("cb_sb", [128, 16], F32)
        lam = P.sb("lam_sb", [128, 16], F32)
        c1 = P.sb("c1_sb", [128, 16], F32)
        gb = P.sb("gb_sb", [128, 32], F32)
        P.dma("sp", cw[:], cw_d, writes=["cw"])
        P.dma("sp", cb[:], cb_d, writes=["cb"])
        P.dma("sp", lam[:], lam_d, writes=["lam"])
        P.dma("sp", gb[:], gb_d, writes=["gb"])
        phase_hT(P, nc, PR, x, gT_sb, ident, hT, gcol=gcol, xsplit=io["xsplit"])
        P.op("act", lambda e: e.activation(out=c1[:], in_=lam[:], func=AF.Exp, scale=-1.0), reads=["lam"], writes=["c1"])
        P.op("act", lambda e: e.activation(out=c1[:], in_=c1[:], func=AF.Ln, bias=1.0), reads=["c1"], writes=["c1"])
        P.op("dve", lambda e: e.tensor_scalar(out=c1[:], in0=c1[:], scalar1=-8.0, scalar2=None, op0=ALU.mult),
             reads=["c1"], writes=["c1"])
        wxs = [P.sb("wxs%d" % i, [128, KC, 128], BF16) for i in range(2)]
        wzs = [P.sb("wzs%d" % i, [128, KC, 128], BF16) for i in range(2)]
        gws = P.sb("gws", [128, 2, 2, 256], BF16)
        xraw = [P.sb("xraw%d" % i, [128, 3 + 512], F32) for i in range(2)]
        xc = [P.sb("xc%d" % i, [128, 512], F32) for i in range(2)]
        xcb = [P.sb("xcb%d" % i, [128, 512], BF16) for i in range(2)]
        gi = [P.sb("gi%d" % i, [128, 512], F32) for i in range(2)]
        ga = [P.sb("ga%d" % i, [128, 512], F32) for i in range(2)]
        gm = [P.sb("gm%d" % i, [128, 512], F32) for i in range(2)]
        hs = [[P.sb("hs%d_%d" % (i, k), [128, 512], F32) for k in range(2)] for i in range(2)]
        zs = [P.sb("zs%d" % i, [128, 512], F32) for i in range(2)]
        ys = [P.sb("ys%d" % i, [128, 512], BF16) for i in range(2)]
        wx_v = wx.rearrange("(c p) n -> p c n", p=128)
        wz_v = wz.rearrange("(c p) n -> p c n", p=128)
        for blk in range(8):
            for j in range(2):
                col0 = blk * 256 + j * 128
                for c0 in range(0, KC, 8):
                    P.dma("pool", wxs[j][:, c0:c0 + 8, :], wx_v[:, c0:c0 + 8, col0:col0 + 128], writes=[("wx", j)])
                for c0 in range(0, KC, 8):
                    P.dma("pool", wzs[j][:, c0:c0 + 8, :], wz_v[:, c0:c0 + 8, col0:col0 + 128], writes=[("wz", j)])
            for k in range(2):
                P.dma("pool", gws[:, k], gw_d[k, blk].rearrange("(jc p) e -> p jc e", p=128), writes=["gw"])
            for j in range(2):
                P.op("dve", lambda e, j=j: e.memset(xraw[j][:, 0:3], 0.0), writes=[("xraw", j)])
            for tb in range(4):
                for j in range(2):
                    ch = blk * 2 + j

                    def ev_x(ps, pk, j=j):
                        P.op("act", lambda e: e.copy(out=xraw[j][:, 3:515], in_=ps[:]), reads=[pk], writes=[("xraw", j)])
                    proj_fm(P, PR, hT, wxs[j], ("wx", j), tb, ev_x)
                    P.op("dve", lambda e, j=j, ch=ch: e.tensor_scalar(
                        out=xc[j][:], in0=xraw[j][:, 0:512], scalar1=cw[:, ch * 4:ch * 4 + 1], scalar2=cb[:, ch:ch + 1],
                        op0=ALU.mult, op1=ALU.add), reads=[("xraw", j), "cw", "cb"], writes=[("xc", j)])
                    for k in range(1, 4):
                        P.op("dve", lambda e, j=j, ch=ch, k=k: e.scalar_tensor_tensor(
                            out=xc[j][:], in0=xraw[j][:, k:k + 512], scalar=cw[:, ch * 4 + k:ch * 4 + k + 1],
                            in1=xc[j][:], op0=ALU.mult, op1=ALU.add), reads=[("xraw", j), ("xc", j), "cw"],
                            writes=[("xc", j)])
                    P.op("act", lambda e, j=j: e.copy(out=xcb[j][:], in_=xc[j][:]), reads=[("xc", j)], writes=[("xcb", j)])
                    P.op("dve", lambda e, j=j: e.tensor_copy(out=xraw[j][:, 0:3], in_=xraw[j][:, 512:515]),
                         reads=[("xraw", j)], writes=[("xraw", j)])
                for je in range(2):
                    def ev_z(ps, pk, je=je):
                        P.op("act", lambda e: e.activation(out=zs[je][:], in_=ps[:], func=AF.Silu), reads=[pk],
                             writes=[("zs", je)])
                    proj_fm(P, PR, hT, wzs[je], ("wz", je), tb, ev_z)
                for je in range(2):
                    ch = blk * 2 + je
                    for k in range(2):
                        ps, pk = PR.next()
                        for jc in range(2):
                            P.op("pe", lambda e, k=k, jc=jc, je=je, ps=ps: e.matmul(
                                ps[:], gws[:, k, jc, je * 128:(je + 1) * 128], xcb[jc][:], start=(jc == 0), stop=(jc == 1)),
                                reads=["gw", ("xcb", jc)], writes=[pk])
                        dst = gi[je] if k == 0 else ga[je]
                        dk = ("gi", je) if k == 0 else ("ga", je)
                        P.op("act", lambda e, ps=ps, dst=dst, k=k, ch=ch: e.activation(
                            out=dst[:], in_=ps[:], func=AF.Sigmoid, bias=gb[:, k * 16 + ch:k * 16 + ch + 1]),
                            reads=[pk, "gb"], writes=[dk])
                    P.op("act", lambda e, je=je, ch=ch: e.activation(out=ga[je][:], in_=ga[je][:], func=AF.Exp,
                                                                   scale=c1[:, ch:ch + 1]),
                         reads=[("ga", je), "c1"], writes=[("ga", je)])
                    P.op("dve", lambda e, je=je: e.tensor_tensor(out=gm[je][:], in0=ga[je][:], in1=ga[je][:], op=ALU.mult),
                         reads=[("ga", je)], writes=[("gm", je)])
                    P.op("act", lambda e, je=je: e.activation(out=gm[je][:], in_=gm[je][:], func=AF.Sqrt, scale=-1.0, bias=1.0),
                         reads=[("gm", je)], writes=[("gm", je)])
                    if tb == 0:
                        P.op("dve", lambda e, je=je: e.memset(gm[je][:, 0:1], 1.0), writes=[("gm", je)])
                    P.op("dve", lambda e, je=je: e.tensor_tensor(out=gi[je][:], in0=gi[je][:], in1=gm[je][:], op=ALU.mult),
                         reads=[("gi", je), ("gm", je)], writes=[("gi", je)])
                    P.op("dve", lambda e, je=je: e.tensor_tensor(out=gi[je][:], in0=gi[je][:], in1=xc[je][:], op=ALU.mult),
                         reads=[("gi", je), ("xc", je)], writes=[("gi", je)])
                    cur = hs[je][tb % 2]
                    prev = hs[je][(tb + 1) % 2]
                    init = 0.0 if tb == 0 else prev[:, 511:512]
                    P.op("dve", lambda e, je=je, cur=cur, init=init: e.tensor_tensor_scan(
                        out=cur[:], data0=ga[je][:], data1=gi[je][:], initial=init, op0=ALU.mult, op1=ALU.add),
                        reads=[("ga", je), ("gi", je), ("hs", je, (tb + 1) % 2)], writes=[("hs", je, tb % 2)])

                    P.op("dve", lambda e, je=je, cur=cur: e.tensor_tensor(out=ys[je][:], in0=cur[:], in1=zs[je][:], op=ALU.mult),
                         reads=[("hs", je, tb % 2), ("zs", je)], writes=[("ys", je)])
                    P.dma("sp", yT[ch * 128:(ch + 1) * 128, tb * 512:(tb + 1) * 512], ys[je][:], reads=[("ys", je)], writes=["ysrc"])


def emit_hg(P, nc, PR, gT_sb, gcol, ident, io, layer=2):
    x = io["x"]
    gT = io["gT"]
    ident_d = io["ident"]
    ones_d = io["ones"]
    rmask_d = io["rmask"]
    bdmask_d = io["bdmask"]
    lbl_d = io["lbl"]
    ng_d = io["ng"]
    yT = io["yT"]
    wd = [io[n] for n in ("wq", "wf", "wv", "wg")]
    KC = D // 128
    if True:
        hT = P.sb("hT", [128, KC, S], BF16)
        ones = P.sb("ones_sb", [128, 128], F32)
        rmask = P.sb("rmask_sb", [128, 512], F32)
        bdmask = P.sb("bdmask_sb", [128, 128], F32)
        lbl = P.sb("lbl_sb", [128, 64], F32)
        lbe = P.sb("lbe_sb", [128, 64], F32)
        lb = P.sb("lb_sb", [128, 16], F32)
        oml = P.sb("oml_sb", [128, 16], F32)
        lsum = P.sb("lsum_sb", [128, 16], F32)
        ng = P.sb("ng_sb", [128, 1], F32)
        for t, d, k in ((ones, ones_d, "ones"), (rmask, rmask_d, "rmask"),
                        (bdmask, bdmask_d, "bdmask"), (lbl, lbl_d, "lbl"), (ng, ng_d, "ng")):
            P.dma("sp", t[:], d, writes=[k])
        phase_hT(P, nc, PR, x, gT_sb, ident, hT, gcol=gcol, xsplit=io["xsplit"])
        P.op("act", lambda e: e.activation(out=lbe[:], in_=lbl[:], func=AF.Exp), reads=["lbl"], writes=["lbe"])
        P.op("dve", lambda e: e.tensor_tensor(out=lsum[:], in0=lbe[:, 0:16], in1=lbe[:, 16:32], op=ALU.add),
             reads=["lbe"], writes=["lsum"])
        P.op("dve", lambda e: e.tensor_tensor(out=lsum[:], in0=lsum[:], in1=lbe[:, 32:48], op=ALU.add),
             reads=["lbe", "lsum"], writes=["lsum"])
        P.op("dve", lambda e: e.tensor_tensor(out=lsum[:], in0=lsum[:], in1=lbe[:, 48:64], op=ALU.add),
             reads=["lbe", "lsum"], writes=["lsum"])
        P.op("dve", lambda e: e.reciprocal(out=lsum[:], in_=lsum[:]), reads=["lsum"], writes=["lsum"])
        P.op("dve", lambda e: e.memset(lb[:], 0.0), writes=["lb"])
        for l in range(1, layer + 1):
            P.op("dve", lambda e, l=l: e.tensor_tensor(out=lb[:], in0=lb[:], in1=lbe[:, l * 16:(l + 1) * 16], op=ALU.add),
                 reads=["lb", "lbe"], writes=["lb"])
        P.op("dve", lambda e: e.tensor_tensor(out=lb[:], in0=lb[:], in1=lsum[:], op=ALU.mult), reads=["lb", "lsum"], writes=["lb"])
        P.op("dve", lambda e: e.tensor_scalar(out=oml[:], in0=lb[:], scalar1=-1.0, scalar2=1.0, op0=ALU.mult, op1=ALU.add),
             reads=["lb"], writes=["oml"])
        ws = [P.sb("hw%d" % i, [128, KC, 128], BF16) for i in range(4)]
        wv = [w.rearrange("(c p) n -> p c n", p=128) for w in wd]
        f32b = lambda n: P.sb(n, [128, 512], F32)
        qs, ff, logf, kk, bb, eb, enb, kef, gs, osb, sq, rstd, tmpo = [f32b("hgb%d" % i) for i in range(13)]
        qe = P.sb("qe", [128, 512], BF16)
        kebf = P.sb("kebf", [128, 512], BF16)
        ys = P.sb("hys", [128, 512], BF16)
        vtm = [P.sb("vtm%d" % i, [128, 128], BF16) for i in range(4)]
        ketm = [P.sb("ketm%d" % i, [128, 128], BF16) for i in range(2)]
        attT = [P.sb("attT%d" % i, [128, 128], BF16) for i in range(2)]
        Sf = P.sb("Sf", [128, 128], F32)
        Stmp = P.sb("Stmp", [128, 128], F32)
        Sb = P.sb("Sb", [128, 128], BF16)
        for hh in range(16):
            for i in range(4):
                for c0 in range(0, KC, 8):
                    P.dma("pool", ws[i][:, c0:c0 + 8, :], wv[i][:, c0:c0 + 8, hh * 128:(hh + 1) * 128], writes=[("hw", i)])
            P.op("dve", lambda e: e.memset(Sf[:], 0.0), writes=["Sf"])
            P.op("dve", lambda e: e.memset(Sb[:], 0.0), writes=["Sb"])
            for tb in range(4):
                def ev_q(ps, pk):
                    P.op("act", lambda e: e.activation(out=qs[:], in_=ps[:], func=AF.Silu), reads=[pk], writes=["qs"])
                proj_fm(P, PR, hT, ws[0], ("hw", 0), tb, ev_q)

                def ev_f(ps, pk):
                    P.op("act", lambda e: e.activation(out=ff[:], in_=ps[:], func=AF.Sigmoid), reads=[pk], writes=["ff"])
                proj_fm(P, PR, hT, ws[1], ("hw", 1), tb, ev_f)

                def ev_g(ps, pk):
                    P.op("act", lambda e: e.activation(out=gs[:], in_=ps[:], func=AF.Silu), reads=[pk], writes=["gs"])
                proj_fm(P, PR, hT, ws[3], ("hw", 3), tb, ev_g)
                for tt in range(4):
                    ps, pk = PR.next()
                    for c in range(KC):
                        P.op("pe", lambda e, c=c, ps=ps, tt=tt: e.matmul(
                            ps[:, 0:128], hT[:, c, tb * 512 + tt * 128:tb * 512 + (tt + 1) * 128], ws[2][:, c, :],
                            start=(c == 0), stop=(c == KC - 1)), reads=[("hw", 2), ("hT", tb)], writes=[pk], sig=(c == KC - 1))
                    P.op("act", lambda e, ps=ps, tt=tt: e.copy(out=vtm[tt][:], in_=ps[:, 0:128]), reads=[pk], writes=[("vtm", tt)])
                P.op("dve", lambda e: e.tensor_scalar(out=ff[:], in0=ff[:], scalar1=oml[:, hh:hh + 1], scalar2=lb[:, hh:hh + 1],
                                                       op0=ALU.mult, op1=ALU.add), reads=["ff", "oml", "lb"], writes=["ff"])
                P.op("act", lambda e: e.activation(out=logf[:], in_=ff[:], func=AF.Ln), reads=["ff"], writes=["logf"])
                P.op("dve", lambda e: e.tensor_scalar(out=kk[:], in0=ff[:], scalar1=-1.0, scalar2=1.0, op0=ALU.mult, op1=ALU.add),
                     reads=["ff"], writes=["kk"])
                P.op("dve", lambda e: e.tensor_tensor_scan(out=bb[:], data0=rmask[:], data1=logf[:], initial=0.0,
                                                            op0=ALU.mult, op1=ALU.add), reads=["rmask", "logf"], writes=["bb"])
                P.op("act", lambda e: e.activation(out=eb[:], in_=bb[:], func=AF.Exp), reads=["bb"], writes=["eb"])
                P.op("act", lambda e: e.activation(out=enb[:], in_=bb[:], func=AF.Exp, scale=-1.0), reads=["bb"], writes=["enb"])
                P.op("dve", lambda e: e.tensor_tensor(out=qe[:], in0=qs[:], in1=eb[:], op=ALU.mult), reads=["qs", "eb"], writes=["qe"])
                P.op("dve", lambda e: e.tensor_tensor(out=kef[:], in0=kk[:], in1=enb[:], op=ALU.mult), reads=["kk", "enb"], writes=["kef"])
                P.op("act", lambda e: e.copy(out=kebf[:], in_=kef[:]), reads=["kef"], writes=["kebf"])
                for tt in range(4):
                    sl = slice(tt * 128, (tt + 1) * 128)
                    kt = ketm[tt % 2]
                    at = attT[tt % 2]
                    ktk = ("ketm", tt % 2)
                    atk = ("attT", tt % 2)
                    ps, pk = PR.next()
                    P.op("pe", lambda e, ps=ps, sl=sl: e.transpose(out=ps[:, 0:128], in_=kef[:, sl], identity=ident[:]),
                         reads=["kef", "ident"], writes=[pk])
                    P.op("act", lambda e, ps=ps, kt=kt: e.copy(out=kt[:], in_=ps[:, 0:128]), reads=[pk], writes=[ktk])
                    ps2, pk2 = PR.next()
                    P.op("pe", lambda e, ps2=ps2, sl=sl: e.matmul(ps2[:, 0:128], kebf[:, sl], qe[:, sl], start=True, stop=True),
                         reads=["kebf", "qe"], writes=[pk2])
                    P.op("dve", lambda e, ps2=ps2, at=at: e.tensor_tensor(out=at[:], in0=ps2[:, 0:128], in1=bdmask[:], op=ALU.mult),
                         reads=[pk2, "bdmask"], writes=[atk])
                    pso, pko = PR.next()
                    P.op("pe", lambda e, pso=pso, at=at, tt=tt: e.matmul(pso[:, 0:128], vtm[tt][:], at[:], start=True, stop=False),
                         reads=[("vtm", tt), atk], writes=[pko])
                    for cc in range(2):
                        csl = slice(tt * 128 + cc * 64, tt * 128 + (cc + 1) * 64)
                        rows = slice(cc * 64, (cc + 1) * 64)
                        P.op("pe", lambda e, pso=pso, cc=cc, csl=csl: e.matmul(
                            pso[:, cc * 64:(cc + 1) * 64], Sb[:], qe[:, csl], start=False, stop=(cc == 1)),
                            reads=["Sb", "qe"], writes=[pko])
                        pss_, pks = PR.next()
                        P.op("pe", lambda e, pss_=pss_, kt=kt, rows=rows, tt=tt: e.matmul(
                            pss_[:, 0:128], kt[rows, :], vtm[tt][rows, :], start=True, stop=True),
                            reads=[ktk, ("vtm", tt)], writes=[pks])
                        ecol = tt * 128 + (cc + 1) * 64 - 1
                        P.op("dve", lambda e, ecol=ecol: e.tensor_scalar(out=Stmp[:], in0=Sf[:], scalar1=eb[:, ecol:ecol + 1],
                                                                        scalar2=None, op0=ALU.mult),
                             reads=["Sf", "eb"], writes=["Stmp"])
                        P.op("dve", lambda e, ecol=ecol, pss_=pss_: e.scalar_tensor_tensor(
                            out=Sf[:], in0=pss_[:, 0:128], scalar=eb[:, ecol:ecol + 1], in1=Stmp[:], op0=ALU.mult, op1=ALU.add),
                            reads=[pks, "eb", "Stmp"], writes=["Sf"])
                        P.op("act", lambda e: e.copy(out=Sb[:], in_=Sf[:]), reads=["Sf"], writes=["Sb"])
                    P.op("act", lambda e, pso=pso, sl=sl: e.copy(out=osb[:, sl], in_=pso[:, 0:128]), reads=[pko], writes=["osb"])
                P.op("act", lambda e: e.activation(out=sq[:], in_=osb[:], func=AF.Square), reads=["osb"], writes=["sq"])
                psn, pkn = PR.next()
                P.op("pe", lambda e, psn=psn: e.matmul(psn[:], ones[:], sq[:], start=True, stop=True), reads=["ones", "sq"], writes=[pkn])
                P.op("act", lambda e, psn=psn: e.activation(out=rstd[:], in_=psn[:], func=AF.Ln, scale=1.0 / 128, bias=EPS),
                     reads=[pkn], writes=["rstd"])
                P.op("act", lambda e: e.activation(out=rstd[:], in_=rstd[:], func=AF.Exp, scale=-0.5), reads=["rstd"], writes=["rstd"])
                P.op("dve", lambda e: e.scalar_tensor_tensor(out=tmpo[:], in0=osb[:], scalar=ng[:, 0:1], in1=rstd[:],
                                                              op0=ALU.mult, op1=ALU.mult), reads=["osb", "ng", "rstd"], writes=["tmpo"])
                P.op("dve", lambda e: e.tensor_tensor(out=ys[:], in0=tmpo[:], in1=gs[:], op=ALU.mult), reads=["tmpo", "gs"], writes=["ys"])
                P.dma("sp", yT[hh * 128:(hh + 1) * 128, tb * 512:(tb + 1) * 512], ys[:], reads=["ys"], writes=["ysrc"])


NSA_SCALE = 128 ** -0.5
NEGM = -1.0e7
NREL = 19


def emit_nsa(P, nc, PR, psUl, psDl, gT_sb, gcol, ident, io):
    x = io["x"]
    gT = io["gT"]
    ident_d = io["ident"]
    wq = io["wq"]
    wz = io["wz"]
    wkv = io["wkv"]
    wgl = io["wgl"]
    peT_d = io["peT"]
    w1_d = io["w1"]
    w2_d = io["w2"]
    slope_d = io["slope"]
    cst_d = io["cst"]
    R_d = io["Rt"]
    E_d = io["E"]
    selb0_d = io["selb0"]
    Mmat_d = io["Mmat"]
    A_d = io["At"]
    Badd_d = io["Bt"]
    SelR_d = io["SelR"]
    yT = io["yT"]
    qT_s = io["qT_s"]
    zT_s = io["zT_s"]
    kvT_s = io["kvT_s"]
    vtm_s = io["vtm_s"]
    tcmp_s = io["tcmp_s"]
    KC = D // 128
    if True:
        GT = P.sb("GT", [48, S], F32)
        with contextlib.ExitStack() as es1:
            hT = es1.enter_context(nc.sbuf_tensor(P.pre + "hT", [128, KC, S], BF16))
            wsb = [es1.enter_context(nc.sbuf_tensor(P.pre + "nw%d" % i, [128, KC, 128], BF16)) for i in range(2)]
            wglsb = es1.enter_context(nc.sbuf_tensor(P.pre + "wglsb", [128, KC, 48], BF16))
            stg = [es1.enter_context(nc.sbuf_tensor(P.pre + "stg%d" % i, [128, 512], BF16)) for i in range(2)]
            phase_hT(P, nc, PR, x, gT_sb, ident, hT, gcol=gcol, xsplit=io["xsplit"])
            si = [0]
            wi = [0]

            def load_w(src_v, col0):
                b = wi[0] % 2
                wi[0] += 1
                for c0 in range(0, KC, 8):
                    P.dma("pool", wsb[b][:, c0:c0 + 8, :], src_v[:, c0:c0 + 8, col0:col0 + 128], writes=[("nw", b)])
                return wsb[b], ("nw", b)

            def fm_chunk(src_v, col0, dst, func, scale):
                w, wk = load_w(src_v, col0)
                for tb in range(4):
                    def ev(ps, pk, tb=tb):
                        b = si[0] % 2
                        si[0] += 1
                        P.op("act", lambda e: e.activation(out=stg[b][:], in_=ps[:], func=func, scale=scale),
                             reads=[pk], writes=[("stg", b)])
                        P.dma("sp", dst[:, tb * 512:(tb + 1) * 512], stg[b][:], reads=[("stg", b)], writes=["scratch"])
                    proj_fm(P, PR, hT, w, wk, tb, ev)

            wq_v = wq.rearrange("(c p) n -> p c n", p=128)
            wz_v = wz.rearrange("(c p) n -> p c n", p=128)
            wkv_v = wkv.rearrange("(c p) n -> p c n", p=128)
            for hl in range(16):
                fm_chunk(wq_v, hl * 128, qT_s[hl], AF.Copy, NSA_SCALE)
                fm_chunk(wz_v, hl * 128, zT_s[hl], AF.Silu, 1.0)
            for gl in range(2):
                for n, i in enumerate((0, 1, 2, 4)):
                    fm_chunk(wkv_v, i * 256 + gl * 128, kvT_s[gl, n], AF.Copy, 1.0)
                for n, i in enumerate((3, 5)):
                    w, wk = load_w(wkv_v, i * 256 + gl * 128)
                    for tt in range(16):
                        ps, pk = PR.next()
                        for c in range(KC):
                            P.op("pe", lambda e, c=c, ps=ps, tt=tt, w=w: e.matmul(
                                ps[:, 0:128], hT[:, c, tt * 128:(tt + 1) * 128], w[:, c, :],
                                start=(c == 0), stop=(c == KC - 1)), reads=[wk, ("hT", tt // 4)], writes=[pk], sig=(c == KC - 1))
                        b = si[0] % 2
                        si[0] += 1
                        P.op("act", lambda e, ps=ps, b=b: e.copy(out=stg[b][:, 0:128], in_=ps[:, 0:128]), reads=[pk],
                             writes=[("stg", b)])
                        P.dma("sp", vtm_s[gl, n, tt * 128:(tt + 1) * 128, :], stg[b][:, 0:128], reads=[("stg", b)],
                              writes=["scratch"])
            for c0 in range(0, KC, 8):
                P.dma("pool", wglsb[:, c0:c0 + 8, :], wgl.rearrange("(c p) n -> p c n", p=128)[:, c0:c0 + 8, :], writes=["wgl"])
            for tb in range(4):
                ps, pk = PR.next()
                for c in range(KC):
                    P.op("pe", lambda e, c=c, ps=ps, tb=tb: e.matmul(ps[0:48, :], wglsb[:, c, :], hT[:, c, tb * 512:(tb + 1) * 512],
                                                                    start=(c == 0), stop=(c == KC - 1)),
                         reads=["wgl", ("hT", tb)], writes=[pk])
                P.op("act", lambda e, ps=ps, tb=tb: e.activation(out=GT[:, tb * 512:(tb + 1) * 512], in_=ps[0:48, :], func=AF.Sigmoid),
                     reads=[pk], writes=["GT"])
            barrier(P)
        Rt = P.sb("Rt_sb", [128, 13, 512], F32)
        SelR = P.sb("SelR_sb", [48, 48 * 128], F32)
        slope = P.sb("slope_sb", [128, 16], F32)
        cst = P.sb("cst_sb", [128, 16 * NREL], F32)
        E = P.sb("E_sb", [32, 2048], BF16)
        selbT = P.sb("selbT", [32, 2048], BF16)
        Mmat = P.sb("Mmat_sb", [128, 32], F32)
        At = P.sb("At_sb", [128, 256], F32)
        Bt = P.sb("Bt_sb", [128, 256], F32)
        onesb = P.sb("onesb", [128, 128], BF16)
        for k in range(13):
            P.dma("sp", Rt[:, k, :], R_d[k], writes=["Rt"])
        for t, d, k in ((SelR, SelR_d, "SelR"), (slope, slope_d, "slope"), (cst, cst_d, "cst"), (E, E_d, "E"),
                        (Mmat, Mmat_d, "Mmat"), (At, A_d, "At"), (Bt, Badd_d, "Bt")):
            P.dma("sp", t[:], d, writes=[k])
        P.op("dve", lambda e: e.memset(onesb[:], 1.0), writes=["onesb"])
        kvT = P.sb("kvT", [128, 4, S], BF16)
        vtm = P.sb("vtm", [128, 2, 16, 128], BF16)
        w1sb = P.sb("w1sb", [128, 32, 128], BF16)
        w2sb = P.sb("w2sb", [128, 128], BF16)
        peT = P.sb("peT_sb", [128, 32], F32)
        peTb = P.sb("peTb", [128, 32], BF16)
        cb_ = P.sb("cmpb", [128, 1], F32)
        xg = P.sb("xg", [128, 128], F32)
        xg2 = P.sb("xg2", [128, 128], F32)
        hidT = P.sb("hidT", [128, 128], BF16)
        KcT = P.sb("KcT", [128, 128], BF16)
        Vc = P.sb("Vc", [128, 128], BF16)
        qTb = P.sb("qTb", [128, S], BF16)
        zTb = P.sb("zTb", [128, S], BF16)
        tmp = [P.sb("atmp%d" % i, [128, 512], F32) for i in range(4)]
        Pf = P.sb("Pf", [128, 512], F32)
        Pb = [P.sb("Pb%d" % i, [128, 512], BF16) for i in range(4)]
        rden = P.sb("rden", [128, 512], F32)
        Ff = P.sb("Ff", [128, 512], F32)
        Tt = P.sb("Tt", [128, 512], F32)
        acc = P.sb("acc", [128, 512], F32)
        pn = P.sb("pn", [128, 512], F32)
        ys = P.sb("nys", [128, 512], BF16)
        pslc = P.sb("pslc", [128, 256], F32)
        sc = P.sb("sc", [128, 32], F32)
        sc2 = P.sb("sc2", [128, 32], F32)
        m8 = P.sb("m8", [128, 16], F32)
        selb = P.sb("selb", [128, 32], F32)
        ti_ = [0]
        bi_ = [0]
        pend = []

        def drain(keep=0):
            while len(pend) > keep:
                pend.pop(0)()

        def branch_step(kT_ap, kkey, v_ap, vkey, qsl, hl, rvar, cidx, nk, extra=None, first=False, last=False, want_f32=False):
            ps, pk = PR.next()
            P.op("pe", lambda e: e.matmul(ps[0:nk, :], kT_ap, qTb[:, qsl], start=True, stop=(extra is None)),
                 reads=[kkey, "qTb"], writes=[pk])
            if extra is not None:
                P.op("pe", lambda e: e.matmul(ps[0:nk, :], extra, selbT[:, qsl], start=False, stop=True),
                     reads=["E", "selbT"], writes=[pk])
            i = ti_[0] % 4
            ti_[0] += 1
            psU, psD = psUl[bi_[0] % 2], psDl[bi_[0] % 2]
            uk, dk = ("psU", bi_[0] % 2), ("psD", bi_[0] % 2)
            P.op("dve", lambda e: e.scalar_tensor_tensor(out=tmp[i][0:nk, :], in0=Rt[0:nk, rvar, :], scalar=slope[0:nk, hl:hl + 1],
                                                          in1=ps[0:nk, :], op0=ALU.mult, op1=ALU.add),
                 reads=["Rt", "slope", pk], writes=[("atmp", i)])
            if want_f32:
                P.op("act", lambda e: e.activation(out=Pf[0:nk, :], in_=tmp[i][0:nk, :], func=AF.Exp), reads=[("atmp", i)], writes=["Pf"])
                P.op("act", lambda e: e.copy(out=Pb[i][0:nk, :], in_=Pf[0:nk, :]), reads=["Pf"], writes=[("Pb", i)])
            elif cidx is None:
                P.op("act", lambda e: e.activation(out=Pb[i][0:nk, :], in_=tmp[i][0:nk, :], func=AF.Exp), reads=[("atmp", i)],
                     writes=[("Pb", i)])
            else:
                P.op("act", lambda e: e.activation(out=Pb[i][0:nk, :], in_=tmp[i][0:nk, :], func=AF.Exp,
                                                   bias=cst[0:nk, cidx:cidx + 1]), reads=[("atmp", i), "cst"], writes=[("Pb", i)])

            def stage_b():
                P.op("pe", lambda e: e.matmul(psU[:], v_ap, Pb[i][0:nk, :], start=first, stop=last), reads=[vkey, ("Pb", i)], writes=[uk])
                P.op("pe", lambda e: e.matmul(psD[:], onesb[0:nk, :], Pb[i][0:nk, :], start=first, stop=last),
                     reads=["onesb", ("Pb", i)], writes=[dk])
            pend.append(stage_b)

        def finish_branch(br, hl, qsl, mode):
            psU, psD = psUl[bi_[0] % 2], psDl[bi_[0] % 2]
            uk, dk = ("psU", bi_[0] % 2), ("psD", bi_[0] % 2)
            bi_[0] += 1

            def fin():
                if br == 0:
                    P.op("dve", lambda e: e.tensor_scalar(out=rden[:], in0=psD[:], scalar1=1e-18, scalar2=None, op0=ALU.max),
                         reads=[dk], writes=["rden"])
                    P.op("act", lambda e: e.activation(out=rden[:], in_=rden[:], func=AF.Ln), reads=["rden"], writes=["rden"])
                else:
                    P.op("act", lambda e: e.activation(out=rden[:], in_=psD[:], func=AF.Ln), reads=[dk], writes=["rden"])
                P.op("act", lambda e: e.activation(out=rden[:], in_=rden[:], func=AF.Exp, scale=-1.0), reads=["rden"], writes=["rden"])
                r = br * 16 + hl
                psG, gk = PR.next()
                P.op("pe", lambda e: e.matmul(psG[:], SelR[:, r * 128:(r + 1) * 128], GT[:, qsl], start=True, stop=True),
                     reads=["SelR", "GT"], writes=[gk])
                P.op("dve", lambda e: e.tensor_tensor(out=Ff[:], in0=rden[:], in1=psG[:], op=ALU.mult), reads=["rden", gk], writes=["Ff"])
                if mode == "ret":
                    P.op("dve", lambda e: e.tensor_tensor(out=Tt[:], in0=psU[:], in1=Ff[:], op=ALU.mult), reads=[uk, "Ff"], writes=["Tt"])
                elif mode == "set":
                    P.op("dve", lambda e: e.tensor_tensor(out=acc[:], in0=psU[:], in1=Ff[:], op=ALU.mult), reads=[uk, "Ff"], writes=["acc"])
                else:
                    P.op("dve", lambda e: e.tensor_tensor(out=Tt[:], in0=psU[:], in1=Ff[:], op=ALU.mult), reads=[uk, "Ff"], writes=["Tt"])
                    P.op("pool", lambda e: e.tensor_tensor(out=acc[:], in0=acc[:], in1=Tt[:], op=ALU.add), reads=["acc", "Tt"], writes=["acc"])
            pend.append(fin)

        def cidx_of(hl, m):
            return hl * NREL + (m + 15)

        for gl in range(2):
            for n in range(4):
                P.dma("sp", kvT[:, n, :], kvT_s[gl, n], reads=["scratch"], writes=["kvT"])
            for n in range(2):
                P.dma("sp", vtm[:, n], vtm_s[gl, n].rearrange("(t p) d -> p t d", p=128), reads=["scratch"], writes=["vtm"])
            P.dma("sp", selbT[:, 0:1024], selb0_d, writes=["selbT"])
            for which in range(2):
                for c0 in range(0, 32, 8):
                    P.dma("pool", w1sb[:, c0:c0 + 8, :], w1_d[which].rearrange("(l d) e -> d l e", d=128)[:, c0:c0 + 8, :], writes=["w1sb"])
                P.dma("pool", w2sb[:], w2_d[which], writes=["w2sb"])
                P.dma("sp", peT[:], peT_d[which], writes=["peT"])
                P.op("act", lambda e: e.copy(out=peTb[:], in_=peT[:]), reads=["peT"], writes=["peTb"])
                ps, pk = PR.next()
                for l in range(32):
                    P.op("pe", lambda e, l=l, ps=ps: e.matmul(ps[:, 0:127], w1sb[:, l, :], kvT[:, which, l:l + 16 * 126 + 1:16],
                                                             start=(l == 0), stop=(l == 31)), reads=["w1sb", "kvT"], writes=[pk])
                ps2, pk2 = PR.next()
                for l in range(32):
                    P.op("pe", lambda e, l=l, ps2=ps2: e.matmul(ps2[:, 0:1], w1sb[:, l, :], peTb[:, l:l + 1],
                                                               start=(l == 0), stop=(l == 31)), reads=["w1sb", "peTb"], writes=[pk2])
                P.op("act", lambda e, ps2=ps2: e.copy(out=cb_[:], in_=ps2[:, 0:1]), reads=[pk2], writes=["cmpb"])
                P.op("act", lambda e, ps=ps: e.activation(out=xg[:, 0:127], in_=ps[:, 0:127], func=AF.Identity, bias=cb_[:, 0:1]),
                     reads=[pk, "cmpb"], writes=["xg"])
                P.op("dve", lambda e: e.tensor_tensor(out=xg2[:, 0:127], in0=xg[:, 0:127], in1=xg[:, 0:127], op=ALU.mult), reads=["xg"], writes=["xg2"])
                P.op("dve", lambda e: e.tensor_scalar(out=xg2[:, 0:127], in0=xg2[:, 0:127], scalar1=0.044715, scalar2=1.0,
                                                       op0=ALU.mult, op1=ALU.add), reads=["xg2"], writes=["xg2"])
                P.op("dve", lambda e: e.tensor_tensor(out=xg2[:, 0:127], in0=xg2[:, 0:127], in1=xg[:, 0:127], op=ALU.mult),
                     reads=["xg", "xg2"], writes=["xg2"])
                P.op("act", lambda e: e.activation(out=xg2[:, 0:127], in_=xg2[:, 0:127], func=AF.Sigmoid, scale=1.5957691216),
                     reads=["xg2"], writes=["xg2"])
                P.op("dve", lambda e: e.tensor_tensor(out=hidT[:, 0:127], in0=xg[:, 0:127], in1=xg2[:, 0:127], op=ALU.mult),
                     reads=["xg", "xg2"], writes=["hidT"])
                ps3, pk3 = PR.next()
                if which == 0:
                    P.op("pe", lambda e, ps3=ps3: e.matmul(ps3[:, 0:127], w2sb[:], hidT[:, 0:127], start=True, stop=True),
                         reads=["w2sb", "hidT"], writes=[pk3])
                    P.op("act", lambda e, ps3=ps3: e.copy(out=KcT[:, 0:127], in_=ps3[:, 0:127]), reads=[pk3], writes=["KcT"])
                else:
                    P.op("pe", lambda e, ps3=ps3: e.matmul(ps3[0:127, 0:128], hidT[:, 0:127], w2sb[:], start=True, stop=True),
                         reads=["w2sb", "hidT"], writes=[pk3])
                    P.op("act", lambda e, ps3=ps3: e.copy(out=Vc[0:127, :], in_=ps3[0:127, 0:128]), reads=[pk3], writes=["Vc"])
            for j in range(8):
                hl = gl * 8 + j
                P.dma("sp", qTb[:], qT_s[hl], reads=["scratch"], writes=["qTb"])
                for qb in range(4):
                    qsl = slice(qb * 512, (qb + 1) * 512)
                    branch_step(KcT[:, 0:127], "KcT", Vc[0:127, :], "Vc", qsl, hl, 9 + qb, None, 127, first=True, last=True,
                                want_f32=(qb >= 2))
                    finish_branch(0, hl, qsl, "ret")
                    drain(0)
                    P.dma("sp", tcmp_s[j, :, qsl], Tt[:], reads=["Tt"], writes=["tcmp"])
                    if qb >= 2:
                        P.op("dve", lambda e: e.tensor_tensor(out=pn[0:127, :], in0=Pf[0:127, :], in1=rden[0:127, :], op=ALU.mult),
                             reads=["Pf", "rden"], writes=["pn"])
                        ps, pk = PR.next()
                        for k in range(4):
                            P.op("pe", lambda e, k=k, ps=ps: e.matmul(ps[:, k * 32:(k + 1) * 32], pn[0:127, k * 128:(k + 1) * 128],
                                                                     Mmat[0:127, :], start=True, stop=True),
                                 reads=["pn", "Mmat"], writes=[pk])
                        dst = pslc[:, (qb - 2) * 128:(qb - 1) * 128]
                        if j == 0:
                            P.op("dve", lambda e, ps=ps, dst=dst: e.tensor_copy(out=dst, in_=ps[:, 0:128]), reads=[pk], writes=["pslc"])
                        else:
                            P.op("dve", lambda e, ps=ps, dst=dst: e.tensor_tensor(out=dst, in0=dst, in1=ps[:, 0:128], op=ALU.add),
                                 reads=[pk, "pslc"], writes=["pslc"])
            for ti in range(8):
                csl = slice(ti * 32, (ti + 1) * 32)
                P.op("dve", lambda e, csl=csl: e.tensor_tensor(out=sc[:], in0=pslc[:, csl], in1=At[:, csl], op=ALU.mult),
                     reads=["pslc", "At"], writes=["sc"])
                P.op("dve", lambda e, csl=csl: e.tensor_tensor(out=sc[:], in0=sc[:], in1=Bt[:, csl], op=ALU.add), reads=["sc", "Bt"], writes=["sc"])
                P.op("dve", lambda e: e.max(out=m8[:, 0:8], in_=sc[:]), reads=["sc"], writes=["m8"])
                P.op("dve", lambda e: e.match_replace(out=sc2[:], in_to_replace=m8[:, 0:8], in_values=sc[:], imm_value=-1e30),
                     reads=["sc", "m8"], writes=["sc2"])
                P.op("dve", lambda e: e.max(out=m8[:, 8:16], in_=sc2[:]), reads=["sc2"], writes=["m8"])
                P.op("dve", lambda e: e.tensor_scalar(out=selb[:], in0=sc[:], scalar1=m8[:, 15:16], scalar2=None, op0=ALU.is_ge),
                     reads=["sc", "m8"], writes=["selb"])
                P.op("dve", lambda e: e.tensor_scalar(out=selb[:], in0=selb[:], scalar1=30000.0, scalar2=-30000.0, op0=ALU.mult, op1=ALU.add),
                     reads=["selb"], writes=["selb"])
                ps, pk = PR.next()
                P.op("pe", lambda e, ps=ps: e.transpose(out=ps[0:32, 0:128], in_=selb[:], identity=ident[:]), reads=["selb", "ident"], writes=[pk])
                P.op("act", lambda e, ps=ps, ti=ti: e.copy(out=selbT[:, 1024 + ti * 128:1024 + (ti + 1) * 128], in_=ps[0:32, 0:128]),
                     reads=[pk], writes=["selbT"])
            for j in range(8):
                hl = gl * 8 + j
                drain(0)
                P.dma("sp", qTb[:], qT_s[hl], reads=["scratch"], writes=["qTb"])
                P.dma("sp", zTb[:], zT_s[hl], reads=["scratch"], writes=["zTb"])
                for qb in range(4):
                    qsl = slice(qb * 512, (qb + 1) * 512)
                    drain(0)
                    P.dma("sp", pn[:], tcmp_s[j, :, qsl], reads=["tcmp"], writes=["pn"])
                    kts = list(range(0, 4 * qb + 4))
                    for kt in kts:
                        m = 4 * qb - kt
                        rvar, ci = (0, cidx_of(hl, -m)) if m >= 1 else (1 + (-m), cidx_of(hl, -m))
                        branch_step(kvT[:, 2, kt * 128:(kt + 1) * 128], "kvT", vtm[:, 0, kt, :], "vtm", qsl, hl, rvar, ci, 128,
                                    extra=(E[:, kt * 128:(kt + 1) * 128] if qb >= 2 else None), first=(kt == kts[0]),
                                    last=(kt == kts[-1]))
                        drain(2)
                    finish_branch(1, hl, qsl, "set")
                    kts = list(range(max(0, 4 * qb - 4), 4 * qb + 4))
                    for kt in kts:
                        m = 4 * qb - kt
                        if m >= 1:
                            rvar = 5 + (4 - m)
                        else:
                            rvar = 1 + (-m)
                        branch_step(kvT[:, 3, kt * 128:(kt + 1) * 128], "kvT", vtm[:, 1, kt, :], "vtm", qsl, hl, rvar, cidx_of(hl, -m), 128,
                                    first=(kt == kts[0]), last=(kt == kts[-1]))
                        drain(2)
                    finish_branch(2, hl, qsl, "add")
                    drain(0)
                    P.op("pool", lambda e: e.tensor_tensor(out=acc[:], in0=acc[:], in1=pn[:], op=ALU.add), reads=["acc", "pn"], writes=["acc"])
                    P.op("pool", lambda e, qsl=qsl: e.tensor_tensor(out=ys[:], in0=acc[:], in1=zTb[:, qsl], op=ALU.mult),
                         reads=["acc", "zTb"], writes=["nys"])
                    P.dma("sp", yT[hl * 128:(hl + 1) * 128, qsl], ys[:], reads=["nys"], writes=["ysrc"])


IDENT = np.eye(128, dtype=np.float32)


def gainT(g):
    return np.ascontiguousarray(g.reshape(32, 128).T).astype(np.float32)


def colT(v, h, n=16):
    return np.ascontiguousarray(v[h * n * 128:(h + 1) * n * 128].reshape(n, 128).T).astype(np.float32)

def nsa_consts(h):
    heads = np.arange(16) + 16 * h
    sl = (2.0 ** (-8.0 * (heads + 1) / 32)).astype(np.float64)
    slope = np.broadcast_to(sl[None, :], (128, 16)).astype(np.float32)
    cst = np.zeros((128, 16, NREL), np.float64)
    for m in range(-15, 4):
        cst[:, :, m + 15] = sl[None, :] * 128.0 * m
    kj = np.arange(128)[:, None].astype(np.float64)
    qi = np.arange(512)[None, :].astype(np.float64)
    base = kj - qi
    Rt = np.zeros((13, 128, 512), np.float64)
    Rt[0] = base
    for v in range(4):
        Rt[1 + v] = np.where(-128 * v + qi - kj >= 0, base, NEGM)
    for n, rel in enumerate((512, 384, 256, 128)):
        Rt[5 + n] = np.where(rel + qi - kj < 512, base, NEGM)
    for qb in range(4):
        dist = 512 * qb + qi - 16 * kj - 31
        Rt[9 + qb] = np.where(dist >= 0, -dist, NEGM)
    key = np.arange(2048)
    E = (key[None, :] // 64 == np.arange(32)[:, None]).astype(np.float32)
    t = np.arange(1024)
    selb0 = np.where(np.arange(32)[:, None] <= (t[None, :] // 64), 0.0, -30000.0)
    Mmat = np.zeros((128, 32), np.float32)
    for i in range(32):
        for off, wgt in ((-1, 1), (0, 2), (1, 2), (2, 2), (3, 1)):
            n = 4 * i + off
            if 0 <= n < 127:
                Mmat[n, i] += wgt
    At = np.zeros((128, 8, 32), np.float32)
    Bt = np.zeros((128, 8, 32), np.float32)
    for ti in range(8):
        tt = 1024 + ti * 128 + np.arange(128)
        cur = (tt // 64)[:, None]
        blk = np.arange(32)[None, :]
        forced = (blk == 0) | (blk == cur) | (blk == cur - 1)
        future = blk > cur
        At[:, ti] = (~forced & ~future)
        Bt[:, ti] = np.where(forced, 1e6, np.where(future, -1.0, 0.0))
    SelR = np.zeros((48, 48, 128), np.float32)
    for r in range(48):
        SelR[r, r, :] = 1.0
    return {"slope": slope, "cst": np.ascontiguousarray(cst.reshape(128, 16 * NREL)).astype(np.float32),
            "Rt": Rt.astype(np.float32), "E": E.astype(ml_dtypes.bfloat16), "selb0": selb0.astype(ml_dtypes.bfloat16),
            "Mmat": Mmat, "At": np.ascontiguousarray(At.reshape(128, 256)), "Bt": np.ascontiguousarray(Bt.reshape(128, 256)),
            "SelR": np.ascontiguousarray(SelR.reshape(48, 48 * 128))}


PAIRS = [[0, 1], [2, 3], [4, 5], [6, 7]]
NSA_IN = ("wq", [D, 2048]), ("wz", [D, 2048]), ("wkv", [D, 1536]), ("wgl", [D, 48]), ("peT", [2, 128, 32]), \
    ("w1", [2, 4096, 128]), ("w2", [2, 128, 128])
NSA_CONST = ("slope", [128, 16], F32), ("cst", [128, 16 * NREL], F32), ("Rt", [13, 128, 512], F32), ("E", [32, 2048], BF16), \
    ("selb0", [32, 1024], BF16), ("Mmat", [128, 32], F32), ("At", [128, 256], F32), ("Bt", [128, 256], F32), \
    ("SelR", [48, 48 * 128], F32)
RG_IN = ("wx", [D, 2048]), ("wz", [D, 2048]), ("cw", [128, 64]), ("cb", [128, 16]), ("lam", [128, 16]), ("gb", [128, 32]), \
    ("gw", [2, 8, 256, 256])
HG_IN = ("wq", [D, 2048]), ("wf", [D, 2048]), ("wv", [D, 2048]), ("wg", [D, 2048]), ("lbl", [128, 64]), ("ng", [128, 1]), \
    ("ones", [128, 128]), ("rmask", [128, 512]), ("bdmask", [128, 128])


def build_fused():
    nc = bass.Bass("TRN2", target_bir_lowering=False)
    I = lambda n, s, dt=F32: nc.dram_tensor(n, list(s), dt, kind="ExternalInput").ap()
    T = lambda n, s, dt: nc.dram_tensor(n, list(s), dt, kind="Internal").ap()
    x = I("x", [S, D])
    x_my = I("x_my", [S, 2048])
    gT_d = I("gT", [128, 128])
    post_d = I("post", [4, 128, 2048])
    ident_d = I("ident", [128, 128])
    consts = {n: I(n, s, dt) for n, s, dt in NSA_CONST}
    lio = {}
    for i in range(4):
        spec = (NSA_IN, RG_IN, HG_IN)[i % 3]
        lio[i] = {n: I("L%d_%s" % (i, n), s) for n, s in spec}
        lio[i]["wo"] = I("L%d_wo" % i, [D, 2048])
    out = nc.dram_tensor("out", [S, 2048], F32, kind="ExternalOutput").ap()
    ysrc = T("ysrc", [2048, S], BF16)
    yT_g = T("yT_g", [D, S], BF16)
    xsrc = [T("xsrc0", [S, 2048], F32), T("xsrc1", [S, 2048], F32)]
    x_g = T("x_g", [2 * S, 2048], F32)
    ss_src = T("ss_src", [128, 8], F32)
    ss_g = T("ss_g", [256, 8], F32)
    scratch = {"qT_s": T("qT_s", [16, 128, S], BF16), "zT_s": T("zT_s", [16, 128, S], BF16),
               "kvT_s": T("kvT_s", [2, 4, 128, S], BF16), "vtm_s": T("vtm_s", [2, 2, S, 128], BF16),
               "tcmp_s": T("tcmp_s", [8, 128, S], F32)}
    with contextlib.ExitStack() as es:
        P = Prog(nc, es)
        PR = PsumRing(P, 4)
        psUl = [P.ps("psU%d" % i, [128, 512]) for i in range(2)]
        psDl = [P.ps("psD%d" % i, [128, 512]) for i in range(2)]
        gT_sb = P.sb("gT_sb", [128, 128], F32)
        ident = P.sb("ident_sb", [128, 128], F32)
        P.dma("sp", gT_sb[:], gT_d, writes=["gT"])
        P.dma("sp", ident[:], ident_d, writes=["ident"])
        for i in range(4):
            kind = i % 3
            io = dict(lio[i])
            io.update(gT=None, ident=None, x=(x if i == 0 else x_g), yT=ysrc, xsplit=(i > 0))
            with contextlib.ExitStack() as esl:
                P.es_cur = esl
                P.pre = "L%dm_" % i
                if kind == 0:
                    io.update(consts)
                    io.update(scratch)
                    emit_nsa(P, nc, PR, psUl, psDl, gT_sb, i * 32, ident, io)
                elif kind == 1:
                    emit_rg(P, nc, PR, gT_sb, i * 32, ident, io)
                else:
                    emit_hg(P, nc, PR, gT_sb, i * 32, ident, io, layer=i)
                barrier(P)
            for k in range(4):
                P.cc("AllGather", PAIRS, ysrc[k * 512:(k + 1) * 512, :], yT_g[k * 1024:(k + 1) * 1024, :], reads=["ysrc"], writes=["yT_g"])
            barrier(P)
            xo = out if i == 3 else xsrc[i % 2]

            def gather_x(k0, k1, xo=xo):
                for k in range(k0, k1):
                    P.cc("AllGather", PAIRS, xo[k * 256:(k + 1) * 256, :], x_g[k * 512:(k + 1) * 512, :],
                         reads=["x_out0"], writes=["x_g"])
            with contextlib.ExitStack() as esl:
                P.es_cur = esl
                P.pre = "L%do_" % i
                x_res = x_my if i == 0 else xsrc[(i - 1) % 2]
                emit_outproj(P, nc, PR, yT_g, x_res, io["wo"], post_d[i], xo, ss_src, ss_g,
                             mid_cc=((lambda: gather_x(0, 4)) if i < 3 else None))
                barrier(P)
            P.es_cur = None
            if i < 3:
                gather_x(4, 8)
                barrier(P)
        P.finish()
    return nc


def kernel(x, pre_norm_gain, post_norm_gain, nsa_w_in, nsa_cmp_pe, nsa_cmp_w1, nsa_cmp_w2, nsa_w_out,
           rg_w_in, rg_conv_w, rg_conv_b, rg_gate_w, rg_gate_b, rg_lambda, rg_w_out,
           hg_w_in, hg_lb_logits, hg_norm_gain, hg_w_out):
    f = lambda a: np.asarray(a, dtype=np.float32)
    ca = np.ascontiguousarray
    x = f(x)
    pre, post = f(pre_norm_gain), f(post_norm_gain)
    common = {
        "gT": ca(np.concatenate([gainT(pre[i]) for i in range(4)], axis=1)),
        "ident": IDENT,
    }
    rmask = np.ones((128, 512), np.float32)
    rmask[:, ::64] = 0.0
    si = np.arange(128)[:, None]
    ti = np.arange(128)[None, :]
    bdmask = ((si // 64 == ti // 64) & (ti >= si)).astype(np.float32)
    in_maps = []
    for c in range(NCORES):
        b, h = c // 2, c % 2
        m = dict(common)
        m["x"] = ca(x[b])
        m["x_my"] = ca(x[b][:, h * 2048:(h + 1) * 2048])
        m["post"] = ca(np.broadcast_to(post[:, None, h * 2048:(h + 1) * 2048], (4, 128, 2048))).astype(np.float32)
        m.update(nsa_consts(h))
        for i, j in ((0, 0), (3, 1)):
            w_in = f(nsa_w_in[j])
            pfx = "L%d_" % i
            m[pfx + "wq"] = ca(w_in[:, h * 2048:(h + 1) * 2048])
            m[pfx + "wz"] = ca(w_in[:, 7264 + h * 2048:7264 + (h + 1) * 2048])
            m[pfx + "wkv"] = ca(np.concatenate(
                [w_in[:, 4096 + k * 512 + h * 256:4096 + k * 512 + (h + 1) * 256] for k in range(6)], axis=1))
            m[pfx + "wgl"] = ca(np.concatenate(
                [w_in[:, 7168 + br * 32 + h * 16:7168 + br * 32 + (h + 1) * 16] for br in range(3)], axis=1))
            m[pfx + "peT"] = ca(f(nsa_cmp_pe[j]).transpose(0, 2, 1))
            m[pfx + "w1"] = ca(f(nsa_cmp_w1[j]))
            m[pfx + "w2"] = ca(f(nsa_cmp_w2[j]))
            m[pfx + "wo"] = ca(f(nsa_w_out[j])[:, h * 2048:(h + 1) * 2048])
        w_in = f(rg_w_in[0])
        m["L1_wx"] = ca(w_in[:, h * 2048:(h + 1) * 2048])
        m["L1_wz"] = ca(w_in[:, 4096 + h * 2048:4096 + (h + 1) * 2048])
        cw = f(rg_conv_w[0])
        m["L1_cw"] = ca(np.stack([colT(cw[k], h) for k in range(4)], axis=-1).reshape(128, 64))
        m["L1_cb"] = colT(f(rg_conv_b[0]), h)
        m["L1_lam"] = colT(f(rg_lambda[0]), h)
        gbv = f(rg_gate_b[0])
        m["L1_gb"] = ca(np.stack([colT(gbv[k].reshape(-1), h) for k in range(2)], axis=1).reshape(128, 32))
        m["L1_gw"] = ca(f(rg_gate_w[0])[:, h * 8:(h + 1) * 8])
        m["L1_wo"] = ca(f(rg_w_out[0])[:, h * 2048:(h + 1) * 2048])
        w_in = f(hg_w_in[0])
        for k, n in enumerate(("wq", "wf", "wv", "wg")):
            m["L2_" + n] = ca(w_in[:, k * 4096 + h * 2048:k * 4096 + (h + 1) * 2048])
        lbl = f(hg_lb_logits)
        m["L2_lbl"] = ca(np.concatenate([colT(lbl[l], h) for l in range(4)], axis=1))
        m["L2_ng"] = ca(f(hg_norm_gain[0]).reshape(128, 1))
        m["L2_ones"] = np.ones((128, 128), np.float32)
        m["L2_rmask"] = rmask
        m["L2_bdmask"] = bdmask
        m["L2_wo"] = ca(f(hg_w_out[0])[:, h * 2048:(h + 1) * 2048])
        in_maps.append(m)
    nc = build_fused()
    res = run_bass_kernel_spmd(nc, in_maps, core_ids=list(range(NCORES)))
    out = np.empty((B, S, D), np.float32)
    for c in range(NCORES):
        b, h = c // 2, c % 2
        out[b][:, h * 2048:(h + 1) * 2048] = res.results[c]["out"]
    return out
```

```python
import contextlib
import numpy as np
import ml_dtypes
import concourse.bass as bass
import concourse.mybir as mybir
from concourse.bass_utils import run_bass_kernel_spmd

F32 = mybir.dt.float32
BF16 = mybir.dt.bfloat16
AF = mybir.ActivationFunctionType
ALU = mybir.AluOpType
AX = mybir.AxisListType

D = 4096
B = 4
S = 2048
EPS = 1e-6
NCORES = 8


class Prog:
    ENGS = ("pe", "act", "dve", "pool", "sp")
    NDS = 24

    def __init__(self, nc, es, self_sync=True):
        self.nc = nc
        self.es = es
        self.self_sync = self_sync
        self.eng = dict(pe=nc.tensor, act=nc.scalar, dve=nc.vector, pool=nc.gpsimd, sp=nc.sync)
        self.sem = {e: es.enter_context(nc.semaphore("s_" + e)) for e in self.ENGS}
        self.cnt = {e: 0 for e in self.ENGS}
        self.seen = {e: {} for e in self.ENGS}
        self.dsem = [es.enter_context(nc.semaphore("d%d" % i)) for i in range(self.NDS)]
        self.dcnt = [0] * self.NDS
        self.drr = 0
        self.state = {}
        self.ninst = 0
        self.pre = ""
        self.es_cur = None
        self.ccsem = es.enter_context(nc.semaphore("ccsem"))
        self.cccnt = 0

    def sb(self, name, shape, dt):
        es = self.es_cur if self.es_cur is not None else self.es
        return es.enter_context(self.nc.sbuf_tensor(self.pre + name, list(shape), dt))

    def ps(self, name, shape, dt=F32):
        return self.es.enter_context(self.nc.psum_tensor(name, list(shape), dt))

    def _deps(self, reads, writes):
        deps = []
        for k in reads:
            st = self.state.get(k)
            if st is not None and st[0] is not None:
                deps.append(st[0])
        for k in writes:
            st = self.state.get(k)
            if st is not None:
                if st[0] is not None:
                    deps.append(st[0])
                deps.extend(st[1].values())
        return deps

    def _wait(self, e, deps):
        best = {}
        for (sid, sem, val) in deps:
            if sid not in best or best[sid][1] < val:
                best[sid] = (sem, val)
        for sid, (sem, val) in best.items():
            if self.seen[e].get(sid, 0) >= val:
                continue
            if sid == e and (e == "pe" or not self.self_sync):
                continue
            self.eng[e].wait_ge(sem, val)
            self.ninst += 1
            self.seen[e][sid] = val

    def _record(self, tok, reads, writes):
        for k in reads:
            st = self.state.get(k)
            if st is None:
                st = [None, {}]
                self.state[k] = st
            st[1][tok[0]] = tok
        for k in writes:
            self.state[k] = [tok, {}]

    def op(self, e, fn, reads=(), writes=(), sig=True):
        self._wait(e, self._deps(reads, writes))
        ins = fn(self.eng[e])
        self.ninst += 1
        if sig:
            self.cnt[e] += 1
            ins.then_inc(self.sem[e], 1)
            self._record((e, self.sem[e], self.cnt[e]), reads, writes)
        else:
            self._record((e, self.sem[e], self.cnt[e] + 1), reads, writes)

    def dma(self, q, out, in_, reads=(), writes=()):
        i = self.drr
        self.drr = (i + 1) % self.NDS
        sem = self.dsem[i]
        sid = "d%d" % i
        deps = self._deps(reads, writes)
        if self.dcnt[i] > 0:
            deps.append((sid, sem, self.dcnt[i]))
        self._wait(q, deps)
        self.eng[q].dma_start(out=out, in_=in_).then_inc(sem, 16)
        self.ninst += 1
        self.dcnt[i] += 16
        self._record((sid, sem, self.dcnt[i]), reads, writes)

    def cc(self, kind, groups, in_, out, reads=(), writes=()):
        deps = self._deps(reads, writes)
        self._wait("pool", deps)
        self.eng["pool"].collective_compute(kind, ALU.bypass, replica_groups=groups, ins=[in_.opt()], outs=[out.opt()]).then_inc(self.ccsem)
        self.ninst += 1
        self.cccnt += 1
        self._record(("cc", self.ccsem, self.cccnt), reads, writes)

    def finish(self):
        for i in range(self.NDS):
            if self.dcnt[i] > 0:
                self.eng["sp"].wait_ge(self.dsem[i], self.dcnt[i])
        for e in self.ENGS:
            if e != "sp" and self.cnt[e] > 0:
                self.eng["sp"].wait_ge(self.sem[e], self.cnt[e])
        if self.cccnt > 0:
            self.eng["sp"].wait_ge(self.ccsem, self.cccnt)


class PsumRing:
    def __init__(self, P, n=8):
        self.P = P
        self.t = [P.ps("psr%d" % i, [128, 512]) for i in range(n)]
        self.i = 0

    def next(self):
        i = self.i % len(self.t)
        self.i += 1
        return self.t[i], ("psr", i)


def barrier(P):
    toks = []
    for e in P.ENGS:
        if P.cnt[e] > 0:
            toks.append((e, P.sem[e], P.cnt[e]))
    for i in range(P.NDS):
        if P.dcnt[i] > 0:
            toks.append(("d%d" % i, P.dsem[i], P.dcnt[i]))
    if P.cccnt > 0:
        toks.append(("cc", P.ccsem, P.cccnt))
    for e in P.ENGS:
        P._wait(e, [t for t in toks if t[0] != e])


def phase_hT(P, nc, PR, x, gR, ident, hT, ntok=S, xsplit=False):
    KC = D // 128
    with contextlib.ExitStack() as es2:
        xs = [es2.enter_context(nc.sbuf_tensor(P.pre + "hx%d" % i, [128, D], F32)) for i in range(2)]
        junk = es2.enter_context(nc.sbuf_tensor(P.pre + "hjunk", [128, D], BF16))
        grep_ = es2.enter_context(nc.sbuf_tensor(P.pre + "hgrep", [128, D], F32))
        st = es2.enter_context(nc.sbuf_tensor(P.pre + "hst", [128, 8], F32))
        P.dma("sp", grep_[:], gR, writes=["hgrep"])
        gi = 0
        for tt in range(ntok // 128):
            xb = xs[tt % 2]
            xk = ("hx", tt % 2)
            if xsplit:
                for r_ in range(2):
                    row = (tt // 2) * 512 + r_ * 256 + (tt % 2) * 128
                    P.dma("sp", xb[:, r_ * 2048:(r_ + 1) * 2048], x[row:row + 128, :], reads=["x_g"], writes=[xk])
            else:
                P.dma("sp", xb[:], x[tt * 128:(tt + 1) * 128, :], reads=["x_g"], writes=[xk])
            P.op("act", lambda e: e.activation(out=junk[:], in_=xb[:], func=AF.Square, accum_out=st[:, 0:1]),
                 reads=[xk], writes=["hjunk", "hst0"])
            P.op("dve", lambda e: e.tensor_scalar(out=st[:, 1:2], in0=st[:, 0:1], scalar1=1.0 / D, scalar2=EPS,
                                                   op0=ALU.mult, op1=ALU.add), reads=["hst0"], writes=["hst1"])
            P.op("act", lambda e: e.activation(out=st[:, 2:3], in_=st[:, 1:2], func=AF.Sqrt),
                 reads=["hst1"], writes=["hst2"])
            P.op("dve", lambda e: e.reciprocal(out=st[:, 3:4], in_=st[:, 2:3]), reads=["hst2"], writes=["hst3"])
            P.op("dve", lambda e: e.scalar_tensor_tensor(out=xb[:], in0=xb[:], scalar=st[:, 3:4], in1=grep_[:],
                                                          op0=ALU.mult, op1=ALU.mult), reads=[xk, "hst3", "hgrep"], writes=[xk])
            for c0 in range(0, KC, 4):
                ps, pk = PR.next()
                for k in range(4):
                    c = c0 + k
                    P.op("pe", lambda e, c=c, k=k, ps=ps: e.transpose(
                        out=ps[:, k * 128:(k + 1) * 128], in_=xb[:, c * 128:(c + 1) * 128], identity=ident[:]),
                        reads=[xk, "ident"], writes=[pk], sig=(k == 3))
                dst = hT[:, c0:c0 + 4, tt * 128:(tt + 1) * 128]
                src_ = ps[:].rearrange("p (k t) -> p k t", k=4)
                if gi % 2 == 0:
                    P.op("act", lambda e, dst=dst, src_=src_: e.copy(out=dst, in_=src_), reads=[pk], writes=[("hT", tt // 4)])
                else:
                    P.op("dve", lambda e, dst=dst, src_=src_: e.tensor_copy(out=dst, in_=src_), reads=[pk], writes=[("hT", tt // 4)])
                gi += 1
        barrier(P)


def proj_fm(P, PR, hT, w_chunk, wkey, tb, evac):
    KC = D // 128
    ps, pk = PR.next()
    for c in range(KC):
        P.op("pe", lambda e, c=c, ps=ps: e.matmul(ps[:], w_chunk[:, c, :], hT[:, c, tb * 512:(tb + 1) * 512],
                                                  start=(c == 0), stop=(c == KC - 1)),
             reads=[wkey, ("hT", tb)], writes=[pk], sig=(c == KC - 1))
    evac(ps, pk)


def emit_outproj(P, nc, PR, yT, x, w, gain, out, ss_src, ss_g, mid_cc=None):
    KC = D // 128
    NB = 256
    TP = 1024
    NCL = 2048
    NTT = TP // 128
    if True:
        yT_sb = P.sb("yT_sb", [128, KC, TP], BF16)
        w_sb = [P.sb("w_sb%d" % i, [128, KC, NB], BF16) for i in range(2)]
        o_sb = [P.sb("o_sb%d" % i, [128, NCL], F32) for i in range(NTT)]
        x_sb = [P.sb("x_sb%d" % i, [128, NCL], F32) for i in range(2)]
        g_sb = P.sb("g_sb", [128, NCL], F32)
        junk = P.sb("junk", [128, NCL], BF16)
        ssq = P.sb("ssq", [128, NTT], F32)
        ssg = P.sb("ssg", [128, 2, NTT], F32)
        st = P.sb("ost", [128, 4 * NTT], F32)
        yT_v = yT.rearrange("(c p) t -> p c t", p=128)
        w_v = w.rearrange("(c p) n -> p c n", p=128)
        P.dma("sp", g_sb[:], gain, writes=["g"])
        wi = 0
        for p_ in range(S // TP):
            for r_ in range(2):
                for k_ in range(4):
                    c0_, s0_ = r_ * 16 + k_ * 4, k_ * 8 + r_ * 4
                    P.dma("sp", yT_sb[:, c0_:c0_ + 4, :], yT_v[:, s0_:s0_ + 4, p_ * TP:(p_ + 1) * TP], reads=["yT_g"], writes=["yT"])
            for nb in range(NCL // NB):
                wb = wi % 2
                wi += 1
                for c0 in range(0, KC, 8):
                    P.dma("pool", w_sb[wb][:, c0:c0 + 8, :], w_v[:, c0:c0 + 8, nb * NB:(nb + 1) * NB],
                          writes=[("w", wb, c0)])
                for tt in range(NTT):
                    ps, pk = PR.next()
                    for c in range(KC):
                        P.op("pe", lambda e, c=c, tt=tt, ps=ps, wb=wb: e.matmul(
                            ps[:, 0:NB], yT_sb[:, c, tt * 128:(tt + 1) * 128], w_sb[wb][:, c, :],
                            start=(c == 0), stop=(c == KC - 1)),
                            reads=["yT", ("w", wb, (c // 8) * 8)], writes=[pk], sig=(c == KC - 1))
                    P.op("act", lambda e, tt=tt, ps=ps, nb=nb: e.copy(
                        out=o_sb[tt][:, nb * NB:(nb + 1) * NB], in_=ps[:, 0:NB]),
                        reads=[pk], writes=[("o", tt)])
            if p_ == 1 and mid_cc is not None:
                mid_cc()
            for tt in range(NTT):
                P.op("act", lambda e, tt=tt: e.activation(out=junk[:], in_=o_sb[tt][:], func=AF.Square,
                                                         accum_out=ssq[:, tt:tt + 1]),
                     reads=[("o", tt)], writes=["junk", "ssq"])
            P.dma("sp", ss_src, ssq[:], reads=["ssq"], writes=["ss_src"])
            P.cc("AllGather", PAIRS, ss_src, ss_g, reads=["ss_src"], writes=["ss_g"])
            P.dma("sp", ssg[:], ss_g.rearrange("(r p) c -> p r c", p=128), reads=["ss_g"], writes=["ssg"])
            P.op("dve", lambda e: e.tensor_tensor(out=st[:, 0:NTT], in0=ssg[:, 0, :], in1=ssg[:, 1, :], op=ALU.add),
                 reads=["ssg"], writes=["st0"])
            P.op("dve", lambda e: e.tensor_scalar(out=st[:, NTT:2 * NTT], in0=st[:, 0:NTT], scalar1=1.0 / D, scalar2=EPS,
                                                   op0=ALU.mult, op1=ALU.add), reads=["st0"], writes=["st1"])
            P.op("act", lambda e: e.activation(out=st[:, 2 * NTT:3 * NTT], in_=st[:, NTT:2 * NTT], func=AF.Sqrt),
                 reads=["st1"], writes=["st2"])
            P.op("dve", lambda e: e.reciprocal(out=st[:, 3 * NTT:4 * NTT], in_=st[:, 2 * NTT:3 * NTT]), reads=["st2"], writes=["st3"])
            for tt in range(NTT):
                t0 = p_ * TP + tt * 128
                xb, xk = x_sb[tt % 2], ("x", tt % 2)
                P.dma("sp", xb[:], x[t0:t0 + 128, :], reads=["x_my"], writes=[xk])
                P.op("dve", lambda e, tt=tt: e.scalar_tensor_tensor(
                    out=o_sb[tt][:], in0=o_sb[tt][:], scalar=st[:, 3 * NTT + tt:3 * NTT + tt + 1], in1=g_sb[:],
                    op0=ALU.mult, op1=ALU.mult), reads=[("o", tt), "st3", "g"], writes=[("o", tt)])
                P.op("dve", lambda e, tt=tt, xb=xb: e.tensor_tensor(out=o_sb[tt][:], in0=o_sb[tt][:], in1=xb[:], op=ALU.add),
                     reads=[("o", tt), xk], writes=[("o", tt)])
                P.dma("sp", out[t0:t0 + 128, :], o_sb[tt][:], reads=[("o", tt)], writes=[("x_out", p_ * NTT + tt)])


def emit_rg(P, nc, PR, gT_sb, gcol, ident, io):
    x = io["x"]
    gT = io["gT"]
    ident_d = io["ident"]
    wx = io["wx"]
    wz = io["wz"]
    cw_d = io["cw"]
    cb_d = io["cb"]
    lam_d = io["lam"]
    gb_d = io["gb"]
    gw_d = io["gw"]
    yT = io["yT"]
    KC = D // 128
    if True:
        hT = P.sb("hT", [128, KC, S], BF16)
        cw = P.sb("cw_sb", [128, 64], F32)
        cb = P.sb("cb_sb", [128, 16], F32)
        lam = P.sb("lam_sb", [128, 16], F32)
        c1 = P.sb("c1_sb", [128, 16], F32)
        gb = P.sb("gb_sb", [128, 32], F32)
        P.dma("sp", cw[:], cw_d, writes=["cw"])
        P.dma("sp", cb[:], cb_d, writes=["cb"])
        P.dma("sp", lam[:], lam_d, writes=["lam"])
        P.dma("sp", gb[:], gb_d, writes=["gb"])
        phase_hT(P, nc, PR, x, io["gR"], ident, hT, xsplit=io["xsplit"])
        P.op("act", lambda e: e.activation(out=c1[:], in_=lam[:], func=AF.Exp, scale=-1.0), reads=["lam"], writes=["c1"])
        P.op("act", lambda e: e.activation(out=c1[:], in_=c1[:], func=AF.Ln, bias=1.0), reads=["c1"], writes=["c1"])
        P.op("dve", lambda e: e.tensor_scalar(out=c1[:], in0=c1[:], scalar1=-8.0, scalar2=None, op0=ALU.mult),
             reads=["c1"], writes=["c1"])
        wxs = [P.sb("wxs%d" % i, [128, KC, 128], BF16) for i in range(2)]
        wzs = [P.sb("wzs%d" % i, [128, KC, 128], BF16) for i in range(2)]
        gws = P.sb("gws", [128, 2, 2, 256], BF16)
        xraw = [P.sb("xraw%d" % i, [128, 3 + 512], F32) for i in range(2)]
        xc = [P.sb("xc%d" % i, [128, 512], F32) for i in range(2)]
        xcb = [P.sb("xcb%d" % i, [128, 512], BF16) for i in range(2)]
        gi = [P.sb("gi%d" % i, [128, 512], F32) for i in range(2)]
        ga = [P.sb("ga%d" % i, [128, 512], F32) for i in range(2)]
        gm = [P.sb("gm%d" % i, [128, 512], F32) for i in range(2)]
        hs = [[P.sb("hs%d_%d" % (i, k), [128, 512], F32) for k in range(2)] for i in range(2)]
        zs = [P.sb("zs%d" % i, [128, 512], F32) for i in range(2)]
        ys = [P.sb("ys%d" % i, [128, 512], BF16) for i in range(2)]
        wx_v = wx.rearrange("(c p) n -> p c n", p=128)
        wz_v = wz.rearrange("(c p) n -> p c n", p=128)
        for blk in range(8):
            for j in range(2):
                col0 = blk * 256 + j * 128
                for c0 in range(0, KC, 8):
                    P.dma("pool", wxs[j][:, c0:c0 + 8, :], wx_v[:, c0:c0 + 8, col0:col0 + 128], writes=[("wx", j)])
                for c0 in range(0, KC, 8):
                    P.dma("pool", wzs[j][:, c0:c0 + 8, :], wz_v[:, c0:c0 + 8, col0:col0 + 128], writes=[("wz", j)])
            for k in range(2):
                P.dma("pool", gws[:, k], gw_d[k, blk].rearrange("(jc p) e -> p jc e", p=128), writes=["gw"])
            for j in range(2):
                P.op("dve", lambda e, j=j: e.memset(xraw[j][:, 0:3], 0.0), writes=[("xraw", j)])
            for tb in range(4):
                for j in range(2):
                    ch = blk * 2 + j

                    def ev_x(ps, pk, j=j):
                        P.op("act", lambda e: e.copy(out=xraw[j][:, 3:515], in_=ps[:]), reads=[pk], writes=[("xraw", j)])
                    proj_fm(P, PR, hT, wxs[j], ("wx", j), tb, ev_x)
                    P.op("dve", lambda e, j=j, ch=ch: e.tensor_scalar(
                        out=xc[j][:], in0=xraw[j][:, 0:512], scalar1=cw[:, ch * 4:ch * 4 + 1], scalar2=cb[:, ch:ch + 1],
                        op0=ALU.mult, op1=ALU.add), reads=[("xraw", j), "cw", "cb"], writes=[("xc", j)])
                    for k in range(1, 4):
                        P.op("dve", lambda e, j=j, ch=ch, k=k: e.scalar_tensor_tensor(
                            out=xc[j][:], in0=xraw[j][:, k:k + 512], scalar=cw[:, ch * 4 + k:ch * 4 + k + 1],
                            in1=xc[j][:], op0=ALU.mult, op1=ALU.add), reads=[("xraw", j), ("xc", j), "cw"],
                            writes=[("xc", j)])
                    P.op("act", lambda e, j=j: e.copy(out=xcb[j][:], in_=xc[j][:]), reads=[("xc", j)], writes=[("xcb", j)])
                    P.op("dve", lambda e, j=j: e.tensor_copy(out=xraw[j][:, 0:3], in_=xraw[j][:, 512:515]),
                         reads=[("xraw", j)], writes=[("xraw", j)])
                for je in range(2):
                    def ev_z(ps, pk, je=je):
                        P.op("act", lambda e: e.activation(out=zs[je][:], in_=ps[:], func=AF.Silu), reads=[pk],
                             writes=[("zs", je)])
                    proj_fm(P, PR, hT, wzs[je], ("wz", je), tb, ev_z)
                for je in range(2):
                    ch = blk * 2 + je
                    for k in range(2):
                        ps, pk = PR.next()
                        for jc in range(2):
                            P.op("pe", lambda e, k=k, jc=jc, je=je, ps=ps: e.matmul(
                                ps[:], gws[:, k, jc, je * 128:(je + 1) * 128], xcb[jc][:], start=(jc == 0), stop=(jc == 1)),
                                reads=["gw", ("xcb", jc)], writes=[pk])
                        dst = gi[je] if k == 0 else ga[je]
                        dk = ("gi", je) if k == 0 else ("ga", je)
                        P.op("act", lambda e, ps=ps, dst=dst, k=k, ch=ch: e.activation(
                            out=dst[:], in_=ps[:], func=AF.Sigmoid, bias=gb[:, k * 16 + ch:k * 16 + ch + 1]),
                            reads=[pk, "gb"], writes=[dk])
                    P.op("act", lambda e, je=je, ch=ch: e.activation(out=ga[je][:], in_=ga[je][:], func=AF.Exp,
                                                                   scale=c1[:, ch:ch + 1]),
                         reads=[("ga", je), "c1"], writes=[("ga", je)])
                    P.op("dve", lambda e, je=je: e.tensor_tensor(out=gm[je][:], in0=ga[je][:], in1=ga[je][:], op=ALU.mult),
                         reads=[("ga", je)], writes=[("gm", je)])
                    P.op("act", lambda e, je=je: e.activation(out=gm[je][:], in_=gm[je][:], func=AF.Sqrt, scale=-1.0, bias=1.0),
                         reads=[("gm", je)], writes=[("gm", je)])
                    if tb == 0:
                        P.op("dve", lambda e, je=je: e.memset(gm[je][:, 0:1], 1.0), writes=[("gm", je)])
                    P.op("dve", lambda e, je=je: e.tensor_tensor(out=gi[je][:], in0=gi[je][:], in1=gm[je][:], op=ALU.mult),
                         reads=[("gi", je), ("gm", je)], writes=[("gi", je)])
                    P.op("dve", lambda e, je=je: e.tensor_tensor(out=gi[je][:], in0=gi[je][:], in1=xc[je][:], op=ALU.mult),
                         reads=[("gi", je), ("xc", je)], writes=[("gi", je)])
                    cur = hs[je][tb % 2]
                    prev = hs[je][(tb + 1) % 2]
                    init = 0.0 if tb == 0 else prev[:, 511:512]
                    P.op("dve", lambda e, je=je, cur=cur, init=init: e.tensor_tensor_scan(
                        out=cur[:], data0=ga[je][:], data1=gi[je][:], initial=init, op0=ALU.mult, op1=ALU.add),
                        reads=[("ga", je), ("gi", je), ("hs", je, (tb + 1) % 2)], writes=[("hs", je, tb % 2)])

                    P.op("dve", lambda e, je=je, cur=cur: e.tensor_tensor(out=ys[je][:], in0=cur[:], in1=zs[je][:], op=ALU.mult),
                         reads=[("hs", je, tb % 2), ("zs", je)], writes=[("ys", je)])
                    P.dma("sp", yT[ch * 128:(ch + 1) * 128, tb * 512:(tb + 1) * 512], ys[je][:], reads=[("ys", je)],
                          writes=[("ysrc", ch, tb)])
            io["ydone"](blk * 2)
            io["ydone"](blk * 2 + 1)


def emit_hg(P, nc, PR, gT_sb, gcol, ident, io, layer=2):
    x = io["x"]
    gT = io["gT"]
    ident_d = io["ident"]
    ones_d = io["ones"]
    rmask_d = io["rmask"]
    bdmask_d = io["bdmask"]
    lbl_d = io["lbl"]
    ng_d = io["ng"]
    yT = io["yT"]
    wd = [io[n] for n in ("wq", "wf", "wv", "wg")]
    KC = D // 128
    if True:
        hT = P.sb("hT", [128, KC, S], BF16)
        ones = P.sb("ones_sb", [128, 128], F32)
        rmask = P.sb("rmask_sb", [128, 512], F32)
        bdmask = P.sb("bdmask_sb", [128, 128], F32)
        lbl = P.sb("lbl_sb", [128, 64], F32)
        lbe = P.sb("lbe_sb", [128, 64], F32)
        lb = P.sb("lb_sb", [128, 16], F32)
        oml = P.sb("oml_sb", [128, 16], F32)
        lsum = P.sb("lsum_sb", [128, 16], F32)
        ng = P.sb("ng_sb", [128, 1], F32)
        for t, d, k in ((ones, ones_d, "ones"), (rmask, rmask_d, "rmask"),
                        (bdmask, bdmask_d, "bdmask"), (lbl, lbl_d, "lbl"), (ng, ng_d, "ng")):
            P.dma("sp", t[:], d, writes=[k])
        phase_hT(P, nc, PR, x, io["gR"], ident, hT, xsplit=io["xsplit"])
        P.op("act", lambda e: e.activation(out=lbe[:], in_=lbl[:], func=AF.Exp), reads=["lbl"], writes=["lbe"])
        P.op("dve", lambda e: e.tensor_tensor(out=lsum[:], in0=lbe[:, 0:16], in1=lbe[:, 16:32], op=ALU.add),
             reads=["lbe"], writes=["lsum"])
        P.op("dve", lambda e: e.tensor_tensor(out=lsum[:], in0=lsum[:], in1=lbe[:, 32:48], op=ALU.add),
             reads=["lbe", "lsum"], writes=["lsum"])
        P.op("dve", lambda e: e.tensor_tensor(out=lsum[:], in0=lsum[:], in1=lbe[:, 48:64], op=ALU.add),
             reads=["lbe", "lsum"], writes=["lsum"])
        P.op("dve", lambda e: e.reciprocal(out=lsum[:], in_=lsum[:]), reads=["lsum"], writes=["lsum"])
        P.op("dve", lambda e: e.memset(lb[:], 0.0), writes=["lb"])
        for l in range(1, layer + 1):
            P.op("dve", lambda e, l=l: e.tensor_tensor(out=lb[:], in0=lb[:], in1=lbe[:, l * 16:(l + 1) * 16], op=ALU.add),
                 reads=["lb", "lbe"], writes=["lb"])
        P.op("dve", lambda e: e.tensor_tensor(out=lb[:], in0=lb[:], in1=lsum[:], op=ALU.mult), reads=["lb", "lsum"], writes=["lb"])
        P.op("dve", lambda e: e.tensor_scalar(out=oml[:], in0=lb[:], scalar1=-1.0, scalar2=1.0, op0=ALU.mult, op1=ALU.add),
             reads=["lb"], writes=["oml"])
        ws = [P.sb("hw%d" % i, [128, KC, 128], BF16) for i in range(4)]
        wv = [w.rearrange("(c p) n -> p c n", p=128) for w in wd]
        f32b = lambda n: P.sb(n, [128, 512], F32)
        qs, ff, logf, kk, bb, eb, enb, kef, gs, osb, sq, rstd, tmpo = [f32b("hgb%d" % i) for i in range(13)]
        qe = P.sb("qe", [128, 512], BF16)
        kebf = P.sb("kebf", [128, 512], BF16)
        ys = P.sb("hys", [128, 512], BF16)
        vtm = [P.sb("vtm%d" % i, [128, 128], BF16) for i in range(4)]
        ketm = [P.sb("ketm%d" % i, [128, 128], BF16) for i in range(2)]
        attT = [P.sb("attT%d" % i, [128, 128], BF16) for i in range(2)]
        Sf = P.sb("Sf", [128, 128], F32)
        Stmp = P.sb("Stmp", [128, 128], F32)
        Sb = P.sb("Sb", [128, 128], BF16)
        for hh in range(16):
            for i in range(4):
                for c0 in range(0, KC, 8):
                    P.dma("pool", ws[i][:, c0:c0 + 8, :], wv[i][:, c0:c0 + 8, hh * 128:(hh + 1) * 128], writes=[("hw", i)])
            P.op("dve", lambda e: e.memset(Sf[:], 0.0), writes=["Sf"])
            P.op("dve", lambda e: e.memset(Sb[:], 0.0), writes=["Sb"])
            for tb in range(4):
                def ev_q(ps, pk):
                    P.op("act", lambda e: e.activation(out=qs[:], in_=ps[:], func=AF.Silu), reads=[pk], writes=["qs"])
                proj_fm(P, PR, hT, ws[0], ("hw", 0), tb, ev_q)

                def ev_f(ps, pk):
                    P.op("act", lambda e: e.activation(out=ff[:], in_=ps[:], func=AF.Sigmoid), reads=[pk], writes=["ff"])
                proj_fm(P, PR, hT, ws[1], ("hw", 1), tb, ev_f)

                def ev_g(ps, pk):
                    P.op("act", lambda e: e.activation(out=gs[:], in_=ps[:], func=AF.Silu), reads=[pk], writes=["gs"])
                proj_fm(P, PR, hT, ws[3], ("hw", 3), tb, ev_g)
                for tt in range(4):
                    ps, pk = PR.next()
                    for c in range(KC):
                        P.op("pe", lambda e, c=c, ps=ps, tt=tt: e.matmul(
                            ps[:, 0:128], hT[:, c, tb * 512 + tt * 128:tb * 512 + (tt + 1) * 128], ws[2][:, c, :],
                            start=(c == 0), stop=(c == KC - 1)), reads=[("hw", 2), ("hT", tb)], writes=[pk], sig=(c == KC - 1))
                    P.op("act", lambda e, ps=ps, tt=tt: e.copy(out=vtm[tt][:], in_=ps[:, 0:128]), reads=[pk], writes=[("vtm", tt)])
                P.op("dve", lambda e: e.tensor_scalar(out=ff[:], in0=ff[:], scalar1=oml[:, hh:hh + 1], scalar2=lb[:, hh:hh + 1],
                                                       op0=ALU.mult, op1=ALU.add), reads=["ff", "oml", "lb"], writes=["ff"])
                P.op("act", lambda e: e.activation(out=logf[:], in_=ff[:], func=AF.Ln), reads=["ff"], writes=["logf"])
                P.op("dve", lambda e: e.tensor_scalar(out=kk[:], in0=ff[:], scalar1=-1.0, scalar2=1.0, op0=ALU.mult, op1=ALU.add),
                     reads=["ff"], writes=["kk"])
                P.op("dve", lambda e: e.tensor_tensor_scan(out=bb[:], data0=rmask[:], data1=logf[:], initial=0.0,
                                                            op0=ALU.mult, op1=ALU.add), reads=["rmask", "logf"], writes=["bb"])
                P.op("act", lambda e: e.activation(out=eb[:], in_=bb[:], func=AF.Exp), reads=["bb"], writes=["eb"])
                P.op("act", lambda e: e.activation(out=enb[:], in_=bb[:], func=AF.Exp, scale=-1.0), reads=["bb"], writes=["enb"])
                P.op("dve", lambda e: e.tensor_tensor(out=qe[:], in0=qs[:], in1=eb[:], op=ALU.mult), reads=["qs", "eb"], writes=["qe"])
                P.op("dve", lambda e: e.tensor_tensor(out=kef[:], in0=kk[:], in1=enb[:], op=ALU.mult), reads=["kk", "enb"], writes=["kef"])
                P.op("act", lambda e: e.copy(out=kebf[:], in_=kef[:]), reads=["kef"], writes=["kebf"])
                for tt in range(4):
                    sl = slice(tt * 128, (tt + 1) * 128)
                    kt = ketm[tt % 2]
                    at = attT[tt % 2]
                    ktk = ("ketm", tt % 2)
                    atk = ("attT", tt % 2)
                    ps, pk = PR.next()
                    P.op("pe", lambda e, ps=ps, sl=sl: e.transpose(out=ps[:, 0:128], in_=kef[:, sl], identity=ident[:]),
                         reads=["kef", "ident"], writes=[pk])
                    P.op("act", lambda e, ps=ps, kt=kt: e.copy(out=kt[:], in_=ps[:, 0:128]), reads=[pk], writes=[ktk])
                    ps2, pk2 = PR.next()
                    P.op("pe", lambda e, ps2=ps2, sl=sl: e.matmul(ps2[:, 0:128], kebf[:, sl], qe[:, sl], start=True, stop=True),
                         reads=["kebf", "qe"], writes=[pk2])
                    P.op("dve", lambda e, ps2=ps2, at=at: e.tensor_tensor(out=at[:], in0=ps2[:, 0:128], in1=bdmask[:], op=ALU.mult),
                         reads=[pk2, "bdmask"], writes=[atk])
                    pso, pko = PR.next()
                    P.op("pe", lambda e, pso=pso, at=at, tt=tt: e.matmul(pso[:, 0:128], vtm[tt][:], at[:], start=True, stop=False),
                         reads=[("vtm", tt), atk], writes=[pko])
                    for cc in range(2):
                        csl = slice(tt * 128 + cc * 64, tt * 128 + (cc + 1) * 64)
                        rows = slice(cc * 64, (cc + 1) * 64)
                        P.op("pe", lambda e, pso=pso, cc=cc, csl=csl: e.matmul(
                            pso[:, cc * 64:(cc + 1) * 64], Sb[:], qe[:, csl], start=False, stop=(cc == 1)),
                            reads=["Sb", "qe"], writes=[pko])
                        pss_, pks = PR.next()
                        P.op("pe", lambda e, pss_=pss_, kt=kt, rows=rows, tt=tt: e.matmul(
                            pss_[:, 0:128], kt[rows, :], vtm[tt][rows, :], start=True, stop=True),
                            reads=[ktk, ("vtm", tt)], writes=[pks])
                        ecol = tt * 128 + (cc + 1) * 64 - 1
                        P.op("dve", lambda e, ecol=ecol: e.tensor_scalar(out=Stmp[:], in0=Sf[:], scalar1=eb[:, ecol:ecol + 1],
                                                                        scalar2=None, op0=ALU.mult),
                             reads=["Sf", "eb"], writes=["Stmp"])
                        P.op("dve", lambda e, ecol=ecol, pss_=pss_: e.scalar_tensor_tensor(
                            out=Sf[:], in0=pss_[:, 0:128], scalar=eb[:, ecol:ecol + 1], in1=Stmp[:], op0=ALU.mult, op1=ALU.add),
                            reads=[pks, "eb", "Stmp"], writes=["Sf"])
                        P.op("act", lambda e: e.copy(out=Sb[:], in_=Sf[:]), reads=["Sf"], writes=["Sb"])
                    P.op("act", lambda e, pso=pso, sl=sl: e.copy(out=osb[:, sl], in_=pso[:, 0:128]), reads=[pko], writes=["osb"])
                P.op("act", lambda e: e.activation(out=sq[:], in_=osb[:], func=AF.Square), reads=["osb"], writes=["sq"])
                psn, pkn = PR.next()
                P.op("pe", lambda e, psn=psn: e.matmul(psn[:], ones[:], sq[:], start=True, stop=True), reads=["ones", "sq"], writes=[pkn])
                P.op("act", lambda e, psn=psn: e.activation(out=rstd[:], in_=psn[:], func=AF.Ln, scale=1.0 / 128, bias=EPS),
                     reads=[pkn], writes=["rstd"])
                P.op("act", lambda e: e.activation(out=rstd[:], in_=rstd[:], func=AF.Exp, scale=-0.5), reads=["rstd"], writes=["rstd"])
                P.op("dve", lambda e: e.scalar_tensor_tensor(out=tmpo[:], in0=osb[:], scalar=ng[:, 0:1], in1=rstd[:],
                                                              op0=ALU.mult, op1=ALU.mult), reads=["osb", "ng", "rstd"], writes=["tmpo"])
                P.op("dve", lambda e: e.tensor_tensor(out=ys[:], in0=tmpo[:], in1=gs[:], op=ALU.mult), reads=["tmpo", "gs"], writes=["ys"])
                P.dma("sp", yT[hh * 128:(hh + 1) * 128, tb * 512:(tb + 1) * 512], ys[:], reads=["ys"], writes=[("ysrc", hh, tb)])
            io["ydone"](hh)


NSA_SCALE = 128 ** -0.5
NEGM = -1.0e7
NREL = 19


def emit_nsa(P, nc, PR, psUl, psDl, gT_sb, gcol, ident, io):
    x = io["x"]
    gT = io["gT"]
    ident_d = io["ident"]
    wq = io["wq"]
    wz = io["wz"]
    wkv = io["wkv"]
    wgl = io["wgl"]
    peT_d = io["peT"]
    w1_d = io["w1"]
    w2_d = io["w2"]
    slope_d = io["slope"]
    cst_d = io["cst"]
    R_d = io["Rt"]
    E_d = io["E"]
    selb0_d = io["selb0"]
    Mmat_d = io["Mmat"]
    A_d = io["At"]
    Badd_d = io["Bt"]
    SelR_d = io["SelR"]
    yT = io["yT"]
    qT_s = io["qT_s"]
    zT_s = io["zT_s"]
    kvT_s = io["kvT_s"]
    vtm_s = io["vtm_s"]
    tcmp_s = io["tcmp_s"]
    KC = D // 128
    if True:
        GT = P.sb("GT", [48, S], F32)
        with contextlib.ExitStack() as es1:
            hT = es1.enter_context(nc.sbuf_tensor(P.pre + "hT", [128, KC, S], BF16))
            phase_hT(P, nc, PR, x, io["gR"], ident, hT, xsplit=io["xsplit"])
            wsb = [es1.enter_context(nc.sbuf_tensor(P.pre + "nw%d" % i, [128, KC, 128], BF16)) for i in range(2)]
            wglsb = es1.enter_context(nc.sbuf_tensor(P.pre + "wglsb", [128, KC, 48], BF16))
            stg = [es1.enter_context(nc.sbuf_tensor(P.pre + "stg%d" % i, [128, 512], BF16)) for i in range(2)]
            si = [0]
            wi = [0]

            def load_w(src_v, col0):
                b = wi[0] % 2
                wi[0] += 1
                for c0 in range(0, KC, 8):
                    P.dma("pool", wsb[b][:, c0:c0 + 8, :], src_v[:, c0:c0 + 8, col0:col0 + 128], writes=[("nw", b)])
                return wsb[b], ("nw", b)

            def fm_chunk(src_v, col0, dst, func, scale):
                w, wk = load_w(src_v, col0)
                for tb in range(4):
                    def ev(ps, pk, tb=tb):
                        b = si[0] % 2
                        si[0] += 1
                        P.op("act", lambda e: e.activation(out=stg[b][:], in_=ps[:], func=func, scale=scale),
                             reads=[pk], writes=[("stg", b)])
                        P.dma("sp", dst[:, tb * 512:(tb + 1) * 512], stg[b][:], reads=[("stg", b)], writes=["scratch"])
                    proj_fm(P, PR, hT, w, wk, tb, ev)

            wq_v = wq.rearrange("(c p) n -> p c n", p=128)
            wz_v = wz.rearrange("(c p) n -> p c n", p=128)
            wkv_v = wkv.rearrange("(c p) n -> p c n", p=128)
            for hl in range(16):
                fm_chunk(wq_v, hl * 128, qT_s[hl], AF.Copy, NSA_SCALE)
                fm_chunk(wz_v, hl * 128, zT_s[hl], AF.Silu, 1.0)
            for gl in range(2):
                for n, i in enumerate((0, 1, 2, 4)):
                    fm_chunk(wkv_v, i * 256 + gl * 128, kvT_s[gl, n], AF.Copy, 1.0)
                for n, i in enumerate((3, 5)):
                    w, wk = load_w(wkv_v, i * 256 + gl * 128)
                    for tt in range(16):
                        ps, pk = PR.next()
                        for c in range(KC):
                            P.op("pe", lambda e, c=c, ps=ps, tt=tt, w=w: e.matmul(
                                ps[:, 0:128], hT[:, c, tt * 128:(tt + 1) * 128], w[:, c, :],
                                start=(c == 0), stop=(c == KC - 1)), reads=[wk, ("hT", tt // 4)], writes=[pk], sig=(c == KC - 1))
                        b = si[0] % 2
                        si[0] += 1
                        P.op("act", lambda e, ps=ps, b=b: e.copy(out=stg[b][:, 0:128], in_=ps[:, 0:128]), reads=[pk],
                             writes=[("stg", b)])
                        P.dma("sp", vtm_s[gl, n, tt * 128:(tt + 1) * 128, :], stg[b][:, 0:128], reads=[("stg", b)],
                              writes=["scratch"])
            for c0 in range(0, KC, 8):
                P.dma("pool", wglsb[:, c0:c0 + 8, :], wgl.rearrange("(c p) n -> p c n", p=128)[:, c0:c0 + 8, :], writes=["wgl"])
            for tb in range(4):
                ps, pk = PR.next()
                for c in range(KC):
                    P.op("pe", lambda e, c=c, ps=ps, tb=tb: e.matmul(ps[0:48, :], wglsb[:, c, :], hT[:, c, tb * 512:(tb + 1) * 512],
                                                                    start=(c == 0), stop=(c == KC - 1)),
                         reads=["wgl", ("hT", tb)], writes=[pk])
                P.op("act", lambda e, ps=ps, tb=tb: e.activation(out=GT[:, tb * 512:(tb + 1) * 512], in_=ps[0:48, :], func=AF.Sigmoid),
                     reads=[pk], writes=["GT"])
            barrier(P)
        Rt = P.sb("Rt_sb", [128, 13, 512], F32)
        SelR = P.sb("SelR_sb", [48, 48 * 128], F32)
        slope = P.sb("slope_sb", [128, 16], F32)
        cst = P.sb("cst_sb", [128, 16 * NREL], F32)
        E = P.sb("E_sb", [32, 2048], BF16)
        selbT = P.sb("selbT", [32, 2048], BF16)
        Mmat = P.sb("Mmat_sb", [128, 32], F32)
        At = P.sb("At_sb", [128, 256], F32)
        Bt = P.sb("Bt_sb", [128, 256], F32)
        onesb = P.sb("onesb", [128, 128], BF16)
        for k in range(13):
            P.dma("sp", Rt[:, k, :], R_d[k], writes=["Rt"])
        for t, d, k in ((SelR, SelR_d, "SelR"), (slope, slope_d, "slope"), (cst, cst_d, "cst"), (E, E_d, "E"),
                        (Mmat, Mmat_d, "Mmat"), (At, A_d, "At"), (Bt, Badd_d, "Bt")):
            P.dma("sp", t[:], d, writes=[k])
        P.op("dve", lambda e: e.memset(onesb[:], 1.0), writes=["onesb"])
        kvT = P.sb("kvT", [128, 4, S], BF16)
        vtm = P.sb("vtm", [128, 2, 16, 128], BF16)
        w1sb = P.sb("w1sb", [128, 32, 128], BF16)
        w2sb = P.sb("w2sb", [128, 128], BF16)
        peT = P.sb("peT_sb", [128, 32], F32)
        peTb = P.sb("peTb", [128, 32], BF16)
        cb_ = P.sb("cmpb", [128, 1], F32)
        xg = P.sb("xg", [128, 128], F32)
        xg2 = P.sb("xg2", [128, 128], F32)
        hidT = P.sb("hidT", [128, 128], BF16)
        KcT = P.sb("KcT", [128, 128], BF16)
        Vc = P.sb("Vc", [128, 128], BF16)
        qTb = P.sb("qTb", [128, S], BF16)
        zTb = P.sb("zTb", [128, S], BF16)
        tmp = [P.sb("atmp%d" % i, [128, 512], F32) for i in range(4)]
        Pf = P.sb("Pf", [128, 512], F32)
        Pb = [P.sb("Pb%d" % i, [128, 512], BF16) for i in range(4)]
        rden = P.sb("rden", [128, 512], F32)
        Ff = P.sb("Ff", [128, 512], F32)
        Tt = P.sb("Tt", [128, 512], F32)
        acc = P.sb("acc", [128, 512], F32)
        pn = P.sb("pn", [128, 512], F32)
        ys = P.sb("nys", [128, 512], BF16)
        pslc = P.sb("pslc", [128, 256], F32)
        sc = P.sb("sc", [128, 32], F32)
        sc2 = P.sb("sc2", [128, 32], F32)
        m8 = P.sb("m8", [128, 16], F32)
        selb = P.sb("selb", [128, 32], F32)
        ti_ = [0]
        bi_ = [0]
        pend = []

        def drain(keep=0):
            while len(pend) > keep:
                pend.pop(0)()

        def branch_step(kT_ap, kkey, v_ap, vkey, qsl, hl, rvar, cidx, nk, extra=None, first=False, last=False, want_f32=False):
            ps, pk = PR.next()
            P.op("pe", lambda e: e.matmul(ps[0:nk, :], kT_ap, qTb[:, qsl], start=True, stop=(extra is None)),
                 reads=[kkey, "qTb"], writes=[pk])
            if extra is not None:
                P.op("pe", lambda e: e.matmul(ps[0:nk, :], extra, selbT[:, qsl], start=False, stop=True),
                     reads=["E", "selbT"], writes=[pk])
            i = ti_[0] % 4
            ti_[0] += 1
            psU, psD = psUl[bi_[0] % 2], psDl[bi_[0] % 2]
            uk, dk = ("psU", bi_[0] % 2), ("psD", bi_[0] % 2)
            P.op("dve", lambda e: e.scalar_tensor_tensor(out=tmp[i][0:nk, :], in0=Rt[0:nk, rvar, :], scalar=slope[0:nk, hl:hl + 1],
                                                          in1=ps[0:nk, :], op0=ALU.mult, op1=ALU.add),
                 reads=["Rt", "slope", pk], writes=[("atmp", i)])
            if want_f32:
                P.op("act", lambda e: e.activation(out=Pf[0:nk, :], in_=tmp[i][0:nk, :], func=AF.Exp), reads=[("atmp", i)], writes=["Pf"])
                P.op("act", lambda e: e.copy(out=Pb[i][0:nk, :], in_=Pf[0:nk, :]), reads=["Pf"], writes=[("Pb", i)])
            elif cidx is None:
                P.op("act", lambda e: e.activation(out=Pb[i][0:nk, :], in_=tmp[i][0:nk, :], func=AF.Exp), reads=[("atmp", i)],
                     writes=[("Pb", i)])
            else:
                P.op("act", lambda e: e.activation(out=Pb[i][0:nk, :], in_=tmp[i][0:nk, :], func=AF.Exp,
                                                   bias=cst[0:nk, cidx:cidx + 1]), reads=[("atmp", i), "cst"], writes=[("Pb", i)])

            def stage_b():
                P.op("pe", lambda e: e.matmul(psU[:], v_ap, Pb[i][0:nk, :], start=first, stop=last), reads=[vkey, ("Pb", i)], writes=[uk])
                P.op("pe", lambda e: e.matmul(psD[:], onesb[0:nk, :], Pb[i][0:nk, :], start=first, stop=last),
                     reads=["onesb", ("Pb", i)], writes=[dk])
            pend.append(stage_b)

        def finish_branch(br, hl, qsl, mode):
            psU, psD = psUl[bi_[0] % 2], psDl[bi_[0] % 2]
            uk, dk = ("psU", bi_[0] % 2), ("psD", bi_[0] % 2)
            bi_[0] += 1

            def fin():
                if br == 0:
                    P.op("dve", lambda e: e.tensor_scalar(out=rden[:], in0=psD[:], scalar1=1e-18, scalar2=None, op0=ALU.max),
                         reads=[dk], writes=["rden"])
                    P.op("act", lambda e: e.activation(out=rden[:], in_=rden[:], func=AF.Ln), reads=["rden"], writes=["rden"])
                else:
                    P.op("act", lambda e: e.activation(out=rden[:], in_=psD[:], func=AF.Ln), reads=[dk], writes=["rden"])
                P.op("act", lambda e: e.activation(out=rden[:], in_=rden[:], func=AF.Exp, scale=-1.0), reads=["rden"], writes=["rden"])
                r = br * 16 + hl
                psG, gk = PR.next()
                P.op("pe", lambda e: e.matmul(psG[:], SelR[:, r * 128:(r + 1) * 128], GT[:, qsl], start=True, stop=True),
                     reads=["SelR", "GT"], writes=[gk])
                P.op("dve", lambda e: e.tensor_tensor(out=Ff[:], in0=rden[:], in1=psG[:], op=ALU.mult), reads=["rden", gk], writes=["Ff"])
                if mode == "ret":
                    P.op("dve", lambda e: e.tensor_tensor(out=Tt[:], in0=psU[:], in1=Ff[:], op=ALU.mult), reads=[uk, "Ff"], writes=["Tt"])
                elif mode == "set":
                    P.op("dve", lambda e: e.tensor_tensor(out=acc[:], in0=psU[:], in1=Ff[:], op=ALU.mult), reads=[uk, "Ff"], writes=["acc"])
                else:
                    P.op("dve", lambda e: e.tensor_tensor(out=Tt[:], in0=psU[:], in1=Ff[:], op=ALU.mult), reads=[uk, "Ff"], writes=["Tt"])
                    P.op("pool", lambda e: e.tensor_tensor(out=acc[:], in0=acc[:], in1=Tt[:], op=ALU.add), reads=["acc", "Tt"], writes=["acc"])
            pend.append(fin)

        def cidx_of(hl, m):
            return hl * NREL + (m + 15)

        for gl in range(2):
            for n in range(4):
                P.dma("sp", kvT[:, n, :], kvT_s[gl, n], reads=["scratch"], writes=["kvT"])
            for n in range(2):
                P.dma("sp", vtm[:, n], vtm_s[gl, n].rearrange("(t p) d -> p t d", p=128), reads=["scratch"], writes=["vtm"])
            P.dma("sp", selbT[:, 0:1024], selb0_d, writes=["selbT"])
            for which in range(2):
                for c0 in range(0, 32, 8):
                    P.dma("pool", w1sb[:, c0:c0 + 8, :], w1_d[which].rearrange("(l d) e -> d l e", d=128)[:, c0:c0 + 8, :], writes=["w1sb"])
                P.dma("pool", w2sb[:], w2_d[which], writes=["w2sb"])
                P.dma("sp", peT[:], peT_d[which], writes=["peT"])
                P.op("act", lambda e: e.copy(out=peTb[:], in_=peT[:]), reads=["peT"], writes=["peTb"])
                ps, pk = PR.next()
                for l in range(32):
                    P.op("pe", lambda e, l=l, ps=ps: e.matmul(ps[:, 0:127], w1sb[:, l, :], kvT[:, which, l:l + 16 * 126 + 1:16],
                                                             start=(l == 0), stop=(l == 31)), reads=["w1sb", "kvT"], writes=[pk])
                ps2, pk2 = PR.next()
                for l in range(32):
                    P.op("pe", lambda e, l=l, ps2=ps2: e.matmul(ps2[:, 0:1], w1sb[:, l, :], peTb[:, l:l + 1],
                                                               start=(l == 0), stop=(l == 31)), reads=["w1sb", "peTb"], writes=[pk2])
                P.op("act", lambda e, ps2=ps2: e.copy(out=cb_[:], in_=ps2[:, 0:1]), reads=[pk2], writes=["cmpb"])
                P.op("act", lambda e, ps=ps: e.activation(out=xg[:, 0:127], in_=ps[:, 0:127], func=AF.Identity, bias=cb_[:, 0:1]),
                     reads=[pk, "cmpb"], writes=["xg"])
                P.op("dve", lambda e: e.tensor_tensor(out=xg2[:, 0:127], in0=xg[:, 0:127], in1=xg[:, 0:127], op=ALU.mult), reads=["xg"], writes=["xg2"])
                P.op("dve", lambda e: e.tensor_scalar(out=xg2[:, 0:127], in0=xg2[:, 0:127], scalar1=0.044715, scalar2=1.0,
                                                       op0=ALU.mult, op1=ALU.add), reads=["xg2"], writes=["xg2"])
                P.op("dve", lambda e: e.tensor_tensor(out=xg2[:, 0:127], in0=xg2[:, 0:127], in1=xg[:, 0:127], op=ALU.mult),
                     reads=["xg", "xg2"], writes=["xg2"])
                P.op("act", lambda e: e.activation(out=xg2[:, 0:127], in_=xg2[:, 0:127], func=AF.Sigmoid, scale=1.5957691216),
                     reads=["xg2"], writes=["xg2"])
                P.op("dve", lambda e: e.tensor_tensor(out=hidT[:, 0:127], in0=xg[:, 0:127], in1=xg2[:, 0:127], op=ALU.mult),
                     reads=["xg", "xg2"], writes=["hidT"])
                ps3, pk3 = PR.next()
                if which == 0:
                    P.op("pe", lambda e, ps3=ps3: e.matmul(ps3[:, 0:127], w2sb[:], hidT[:, 0:127], start=True, stop=True),
                         reads=["w2sb", "hidT"], writes=[pk3])
                    P.op("act", lambda e, ps3=ps3: e.copy(out=KcT[:, 0:127], in_=ps3[:, 0:127]), reads=[pk3], writes=["KcT"])
                else:
                    P.op("pe", lambda e, ps3=ps3: e.matmul(ps3[0:127, 0:128], hidT[:, 0:127], w2sb[:], start=True, stop=True),
                         reads=["w2sb", "hidT"], writes=[pk3])
                    P.op("act", lambda e, ps3=ps3: e.copy(out=Vc[0:127, :], in_=ps3[0:127, 0:128]), reads=[pk3], writes=["Vc"])
            for j in range(8):
                hl = gl * 8 + j
                P.dma("sp", qTb[:], qT_s[hl], reads=["scratch"], writes=["qTb"])
                for qb in range(4):
                    qsl = slice(qb * 512, (qb + 1) * 512)
                    branch_step(KcT[:, 0:127], "KcT", Vc[0:127, :], "Vc", qsl, hl, 9 + qb, None, 127, first=True, last=True,
                                want_f32=(qb >= 2))
                    finish_branch(0, hl, qsl, "ret")
                    drain(0)
                    P.dma("sp", tcmp_s[j, :, qsl], Tt[:], reads=["Tt"], writes=["tcmp"])
                    if qb >= 2:
                        P.op("dve", lambda e: e.tensor_tensor(out=pn[0:127, :], in0=Pf[0:127, :], in1=rden[0:127, :], op=ALU.mult),
                             reads=["Pf", "rden"], writes=["pn"])
                        ps, pk = PR.next()
                        for k in range(4):
                            P.op("pe", lambda e, k=k, ps=ps: e.matmul(ps[:, k * 32:(k + 1) * 32], pn[0:127, k * 128:(k + 1) * 128],
                                                                     Mmat[0:127, :], start=True, stop=True),
                                 reads=["pn", "Mmat"], writes=[pk])
                        dst = pslc[:, (qb - 2) * 128:(qb - 1) * 128]
                        if j == 0:
                            P.op("dve", lambda e, ps=ps, dst=dst: e.tensor_copy(out=dst, in_=ps[:, 0:128]), reads=[pk], writes=["pslc"])
                        else:
                            P.op("dve", lambda e, ps=ps, dst=dst: e.tensor_tensor(out=dst, in0=dst, in1=ps[:, 0:128], op=ALU.add),
                                 reads=[pk, "pslc"], writes=["pslc"])
            for ti in range(8):
                csl = slice(ti * 32, (ti + 1) * 32)
                P.op("dve", lambda e, csl=csl: e.tensor_tensor(out=sc[:], in0=pslc[:, csl], in1=At[:, csl], op=ALU.mult),
                     reads=["pslc", "At"], writes=["sc"])
                P.op("dve", lambda e, csl=csl: e.tensor_tensor(out=sc[:], in0=sc[:], in1=Bt[:, csl], op=ALU.add), reads=["sc", "Bt"], writes=["sc"])
                P.op("dve", lambda e: e.max(out=m8[:, 0:8], in_=sc[:]), reads=["sc"], writes=["m8"])
                P.op("dve", lambda e: e.match_replace(out=sc2[:], in_to_replace=m8[:, 0:8], in_values=sc[:], imm_value=-1e30),
                     reads=["sc", "m8"], writes=["sc2"])
                P.op("dve", lambda e: e.max(out=m8[:, 8:16], in_=sc2[:]), reads=["sc2"], writes=["m8"])
                P.op("dve", lambda e: e.tensor_scalar(out=selb[:], in0=sc[:], scalar1=m8[:, 15:16], scalar2=None, op0=ALU.is_ge),
                     reads=["sc", "m8"], writes=["selb"])
                P.op("dve", lambda e: e.tensor_scalar(out=selb[:], in0=selb[:], scalar1=30000.0, scalar2=-30000.0, op0=ALU.mult, op1=ALU.add),
                     reads=["selb"], writes=["selb"])
                ps, pk = PR.next()
                P.op("pe", lambda e, ps=ps: e.transpose(out=ps[0:32, 0:128], in_=selb[:], identity=ident[:]), reads=["selb", "ident"], writes=[pk])
                P.op("act", lambda e, ps=ps, ti=ti: e.copy(out=selbT[:, 1024 + ti * 128:1024 + (ti + 1) * 128], in_=ps[0:32, 0:128]),
                     reads=[pk], writes=["selbT"])
            for j in range(8):
                hl = gl * 8 + j
                drain(0)
                P.dma("sp", qTb[:], qT_s[hl], reads=["scratch"], writes=["qTb"])
                P.dma("sp", zTb[:], zT_s[hl], reads=["scratch"], writes=["zTb"])
                for qb in range(4):
                    qsl = slice(qb * 512, (qb + 1) * 512)
                    drain(0)
                    P.dma("sp", pn[:], tcmp_s[j, :, qsl], reads=["tcmp"], writes=["pn"])
                    kts = list(range(0, 4 * qb + 4))
                    for kt in kts:
                        m = 4 * qb - kt
                        rvar, ci = (0, cidx_of(hl, -m)) if m >= 1 else (1 + (-m), cidx_of(hl, -m))
                        branch_step(kvT[:, 2, kt * 128:(kt + 1) * 128], "kvT", vtm[:, 0, kt, :], "vtm", qsl, hl, rvar, ci, 128,
                                    extra=(E[:, kt * 128:(kt + 1) * 128] if qb >= 2 else None), first=(kt == kts[0]),
                                    last=(kt == kts[-1]))
                        drain(2)
                    finish_branch(1, hl, qsl, "set")
                    kts = list(range(max(0, 4 * qb - 4), 4 * qb + 4))
                    for kt in kts:
                        m = 4 * qb - kt
                        if m >= 1:
                            rvar = 5 + (4 - m)
                        else:
                            rvar = 1 + (-m)
                        branch_step(kvT[:, 3, kt * 128:(kt + 1) * 128], "kvT", vtm[:, 1, kt, :], "vtm", qsl, hl, rvar, cidx_of(hl, -m), 128,
                                    first=(kt == kts[0]), last=(kt == kts[-1]))
                        drain(2)
                    finish_branch(2, hl, qsl, "add")
                    drain(0)
                    P.op("pool", lambda e: e.tensor_tensor(out=acc[:], in0=acc[:], in1=pn[:], op=ALU.add), reads=["acc", "pn"], writes=["acc"])
                    P.op("pool", lambda e, qsl=qsl: e.tensor_tensor(out=ys[:], in0=acc[:], in1=zTb[:, qsl], op=ALU.mult),
                         reads=["acc", "zTb"], writes=["nys"])
                    P.dma("sp", yT[hl * 128:(hl + 1) * 128, qsl], ys[:], reads=["nys"], writes=[("ysrc", hl, qb)])
                io["ydone"](hl)


IDENT = np.eye(128, dtype=np.float32)


def gainT(g):
    return np.ascontiguousarray(g.reshape(32, 128).T).astype(np.float32)


def colT(v, h, n=16):
    return np.ascontiguousarray(v[h * n * 128:(h + 1) * n * 128].reshape(n, 128).T).astype(np.float32)

def nsa_consts(h):
    heads = np.arange(16) + 16 * h
    sl = (2.0 ** (-8.0 * (heads + 1) / 32)).astype(np.float64)
    slope = np.broadcast_to(sl[None, :], (128, 16)).astype(np.float32)
    cst = np.zeros((128, 16, NREL), np.float64)
    for m in range(-15, 4):
        cst[:, :, m + 15] = sl[None, :] * 128.0 * m
    kj = np.arange(128)[:, None].astype(np.float64)
    qi = np.arange(512)[None, :].astype(np.float64)
    base = kj - qi
    Rt = np.zeros((13, 128, 512), np.float64)
    Rt[0] = base
    for v in range(4):
        Rt[1 + v] = np.where(-128 * v + qi - kj >= 0, base, NEGM)
    for n, rel in enumerate((512, 384, 256, 128)):
        Rt[5 + n] = np.where(rel + qi - kj < 512, base, NEGM)
    for qb in range(4):
        dist = 512 * qb + qi - 16 * kj - 31
        Rt[9 + qb] = np.where(dist >= 0, -dist, NEGM)
    key = np.arange(2048)
    E = (key[None, :] // 64 == np.arange(32)[:, None]).astype(np.float32)
    t = np.arange(1024)
    selb0 = np.where(np.arange(32)[:, None] <= (t[None, :] // 64), 0.0, -30000.0)
    Mmat = np.zeros((128, 32), np.float32)
    for i in range(32):
        for off, wgt in ((-1, 1), (0, 2), (1, 2), (2, 2), (3, 1)):
            n = 4 * i + off
            if 0 <= n < 127:
                Mmat[n, i] += wgt
    At = np.zeros((128, 8, 32), np.float32)
    Bt = np.zeros((128, 8, 32), np.float32)
    for ti in range(8):
        tt = 1024 + ti * 128 + np.arange(128)
        cur = (tt // 64)[:, None]
        blk = np.arange(32)[None, :]
        forced = (blk == 0) | (blk == cur) | (blk == cur - 1)
        future = blk > cur
        At[:, ti] = (~forced & ~future)
        Bt[:, ti] = np.where(forced, 1e6, np.where(future, -1.0, 0.0))
    SelR = np.zeros((48, 48, 128), np.float32)
    for r in range(48):
        SelR[r, r, :] = 1.0
    return {"slope": slope, "cst": np.ascontiguousarray(cst.reshape(128, 16 * NREL)).astype(np.float32),
            "Rt": Rt.astype(np.float32), "E": E.astype(ml_dtypes.bfloat16), "selb0": selb0.astype(ml_dtypes.bfloat16),
            "Mmat": Mmat, "At": np.ascontiguousarray(At.reshape(128, 256)), "Bt": np.ascontiguousarray(Bt.reshape(128, 256)),
            "SelR": np.ascontiguousarray(SelR.reshape(48, 48 * 128))}


PAIRS = [[0, 1], [2, 3], [4, 5], [6, 7]]
NSA_IN = ("wq", [D, 2048]), ("wz", [D, 2048]), ("wkv", [D, 1536]), ("wgl", [D, 48]), ("peT", [2, 128, 32]), \
    ("w1", [2, 4096, 128]), ("w2", [2, 128, 128])
NSA_CONST = ("slope", [128, 16], F32), ("cst", [128, 16 * NREL], F32), ("Rt", [13, 128, 512], F32), ("E", [32, 2048], BF16), \
    ("selb0", [32, 1024], BF16), ("Mmat", [128, 32], F32), ("At", [128, 256], F32), ("Bt", [128, 256], F32), \
    ("SelR", [48, 48 * 128], F32)
RG_IN = ("wx", [D, 2048]), ("wz", [D, 2048]), ("cw", [128, 64]), ("cb", [128, 16]), ("lam", [128, 16]), ("gb", [128, 32]), \
    ("gw", [2, 8, 256, 256])
HG_IN = ("wq", [D, 2048]), ("wf", [D, 2048]), ("wv", [D, 2048]), ("wg", [D, 2048]), ("lbl", [128, 64]), ("ng", [128, 1]), \
    ("ones", [128, 128]), ("rmask", [128, 512]), ("bdmask", [128, 128])


def build_fused():
    nc = bass.Bass("TRN2", target_bir_lowering=False)
    I = lambda n, s, dt=F32: nc.dram_tensor(n, list(s), dt, kind="ExternalInput").ap()
    T = lambda n, s, dt: nc.dram_tensor(n, list(s), dt, kind="Internal").ap()
    x = I("x", [S, D])
    x_my = I("x_my", [S, 2048])
    gT_d = I("gT", [128, 128])
    gR_d = I("gR", [4, 128, D])
    post_d = I("post", [4, 128, 2048])
    ident_d = I("ident", [128, 128])
    consts = {n: I(n, s, dt) for n, s, dt in NSA_CONST}
    lio = {}
    for i in range(4):
        spec = (NSA_IN, RG_IN, HG_IN)[i % 3]
        lio[i] = {n: I("L%d_%s" % (i, n), s) for n, s in spec}
        lio[i]["wo"] = I("L%d_wo" % i, [D, 2048])
    out = nc.dram_tensor("out", [S, 2048], F32, kind="ExternalOutput").ap()
    ysrc = T("ysrc", [2048, S], BF16)
    yT_g = T("yT_g", [D, S], BF16)
    xsrc = [T("xsrc0", [S, 2048], F32), T("xsrc1", [S, 2048], F32)]
    x_g = T("x_g", [2 * S, 2048], F32)
    ss_src = T("ss_src", [128, 8], F32)
    ss_g = T("ss_g", [256, 8], F32)
    scratch = {"qT_s": T("qT_s", [16, 128, S], BF16), "zT_s": T("zT_s", [16, 128, S], BF16),
               "kvT_s": T("kvT_s", [2, 4, 128, S], BF16), "vtm_s": T("vtm_s", [2, 2, S, 128], BF16),
               "tcmp_s": T("tcmp_s", [8, 128, S], F32)}
    with contextlib.ExitStack() as es:
        P = Prog(nc, es)
        PR = PsumRing(P, 4)
        psUl = [P.ps("psU%d" % i, [128, 512]) for i in range(2)]
        psDl = [P.ps("psD%d" % i, [128, 512]) for i in range(2)]
        gT_sb = P.sb("gT_sb", [128, 128], F32)
        ident = P.sb("ident_sb", [128, 128], F32)
        P.dma("sp", gT_sb[:], gT_d, writes=["gT"])
        P.dma("sp", ident[:], ident_d, writes=["ident"])
        for i in range(4):
            kind = i % 3
            io = dict(lio[i])
            io.update(gT=None, ident=None, x=(x if i == 0 else x_g), yT=ysrc, xsplit=(i > 0), gR=gR_d[i])

            def ydone(ch):
                if ch % 4 == 3:
                    k = ch // 4
                    P.cc("AllGather", PAIRS, ysrc[k * 512:(k + 1) * 512, :], yT_g[k * 1024:(k + 1) * 1024, :],
                         reads=[("ysrc", c_, t_) for c_ in range(4 * k, 4 * k + 4) for t_ in range(4)], writes=["yT_g"])
            io["ydone"] = ydone
            with contextlib.ExitStack() as esl:
                P.es_cur = esl
                P.pre = "L%dm_" % i
                if kind == 0:
                    io.update(consts)
                    io.update(scratch)
                    emit_nsa(P, nc, PR, psUl, psDl, gT_sb, i * 32, ident, io)
                elif kind == 1:
                    emit_rg(P, nc, PR, gT_sb, i * 32, ident, io)
                else:
                    emit_hg(P, nc, PR, gT_sb, i * 32, ident, io, layer=i)
                barrier(P)
            xo = out if i == 3 else xsrc[i % 2]

            def gather_x(k0, k1, xo=xo):
                for k in range(k0, k1):
                    P.cc("AllGather", PAIRS, xo[k * 256:(k + 1) * 256, :], x_g[k * 512:(k + 1) * 512, :],
                         reads=[("x_out", 2 * k), ("x_out", 2 * k + 1)], writes=["x_g"])
            with contextlib.ExitStack() as esl:
                P.es_cur = esl
                P.pre = "L%do_" % i
                x_res = x_my if i == 0 else xsrc[(i - 1) % 2]
                emit_outproj(P, nc, PR, yT_g, x_res, io["wo"], post_d[i], xo, ss_src, ss_g,
                             mid_cc=((lambda: gather_x(0, 4)) if i < 3 else None))
                barrier(P)
            P.es_cur = None
            if i < 3:
                gather_x(4, 8)
        P.finish()
    return nc


def kernel(x, pre_norm_gain, post_norm_gain, nsa_w_in, nsa_cmp_pe, nsa_cmp_w1, nsa_cmp_w2, nsa_w_out,
           rg_w_in, rg_conv_w, rg_conv_b, rg_gate_w, rg_gate_b, rg_lambda, rg_w_out,
           hg_w_in, hg_lb_logits, hg_norm_gain, hg_w_out):
    f = lambda a: np.asarray(a, dtype=np.float32)
    ca = np.ascontiguousarray
    x = f(x)
    pre, post = f(pre_norm_gain), f(post_norm_gain)
    common = {
        "gT": ca(np.concatenate([gainT(pre[i]) for i in range(4)], axis=1)),
        "ident": IDENT,
        "gR": ca(np.broadcast_to(pre[:, None, :], (4, 128, D))).astype(np.float32),
    }
    rmask = np.ones((128, 512), np.float32)
    rmask[:, ::64] = 0.0
    si = np.arange(128)[:, None]
    ti = np.arange(128)[None, :]
    bdmask = ((si // 64 == ti // 64) & (ti >= si)).astype(np.float32)
    in_maps = []
    for c in range(NCORES):
        b, h = c // 2, c % 2
        m = dict(common)
        m["x"] = ca(x[b])
        m["x_my"] = ca(x[b][:, h * 2048:(h + 1) * 2048])
        m["post"] = ca(np.broadcast_to(post[:, None, h * 2048:(h + 1) * 2048], (4, 128, 2048))).astype(np.float32)
        m.update(nsa_consts(h))
        for i, j in ((0, 0), (3, 1)):
            w_in = f(nsa_w_in[j])
            pfx = "L%d_" % i
            m[pfx + "wq"] = ca(w_in[:, h * 2048:(h + 1) * 2048])
            m[pfx + "wz"] = ca(w_in[:, 7264 + h * 2048:7264 + (h + 1) * 2048])
            m[pfx + "wkv"] = ca(np.concatenate(
                [w_in[:, 4096 + k * 512 + h * 256:4096 + k * 512 + (h + 1) * 256] for k in range(6)], axis=1))
            m[pfx + "wgl"] = ca(np.concatenate(
                [w_in[:, 7168 + br * 32 + h * 16:7168 + br * 32 + (h + 1) * 16] for br in range(3)], axis=1))
            m[pfx + "peT"] = ca(f(nsa_cmp_pe[j]).transpose(0, 2, 1))
            m[pfx + "w1"] = ca(f(nsa_cmp_w1[j]))
            m[pfx + "w2"] = ca(f(nsa_cmp_w2[j]))
            m[pfx + "wo"] = ca(f(nsa_w_out[j])[:, h * 2048:(h + 1) * 2048])
        w_in = f(rg_w_in[0])
        m["L1_wx"] = ca(w_in[:, h * 2048:(h + 1) * 2048])
        m["L1_wz"] = ca(w_in[:, 4096 + h * 2048:4096 + (h + 1) * 2048])
        cw = f(rg_conv_w[0])
        m["L1_cw"] = ca(np.stack([colT(cw[k], h) for k in range(4)], axis=-1).reshape(128, 64))
        m["L1_cb"] = colT(f(rg_conv_b[0]), h)
        m["L1_lam"] = colT(f(rg_lambda[0]), h)
        gbv = f(rg_gate_b[0])
        m["L1_gb"] = ca(np.stack([colT(gbv[k].reshape(-1), h) for k in range(2)], axis=1).reshape(128, 32))
        m["L1_gw"] = ca(f(rg_gate_w[0])[:, h * 8:(h + 1) * 8])
        m["L1_wo"] = ca(f(rg_w_out[0])[:, h * 2048:(h + 1) * 2048])
        w_in = f(hg_w_in[0])
        for k, n in enumerate(("wq", "wf", "wv", "wg")):
            m["L2_" + n] = ca(w_in[:, k * 4096 + h * 2048:k * 4096 + (h + 1) * 2048])
        lbl = f(hg_lb_logits)
        m["L2_lbl"] = ca(np.concatenate([colT(lbl[l], h) for l in range(4)], axis=1))
        m["L2_ng"] = ca(f(hg_norm_gain[0]).reshape(128, 1))
        m["L2_ones"] = np.ones((128, 128), np.float32)
        m["L2_rmask"] = rmask
        m["L2_bdmask"] = bdmask
        m["L2_wo"] = ca(f(hg_w_out[0])[:, h * 2048:(h + 1) * 2048])
        in_maps.append(m)
    nc = build_fused()
    res = run_bass_kernel_spmd(nc, in_maps, core_ids=list(range(NCORES)))
    out = np.empty((B, S, D), np.float32)
    for c in range(NCORES):
        b, h = c // 2, c % 2
        out[b][:, h * 2048:(h + 1) * 2048] = res.results[c]["out"]
    return out
```

```python
import contextlib
import numpy as np
import ml_dtypes
import concourse.bass as bass
import concourse.mybir as mybir
from concourse.bass_utils import run_bass_kernel_spmd

F32 = mybir.dt.float32
BF16 = mybir.dt.bfloat16
AF = mybir.ActivationFunctionType
ALU = mybir.AluOpType
AX = mybir.AxisListType

D = 4096
B = 4
S = 2048
EPS = 1e-6
NCORES = 8


class Prog:
    ENGS = ("pe", "act", "dve", "pool", "sp")
    NDS = 24

    def __init__(self, nc, es, self_sync=True):
        self.nc = nc
        self.es = es
        self.self_sync = self_sync
        self.eng = dict(pe=nc.tensor, act=nc.scalar, dve=nc.vector, pool=nc.gpsimd, sp=nc.sync)
        self.sem = {e: es.enter_context(nc.semaphore("s_" + e)) for e in self.ENGS}
        self.cnt = {e: 0 for e in self.ENGS}
        self.seen = {e: {} for e in self.ENGS}
        self.dsem = [es.enter_context(nc.semaphore("d%d" % i)) for i in range(self.NDS)]
        self.dcnt = [0] * self.NDS
        self.drr = 0
        self.state = {}
        self.ninst = 0
        self.pre = ""
        self.es_cur = None
        self.ccsem = es.enter_context(nc.semaphore("ccsem"))
        self.cccnt = 0

    def sb(self, name, shape, dt):
        es = self.es_cur if self.es_cur is not None else self.es
        return es.enter_context(self.nc.sbuf_tensor(self.pre + name, list(shape), dt))

    def ps(self, name, shape, dt=F32):
        return self.es.enter_context(self.nc.psum_tensor(name, list(shape), dt))

    def _deps(self, reads, writes):
        deps = []
        for k in reads:
            st = self.state.get(k)
            if st is not None and st[0] is not None:
                deps.append(st[0])
        for k in writes:
            st = self.state.get(k)
            if st is not None:
                if st[0] is not None:
                    deps.append(st[0])
                deps.extend(st[1].values())
        return deps

    def _wait(self, e, deps):
        best = {}
        for (sid, sem, val) in deps:
            if sid not in best or best[sid][1] < val:
                best[sid] = (sem, val)
        for sid, (sem, val) in best.items():
            if self.seen[e].get(sid, 0) >= val:
                continue
            if sid == e and (e == "pe" or not self.self_sync):
                continue
            self.eng[e].wait_ge(sem, val)
            self.ninst += 1
            self.seen[e][sid] = val

    def _record(self, tok, reads, writes):
        for k in reads:
            st = self.state.get(k)
            if st is None:
                st = [None, {}]
                self.state[k] = st
            st[1][tok[0]] = tok
        for k in writes:
            self.state[k] = [tok, {}]

    def op(self, e, fn, reads=(), writes=(), sig=True):
        self._wait(e, self._deps(reads, writes))
        ins = fn(self.eng[e])
        self.ninst += 1
        if sig:
            self.cnt[e] += 1
            ins.then_inc(self.sem[e], 1)
            self._record((e, self.sem[e], self.cnt[e]), reads, writes)
        else:
            self._record((e, self.sem[e], self.cnt[e] + 1), reads, writes)

    def dma(self, q, out, in_, reads=(), writes=()):
        i = self.drr
        self.drr = (i + 1) % self.NDS
        sem = self.dsem[i]
        sid = "d%d" % i
        deps = self._deps(reads, writes)
        if self.dcnt[i] > 0:
            deps.append((sid, sem, self.dcnt[i]))
        self._wait(q, deps)
        self.eng[q].dma_start(out=out, in_=in_).then_inc(sem, 16)
        self.ninst += 1
        self.dcnt[i] += 16
        self._record((sid, sem, self.dcnt[i]), reads, writes)

    def cc(self, kind, groups, in_, out, reads=(), writes=()):
        deps = self._deps(reads, writes)
        self._wait("pool", deps)
        self.eng["pool"].collective_compute(kind, ALU.bypass, replica_groups=groups, ins=[in_.opt()], outs=[out.opt()]).then_inc(self.ccsem)
        self.ninst += 1
        self.cccnt += 1
        self._record(("cc", self.ccsem, self.cccnt), reads, writes)

    def finish(self):
        for i in range(self.NDS):
            if self.dcnt[i] > 0:
                self.eng["sp"].wait_ge(self.dsem[i], self.dcnt[i])
        for e in self.ENGS:
            if e != "sp" and self.cnt[e] > 0:
                self.eng["sp"].wait_ge(self.sem[e], self.cnt[e])
        if self.cccnt > 0:
            self.eng["sp"].wait_ge(self.ccsem, self.cccnt)


class PsumRing:
    def __init__(self, P, n=8):
        self.P = P
        self.t = [P.ps("psr%d" % i, [128, 512]) for i in range(n)]
        self.i = 0

    def next(self):
        i = self.i % len(self.t)
        self.i += 1
        return self.t[i], ("psr", i)


def barrier(P):
    toks = []
    for e in P.ENGS:
        if P.cnt[e] > 0:
            toks.append((e, P.sem[e], P.cnt[e]))
    for i in range(P.NDS):
        if P.dcnt[i] > 0:
            toks.append(("d%d" % i, P.dsem[i], P.dcnt[i]))
    if P.cccnt > 0:
        toks.append(("cc", P.ccsem, P.cccnt))
    for e in P.ENGS:
        P._wait(e, [t for t in toks if t[0] != e])


def phase_hT(P, nc, PR, x, gR, ident, hT, ntok=S, xsplit=False):
    KC = D // 128
    with contextlib.ExitStack() as es2:
        xs = [es2.enter_context(nc.sbuf_tensor(P.pre + "hx%d" % i, [128, D], F32)) for i in range(2)]
        junk = es2.enter_context(nc.sbuf_tensor(P.pre + "hjunk", [128, D], BF16))
        grep_ = es2.enter_context(nc.sbuf_tensor(P.pre + "hgrep", [128, D], F32))
        st = es2.enter_context(nc.sbuf_tensor(P.pre + "hst", [128, 8], F32))
        P.dma("sp", grep_[:], gR, writes=["hgrep"])
        gi = 0
        for tt in range(ntok // 128):
            xb = xs[tt % 2]
            xk = ("hx", tt % 2)
            if xsplit:
                for r_ in range(2):
                    row = (tt // 2) * 512 + r_ * 256 + (tt % 2) * 128
                    P.dma("sp", xb[:, r_ * 2048:(r_ + 1) * 2048], x[row:row + 128, :], reads=["x_g"], writes=[xk])
            else:
                P.dma("sp", xb[:], x[tt * 128:(tt + 1) * 128, :], reads=["x_g"], writes=[xk])
            P.op("act", lambda e: e.activation(out=junk[:], in_=xb[:], func=AF.Square, accum_out=st[:, 0:1]),
                 reads=[xk], writes=["hjunk", "hst0"])
            P.op("dve", lambda e: e.tensor_scalar(out=st[:, 1:2], in0=st[:, 0:1], scalar1=1.0 / D, scalar2=EPS,
                                                   op0=ALU.mult, op1=ALU.add), reads=["hst0"], writes=["hst1"])
            P.op("act", lambda e: e.activation(out=st[:, 2:3], in_=st[:, 1:2], func=AF.Sqrt),
                 reads=["hst1"], writes=["hst2"])
            P.op("dve", lambda e: e.reciprocal(out=st[:, 3:4], in_=st[:, 2:3]), reads=["hst2"], writes=["hst3"])
            P.op("dve", lambda e: e.scalar_tensor_tensor(out=xb[:], in0=xb[:], scalar=st[:, 3:4], in1=grep_[:],
                                                          op0=ALU.mult, op1=ALU.mult), reads=[xk, "hst3", "hgrep"], writes=[xk])
            for c0 in range(0, KC, 4):
                ps, pk = PR.next()
                for k in range(4):
                    c = c0 + k
                    P.op("pe", lambda e, c=c, k=k, ps=ps: e.transpose(
                        out=ps[:, k * 128:(k + 1) * 128], in_=xb[:, c * 128:(c + 1) * 128], identity=ident[:]),
                        reads=[xk, "ident"], writes=[pk], sig=(k == 3))
                dst = hT[:, c0:c0 + 4, tt * 128:(tt + 1) * 128]
                src_ = ps[:].rearrange("p (k t) -> p k t", k=4)
                if gi % 2 == 0:
                    P.op("act", lambda e, dst=dst, src_=src_: e.copy(out=dst, in_=src_), reads=[pk], writes=[("hT", tt // 4)])
                else:
                    P.op("dve", lambda e, dst=dst, src_=src_: e.tensor_copy(out=dst, in_=src_), reads=[pk], writes=[("hT", tt // 4)])
                gi += 1
        barrier(P)


def proj_fm(P, PR, hT, w_chunk, wkey, tb, evac):
    KC = D // 128
    ps, pk = PR.next()
    for c in range(KC):
        P.op("pe", lambda e, c=c, ps=ps: e.matmul(ps[:], w_chunk[:, c, :], hT[:, c, tb * 512:(tb + 1) * 512],
                                                  start=(c == 0), stop=(c == KC - 1)),
             reads=[wkey, ("hT", tb)], writes=[pk], sig=(c == KC - 1))
    evac(ps, pk)


def emit_outproj(P, nc, PR, yT, x, w, gain, out, ss_src, ss_g, mid_cc=None):
    KC = D // 128
    NB = 256
    TP = 1024
    NCL = 2048
    NTT = TP // 128
    if True:
        yT_sb = P.sb("yT_sb", [128, KC, TP], BF16)
        w_sb = [P.sb("w_sb%d" % i, [128, KC, NB], BF16) for i in range(2)]
        o_sb = [P.sb("o_sb%d" % i, [128, NCL], F32) for i in range(NTT)]
        x_sb = [P.sb("x_sb%d" % i, [128, NCL], F32) for i in range(2)]
        g_sb = P.sb("g_sb", [128, NCL], F32)
        junk = P.sb("junk", [128, NCL], BF16)
        ssq = P.sb("ssq", [128, NTT], F32)
        ssg = P.sb("ssg", [128, 2, NTT], F32)
        st = P.sb("ost", [128, 4 * NTT], F32)
        yT_v = yT.rearrange("(c p) t -> p c t", p=128)
        w_v = w.rearrange("(c p) n -> p c n", p=128)
        P.dma("sp", g_sb[:], gain, writes=["g"])
        wi = 0
        for p_ in range(S // TP):
            for r_ in range(2):
                for k_ in range(4):
                    c0_, s0_ = r_ * 16 + k_ * 4, k_ * 8 + r_ * 4
                    P.dma("sp", yT_sb[:, c0_:c0_ + 4, :], yT_v[:, s0_:s0_ + 4, p_ * TP:(p_ + 1) * TP], reads=["yT_g"], writes=["yT"])
            for nb in range(NCL // NB):
                wb = wi % 2
                wi += 1
                for c0 in range(0, KC, 8):
                    P.dma("pool", w_sb[wb][:, c0:c0 + 8, :], w_v[:, c0:c0 + 8, nb * NB:(nb + 1) * NB],
                          writes=[("w", wb, c0)])
                for tt in range(NTT):
                    ps, pk = PR.next()
                    for c in range(KC):
                        P.op("pe", lambda e, c=c, tt=tt, ps=ps, wb=wb: e.matmul(
                            ps[:, 0:NB], yT_sb[:, c, tt * 128:(tt + 1) * 128], w_sb[wb][:, c, :],
                            start=(c == 0), stop=(c == KC - 1)),
                            reads=["yT", ("w", wb, (c // 8) * 8)], writes=[pk], sig=(c == KC - 1))
                    P.op("act", lambda e, tt=tt, ps=ps, nb=nb: e.copy(
                        out=o_sb[tt][:, nb * NB:(nb + 1) * NB], in_=ps[:, 0:NB]),
                        reads=[pk], writes=[("o", tt)])
            if p_ == 1 and mid_cc is not None:
                mid_cc()
            for tt in range(NTT):
                P.op("act", lambda e, tt=tt: e.activation(out=junk[:], in_=o_sb[tt][:], func=AF.Square,
                                                         accum_out=ssq[:, tt:tt + 1]),
                     reads=[("o", tt)], writes=["junk", "ssq"])
            P.dma("sp", ss_src, ssq[:], reads=["ssq"], writes=["ss_src"])
            P.cc("AllGather", PAIRS, ss_src, ss_g, reads=["ss_src"], writes=["ss_g"])
            P.dma("sp", ssg[:], ss_g.rearrange("(r p) c -> p r c", p=128), reads=["ss_g"], writes=["ssg"])
            P.op("dve", lambda e: e.tensor_tensor(out=st[:, 0:NTT], in0=ssg[:, 0, :], in1=ssg[:, 1, :], op=ALU.add),
                 reads=["ssg"], writes=["st0"])
            P.op("dve", lambda e: e.tensor_scalar(out=st[:, NTT:2 * NTT], in0=st[:, 0:NTT], scalar1=1.0 / D, scalar2=EPS,
                                                   op0=ALU.mult, op1=ALU.add), reads=["st0"], writes=["st1"])
            P.op("act", lambda e: e.activation(out=st[:, 2 * NTT:3 * NTT], in_=st[:, NTT:2 * NTT], func=AF.Sqrt),
                 reads=["st1"], writes=["st2"])
            P.op("dve", lambda e: e.reciprocal(out=st[:, 3 * NTT:4 * NTT], in_=st[:, 2 * NTT:3 * NTT]), reads=["st2"], writes=["st3"])
            for tt in range(NTT):
                t0 = p_ * TP + tt * 128
                xb, xk = x_sb[tt % 2], ("x", tt % 2)
                P.dma("sp", xb[:], x[t0:t0 + 128, :], reads=["x_my"], writes=[xk])
                P.op("dve", lambda e, tt=tt: e.scalar_tensor_tensor(
                    out=o_sb[tt][:], in0=o_sb[tt][:], scalar=st[:, 3 * NTT + tt:3 * NTT + tt + 1], in1=g_sb[:],
                    op0=ALU.mult, op1=ALU.mult), reads=[("o", tt), "st3", "g"], writes=[("o", tt)])
                P.op("dve", lambda e, tt=tt, xb=xb: e.tensor_tensor(out=o_sb[tt][:], in0=o_sb[tt][:], in1=xb[:], op=ALU.add),
                     reads=[("o", tt), xk], writes=[("o", tt)])
                P.dma("sp", out[t0:t0 + 128, :], o_sb[tt][:], reads=[("o", tt)], writes=[("x_out", p_ * NTT + tt)])


def emit_rg(P, nc, PR, gT_sb, gcol, ident, io):
    x = io["x"]
    gT = io["gT"]
    ident_d = io["ident"]
    wx = io["wx"]
    wz = io["wz"]
    cw_d = io["cw"]
    cb_d = io["cb"]
    lam_d = io["lam"]
    gb_d = io["gb"]
    gw_d = io["gw"]
    yT = io["yT"]
    KC = D // 128
    if True:
        hT = P.sb("hT", [128, KC, S], BF16)
        cw = P.sb("cw_sb", [128, 64], F32)
        cb = P.sb("cb_sb", [128, 16], F32)
        lam = P.sb("lam_sb", [128, 16], F32)
        c1 = P.sb("c1_sb", [128, 16], F32)
        gb = P.sb("gb_sb", [128, 32], F32)
        P.dma("sp", cw[:], cw_d, writes=["cw"])
        P.dma("sp", cb[:], cb_d, writes=["cb"])
        P.dma("sp", lam[:], lam_d, writes=["lam"])
        P.dma("sp", gb[:], gb_d, writes=["gb"])
        phase_hT(P, nc, PR, x, io["gR"], ident, hT, xsplit=io["xsplit"])
        P.op("act", lambda e: e.activation(out=c1[:], in_=lam[:], func=AF.Exp, scale=-1.0), reads=["lam"], writes=["c1"])
        P.op("act", lambda e: e.activation(out=c1[:], in_=c1[:], func=AF.Ln, bias=1.0), reads=["c1"], writes=["c1"])
        P.op("dve", lambda e: e.tensor_scalar(out=c1[:], in0=c1[:], scalar1=-8.0, scalar2=None, op0=ALU.mult),
             reads=["c1"], writes=["c1"])
        wxs = [P.sb("wxs%d" % i, [128, KC, 128], BF16) for i in range(2)]
        wzs = [P.sb("wzs%d" % i, [128, KC, 128], BF16) for i in range(2)]
        gws = P.sb("gws", [128, 2, 2, 256], BF16)
        xraw = [P.sb("xraw%d" % i, [128, 3 + 512], F32) for i in range(2)]
        xc = [P.sb("xc%d" % i, [128, 512], F32) for i in range(2)]
        xcb = [P.sb("xcb%d" % i, [128, 512], BF16) for i in range(2)]
        gi = [P.sb("gi%d" % i, [128, 512], F32) for i in range(2)]
        ga = [P.sb("ga%d" % i, [128, 512], F32) for i in range(2)]
        gm = [P.sb("gm%d" % i, [128, 512], F32) for i in range(2)]
        hs = [[P.sb("hs%d_%d" % (i, k), [128, 512], F32) for k in range(2)] for i in range(2)]
        zs = [P.sb("zs%d" % i, [128, 512], F32) for i in range(2)]
        ys = [P.sb("ys%d" % i, [128, 512], BF16) for i in range(2)]
        wx_v = wx.rearrange("(c p) n -> p c n", p=128)
        wz_v = wz.rearrange("(c p) n -> p c n", p=128)
        for blk in range(8):
            for j in range(2):
                col0 = blk * 256 + j * 128
                for c0 in range(0, KC, 8):
                    P.dma("pool", wxs[j][:, c0:c0 + 8, :], wx_v[:, c0:c0 + 8, col0:col0 + 128], writes=[("wx", j)])
                for c0 in range(0, KC, 8):
                    P.dma("pool", wzs[j][:, c0:c0 + 8, :], wz_v[:, c0:c0 + 8, col0:col0 + 128], writes=[("wz", j)])
            for k in range(2):
                P.dma("pool", gws[:, k], gw_d[k, blk].rearrange("(jc p) e -> p jc e", p=128), writes=["gw"])
            for j in range(2):
                P.op("dve", lambda e, j=j: e.memset(xraw[j][:, 0:3], 0.0), writes=[("xraw", j)])
            for tb in range(4):
                for j in range(2):
                    ch = blk * 2 + j

                    def ev_x(ps, pk, j=j):
                        P.op("act", lambda e: e.copy(out=xraw[j][:, 3:515], in_=ps[:]), reads=[pk], writes=[("xraw", j)])
                    proj_fm(P, PR, hT, wxs[j], ("wx", j), tb, ev_x)
                    P.op("dve", lambda e, j=j, ch=ch: e.tensor_scalar(
                        out=xc[j][:], in0=xraw[j][:, 0:512], scalar1=cw[:, ch * 4:ch * 4 + 1], scalar2=cb[:, ch:ch + 1],
                        op0=ALU.mult, op1=ALU.add), reads=[("xraw", j), "cw", "cb"], writes=[("xc", j)])
                    for k in range(1, 4):
                        P.op("dve", lambda e, j=j, ch=ch, k=k: e.scalar_tensor_tensor(
                            out=xc[j][:], in0=xraw[j][:, k:k + 512], scalar=cw[:, ch * 4 + k:ch * 4 + k + 1],
                            in1=xc[j][:], op0=ALU.mult, op1=ALU.add), reads=[("xraw", j), ("xc", j), "cw"],
                            writes=[("xc", j)])
                    P.op("act", lambda e, j=j: e.copy(out=xcb[j][:], in_=xc[j][:]), reads=[("xc", j)], writes=[("xcb", j)])
                    P.op("dve", lambda e, j=j: e.tensor_copy(out=xraw[j][:, 0:3], in_=xraw[j][:, 512:515]),
                         reads=[("xraw", j)], writes=[("xraw", j)])
                for je in range(2):
                    def ev_z(ps, pk, je=je):
                        P.op("act", lambda e: e.activation(out=zs[je][:], in_=ps[:], func=AF.Silu), reads=[pk],
                             writes=[("zs", je)])
                    proj_fm(P, PR, hT, wzs[je], ("wz", je), tb, ev_z)
                for je in range(2):
                    ch = blk * 2 + je
                    for k in range(2):
                        ps, pk = PR.next()
                        for jc in range(2):
                            P.op("pe", lambda e, k=k, jc=jc, je=je, ps=ps: e.matmul(
                                ps[:], gws[:, k, jc, je * 128:(je + 1) * 128], xcb[jc][:], start=(jc == 0), stop=(jc == 1)),
                                reads=["gw", ("xcb", jc)], writes=[pk])
                        dst = gi[je] if k == 0 else ga[je]
                        dk = ("gi", je) if k == 0 else ("ga", je)
                        P.op("act", lambda e, ps=ps, dst=dst, k=k, ch=ch: e.activation(
                            out=dst[:], in_=ps[:], func=AF.Sigmoid, bias=gb[:, k * 16 + ch:k * 16 + ch + 1]),
                            reads=[pk, "gb"], writes=[dk])
                    P.op("act", lambda e, je=je, ch=ch: e.activation(out=ga[je][:], in_=ga[je][:], func=AF.Exp,
                                                                   scale=c1[:, ch:ch + 1]),
                         reads=[("ga", je), "c1"], writes=[("ga", je)])
                    P.op("dve", lambda e, je=je: e.tensor_tensor(out=gm[je][:], in0=ga[je][:], in1=ga[je][:], op=ALU.mult),
                         reads=[("ga", je)], writes=[("gm", je)])
                    P.op("act", lambda e, je=je: e.activation(out=gm[je][:], in_=gm[je][:], func=AF.Sqrt, scale=-1.0, bias=1.0),
                         reads=[("gm", je)], writes=[("gm", je)])
                    if tb == 0:
                        P.op("dve", lambda e, je=je: e.memset(gm[je][:, 0:1], 1.0), writes=[("gm", je)])
                    P.op("dve", lambda e, je=je: e.tensor_tensor(out=gi[je][:], in0=gi[je][:], in1=gm[je][:], op=ALU.mult),
                         reads=[("gi", je), ("gm", je)], writes=[("gi", je)])
                    P.op("dve", lambda e, je=je: e.tensor_tensor(out=gi[je][:], in0=gi[je][:], in1=xc[je][:], op=ALU.mult),
                         reads=[("gi", je), ("xc", je)], writes=[("gi", je)])
                    cur = hs[je][tb % 2]
                    prev = hs[je][(tb + 1) % 2]
                    init = 0.0 if tb == 0 else prev[:, 511:512]
                    P.op("dve", lambda e, je=je, cur=cur, init=init: e.tensor_tensor_scan(
                        out=cur[:], data0=ga[je][:], data1=gi[je][:], initial=init, op0=ALU.mult, op1=ALU.add),
                        reads=[("ga", je), ("gi", je), ("hs", je, (tb + 1) % 2)], writes=[("hs", je, tb % 2)])

                    P.op("dve", lambda e, je=je, cur=cur: e.tensor_tensor(out=ys[je][:], in0=cur[:], in1=zs[je][:], op=ALU.mult),
                         reads=[("hs", je, tb % 2), ("zs", je)], writes=[("ys", je)])
                    P.dma("sp", yT[ch * 128:(ch + 1) * 128, tb * 512:(tb + 1) * 512], ys[je][:], reads=[("ys", je)],
                          writes=[("ysrc", ch, tb)])
            io["ydone"](blk * 2)
            io["ydone"](blk * 2 + 1)


def emit_hg(P, nc, PR, gT_sb, gcol, ident, io, layer=2):
    x = io["x"]
    gT = io["gT"]
    ident_d = io["ident"]
    ones_d = io["ones"]
    rmask_d = io["rmask"]
    bdmask_d = io["bdmask"]
    lbl_d = io["lbl"]
    ng_d = io["ng"]
    yT = io["yT"]
    wd = [io[n] for n in ("wq", "wf", "wv", "wg")]
    KC = D // 128
    if True:
        hT = P.sb("hT", [128, KC, S], BF16)
        ones = P.sb("ones_sb", [128, 128], F32)
        rmask = P.sb("rmask_sb", [128, 512], F32)
        bdmask = P.sb("bdmask_sb", [128, 128], F32)
        lbl = P.sb("lbl_sb", [128, 64], F32)
        lbe = P.sb("lbe_sb", [128, 64], F32)
        lb = P.sb("lb_sb", [128, 16], F32)
        oml = P.sb("oml_sb", [128, 16], F32)
        lsum = P.sb("lsum_sb", [128, 16], F32)
        ng = P.sb("ng_sb", [128, 1], F32)
        for t, d, k in ((ones, ones_d, "ones"), (rmask, rmask_d, "rmask"),
                        (bdmask, bdmask_d, "bdmask"), (lbl, lbl_d, "lbl"), (ng, ng_d, "ng")):
            P.dma("sp", t[:], d, writes=[k])
        phase_hT(P, nc, PR, x, io["gR"], ident, hT, xsplit=io["xsplit"])
        P.op("act", lambda e: e.activation(out=lbe[:], in_=lbl[:], func=AF.Exp), reads=["lbl"], writes=["lbe"])
        P.op("dve", lambda e: e.tensor_tensor(out=lsum[:], in0=lbe[:, 0:16], in1=lbe[:, 16:32], op=ALU.add),
             reads=["lbe"], writes=["lsum"])
        P.op("dve", lambda e: e.tensor_tensor(out=lsum[:], in0=lsum[:], in1=lbe[:, 32:48], op=ALU.add),
             reads=["lbe", "lsum"], writes=["lsum"])
        P.op("dve", lambda e: e.tensor_tensor(out=lsum[:], in0=lsum[:], in1=lbe[:, 48:64], op=ALU.add),
             reads=["lbe", "lsum"], writes=["lsum"])
        P.op("dve", lambda e: e.reciprocal(out=lsum[:], in_=lsum[:]), reads=["lsum"], writes=["lsum"])
        P.op("dve", lambda e: e.memset(lb[:], 0.0), writes=["lb"])
        for l in range(1, layer + 1):
            P.op("dve", lambda e, l=l: e.tensor_tensor(out=lb[:], in0=lb[:], in1=lbe[:, l * 16:(l + 1) * 16], op=ALU.add),
                 reads=["lb", "lbe"], writes=["lb"])
        P.op("dve", lambda e: e.tensor_tensor(out=lb[:], in0=lb[:], in1=lsum[:], op=ALU.mult), reads=["lb", "lsum"], writes=["lb"])
        P.op("dve", lambda e: e.tensor_scalar(out=oml[:], in0=lb[:], scalar1=-1.0, scalar2=1.0, op0=ALU.mult, op1=ALU.add),
             reads=["lb"], writes=["oml"])
        ws = [P.sb("hw%d" % i, [128, KC, 128], BF16) for i in range(4)]
        wv = [w.rearrange("(c p) n -> p c n", p=128) for w in wd]
        f32b = lambda n: P.sb(n, [128, 512], F32)
        qs, ff, logf, kk, bb, eb, enb, kef, gs, osb, sq, rstd, tmpo = [f32b("hgb%d" % i) for i in range(13)]
        qe = P.sb("qe", [128, 512], BF16)
        kebf = P.sb("kebf", [128, 512], BF16)
        ys = P.sb("hys", [128, 512], BF16)
        vtm = [P.sb("vtm%d" % i, [128, 128], BF16) for i in range(4)]
        ketm = [P.sb("ketm%d" % i, [128, 128], BF16) for i in range(2)]
        attT = [P.sb("attT%d" % i, [128, 128], BF16) for i in range(2)]
        Sf = P.sb("Sf", [128, 128], F32)
        Stmp = P.sb("Stmp", [128, 128], F32)
        Sb = P.sb("Sb", [128, 128], BF16)
        for hh in range(16):
            for i in range(4):
                for c0 in range(0, KC, 8):
                    P.dma("pool", ws[i][:, c0:c0 + 8, :], wv[i][:, c0:c0 + 8, hh * 128:(hh + 1) * 128], writes=[("hw", i)])
            P.op("dve", lambda e: e.memset(Sf[:], 0.0), writes=["Sf"])
            P.op("dve", lambda e: e.memset(Sb[:], 0.0), writes=["Sb"])
            for tb in range(4):
                def ev_q(ps, pk):
                    P.op("act", lambda e: e.activation(out=qs[:], in_=ps[:], func=AF.Silu), reads=[pk], writes=["qs"])
                proj_fm(P, PR, hT, ws[0], ("hw", 0), tb, ev_q)

                def ev_f(ps, pk):
                    P.op("act", lambda e: e.activation(out=ff[:], in_=ps[:], func=AF.Sigmoid), reads=[pk], writes=["ff"])
                proj_fm(P, PR, hT, ws[1], ("hw", 1), tb, ev_f)

                def ev_g(ps, pk):
                    P.op("act", lambda e: e.activation(out=gs[:], in_=ps[:], func=AF.Silu), reads=[pk], writes=["gs"])
                proj_fm(P, PR, hT, ws[3], ("hw", 3), tb, ev_g)
                for tt in range(4):
                    ps, pk = PR.next()
                    for c in range(KC):
                        P.op("pe", lambda e, c=c, ps=ps, tt=tt: e.matmul(
                            ps[:, 0:128], hT[:, c, tb * 512 + tt * 128:tb * 512 + (tt + 1) * 128], ws[2][:, c, :],
                            start=(c == 0), stop=(c == KC - 1)), reads=[("hw", 2), ("hT", tb)], writes=[pk], sig=(c == KC - 1))
                    P.op("act", lambda e, ps=ps, tt=tt: e.copy(out=vtm[tt][:], in_=ps[:, 0:128]), reads=[pk], writes=[("vtm", tt)])
                P.op("dve", lambda e: e.tensor_scalar(out=ff[:], in0=ff[:], scalar1=oml[:, hh:hh + 1], scalar2=lb[:, hh:hh + 1],
                                                       op0=ALU.mult, op1=ALU.add), reads=["ff", "oml", "lb"], writes=["ff"])
                P.op("act", lambda e: e.activation(out=logf[:], in_=ff[:], func=AF.Ln), reads=["ff"], writes=["logf"])
                P.op("dve", lambda e: e.tensor_scalar(out=kk[:], in0=ff[:], scalar1=-1.0, scalar2=1.0, op0=ALU.mult, op1=ALU.add),
                     reads=["ff"], writes=["kk"])
                P.op("dve", lambda e: e.tensor_tensor_scan(out=bb[:], data0=rmask[:], data1=logf[:], initial=0.0,
                                                            op0=ALU.mult, op1=ALU.add), reads=["rmask", "logf"], writes=["bb"])
                P.op("act", lambda e: e.activation(out=eb[:], in_=bb[:], func=AF.Exp), reads=["bb"], writes=["eb"])
                P.op("act", lambda e: e.activation(out=enb[:], in_=bb[:], func=AF.Exp, scale=-1.0), reads=["bb"], writes=["enb"])
                P.op("dve", lambda e: e.tensor_tensor(out=qe[:], in0=qs[:], in1=eb[:], op=ALU.mult), reads=["qs", "eb"], writes=["qe"])
                P.op("dve", lambda e: e.tensor_tensor(out=kef[:], in0=kk[:], in1=enb[:], op=ALU.mult), reads=["kk", "enb"], writes=["kef"])
                P.op("act", lambda e: e.copy(out=kebf[:], in_=kef[:]), reads=["kef"], writes=["kebf"])
                for tt in range(4):
                    sl = slice(tt * 128, (tt + 1) * 128)
                    kt = ketm[tt % 2]
                    at = attT[tt % 2]
                    ktk = ("ketm", tt % 2)
                    atk = ("attT", tt % 2)
                    ps, pk = PR.next()
                    P.op("pe", lambda e, ps=ps, sl=sl: e.transpose(out=ps[:, 0:128], in_=kef[:, sl], identity=ident[:]),
                         reads=["kef", "ident"], writes=[pk])
                    P.op("act", lambda e, ps=ps, kt=kt: e.copy(out=kt[:], in_=ps[:, 0:128]), reads=[pk], writes=[ktk])
                    ps2, pk2 = PR.next()
                    P.op("pe", lambda e, ps2=ps2, sl=sl: e.matmul(ps2[:, 0:128], kebf[:, sl], qe[:, sl], start=True, stop=True),
                         reads=["kebf", "qe"], writes=[pk2])
                    P.op("dve", lambda e, ps2=ps2, at=at: e.tensor_tensor(out=at[:], in0=ps2[:, 0:128], in1=bdmask[:], op=ALU.mult),
                         reads=[pk2, "bdmask"], writes=[atk])
                    pso, pko = PR.next()
                    P.op("pe", lambda e, pso=pso, at=at, tt=tt: e.matmul(pso[:, 0:128], vtm[tt][:], at[:], start=True, stop=False),
                         reads=[("vtm", tt), atk], writes=[pko])
                    for cc in range(2):
                        csl = slice(tt * 128 + cc * 64, tt * 128 + (cc + 1) * 64)
                        rows = slice(cc * 64, (cc + 1) * 64)
                        P.op("pe", lambda e, pso=pso, cc=cc, csl=csl: e.matmul(
                            pso[:, cc * 64:(cc + 1) * 64], Sb[:], qe[:, csl], start=False, stop=(cc == 1)),
                            reads=["Sb", "qe"], writes=[pko])
                        pss_, pks = PR.next()
                        P.op("pe", lambda e, pss_=pss_, kt=kt, rows=rows, tt=tt: e.matmul(
                            pss_[:, 0:128], kt[rows, :], vtm[tt][rows, :], start=True, stop=True),
                            reads=[ktk, ("vtm", tt)], writes=[pks])
                        ecol = tt * 128 + (cc + 1) * 64 - 1
                        P.op("dve", lambda e, ecol=ecol: e.tensor_scalar(out=Stmp[:], in0=Sf[:], scalar1=eb[:, ecol:ecol + 1],
                                                                        scalar2=None, op0=ALU.mult),
                             reads=["Sf", "eb"], writes=["Stmp"])
                        P.op("dve", lambda e, ecol=ecol, pss_=pss_: e.scalar_tensor_tensor(
                            out=Sf[:], in0=pss_[:, 0:128], scalar=eb[:, ecol:ecol + 1], in1=Stmp[:], op0=ALU.mult, op1=ALU.add),
                            reads=[pks, "eb", "Stmp"], writes=["Sf"])
                        P.op("act", lambda e: e.copy(out=Sb[:], in_=Sf[:]), reads=["Sf"], writes=["Sb"])
                    P.op("act", lambda e, pso=pso, sl=sl: e.copy(out=osb[:, sl], in_=pso[:, 0:128]), reads=[pko], writes=["osb"])
                P.op("act", lambda e: e.activation(out=sq[:], in_=osb[:], func=AF.Square), reads=["osb"], writes=["sq"])
                psn, pkn = PR.next()
                P.op("pe", lambda e, psn=psn: e.matmul(psn[:], ones[:], sq[:], start=True, stop=True), reads=["ones", "sq"], writes=[pkn])
                P.op("act", lambda e, psn=psn: e.activation(out=rstd[:], in_=psn[:], func=AF.Ln, scale=1.0 / 128, bias=EPS),
                     reads=[pkn], writes=["rstd"])
                P.op("act", lambda e: e.activation(out=rstd[:], in_=rstd[:], func=AF.Exp, scale=-0.5), reads=["rstd"], writes=["rstd"])
                P.op("dve", lambda e: e.scalar_tensor_tensor(out=tmpo[:], in0=osb[:], scalar=ng[:, 0:1], in1=rstd[:],
                                                              op0=ALU.mult, op1=ALU.mult), reads=["osb", "ng", "rstd"], writes=["tmpo"])
                P.op("dve", lambda e: e.tensor_tensor(out=ys[:], in0=tmpo[:], in1=gs[:], op=ALU.mult), reads=["tmpo", "gs"], writes=["ys"])
                P.dma("sp", yT[hh * 128:(hh + 1) * 128, tb * 512:(tb + 1) * 512], ys[:], reads=["ys"], writes=[("ysrc", hh, tb)])
            io["ydone"](hh)


NSA_SCALE = 128 ** -0.5
NEGM = -1.0e7
NREL = 19


def emit_nsa(P, nc, PR, psUl, psDl, gT_sb, gcol, ident, io):
    x = io["x"]
    gT = io["gT"]
    ident_d = io["ident"]
    wq = io["wq"]
    wz = io["wz"]
    wkv = io["wkv"]
    wgl = io["wgl"]
    peT_d = io["peT"]
    w1_d = io["w1"]
    w2_d = io["w2"]
    slope_d = io["slope"]
    cst_d = io["cst"]
    R_d = io["Rt"]
    E_d = io["E"]
    selb0_d = io["selb0"]
    Mmat_d = io["Mmat"]
    A_d = io["At"]
    Badd_d = io["Bt"]
    SelR_d = io["SelR"]
    yT = io["yT"]
    qT_s = io["qT_s"]
    zT_s = io["zT_s"]
    kvT_s = io["kvT_s"]
    vtm_s = io["vtm_s"]
    tcmp_s = io["tcmp_s"]
    KC = D // 128
    if True:
        GT = P.sb("GT", [48, S], F32)
        with contextlib.ExitStack() as es1:
            hT = es1.enter_context(nc.sbuf_tensor(P.pre + "hT", [128, KC, S], BF16))
            phase_hT(P, nc, PR, x, io["gR"], ident, hT, xsplit=io["xsplit"])
            wsb = [es1.enter_context(nc.sbuf_tensor(P.pre + "nw%d" % i, [128, KC, 128], BF16)) for i in range(2)]
            wglsb = es1.enter_context(nc.sbuf_tensor(P.pre + "wglsb", [128, KC, 48], BF16))
            stg = [es1.enter_context(nc.sbuf_tensor(P.pre + "stg%d" % i, [128, 512], BF16)) for i in range(2)]
            si = [0]
            wi = [0]

            def load_w(src_v, col0):
                b = wi[0] % 2
                wi[0] += 1
                for c0 in range(0, KC, 8):
                    P.dma("pool", wsb[b][:, c0:c0 + 8, :], src_v[:, c0:c0 + 8, col0:col0 + 128], writes=[("nw", b)])
                return wsb[b], ("nw", b)

            def fm_chunk(src_v, col0, dst, func, scale):
                w, wk = load_w(src_v, col0)
                for tb in range(4):
                    def ev(ps, pk, tb=tb):
                        b = si[0] % 2
                        si[0] += 1
                        P.op("act", lambda e: e.activation(out=stg[b][:], in_=ps[:], func=func, scale=scale),
                             reads=[pk], writes=[("stg", b)])
                        P.dma("sp", dst[:, tb * 512:(tb + 1) * 512], stg[b][:], reads=[("stg", b)], writes=["scratch"])
                    proj_fm(P, PR, hT, w, wk, tb, ev)

            wq_v = wq.rearrange("(c p) n -> p c n", p=128)
            wz_v = wz.rearrange("(c p) n -> p c n", p=128)
            wkv_v = wkv.rearrange("(c p) n -> p c n", p=128)
            for hl in range(16):
                fm_chunk(wq_v, hl * 128, qT_s[hl], AF.Copy, NSA_SCALE)
                fm_chunk(wz_v, hl * 128, zT_s[hl], AF.Silu, 1.0)
            for gl in range(2):
                for n, i in enumerate((0, 1, 2, 4)):
                    fm_chunk(wkv_v, i * 256 + gl * 128, kvT_s[gl, n], AF.Copy, 1.0)
                for n, i in enumerate((3, 5)):
                    w, wk = load_w(wkv_v, i * 256 + gl * 128)
                    for tt in range(16):
                        ps, pk = PR.next()
                        for c in range(KC):
                            P.op("pe", lambda e, c=c, ps=ps, tt=tt, w=w: e.matmul(
                                ps[:, 0:128], hT[:, c, tt * 128:(tt + 1) * 128], w[:, c, :],
                                start=(c == 0), stop=(c == KC - 1)), reads=[wk, ("hT", tt // 4)], writes=[pk], sig=(c == KC - 1))
                        b = si[0] % 2
                        si[0] += 1
                        P.op("act", lambda e, ps=ps, b=b: e.copy(out=stg[b][:, 0:128], in_=ps[:, 0:128]), reads=[pk],
                             writes=[("stg", b)])
                        P.dma("sp", vtm_s[gl, n, tt * 128:(tt + 1) * 128, :], stg[b][:, 0:128], reads=[("stg", b)],
                              writes=["scratch"])
            for c0 in range(0, KC, 8):
                P.dma("pool", wglsb[:, c0:c0 + 8, :], wgl.rearrange("(c p) n -> p c n", p=128)[:, c0:c0 + 8, :], writes=["wgl"])
            for tb in range(4):
                ps, pk = PR.next()
                for c in range(KC):
                    P.op("pe", lambda e, c=c, ps=ps, tb=tb: e.matmul(ps[0:48, :], wglsb[:, c, :], hT[:, c, tb * 512:(tb + 1) * 512],
                                                                    start=(c == 0), stop=(c == KC - 1)),
                         reads=["wgl", ("hT", tb)], writes=[pk])
                P.op("act", lambda e, ps=ps, tb=tb: e.activation(out=GT[:, tb * 512:(tb + 1) * 512], in_=ps[0:48, :], func=AF.Sigmoid),
                     reads=[pk], writes=["GT"])
            barrier(P)
        Rt = P.sb("Rt_sb", [128, 13, 512], F32)
        SelR = P.sb("SelR_sb", [48, 48 * 128], F32)
        slope = P.sb("slope_sb", [128, 16], F32)
        cst = P.sb("cst_sb", [128, 16 * NREL], F32)
        E = P.sb("E_sb", [32, 2048], BF16)
        selbT = P.sb("selbT", [32, 2048], BF16)
        Mmat = P.sb("Mmat_sb", [128, 32], F32)
        At = P.sb("At_sb", [128, 256], F32)
        Bt = P.sb("Bt_sb", [128, 256], F32)
        onesb = P.sb("onesb", [128, 128], BF16)
        for k in range(13):
            P.dma("sp", Rt[:, k, :], R_d[k], writes=["Rt"])
        for t, d, k in ((SelR, SelR_d, "SelR"), (slope, slope_d, "slope"), (cst, cst_d, "cst"), (E, E_d, "E"),
                        (Mmat, Mmat_d, "Mmat"), (At, A_d, "At"), (Bt, Badd_d, "Bt")):
            P.dma("sp", t[:], d, writes=[k])
        P.op("dve", lambda e: e.memset(onesb[:], 1.0), writes=["onesb"])
        kvT = P.sb("kvT", [128, 4, S], BF16)
        vtm = P.sb("vtm", [128, 2, 16, 128], BF16)
        w1sb = P.sb("w1sb", [128, 32, 128], BF16)
        w2sb = P.sb("w2sb", [128, 128], BF16)
        peT = P.sb("peT_sb", [128, 32], F32)
        peTb = P.sb("peTb", [128, 32], BF16)
        cb_ = P.sb("cmpb", [128, 1], F32)
        xg = P.sb("xg", [128, 128], F32)
        xg2 = P.sb("xg2", [128, 128], F32)
        hidT = P.sb("hidT", [128, 128], BF16)
        KcT = P.sb("KcT", [128, 128], BF16)
        Vc = P.sb("Vc", [128, 128], BF16)
        qTb = P.sb("qTb", [128, S], BF16)
        zTb = P.sb("zTb", [128, S], BF16)
        tmp = [P.sb("atmp%d" % i, [128, 512], F32) for i in range(4)]
        Pf = P.sb("Pf", [128, 512], F32)
        Pb = [P.sb("Pb%d" % i, [128, 512], BF16) for i in range(4)]
        rden = P.sb("rden", [128, 512], F32)
        Ff = P.sb("Ff", [128, 512], F32)
        Tt = P.sb("Tt", [128, 512], F32)
        acc = P.sb("acc", [128, 512], F32)
        pn = P.sb("pn", [128, 512], F32)
        ys = P.sb("nys", [128, 512], BF16)
        pslc = P.sb("pslc", [128, 256], F32)
        sc = P.sb("sc", [128, 32], F32)
        sc2 = P.sb("sc2", [128, 32], F32)
        m8 = P.sb("m8", [128, 16], F32)
        selb = P.sb("selb", [128, 32], F32)
        ti_ = [0]
        bi_ = [0]
        pend = []

        def drain(keep=0):
            while len(pend) > keep:
                pend.pop(0)()

        def branch_step(kT_ap, kkey, v_ap, vkey, qsl, hl, rvar, cidx, nk, extra=None, first=False, last=False, want_f32=False):
            ps, pk = PR.next()
            P.op("pe", lambda e: e.matmul(ps[0:nk, :], kT_ap, qTb[:, qsl], start=True, stop=(extra is None)),
                 reads=[kkey, "qTb"], writes=[pk])
            if extra is not None:
                P.op("pe", lambda e: e.matmul(ps[0:nk, :], extra, selbT[:, qsl], start=False, stop=True),
                     reads=["E", "selbT"], writes=[pk])
            i = ti_[0] % 4
            ti_[0] += 1
            psU, psD = psUl[bi_[0] % 2], psDl[bi_[0] % 2]
            uk, dk = ("psU", bi_[0] % 2), ("psD", bi_[0] % 2)
            P.op("dve", lambda e: e.scalar_tensor_tensor(out=tmp[i][0:nk, :], in0=Rt[0:nk, rvar, :], scalar=slope[0:nk, hl:hl + 1],
                                                          in1=ps[0:nk, :], op0=ALU.mult, op1=ALU.add),
                 reads=["Rt", "slope", pk], writes=[("atmp", i)])
            if want_f32:
                P.op("act", lambda e: e.activation(out=Pf[0:nk, :], in_=tmp[i][0:nk, :], func=AF.Exp), reads=[("atmp", i)], writes=["Pf"])
                P.op("act", lambda e: e.copy(out=Pb[i][0:nk, :], in_=Pf[0:nk, :]), reads=["Pf"], writes=[("Pb", i)])
            elif cidx is None:
                P.op("act", lambda e: e.activation(out=Pb[i][0:nk, :], in_=tmp[i][0:nk, :], func=AF.Exp), reads=[("atmp", i)],
                     writes=[("Pb", i)])
            else:
                P.op("act", lambda e: e.activation(out=Pb[i][0:nk, :], in_=tmp[i][0:nk, :], func=AF.Exp,
                                                   bias=cst[0:nk, cidx:cidx + 1]), reads=[("atmp", i), "cst"], writes=[("Pb", i)])

            def stage_b():
                P.op("pe", lambda e: e.matmul(psU[:], v_ap, Pb[i][0:nk, :], start=first, stop=last), reads=[vkey, ("Pb", i)], writes=[uk])
                P.op("pe", lambda e: e.matmul(psD[:], onesb[0:nk, :], Pb[i][0:nk, :], start=first, stop=last),
                     reads=["onesb", ("Pb", i)], writes=[dk])
            pend.append(stage_b)

        def finish_branch(br, hl, qsl, mode):
            psU, psD = psUl[bi_[0] % 2], psDl[bi_[0] % 2]
            uk, dk = ("psU", bi_[0] % 2), ("psD", bi_[0] % 2)
            bi_[0] += 1

            def fin():
                if br == 0:
                    P.op("dve", lambda e: e.tensor_scalar(out=rden[:], in0=psD[:], scalar1=1e-18, scalar2=None, op0=ALU.max),
                         reads=[dk], writes=["rden"])
                    P.op("act", lambda e: e.activation(out=rden[:], in_=rden[:], func=AF.Ln), reads=["rden"], writes=["rden"])
                else:
                    P.op("act", lambda e: e.activation(out=rden[:], in_=psD[:], func=AF.Ln), reads=[dk], writes=["rden"])
                P.op("act", lambda e: e.activation(out=rden[:], in_=rden[:], func=AF.Exp, scale=-1.0), reads=["rden"], writes=["rden"])
                r = br * 16 + hl
                psG, gk = PR.next()
                P.op("pe", lambda e: e.matmul(psG[:], SelR[:, r * 128:(r + 1) * 128], GT[:, qsl], start=True, stop=True),
                     reads=["SelR", "GT"], writes=[gk])
                P.op("dve", lambda e: e.tensor_tensor(out=Ff[:], in0=rden[:], in1=psG[:], op=ALU.mult), reads=["rden", gk], writes=["Ff"])
                if mode == "ret":
                    P.op("dve", lambda e: e.tensor_tensor(out=Tt[:], in0=psU[:], in1=Ff[:], op=ALU.mult), reads=[uk, "Ff"], writes=["Tt"])
                elif mode == "set":
                    P.op("dve", lambda e: e.tensor_tensor(out=acc[:], in0=psU[:], in1=Ff[:], op=ALU.mult), reads=[uk, "Ff"], writes=["acc"])
                else:
                    P.op("dve", lambda e: e.tensor_tensor(out=Tt[:], in0=psU[:], in1=Ff[:], op=ALU.mult), reads=[uk, "Ff"], writes=["Tt"])
                    P.op("pool", lambda e: e.tensor_tensor(out=acc[:], in0=acc[:], in1=Tt[:], op=ALU.add), reads=["acc", "Tt"], writes=["acc"])
            pend.append(fin)

        def cidx_of(hl, m):
            return hl * NREL + (m + 15)

        for gl in range(2):
            for n in range(4):
                P.dma("sp", kvT[:, n, :], kvT_s[gl, n], reads=["scratch"], writes=["kvT"])
            for n in range(2):
                P.dma("sp", vtm[:, n], vtm_s[gl, n].rearrange("(t p) d -> p t d", p=128), reads=["scratch"], writes=["vtm"])
            P.dma("sp", selbT[:, 0:1024], selb0_d, writes=["selbT"])
            for which in range(2):
                for c0 in range(0, 32, 8):
                    P.dma("pool", w1sb[:, c0:c0 + 8, :], w1_d[which].rearrange("(l d) e -> d l e", d=128)[:, c0:c0 + 8, :], writes=["w1sb"])
                P.dma("pool", w2sb[:], w2_d[which], writes=["w2sb"])
                P.dma("sp", peT[:], peT_d[which], writes=["peT"])
                P.op("act", lambda e: e.copy(out=peTb[:], in_=peT[:]), reads=["peT"], writes=["peTb"])
                ps, pk = PR.next()
                for l in range(32):
                    P.op("pe", lambda e, l=l, ps=ps: e.matmul(ps[:, 0:127], w1sb[:, l, :], kvT[:, which, l:l + 16 * 126 + 1:16],
                                                             start=(l == 0), stop=(l == 31)), reads=["w1sb", "kvT"], writes=[pk])
                ps2, pk2 = PR.next()
                for l in range(32):
                    P.op("pe", lambda e, l=l, ps2=ps2: e.matmul(ps2[:, 0:1], w1sb[:, l, :], peTb[:, l:l + 1],
                                                               start=(l == 0), stop=(l == 31)), reads=["w1sb", "peTb"], writes=[pk2])
                P.op("act", lambda e, ps2=ps2: e.copy(out=cb_[:], in_=ps2[:, 0:1]), reads=[pk2], writes=["cmpb"])
                P.op("act", lambda e, ps=ps: e.activation(out=xg[:, 0:127], in_=ps[:, 0:127], func=AF.Identity, bias=cb_[:, 0:1]),
                     reads=[pk, "cmpb"], writes=["xg"])
                P.op("dve", lambda e: e.tensor_tensor(out=xg2[:, 0:127], in0=xg[:, 0:127], in1=xg[:, 0:127], op=ALU.mult), reads=["xg"], writes=["xg2"])
                P.op("dve", lambda e: e.tensor_scalar(out=xg2[:, 0:127], in0=xg2[:, 0:127], scalar1=0.044715, scalar2=1.0,
                                                       op0=ALU.mult, op1=ALU.add), reads=["xg2"], writes=["xg2"])
                P.op("dve", lambda e: e.tensor_tensor(out=xg2[:, 0:127], in0=xg2[:, 0:127], in1=xg[:, 0:127], op=ALU.mult),
                     reads=["xg", "xg2"], writes=["xg2"])
                P.op("act", lambda e: e.activation(out=xg2[:, 0:127], in_=xg2[:, 0:127], func=AF.Sigmoid, scale=1.5957691216),
                     reads=["xg2"], writes=["xg2"])
                P.op("dve", lambda e: e.tensor_tensor(out=hidT[:, 0:127], in0=xg[:, 0:127], in1=xg2[:, 0:127], op=ALU.mult),
                     reads=["xg", "xg2"], writes=["hidT"])
                ps3, pk3 = PR.next()
                if which == 0:
                    P.op("pe", lambda e, ps3=ps3: e.matmul(ps3[:, 0:127], w2sb[:], hidT[:, 0:127], start=True, stop=True),
                         reads=["w2sb", "hidT"], writes=[pk3])
                    P.op("act", lambda e, ps3=ps3: e.copy(out=KcT[:, 0:127], in_=ps3[:, 0:127]), reads=[pk3], writes=["KcT"])
                else:
                    P.op("pe", lambda e, ps3=ps3: e.matmul(ps3[0:127, 0:128], hidT[:, 0:127], w2sb[:], start=True, stop=True),
                         reads=["w2sb", "hidT"], writes=[pk3])
                    P.op("act", lambda e, ps3=ps3: e.copy(out=Vc[0:127, :], in_=ps3[0:127, 0:128]), reads=[pk3], writes=["Vc"])
            for j in range(8):
                hl = gl * 8 + j
                P.dma("sp", qTb[:], qT_s[hl], reads=["scratch"], writes=["qTb"])
                for qb in range(4):
                    qsl = slice(qb * 512, (qb + 1) * 512)
                    branch_step(KcT[:, 0:127], "KcT", Vc[0:127, :], "Vc", qsl, hl, 9 + qb, None, 127, first=True, last=True,
                                want_f32=(qb >= 2))
                    finish_branch(0, hl, qsl, "ret")
                    drain(0)
                    P.dma("pool", tcmp_s[j, :, qsl], Tt[:], reads=["Tt"], writes=["tcmp"])
                    if qb >= 2:
                        P.op("dve", lambda e: e.tensor_tensor(out=pn[0:127, :], in0=Pf[0:127, :], in1=rden[0:127, :], op=ALU.mult),
                             reads=["Pf", "rden"], writes=["pn"])
                        ps, pk = PR.next()
                        for k in range(4):
                            P.op("pe", lambda e, k=k, ps=ps: e.matmul(ps[:, k * 32:(k + 1) * 32], pn[0:127, k * 128:(k + 1) * 128],
                                                                     Mmat[0:127, :], start=True, stop=True),
                                 reads=["pn", "Mmat"], writes=[pk])
                        dst = pslc[:, (qb - 2) * 128:(qb - 1) * 128]
                        if j == 0:
                            P.op("dve", lambda e, ps=ps, dst=dst: e.tensor_copy(out=dst, in_=ps[:, 0:128]), reads=[pk], writes=["pslc"])
                        else:
                            P.op("dve", lambda e, ps=ps, dst=dst: e.tensor_tensor(out=dst, in0=dst, in1=ps[:, 0:128], op=ALU.add),
                                 reads=[pk, "pslc"], writes=["pslc"])
            for ti in range(8):
                csl = slice(ti * 32, (ti + 1) * 32)
                P.op("dve", lambda e, csl=csl: e.tensor_tensor(out=sc[:], in0=pslc[:, csl], in1=At[:, csl], op=ALU.mult),
                     reads=["pslc", "At"], writes=["sc"])
                P.op("dve", lambda e, csl=csl: e.tensor_tensor(out=sc[:], in0=sc[:], in1=Bt[:, csl], op=ALU.add), reads=["sc", "Bt"], writes=["sc"])
                P.op("dve", lambda e: e.max(out=m8[:, 0:8], in_=sc[:]), reads=["sc"], writes=["m8"])
                P.op("dve", lambda e: e.match_replace(out=sc2[:], in_to_replace=m8[:, 0:8], in_values=sc[:], imm_value=-1e30),
                     reads=["sc", "m8"], writes=["sc2"])
                P.op("dve", lambda e: e.max(out=m8[:, 8:16], in_=sc2[:]), reads=["sc2"], writes=["m8"])
                P.op("dve", lambda e: e.tensor_scalar(out=selb[:], in0=sc[:], scalar1=m8[:, 15:16], scalar2=None, op0=ALU.is_ge),
                     reads=["sc", "m8"], writes=["selb"])
                P.op("dve", lambda e: e.tensor_scalar(out=selb[:], in0=selb[:], scalar1=30000.0, scalar2=-30000.0, op0=ALU.mult, op1=ALU.add),
                     reads=["selb"], writes=["selb"])
                ps, pk = PR.next()
                P.op("pe", lambda e, ps=ps: e.transpose(out=ps[0:32, 0:128], in_=selb[:], identity=ident[:]), reads=["selb", "ident"], writes=[pk])
                P.op("act", lambda e, ps=ps, ti=ti: e.copy(out=selbT[:, 1024 + ti * 128:1024 + (ti + 1) * 128], in_=ps[0:32, 0:128]),
                     reads=[pk], writes=["selbT"])
            for j in range(8):
                hl = gl * 8 + j
                drain(0)
                P.dma("sp", qTb[:], qT_s[hl], reads=["scratch"], writes=["qTb"])
                P.dma("sp", zTb[:], zT_s[hl], reads=["scratch"], writes=["zTb"])
                for qb in range(4):
                    qsl = slice(qb * 512, (qb + 1) * 512)
                    drain(0)
                    P.dma("sp", pn[:], tcmp_s[j, :, qsl], reads=["tcmp"], writes=["pn"])
                    kts = list(range(0, 4 * qb + 4))
                    for kt in kts:
                        m = 4 * qb - kt
                        rvar, ci = (0, cidx_of(hl, -m)) if m >= 1 else (1 + (-m), cidx_of(hl, -m))
                        branch_step(kvT[:, 2, kt * 128:(kt + 1) * 128], "kvT", vtm[:, 0, kt, :], "vtm", qsl, hl, rvar, ci, 128,
                                    extra=(E[:, kt * 128:(kt + 1) * 128] if qb >= 2 else None), first=(kt == kts[0]),
                                    last=(kt == kts[-1]))
                        drain(2)
                    finish_branch(1, hl, qsl, "set")
                    kts = list(range(max(0, 4 * qb - 4), 4 * qb + 4))
                    for kt in kts:
                        m = 4 * qb - kt
                        if m >= 1:
                            rvar = 5 + (4 - m)
                        else:
                            rvar = 1 + (-m)
                        branch_step(kvT[:, 3, kt * 128:(kt + 1) * 128], "kvT", vtm[:, 1, kt, :], "vtm", qsl, hl, rvar, cidx_of(hl, -m), 128,
                                    first=(kt == kts[0]), last=(kt == kts[-1]))
                        drain(2)
                    finish_branch(2, hl, qsl, "add")
                    drain(0)
                    P.op("pool", lambda e: e.tensor_tensor(out=acc[:], in0=acc[:], in1=pn[:], op=ALU.add), reads=["acc", "pn"], writes=["acc"])
                    P.op("pool", lambda e, qsl=qsl: e.tensor_tensor(out=ys[:], in0=acc[:], in1=zTb[:, qsl], op=ALU.mult),
                         reads=["acc", "zTb"], writes=["nys"])
                    P.dma("pool", yT[hl * 128:(hl + 1) * 128, qsl], ys[:], reads=["nys"], writes=[("ysrc", hl, qb)])
                io["ydone"](hl)


IDENT = np.eye(128, dtype=np.float32)


def gainT(g):
    return np.ascontiguousarray(g.reshape(32, 128).T).astype(np.float32)


def colT(v, h, n=16):
    return np.ascontiguousarray(v[h * n * 128:(h + 1) * n * 128].reshape(n, 128).T).astype(np.float32)

def nsa_consts(h):
    heads = np.arange(16) + 16 * h
    sl = (2.0 ** (-8.0 * (heads + 1) / 32)).astype(np.float64)
    slope = np.broadcast_to(sl[None, :], (128, 16)).astype(np.float32)
    cst = np.zeros((128, 16, NREL), np.float64)
    for m in range(-15, 4):
        cst[:, :, m + 15] = sl[None, :] * 128.0 * m
    kj = np.arange(128)[:, None].astype(np.float64)
    qi = np.arange(512)[None, :].astype(np.float64)
    base = kj - qi
    Rt = np.zeros((13, 128, 512), np.float64)
    Rt[0] = base
    for v in range(4):
        Rt[1 + v] = np.where(-128 * v + qi - kj >= 0, base, NEGM)
    for n, rel in enumerate((512, 384, 256, 128)):
        Rt[5 + n] = np.where(rel + qi - kj < 512, base, NEGM)
    for qb in range(4):
        dist = 512 * qb + qi - 16 * kj - 31
        Rt[9 + qb] = np.where(dist >= 0, -dist, NEGM)
    key = np.arange(2048)
    E = (key[None, :] // 64 == np.arange(32)[:, None]).astype(np.float32)
    t = np.arange(1024)
    selb0 = np.where(np.arange(32)[:, None] <= (t[None, :] // 64), 0.0, -30000.0)
    Mmat = np.zeros((128, 32), np.float32)
    for i in range(32):
        for off, wgt in ((-1, 1), (0, 2), (1, 2), (2, 2), (3, 1)):
            n = 4 * i + off
            if 0 <= n < 127:
                Mmat[n, i] += wgt
    At = np.zeros((128, 8, 32), np.float32)
    Bt = np.zeros((128, 8, 32), np.float32)
    for ti in range(8):
        tt = 1024 + ti * 128 + np.arange(128)
        cur = (tt // 64)[:, None]
        blk = np.arange(32)[None, :]
        forced = (blk == 0) | (blk == cur) | (blk == cur - 1)
        future = blk > cur
        At[:, ti] = (~forced & ~future)
        Bt[:, ti] = np.where(forced, 1e6, np.where(future, -1.0, 0.0))
    SelR = np.zeros((48, 48, 128), np.float32)
    for r in range(48):
        SelR[r, r, :] = 1.0
    return {"slope": slope, "cst": np.ascontiguousarray(cst.reshape(128, 16 * NREL)).astype(np.float32),
            "Rt": Rt.astype(np.float32), "E": E.astype(ml_dtypes.bfloat16), "selb0": selb0.astype(ml_dtypes.bfloat16),
            "Mmat": Mmat, "At": np.ascontiguousarray(At.reshape(128, 256)), "Bt": np.ascontiguousarray(Bt.reshape(128, 256)),
            "SelR": np.ascontiguousarray(SelR.reshape(48, 48 * 128))}


PAIRS = [[0, 1], [2, 3], [4, 5], [6, 7]]
NSA_IN = ("wq", [D, 2048]), ("wz", [D, 2048]), ("wkv", [D, 1536]), ("wgl", [D, 48]), ("peT", [2, 128, 32]), \
    ("w1", [2, 4096, 128]), ("w2", [2, 128, 128])
NSA_CONST = ("slope", [128, 16], F32), ("cst", [128, 16 * NREL], F32), ("Rt", [13, 128, 512], F32), ("E", [32, 2048], BF16), \
    ("selb0", [32, 1024], BF16), ("Mmat", [128, 32], F32), ("At", [128, 256], F32), ("Bt", [128, 256], F32), \
    ("SelR", [48, 48 * 128], F32)
RG_IN = ("wx", [D, 2048]), ("wz", [D, 2048]), ("cw", [128, 64]), ("cb", [128, 16]), ("lam", [128, 16]), ("gb", [128, 32]), \
    ("gw", [2, 8, 256, 256])
HG_IN = ("wq", [D, 2048]), ("wf", [D, 2048]), ("wv", [D, 2048]), ("wg", [D, 2048]), ("lbl", [128, 64]), ("ng", [128, 1]), \
    ("ones", [128, 128]), ("rmask", [128, 512]), ("bdmask", [128, 128])


def build_fused():
    nc = bass.Bass("TRN2", target_bir_lowering=False)
    I = lambda n, s, dt=F32: nc.dram_tensor(n, list(s), dt, kind="ExternalInput").ap()
    T = lambda n, s, dt: nc.dram_tensor(n, list(s), dt, kind="Internal").ap()
    x = I("x", [S, D])
    x_my = I("x_my", [S, 2048])
    gT_d = I("gT", [128, 128])
    gR_d = I("gR", [4, 128, D])
    post_d = I("post", [4, 128, 2048])
    ident_d = I("ident", [128, 128])
    consts = {n: I(n, s, dt) for n, s, dt in NSA_CONST}
    lio = {}
    for i in range(4):
        spec = (NSA_IN, RG_IN, HG_IN)[i % 3]
        lio[i] = {n: I("L%d_%s" % (i, n), s) for n, s in spec}
        lio[i]["wo"] = I("L%d_wo" % i, [D, 2048])
    out = nc.dram_tensor("out", [S, 2048], F32, kind="ExternalOutput").ap()
    ysrc = T("ysrc", [2048, S], BF16)
    yT_g = T("yT_g", [D, S], BF16)
    xsrc = [T("xsrc0", [S, 2048], F32), T("xsrc1", [S, 2048], F32)]
    x_g = T("x_g", [2 * S, 2048], F32)
    ss_src = T("ss_src", [128, 8], F32)
    ss_g = T("ss_g", [256, 8], F32)
    scratch = {"qT_s": T("qT_s", [16, 128, S], BF16), "zT_s": T("zT_s", [16, 128, S], BF16),
               "kvT_s": T("kvT_s", [2, 4, 128, S], BF16), "vtm_s": T("vtm_s", [2, 2, S, 128], BF16),
               "tcmp_s": T("tcmp_s", [8, 128, S], F32)}
    with contextlib.ExitStack() as es:
        P = Prog(nc, es)
        PR = PsumRing(P, 4)
        psUl = [P.ps("psU%d" % i, [128, 512]) for i in range(2)]
        psDl = [P.ps("psD%d" % i, [128, 512]) for i in range(2)]
        gT_sb = P.sb("gT_sb", [128, 128], F32)
        ident = P.sb("ident_sb", [128, 128], F32)
        P.dma("sp", gT_sb[:], gT_d, writes=["gT"])
        P.dma("sp", ident[:], ident_d, writes=["ident"])
        for i in range(4):
            kind = i % 3
            io = dict(lio[i])
            io.update(gT=None, ident=None, x=(x if i == 0 else x_g), yT=ysrc, xsplit=(i > 0), gR=gR_d[i])

            def ydone(ch):
                if ch % 4 == 3:
                    k = ch // 4
                    P.cc("AllGather", PAIRS, ysrc[k * 512:(k + 1) * 512, :], yT_g[k * 1024:(k + 1) * 1024, :],
                         reads=[("ysrc", c_, t_) for c_ in range(4 * k, 4 * k + 4) for t_ in range(4)], writes=["yT_g"])
            io["ydone"] = ydone
            with contextlib.ExitStack() as esl:
                P.es_cur = esl
                P.pre = "L%dm_" % i
                if kind == 0:
                    io.update(consts)
                    io.update(scratch)
                    emit_nsa(P, nc, PR, psUl, psDl, gT_sb, i * 32, ident, io)
                elif kind == 1:
                    emit_rg(P, nc, PR, gT_sb, i * 32, ident, io)
                else:
                    emit_hg(P, nc, PR, gT_sb, i * 32, ident, io, layer=i)
                barrier(P)
            xo = out if i == 3 else xsrc[i % 2]

            def gather_x(k0, k1, xo=xo):
                for k in range(k0, k1):
                    P.cc("AllGather", PAIRS, xo[k * 256:(k + 1) * 256, :], x_g[k * 512:(k + 1) * 512, :],
                         reads=[("x_out", 2 * k), ("x_out", 2 * k + 1)], writes=["x_g"])
            with contextlib.ExitStack() as esl:
                P.es_cur = esl
                P.pre = "L%do_" % i
                x_res = x_my if i == 0 else xsrc[(i - 1) % 2]
                emit_outproj(P, nc, PR, yT_g, x_res, io["wo"], post_d[i], xo, ss_src, ss_g,
                             mid_cc=((lambda: gather_x(0, 4)) if i < 3 else None))
                barrier(P)
            P.es_cur = None
            if i < 3:
                gather_x(4, 8)
        P.finish()
    return nc


def kernel(x, pre_norm_gain, post_norm_gain, nsa_w_in, nsa_cmp_pe, nsa_cmp_w1, nsa_cmp_w2, nsa_w_out,
           rg_w_in, rg_conv_w, rg_conv_b, rg_gate_w, rg_gate_b, rg_lambda, rg_w_out,
           hg_w_in, hg_lb_logits, hg_norm_gain, hg_w_out):
    f = lambda a: np.asarray(a, dtype=np.float32)
    ca = np.ascontiguousarray
    x = f(x)
    pre, post = f(pre_norm_gain), f(post_norm_gain)
    common = {
        "gT": ca(np.concatenate([gainT(pre[i]) for i in range(4)], axis=1)),
        "ident": IDENT,
        "gR": ca(np.broadcast_to(pre[:, None, :], (4, 128, D))).astype(np.float32),
    }
    rmask = np.ones((128, 512), np.float32)
    rmask[:, ::64] = 0.0
    si = np.arange(128)[:, None]
    ti = np.arange(128)[None, :]
    bdmask = ((si // 64 == ti // 64) & (ti >= si)).astype(np.float32)
    in_maps = []
    for c in range(NCORES):
        b, h = c // 2, c % 2
        m = dict(common)
        m["x"] = ca(x[b])
        m["x_my"] = ca(x[b][:, h * 2048:(h + 1) * 2048])
        m["post"] = ca(np.broadcast_to(post[:, None, h * 2048:(h + 1) * 2048], (4, 128, 2048))).astype(np.float32)
        m.update(nsa_consts(h))
        for i, j in ((0, 0), (3, 1)):
            w_in = f(nsa_w_in[j])
            pfx = "L%d_" % i
            m[pfx + "wq"] = ca(w_in[:, h * 2048:(h + 1) * 2048])
            m[pfx + "wz"] = ca(w_in[:, 7264 + h * 2048:7264 + (h + 1) * 2048])
            m[pfx + "wkv"] = ca(np.concatenate(
                [w_in[:, 4096 + k * 512 + h * 256:4096 + k * 512 + (h + 1) * 256] for k in range(6)], axis=1))
            m[pfx + "wgl"] = ca(np.concatenate(
                [w_in[:, 7168 + br * 32 + h * 16:7168 + br * 32 + (h + 1) * 16] for br in range(3)], axis=1))
            m[pfx + "peT"] = ca(f(nsa_cmp_pe[j]).transpose(0, 2, 1))
            m[pfx + "w1"] = ca(f(nsa_cmp_w1[j]))
            m[pfx + "w2"] = ca(f(nsa_cmp_w2[j]))
            m[pfx + "wo"] = ca(f(nsa_w_out[j])[:, h * 2048:(h + 1) * 2048])
        w_in = f(rg_w_in[0])
        m["L1_wx"] = ca(w_in[:, h * 2048:(h + 1) * 2048])
        m["L1_wz"] = ca(w_in[:, 4096 + h * 2048:4096 + (h + 1) * 2048])
        cw = f(rg_conv_w[0])
        m["L1_cw"] = ca(np.stack([colT(cw[k], h) for k in range(4)], axis=-1).reshape(128, 64))
        m["L1_cb"] = colT(f(rg_conv_b[0]), h)
        m["L1_lam"] = colT(f(rg_lambda[0]), h)
        gbv = f(rg_gate_b[0])
        m["L1_gb"] = ca(np.stack([colT(gbv[k].reshape(-1), h) for k in range(2)], axis=1).reshape(128, 32))
        m["L1_gw"] = ca(f(rg_gate_w[0])[:, h * 8:(h + 1) * 8])
        m["L1_wo"] = ca(f(rg_w_out[0])[:, h * 2048:(h + 1) * 2048])
        w_in = f(hg_w_in[0])
        for k, n in enumerate(("wq", "wf", "wv", "wg")):
            m["L2_" + n] = ca(w_in[:, k * 4096 + h * 2048:k * 4096 + (h + 1) * 2048])
        lbl = f(hg_lb_logits)
        m["L2_lbl"] = ca(np.concatenate([colT(lbl[l], h) for l in range(4)], axis=1))
        m["L2_ng"] = ca(f(hg_norm_gain[0]).reshape(128, 1))
        m["L2_ones"] = np.ones((128, 128), np.float32)
        m["L2_rmask"] = rmask
        m["L2_bdmask"] = bdmask
        m["L2_wo"] = ca(f(hg_w_out[0])[:, h * 2048:(h + 1) * 2048])
        in_maps.append(m)
    nc = build_fused()
    res = run_bass_kernel_spmd(nc, in_maps, core_ids=list(range(NCORES)))
    out = np.empty((B, S, D), np.float32)
    for c in range(NCORES):
        b, h = c // 2, c % 2
        out[b][:, h * 2048:(h + 1) * 2048] = res.results[c]["out"]
    return out
```

```python
import contextlib
import numpy as np
import ml_dtypes
import concourse.bass as bass
import concourse.mybir as mybir
from concourse.bass_utils import run_bass_kernel_spmd

F32 = mybir.dt.float32
BF16 = mybir.dt.bfloat16
AF = mybir.ActivationFunctionType
ALU = mybir.AluOpType
AX = mybir.AxisListType

D = 4096
B = 4
S = 2048
EPS = 1e-6
NCORES = 8


class Prog:
    ENGS = ("pe", "act", "dve", "pool", "sp")
    NDS = 24

    def __init__(self, nc, es, self_sync=True):
        self.nc = nc
        self.es = es
        self.self_sync = self_sync
        self.eng = dict(pe=nc.tensor, act=nc.scalar, dve=nc.vector, pool=nc.gpsimd, sp=nc.sync)
        self.sem = {e: es.enter_context(nc.semaphore("s_" + e)) for e in self.ENGS}
        self.cnt = {e: 0 for e in self.ENGS}
        self.seen = {e: {} for e in self.ENGS}
        self.dsem = [es.enter_context(nc.semaphore("d%d" % i)) for i in range(self.NDS)]
        self.dcnt = [0] * self.NDS
        self.drr = 0
        self.state = {}
        self.ninst = 0
        self.pre = ""
        self.es_cur = None
        self.ccsem = es.enter_context(nc.semaphore("ccsem"))
        self.cccnt = 0

    def sb(self, name, shape, dt):
        es = self.es_cur if self.es_cur is not None else self.es
        return es.enter_context(self.nc.sbuf_tensor(self.pre + name, list(shape), dt))

    def ps(self, name, shape, dt=F32):
        return self.es.enter_context(self.nc.psum_tensor(name, list(shape), dt))

    def _deps(self, reads, writes):
        deps = []
        for k in reads:
            st = self.state.get(k)
            if st is not None and st[0] is not None:
                deps.append(st[0])
        for k in writes:
            st = self.state.get(k)
            if st is not None:
                if st[0] is not None:
                    deps.append(st[0])
                deps.extend(st[1].values())
        return deps

    def _wait(self, e, deps):
        best = {}
        for (sid, sem, val) in deps:
            if sid not in best or best[sid][1] < val:
                best[sid] = (sem, val)
        for sid, (sem, val) in best.items():
            if self.seen[e].get(sid, 0) >= val:
                continue
            if sid == e and (e == "pe" or not self.self_sync):
                continue
            self.eng[e].wait_ge(sem, val)
            self.ninst += 1
            self.seen[e][sid] = val

    def _record(self, tok, reads, writes):
        for k in reads:
            st = self.state.get(k)
            if st is None:
                st = [None, {}]
                self.state[k] = st
            st[1][tok[0]] = tok
        for k in writes:
            self.state[k] = [tok, {}]

    def op(self, e, fn, reads=(), writes=(), sig=True):
        self._wait(e, self._deps(reads, writes))
        ins = fn(self.eng[e])
        self.ninst += 1
        if sig:
            self.cnt[e] += 1
            ins.then_inc(self.sem[e], 1)
            self._record((e, self.sem[e], self.cnt[e]), reads, writes)
        else:
            self._record((e, self.sem[e], self.cnt[e] + 1), reads, writes)

    def dma(self, q, out, in_, reads=(), writes=()):
        i = self.drr
        self.drr = (i + 1) % self.NDS
        sem = self.dsem[i]
        sid = "d%d" % i
        deps = self._deps(reads, writes)
        if self.dcnt[i] > 0:
            deps.append((sid, sem, self.dcnt[i]))
        self._wait(q, deps)
        self.eng[q].dma_start(out=out, in_=in_).then_inc(sem, 16)
        self.ninst += 1
        self.dcnt[i] += 16
        self._record((sid, sem, self.dcnt[i]), reads, writes)

    def cc(self, kind, groups, in_, out, reads=(), writes=()):
        deps = self._deps(reads, writes)
        self._wait("pool", deps)
        self.eng["pool"].collective_compute(kind, ALU.bypass, replica_groups=groups, ins=[in_.opt()], outs=[out.opt()]).then_inc(self.ccsem)
        self.ninst += 1
        self.cccnt += 1
        self._record(("cc", self.ccsem, self.cccnt), reads, writes)

    def finish(self):
        for i in range(self.NDS):
            if self.dcnt[i] > 0:
                self.eng["sp"].wait_ge(self.dsem[i], self.dcnt[i])
        for e in self.ENGS:
            if e != "sp" and self.cnt[e] > 0:
                self.eng["sp"].wait_ge(self.sem[e], self.cnt[e])
        if self.cccnt > 0:
            self.eng["sp"].wait_ge(self.ccsem, self.cccnt)


class PsumRing:
    def __init__(self, P, n=8):
        self.P = P
        self.t = [P.ps("psr%d" % i, [128, 512]) for i in range(n)]
        self.i = 0

    def next(self):
        i = self.i % len(self.t)
        self.i += 1
        return self.t[i], ("psr", i)


def barrier(P):
    toks = []
    for e in P.ENGS:
        if P.cnt[e] > 0:
            toks.append((e, P.sem[e], P.cnt[e]))
    for i in range(P.NDS):
        if P.dcnt[i] > 0:
            toks.append(("d%d" % i, P.dsem[i], P.dcnt[i]))
    if P.cccnt > 0:
        toks.append(("cc", P.ccsem, P.cccnt))
    for e in P.ENGS:
        P._wait(e, [t for t in toks if t[0] != e])


def phase_hT(P, nc, PR, x, gR, ident, hT, ntok=S, xsplit=False):
    KC = D // 128
    with contextlib.ExitStack() as es2:
        xs = [es2.enter_context(nc.sbuf_tensor(P.pre + "hx%d" % i, [128, D], F32)) for i in range(2)]
        junk = es2.enter_context(nc.sbuf_tensor(P.pre + "hjunk", [128, D], BF16))
        grep_ = es2.enter_context(nc.sbuf_tensor(P.pre + "hgrep", [128, D], F32))
        st = es2.enter_context(nc.sbuf_tensor(P.pre + "hst", [128, 8], F32))
        P.dma("sp", grep_[:], gR, writes=["hgrep"])
        gi = 0
        for tt in range(ntok // 128):
            xb = xs[tt % 2]
            xk = ("hx", tt % 2)
            if xsplit:
                for r_ in range(2):
                    row = (tt // 2) * 512 + r_ * 256 + (tt % 2) * 128
                    P.dma("sp", xb[:, r_ * 2048:(r_ + 1) * 2048], x[row:row + 128, :], reads=["x_g"], writes=[xk])
            else:
                P.dma("sp", xb[:], x[tt * 128:(tt + 1) * 128, :], reads=["x_g"], writes=[xk])
            P.op("act", lambda e: e.activation(out=junk[:], in_=xb[:], func=AF.Square, accum_out=st[:, 0:1]),
                 reads=[xk], writes=["hjunk", "hst0"])
            P.op("dve", lambda e: e.tensor_scalar(out=st[:, 1:2], in0=st[:, 0:1], scalar1=1.0 / D, scalar2=EPS,
                                                   op0=ALU.mult, op1=ALU.add), reads=["hst0"], writes=["hst1"])
            P.op("act", lambda e: e.activation(out=st[:, 2:3], in_=st[:, 1:2], func=AF.Sqrt),
                 reads=["hst1"], writes=["hst2"])
            P.op("dve", lambda e: e.reciprocal(out=st[:, 3:4], in_=st[:, 2:3]), reads=["hst2"], writes=["hst3"])
            P.op("dve", lambda e: e.scalar_tensor_tensor(out=xb[:], in0=xb[:], scalar=st[:, 3:4], in1=grep_[:],
                                                          op0=ALU.mult, op1=ALU.mult), reads=[xk, "hst3", "hgrep"], writes=[xk])
            for c0 in range(0, KC, 4):
                ps, pk = PR.next()
                for k in range(4):
                    c = c0 + k
                    P.op("pe", lambda e, c=c, k=k, ps=ps: e.transpose(
                        out=ps[:, k * 128:(k + 1) * 128], in_=xb[:, c * 128:(c + 1) * 128], identity=ident[:]),
                        reads=[xk, "ident"], writes=[pk], sig=(k == 3))
                dst = hT[:, c0:c0 + 4, tt * 128:(tt + 1) * 128]
                src_ = ps[:].rearrange("p (k t) -> p k t", k=4)
                if gi % 2 == 0:
                    P.op("act", lambda e, dst=dst, src_=src_: e.copy(out=dst, in_=src_), reads=[pk], writes=[("hT", tt // 4)])
                else:
                    P.op("dve", lambda e, dst=dst, src_=src_: e.tensor_copy(out=dst, in_=src_), reads=[pk], writes=[("hT", tt // 4)])
                gi += 1
        barrier(P)


def proj_fm(P, PR, hT, w_chunk, wkey, tb, evac):
    KC = D // 128
    ps, pk = PR.next()
    for c in range(KC):
        P.op("pe", lambda e, c=c, ps=ps: e.matmul(ps[:], w_chunk[:, c, :], hT[:, c, tb * 512:(tb + 1) * 512],
                                                  start=(c == 0), stop=(c == KC - 1)),
             reads=[wkey, ("hT", tb)], writes=[pk], sig=(c == KC - 1))
    evac(ps, pk)


def emit_outproj(P, nc, PR, yT, x, w, gain, out, ss_src, ss_g, mid_cc=None):
    KC = D // 128
    NB = 256
    TP = 1024
    NCL = 2048
    NTT = TP // 128
    if True:
        yT_sb = P.sb("yT_sb", [128, KC, TP], BF16)
        w_sb = [P.sb("w_sb%d" % i, [128, KC, NB], BF16) for i in range(2)]
        o_sb = [P.sb("o_sb%d" % i, [128, NCL], F32) for i in range(NTT)]
        x_sb = [P.sb("x_sb%d" % i, [128, NCL], F32) for i in range(2)]
        g_sb = P.sb("g_sb", [128, NCL], F32)
        junk = P.sb("junk", [128, NCL], BF16)
        ssq = P.sb("ssq", [128, NTT], F32)
        ssg = P.sb("ssg", [128, 2, NTT], F32)
        st = P.sb("ost", [128, 4 * NTT], F32)
        yT_v = yT.rearrange("(c p) t -> p c t", p=128)
        w_v = w.rearrange("(c p) n -> p c n", p=128)
        P.dma("sp", g_sb[:], gain, writes=["g"])
        wi = 0
        for p_ in range(S // TP):
            for r_ in range(2):
                for k_ in range(4):
                    c0_, s0_ = r_ * 16 + k_ * 4, k_ * 8 + r_ * 4
                    P.dma("sp", yT_sb[:, c0_:c0_ + 4, :], yT_v[:, s0_:s0_ + 4, p_ * TP:(p_ + 1) * TP], reads=["yT_g"], writes=["yT"])
            for nb in range(NCL // NB):
                wb = wi % 2
                wi += 1
                for c0 in range(0, KC, 8):
                    P.dma("pool", w_sb[wb][:, c0:c0 + 8, :], w_v[:, c0:c0 + 8, nb * NB:(nb + 1) * NB],
                          writes=[("w", wb, c0)])
                for tt in range(NTT):
                    ps, pk = PR.next()
                    for c in range(KC):
                        P.op("pe", lambda e, c=c, tt=tt, ps=ps, wb=wb: e.matmul(
                            ps[:, 0:NB], yT_sb[:, c, tt * 128:(tt + 1) * 128], w_sb[wb][:, c, :],
                            start=(c == 0), stop=(c == KC - 1)),
                            reads=["yT", ("w", wb, (c // 8) * 8)], writes=[pk], sig=(c == KC - 1))
                    P.op("act", lambda e, tt=tt, ps=ps, nb=nb: e.copy(
                        out=o_sb[tt][:, nb * NB:(nb + 1) * NB], in_=ps[:, 0:NB]),
                        reads=[pk], writes=[("o", tt)])
            if p_ == 1 and mid_cc is not None:
                mid_cc()
            for tt in range(NTT):
                P.op("act", lambda e, tt=tt: e.activation(out=junk[:], in_=o_sb[tt][:], func=AF.Square,
                                                         accum_out=ssq[:, tt:tt + 1]),
                     reads=[("o", tt)], writes=["junk", "ssq"])
            P.dma("sp", ss_src, ssq[:], reads=["ssq"], writes=["ss_src"])
            P.cc("AllGather", PAIRS, ss_src, ss_g, reads=["ss_src"], writes=["ss_g"])
            P.dma("sp", ssg[:], ss_g.rearrange("(r p) c -> p r c", p=128), reads=["ss_g"], writes=["ssg"])
            P.op("dve", lambda e: e.tensor_tensor(out=st[:, 0:NTT], in0=ssg[:, 0, :], in1=ssg[:, 1, :], op=ALU.add),
                 reads=["ssg"], writes=["st0"])
            P.op("dve", lambda e: e.tensor_scalar(out=st[:, NTT:2 * NTT], in0=st[:, 0:NTT], scalar1=1.0 / D, scalar2=EPS,
                                                   op0=ALU.mult, op1=ALU.add), reads=["st0"], writes=["st1"])
            P.op("act", lambda e: e.activation(out=st[:, 2 * NTT:3 * NTT], in_=st[:, NTT:2 * NTT], func=AF.Sqrt),
                 reads=["st1"], writes=["st2"])
            P.op("dve", lambda e: e.reciprocal(out=st[:, 3 * NTT:4 * NTT], in_=st[:, 2 * NTT:3 * NTT]), reads=["st2"], writes=["st3"])
            for tt in range(NTT):
                t0 = p_ * TP + tt * 128
                xb, xk = x_sb[tt % 2], ("x", tt % 2)
                P.dma("sp", xb[:], x[t0:t0 + 128, :], reads=["x_my"], writes=[xk])
                P.op("dve", lambda e, tt=tt: e.scalar_tensor_tensor(
                    out=o_sb[tt][:], in0=o_sb[tt][:], scalar=st[:, 3 * NTT + tt:3 * NTT + tt + 1], in1=g_sb[:],
                    op0=ALU.mult, op1=ALU.mult), reads=[("o", tt), "st3", "g"], writes=[("o", tt)])
                P.op("dve", lambda e, tt=tt, xb=xb: e.tensor_tensor(out=o_sb[tt][:], in0=o_sb[tt][:], in1=xb[:], op=ALU.add),
                     reads=[("o", tt), xk], writes=[("o", tt)])
                P.dma("sp", out[t0:t0 + 128, :], o_sb[tt][:], reads=[("o", tt)], writes=[("x_out", p_ * NTT + tt)])


def emit_rg(P, nc, PR, gT_sb, gcol, ident, io):
    x = io["x"]
    gT = io["gT"]
    ident_d = io["ident"]
    wx = io["wx"]
    wz = io["wz"]
    cw_d = io["cw"]
    cb_d = io["cb"]
    lam_d = io["lam"]
    gb_d = io["gb"]
    gw_d = io["gw"]
    yT = io["yT"]
    KC = D // 128
    if True:
        hT = P.sb("hT", [128, KC, S], BF16)
        cw = P.sb("cw_sb", [128, 64], F32)
        cb = P.sb("cb_sb", [128, 16], F32)
        lam = P.sb("lam_sb", [128, 16], F32)
        c1 = P.sb("c1_sb", [128, 16], F32)
        gb = P.sb("gb_sb", [128, 32], F32)
        P.dma("sp", cw[:], cw_d, writes=["cw"])
        P.dma("sp", cb[:], cb_d, writes=["cb"])
        P.dma("sp", lam[:], lam_d, writes=["lam"])
        P.dma("sp", gb[:], gb_d, writes=["gb"])
        phase_hT(P, nc, PR, x, io["gR"], ident, hT, xsplit=io["xsplit"])
        P.op("act", lambda e: e.activation(out=c1[:], in_=lam[:], func=AF.Exp, scale=-1.0), reads=["lam"], writes=["c1"])
        P.op("act", lambda e: e.activation(out=c1[:], in_=c1[:], func=AF.Ln, bias=1.0), reads=["c1"], writes=["c1"])
        P.op("dve", lambda e: e.tensor_scalar(out=c1[:], in0=c1[:], scalar1=-8.0, scalar2=None, op0=ALU.mult),
             reads=["c1"], writes=["c1"])
        wxs = [P.sb("wxs%d" % i, [128, KC, 128], BF16) for i in range(2)]
        wzs = [P.sb("wzs%d" % i, [128, KC, 128], BF16) for i in range(2)]
        gws = P.sb("gws", [128, 2, 2, 256], BF16)
        xraw = [P.sb("xraw%d" % i, [128, 3 + 512], F32) for i in range(2)]
        xc = [P.sb("xc%d" % i, [128, 512], F32) for i in range(2)]
        xcb = [P.sb("xcb%d" % i, [128, 512], BF16) for i in range(2)]
        gi = [P.sb("gi%d" % i, [128, 512], F32) for i in range(2)]
        ga = [P.sb("ga%d" % i, [128, 512], F32) for i in range(2)]
        gm = [P.sb("gm%d" % i, [128, 512], F32) for i in range(2)]
        hs = [[P.sb("hs%d_%d" % (i, k), [128, 512], F32) for k in range(2)] for i in range(2)]
        zs = [P.sb("zs%d" % i, [128, 512], F32) for i in range(2)]
        ys = [P.sb("ys%d" % i, [128, 512], BF16) for i in range(2)]
        wx_v = wx.rearrange("(c p) n -> p c n", p=128)
        wz_v = wz.rearrange("(c p) n -> p c n", p=128)
        for blk in range(8):
            for j in range(2):
                col0 = blk * 256 + j * 128
                for c0 in range(0, KC, 8):
                    P.dma("pool", wxs[j][:, c0:c0 + 8, :], wx_v[:, c0:c0 + 8, col0:col0 + 128], writes=[("wx", j)])
                for c0 in range(0, KC, 8):
                    P.dma("pool", wzs[j][:, c0:c0 + 8, :], wz_v[:, c0:c0 + 8, col0:col0 + 128], writes=[("wz", j)])
            for k in range(2):
                P.dma("pool", gws[:, k], gw_d[k, blk].rearrange("(jc p) e -> p jc e", p=128), writes=["gw"])
            for j in range(2):
                P.op("dve", lambda e, j=j: e.memset(xraw[j][:, 0:3], 0.0), writes=[("xraw", j)])
            for tb in range(4):
                for j in range(2):
                    ch = blk * 2 + j

                    def ev_x(ps, pk, j=j):
                        P.op("act", lambda e: e.copy(out=xraw[j][:, 3:515], in_=ps[:]), reads=[pk], writes=[("xraw", j)])
                    proj_fm(P, PR, hT, wxs[j], ("wx", j), tb, ev_x)
                    P.op("dve", lambda e, j=j, ch=ch: e.tensor_scalar(
                        out=xc[j][:], in0=xraw[j][:, 0:512], scalar1=cw[:, ch * 4:ch * 4 + 1], scalar2=cb[:, ch:ch + 1],
                        op0=ALU.mult, op1=ALU.add), reads=[("xraw", j), "cw", "cb"], writes=[("xc", j)])
                    for k in range(1, 4):
                        P.op("dve", lambda e, j=j, ch=ch, k=k: e.scalar_tensor_tensor(
                            out=xc[j][:], in0=xraw[j][:, k:k + 512], scalar=cw[:, ch * 4 + k:ch * 4 + k + 1],
                            in1=xc[j][:], op0=ALU.mult, op1=ALU.add), reads=[("xraw", j), ("xc", j), "cw"],
                            writes=[("xc", j)])
                    P.op("act", lambda e, j=j: e.copy(out=xcb[j][:], in_=xc[j][:]), reads=[("xc", j)], writes=[("xcb", j)])
                    P.op("dve", lambda e, j=j: e.tensor_copy(out=xraw[j][:, 0:3], in_=xraw[j][:, 512:515]),
                         reads=[("xraw", j)], writes=[("xraw", j)])
                for je in range(2):
                    def ev_z(ps, pk, je=je):
                        P.op("act", lambda e: e.activation(out=zs[je][:], in_=ps[:], func=AF.Silu), reads=[pk],
                             writes=[("zs", je)])
                    proj_fm(P, PR, hT, wzs[je], ("wz", je), tb, ev_z)
                for je in range(2):
                    ch = blk * 2 + je
                    for k in range(2):
                        ps, pk = PR.next()
                        for jc in range(2):
                            P.op("pe", lambda e, k=k, jc=jc, je=je, ps=ps: e.matmul(
                                ps[:], gws[:, k, jc, je * 128:(je + 1) * 128], xcb[jc][:], start=(jc == 0), stop=(jc == 1)),
                                reads=["gw", ("xcb", jc)], writes=[pk])
                        dst = gi[je] if k == 0 else ga[je]
                        dk = ("gi", je) if k == 0 else ("ga", je)
                        P.op("act", lambda e, ps=ps, dst=dst, k=k, ch=ch: e.activation(
                            out=dst[:], in_=ps[:], func=AF.Sigmoid, bias=gb[:, k * 16 + ch:k * 16 + ch + 1]),
                            reads=[pk, "gb"], writes=[dk])
                    P.op("act", lambda e, je=je, ch=ch: e.activation(out=ga[je][:], in_=ga[je][:], func=AF.Exp,
                                                                   scale=c1[:, ch:ch + 1]),
                         reads=[("ga", je), "c1"], writes=[("ga", je)])
                    P.op("dve", lambda e, je=je: e.tensor_tensor(out=gm[je][:], in0=ga[je][:], in1=ga[je][:], op=ALU.mult),
                         reads=[("ga", je)], writes=[("gm", je)])
                    P.op("act", lambda e, je=je: e.activation(out=gm[je][:], in_=gm[je][:], func=AF.Sqrt, scale=-1.0, bias=1.0),
                         reads=[("gm", je)], writes=[("gm", je)])
                    if tb == 0:
                        P.op("dve", lambda e, je=je: e.memset(gm[je][:, 0:1], 1.0), writes=[("gm", je)])
                    P.op("dve", lambda e, je=je: e.tensor_tensor(out=gi[je][:], in0=gi[je][:], in1=gm[je][:], op=ALU.mult),
                         reads=[("gi", je), ("gm", je)], writes=[("gi", je)])
                    P.op("dve", lambda e, je=je: e.tensor_tensor(out=gi[je][:], in0=gi[je][:], in1=xc[je][:], op=ALU.mult),
                         reads=[("gi", je), ("xc", je)], writes=[("gi", je)])
                    cur = hs[je][tb % 2]
                    prev = hs[je][(tb + 1) % 2]
                    init = 0.0 if tb == 0 else prev[:, 511:512]
                    P.op("dve", lambda e, je=je, cur=cur, init=init: e.tensor_tensor_scan(
                        out=cur[:], data0=ga[je][:], data1=gi[je][:], initial=init, op0=ALU.mult, op1=ALU.add),
                        reads=[("ga", je), ("gi", je), ("hs", je, (tb + 1) % 2)], writes=[("hs", je, tb % 2)])

                    P.op("dve", lambda e, je=je, cur=cur: e.tensor_tensor(out=ys[je][:], in0=cur[:], in1=zs[je][:], op=ALU.mult),
                         reads=[("hs", je, tb % 2), ("zs", je)], writes=[("ys", je)])
                    P.dma("sp", yT[ch * 128:(ch + 1) * 128, tb * 512:(tb + 1) * 512], ys[je][:], reads=[("ys", je)],
                          writes=[("ysrc", ch, tb)])
            io["ydone"](blk * 2)
            io["ydone"](blk * 2 + 1)


def emit_hg(P, nc, PR, gT_sb, gcol, ident, io, layer=2):
    x = io["x"]
    gT = io["gT"]
    ident_d = io["ident"]
    ones_d = io["ones"]
    rmask_d = io["rmask"]
    bdmask_d = io["bdmask"]
    lbl_d = io["lbl"]
    ng_d = io["ng"]
    yT = io["yT"]
    wd = [io[n] for n in ("wq", "wf", "wv", "wg")]
    KC = D // 128
    if True:
        hT = P.sb("hT", [128, KC, S], BF16)
        ones = P.sb("ones_sb", [128, 128], F32)
        rmask = P.sb("rmask_sb", [128, 512], F32)
        bdmask = P.sb("bdmask_sb", [128, 128], F32)
        lbl = P.sb("lbl_sb", [128, 64], F32)
        lbe = P.sb("lbe_sb", [128, 64], F32)
        lb = P.sb("lb_sb", [128, 16], F32)
        oml = P.sb("oml_sb", [128, 16], F32)
        lsum = P.sb("lsum_sb", [128, 16], F32)
        ng = P.sb("ng_sb", [128, 1], F32)
        for t, d, k in ((ones, ones_d, "ones"), (rmask, rmask_d, "rmask"),
                        (bdmask, bdmask_d, "bdmask"), (lbl, lbl_d, "lbl"), (ng, ng_d, "ng")):
            P.dma("sp", t[:], d, writes=[k])
        phase_hT(P, nc, PR, x, io["gR"], ident, hT, xsplit=io["xsplit"])
        P.op("act", lambda e: e.activation(out=lbe[:], in_=lbl[:], func=AF.Exp), reads=["lbl"], writes=["lbe"])
        P.op("dve", lambda e: e.tensor_tensor(out=lsum[:], in0=lbe[:, 0:16], in1=lbe[:, 16:32], op=ALU.add),
             reads=["lbe"], writes=["lsum"])
        P.op("dve", lambda e: e.tensor_tensor(out=lsum[:], in0=lsum[:], in1=lbe[:, 32:48], op=ALU.add),
             reads=["lbe", "lsum"], writes=["lsum"])
        P.op("dve", lambda e: e.tensor_tensor(out=lsum[:], in0=lsum[:], in1=lbe[:, 48:64], op=ALU.add),
             reads=["lbe", "lsum"], writes=["lsum"])
        P.op("dve", lambda e: e.reciprocal(out=lsum[:], in_=lsum[:]), reads=["lsum"], writes=["lsum"])
        P.op("dve", lambda e: e.memset(lb[:], 0.0), writes=["lb"])
        for l in range(1, layer + 1):
            P.op("dve", lambda e, l=l: e.tensor_tensor(out=lb[:], in0=lb[:], in1=lbe[:, l * 16:(l + 1) * 16], op=ALU.add),
                 reads=["lb", "lbe"], writes=["lb"])
        P.op("dve", lambda e: e.tensor_tensor(out=lb[:], in0=lb[:], in1=lsum[:], op=ALU.mult), reads=["lb", "lsum"], writes=["lb"])
        P.op("dve", lambda e: e.tensor_scalar(out=oml[:], in0=lb[:], scalar1=-1.0, scalar2=1.0, op0=ALU.mult, op1=ALU.add),
             reads=["lb"], writes=["oml"])
        ws = [P.sb("hw%d" % i, [128, KC, 128], BF16) for i in range(4)]
        wv = [w.rearrange("(c p) n -> p c n", p=128) for w in wd]
        f32b = lambda n: P.sb(n, [128, 512], F32)
        qs, ff, logf, kk, bb, eb, enb, kef, gs, osb, sq, rstd, tmpo = [f32b("hgb%d" % i) for i in range(13)]
        qe = P.sb("qe", [128, 512], BF16)
        kebf = P.sb("kebf", [128, 512], BF16)
        ys = P.sb("hys", [128, 512], BF16)
        vtm = [P.sb("vtm%d" % i, [128, 128], BF16) for i in range(4)]
        ketm = [P.sb("ketm%d" % i, [128, 128], BF16) for i in range(2)]
        attT = [P.sb("attT%d" % i, [128, 128], BF16) for i in range(2)]
        Sf = P.sb("Sf", [128, 128], F32)
        Stmp = P.sb("Stmp", [128, 128], F32)
        Sb = P.sb("Sb", [128, 128], BF16)
        for hh in range(16):
            for i in range(4):
                for c0 in range(0, KC, 8):
                    P.dma("pool", ws[i][:, c0:c0 + 8, :], wv[i][:, c0:c0 + 8, hh * 128:(hh + 1) * 128], writes=[("hw", i)])
            P.op("dve", lambda e: e.memset(Sf[:], 0.0), writes=["Sf"])
            P.op("dve", lambda e: e.memset(Sb[:], 0.0), writes=["Sb"])
            for tb in range(4):
                def ev_q(ps, pk):
                    P.op("act", lambda e: e.activation(out=qs[:], in_=ps[:], func=AF.Silu), reads=[pk], writes=["qs"])
                proj_fm(P, PR, hT, ws[0], ("hw", 0), tb, ev_q)

                def ev_f(ps, pk):
                    P.op("act", lambda e: e.activation(out=ff[:], in_=ps[:], func=AF.Sigmoid), reads=[pk], writes=["ff"])
                proj_fm(P, PR, hT, ws[1], ("hw", 1), tb, ev_f)

                def ev_g(ps, pk):
                    P.op("act", lambda e: e.activation(out=gs[:], in_=ps[:], func=AF.Silu), reads=[pk], writes=["gs"])
                proj_fm(P, PR, hT, ws[3], ("hw", 3), tb, ev_g)
                for tt in range(4):
                    ps, pk = PR.next()
                    for c in range(KC):
                        P.op("pe", lambda e, c=c, ps=ps, tt=tt: e.matmul(
                            ps[:, 0:128], hT[:, c, tb * 512 + tt * 128:tb * 512 + (tt + 1) * 128], ws[2][:, c, :],
                            start=(c == 0), stop=(c == KC - 1)), reads=[("hw", 2), ("hT", tb)], writes=[pk], sig=(c == KC - 1))
                    P.op("act", lambda e, ps=ps, tt=tt: e.copy(out=vtm[tt][:], in_=ps[:, 0:128]), reads=[pk], writes=[("vtm", tt)])
                P.op("dve", lambda e: e.tensor_scalar(out=ff[:], in0=ff[:], scalar1=oml[:, hh:hh + 1], scalar2=lb[:, hh:hh + 1],
                                                       op0=ALU.mult, op1=ALU.add), reads=["ff", "oml", "lb"], writes=["ff"])
                P.op("act", lambda e: e.activation(out=logf[:], in_=ff[:], func=AF.Ln), reads=["ff"], writes=["logf"])
                P.op("dve", lambda e: e.tensor_scalar(out=kk[:], in0=ff[:], scalar1=-1.0, scalar2=1.0, op0=ALU.mult, op1=ALU.add),
                     reads=["ff"], writes=["kk"])
                P.op("dve", lambda e: e.tensor_tensor_scan(out=bb[:], data0=rmask[:], data1=logf[:], initial=0.0,
                                                            op0=ALU.mult, op1=ALU.add), reads=["rmask", "logf"], writes=["bb"])
                P.op("act", lambda e: e.activation(out=eb[:], in_=bb[:], func=AF.Exp), reads=["bb"], writes=["eb"])
                P.op("act", lambda e: e.activation(out=enb[:], in_=bb[:], func=AF.Exp, scale=-1.0), reads=["bb"], writes=["enb"])
                P.op("dve", lambda e: e.tensor_tensor(out=qe[:], in0=qs[:], in1=eb[:], op=ALU.mult), reads=["qs", "eb"], writes=["qe"])
                P.op("dve", lambda e: e.tensor_tensor(out=kef[:], in0=kk[:], in1=enb[:], op=ALU.mult), reads=["kk", "enb"], writes=["kef"])
                P.op("act", lambda e: e.copy(out=kebf[:], in_=kef[:]), reads=["kef"], writes=["kebf"])
                for tt in range(4):
                    sl = slice(tt * 128, (tt + 1) * 128)
                    kt = ketm[tt % 2]
                    at = attT[tt % 2]
                    ktk = ("ketm", tt % 2)
                    atk = ("attT", tt % 2)
                    ps, pk = PR.next()
                    P.op("pe", lambda e, ps=ps, sl=sl: e.transpose(out=ps[:, 0:128], in_=kef[:, sl], identity=ident[:]),
                         reads=["kef", "ident"], writes=[pk])
                    P.op("act", lambda e, ps=ps, kt=kt: e.copy(out=kt[:], in_=ps[:, 0:128]), reads=[pk], writes=[ktk])
                    ps2, pk2 = PR.next()
                    P.op("pe", lambda e, ps2=ps2, sl=sl: e.matmul(ps2[:, 0:128], kebf[:, sl], qe[:, sl], start=True, stop=True),
                         reads=["kebf", "qe"], writes=[pk2])
                    P.op("dve", lambda e, ps2=ps2, at=at: e.tensor_tensor(out=at[:], in0=ps2[:, 0:128], in1=bdmask[:], op=ALU.mult),
                         reads=[pk2, "bdmask"], writes=[atk])
                    pso, pko = PR.next()
                    P.op("pe", lambda e, pso=pso, at=at, tt=tt: e.matmul(pso[:, 0:128], vtm[tt][:], at[:], start=True, stop=False),
                         reads=[("vtm", tt), atk], writes=[pko])
                    for cc in range(2):
                        csl = slice(tt * 128 + cc * 64, tt * 128 + (cc + 1) * 64)
                        rows = slice(cc * 64, (cc + 1) * 64)
                        P.op("pe", lambda e, pso=pso, cc=cc, csl=csl: e.matmul(
                            pso[:, cc * 64:(cc + 1) * 64], Sb[:], qe[:, csl], start=False, stop=(cc == 1)),
                            reads=["Sb", "qe"], writes=[pko])
                        pss_, pks = PR.next()
                        P.op("pe", lambda e, pss_=pss_, kt=kt, rows=rows, tt=tt: e.matmul(
                            pss_[:, 0:128], kt[rows, :], vtm[tt][rows, :], start=True, stop=True),
                            reads=[ktk, ("vtm", tt)], writes=[pks])
                        ecol = tt * 128 + (cc + 1) * 64 - 1
                        P.op("dve", lambda e, ecol=ecol: e.tensor_scalar(out=Stmp[:], in0=Sf[:], scalar1=eb[:, ecol:ecol + 1],
                                                                        scalar2=None, op0=ALU.mult),
                             reads=["Sf", "eb"], writes=["Stmp"])
                        P.op("dve", lambda e, ecol=ecol, pss_=pss_: e.scalar_tensor_tensor(
                            out=Sf[:], in0=pss_[:, 0:128], scalar=eb[:, ecol:ecol + 1], in1=Stmp[:], op0=ALU.mult, op1=ALU.add),
                            reads=[pks, "eb", "Stmp"], writes=["Sf"])
                        P.op("act", lambda e: e.copy(out=Sb[:], in_=Sf[:]), reads=["Sf"], writes=["Sb"])
                    P.op("act", lambda e, pso=pso, sl=sl: e.copy(out=osb[:, sl], in_=pso[:, 0:128]), reads=[pko], writes=["osb"])
                P.op("act", lambda e: e.activation(out=sq[:], in_=osb[:], func=AF.Square), reads=["osb"], writes=["sq"])
                psn, pkn = PR.next()
                P.op("pe", lambda e, psn=psn: e.matmul(psn[:], ones[:], sq[:], start=True, stop=True), reads=["ones", "sq"], writes=[pkn])
                P.op("act", lambda e, psn=psn: e.activation(out=rstd[:], in_=psn[:], func=AF.Ln, scale=1.0 / 128, bias=EPS),
                     reads=[pkn], writes=["rstd"])
                P.op("act", lambda e: e.activation(out=rstd[:], in_=rstd[:], func=AF.Exp, scale=-0.5), reads=["rstd"], writes=["rstd"])
                P.op("dve", lambda e: e.scalar_tensor_tensor(out=tmpo[:], in0=osb[:], scalar=ng[:, 0:1], in1=rstd[:],
                                                              op0=ALU.mult, op1=ALU.mult), reads=["osb", "ng", "rstd"], writes=["tmpo"])
                P.op("dve", lambda e: e.tensor_tensor(out=ys[:], in0=tmpo[:], in1=gs[:], op=ALU.mult), reads=["tmpo", "gs"], writes=["ys"])
                P.dma("sp", yT[hh * 128:(hh + 1) * 128, tb * 512:(tb + 1) * 512], ys[:], reads=["ys"], writes=[("ysrc", hh, tb)])
            io["ydone"](hh)


NSA_SCALE = 128 ** -0.5
NEGM = -1.0e7
NREL = 19


def emit_nsa(P, nc, PR, psUl, psDl, gT_sb, gcol, ident, io):
    x = io["x"]
    gT = io["gT"]
    ident_d = io["ident"]
    wq = io["wq"]
    wz = io["wz"]
    wkv = io["wkv"]
    wgl = io["wgl"]
    peT_d = io["peT"]
    w1_d = io["w1"]
    w2_d = io["w2"]
    slope_d = io["slope"]
    cst_d = io["cst"]
    R_d = io["Rt"]
    E_d = io["E"]
    selb0_d = io["selb0"]
    Mmat_d = io["Mmat"]
    A_d = io["At"]
    Badd_d = io["Bt"]
    SelR_d = io["SelR"]
    yT = io["yT"]
    qT_s = io["qT_s"]
    zT_s = io["zT_s"]
    kvT_s = io["kvT_s"]
    vtm_s = io["vtm_s"]
    tcmp_s = io["tcmp_s"]
    KC = D // 128
    if True:
        GT = P.sb("GT", [48, S], F32)
        with contextlib.ExitStack() as es1:
            hT = es1.enter_context(nc.sbuf_tensor(P.pre + "hT", [128, KC, S], BF16))
            phase_hT(P, nc, PR, x, io["gR"], ident, hT, xsplit=io["xsplit"])
            wsb = [es1.enter_context(nc.sbuf_tensor(P.pre + "nw%d" % i, [128, KC, 128], BF16)) for i in range(2)]
            wglsb = es1.enter_context(nc.sbuf_tensor(P.pre + "wglsb", [128, KC, 48], BF16))
            stg = [es1.enter_context(nc.sbuf_tensor(P.pre + "stg%d" % i, [128, 512], BF16)) for i in range(2)]
            si = [0]
            wi = [0]

            def load_w(src_v, col0):
                b = wi[0] % 2
                wi[0] += 1
                for c0 in range(0, KC, 8):
                    P.dma("pool", wsb[b][:, c0:c0 + 8, :], src_v[:, c0:c0 + 8, col0:col0 + 128], writes=[("nw", b)])
                return wsb[b], ("nw", b)

            def fm_chunk(src_v, col0, dst, func, scale):
                w, wk = load_w(src_v, col0)
                for tb in range(4):
                    def ev(ps, pk, tb=tb):
                        b = si[0] % 2
                        si[0] += 1
                        P.op("act", lambda e: e.activation(out=stg[b][:], in_=ps[:], func=func, scale=scale),
                             reads=[pk], writes=[("stg", b)])
                        P.dma("sp", dst[:, tb * 512:(tb + 1) * 512], stg[b][:], reads=[("stg", b)], writes=["scratch"])
                    proj_fm(P, PR, hT, w, wk, tb, ev)

            wq_v = wq.rearrange("(c p) n -> p c n", p=128)
            wz_v = wz.rearrange("(c p) n -> p c n", p=128)
            wkv_v = wkv.rearrange("(c p) n -> p c n", p=128)
            for hl in range(16):
                fm_chunk(wq_v, hl * 128, qT_s[hl], AF.Copy, NSA_SCALE)
                fm_chunk(wz_v, hl * 128, zT_s[hl], AF.Silu, 1.0)
            for gl in range(2):
                for n, i in enumerate((0, 1, 2, 4)):
                    fm_chunk(wkv_v, i * 256 + gl * 128, kvT_s[gl, n], AF.Copy, 1.0)
                for n, i in enumerate((3, 5)):
                    w, wk = load_w(wkv_v, i * 256 + gl * 128)
                    for tt in range(16):
                        ps, pk = PR.next()
                        for c in range(KC):
                            P.op("pe", lambda e, c=c, ps=ps, tt=tt, w=w: e.matmul(
                                ps[:, 0:128], hT[:, c, tt * 128:(tt + 1) * 128], w[:, c, :],
                                start=(c == 0), stop=(c == KC - 1)), reads=[wk, ("hT", tt // 4)], writes=[pk], sig=(c == KC - 1))
                        b = si[0] % 2
                        si[0] += 1
                        P.op("act", lambda e, ps=ps, b=b: e.copy(out=stg[b][:, 0:128], in_=ps[:, 0:128]), reads=[pk],
                             writes=[("stg", b)])
                        P.dma("sp", vtm_s[gl, n, tt * 128:(tt + 1) * 128, :], stg[b][:, 0:128], reads=[("stg", b)],
                              writes=["scratch"])
            for c0 in range(0, KC, 8):
                P.dma("pool", wglsb[:, c0:c0 + 8, :], wgl.rearrange("(c p) n -> p c n", p=128)[:, c0:c0 + 8, :], writes=["wgl"])
            for tb in range(4):
                ps, pk = PR.next()
                for c in range(KC):
                    P.op("pe", lambda e, c=c, ps=ps, tb=tb: e.matmul(ps[0:48, :], wglsb[:, c, :], hT[:, c, tb * 512:(tb + 1) * 512],
                                                                    start=(c == 0), stop=(c == KC - 1)),
                         reads=["wgl", ("hT", tb)], writes=[pk])
                P.op("act", lambda e, ps=ps, tb=tb: e.activation(out=GT[:, tb * 512:(tb + 1) * 512], in_=ps[0:48, :], func=AF.Sigmoid),
                     reads=[pk], writes=["GT"])
            barrier(P)
        Rt = P.sb("Rt_sb", [128, 13, 512], F32)
        SelR = P.sb("SelR_sb", [48, 48 * 128], F32)
        slope = P.sb("slope_sb", [128, 16], F32)
        cst = P.sb("cst_sb", [128, 16 * NREL], F32)
        E = P.sb("E_sb", [32, 2048], BF16)
        selbT = P.sb("selbT", [32, 2048], BF16)
        Mmat = P.sb("Mmat_sb", [128, 32], F32)
        At = P.sb("At_sb", [128, 256], F32)
        Bt = P.sb("Bt_sb", [128, 256], F32)
        onesb = P.sb("onesb", [128, 128], BF16)
        for k in range(13):
            P.dma("sp", Rt[:, k, :], R_d[k], writes=["Rt"])
        for t, d, k in ((SelR, SelR_d, "SelR"), (slope, slope_d, "slope"), (cst, cst_d, "cst"), (E, E_d, "E"),
                        (Mmat, Mmat_d, "Mmat"), (At, A_d, "At"), (Bt, Badd_d, "Bt")):
            P.dma("sp", t[:], d, writes=[k])
        P.op("dve", lambda e: e.memset(onesb[:], 1.0), writes=["onesb"])
        kvT = P.sb("kvT", [128, 4, S], BF16)
        vtm = P.sb("vtm", [128, 2, 16, 128], BF16)
        w1sb = P.sb("w1sb", [128, 32, 128], BF16)
        w2sb = P.sb("w2sb", [128, 128], BF16)
        peT = P.sb("peT_sb", [128, 32], F32)
        peTb = P.sb("peTb", [128, 32], BF16)
        cb_ = P.sb("cmpb", [128, 1], F32)
        xg = P.sb("xg", [128, 128], F32)
        xg2 = P.sb("xg2", [128, 128], F32)
        hidT = P.sb("hidT", [128, 128], BF16)
        KcT = P.sb("KcT", [128, 128], BF16)
        Vc = P.sb("Vc", [128, 128], BF16)
        qTbs = [P.sb("qTb%d" % i, [128, S], BF16) for i in range(2)]
        zTbs = [P.sb("zTb%d" % i, [128, S], BF16) for i in range(2)]
        qc = [0]
        tmp = [P.sb("atmp%d" % i, [128, 512], F32) for i in range(4)]
        Pfs = [P.sb("Pf%d" % i, [128, 512], F32) for i in range(2)]
        pfi = [0]
        Pb = [P.sb("Pb%d" % i, [128, 512], BF16) for i in range(4)]
        rden = P.sb("rden", [128, 512], F32)
        Ff = P.sb("Ff", [128, 512], F32)
        Tt = P.sb("Tt", [128, 512], F32)
        acc = P.sb("acc", [128, 512], F32)
        pn = P.sb("pn", [128, 512], F32)
        ys = P.sb("nys", [128, 512], BF16)
        pslc = P.sb("pslc", [128, 256], F32)
        sc = P.sb("sc", [128, 32], F32)
        sc2 = P.sb("sc2", [128, 32], F32)
        m8 = P.sb("m8", [128, 16], F32)
        selb = P.sb("selb", [128, 32], F32)
        ti_ = [0]
        bi_ = [0]
        pend = []

        def drain(keep=0):
            while len(pend) > keep:
                pend.pop(0)()

        def branch_step(kT_ap, kkey, v_ap, vkey, qsl, hl, rvar, cidx, nk, extra=None, first=False, last=False, want_f32=False):
            ps, pk = PR.next()
            qTb, qkey = qTbs[qc[0]], ("qTb", qc[0])
            P.op("pe", lambda e: e.matmul(ps[0:nk, :], kT_ap, qTb[:, qsl], start=True, stop=(extra is None)),
                 reads=[kkey, qkey], writes=[pk])
            if extra is not None:
                P.op("pe", lambda e: e.matmul(ps[0:nk, :], extra, selbT[:, qsl], start=False, stop=True),
                     reads=["E", "selbT"], writes=[pk])
            i = ti_[0] % 4
            ti_[0] += 1
            psU, psD = psUl[bi_[0] % 2], psDl[bi_[0] % 2]
            uk, dk = ("psU", bi_[0] % 2), ("psD", bi_[0] % 2)
            P.op("dve", lambda e: e.scalar_tensor_tensor(out=tmp[i][0:nk, :], in0=Rt[0:nk, rvar, :], scalar=slope[0:nk, hl:hl + 1],
                                                          in1=ps[0:nk, :], op0=ALU.mult, op1=ALU.add),
                 reads=["Rt", "slope", pk], writes=[("atmp", i)])
            if want_f32:
                Pf, pfk = Pfs[pfi[0]], ("Pf", pfi[0])
                P.op("act", lambda e: e.activation(out=Pf[0:nk, :], in_=tmp[i][0:nk, :], func=AF.Exp), reads=[("atmp", i)], writes=[pfk])
                P.op("act", lambda e: e.copy(out=Pb[i][0:nk, :], in_=Pf[0:nk, :]), reads=[pfk], writes=[("Pb", i)])
            elif cidx is None:
                P.op("act", lambda e: e.activation(out=Pb[i][0:nk, :], in_=tmp[i][0:nk, :], func=AF.Exp), reads=[("atmp", i)],
                     writes=[("Pb", i)])
            else:
                P.op("act", lambda e: e.activation(out=Pb[i][0:nk, :], in_=tmp[i][0:nk, :], func=AF.Exp,
                                                   bias=cst[0:nk, cidx:cidx + 1]), reads=[("atmp", i), "cst"], writes=[("Pb", i)])

            def stage_b():
                P.op("pe", lambda e: e.matmul(psU[:], v_ap, Pb[i][0:nk, :], start=first, stop=last), reads=[vkey, ("Pb", i)], writes=[uk])
                P.op("pe", lambda e: e.matmul(psD[:], onesb[0:nk, :], Pb[i][0:nk, :], start=first, stop=last),
                     reads=["onesb", ("Pb", i)], writes=[dk])
            pend.append(stage_b)

        def finish_branch(br, hl, qsl, mode):
            psU, psD = psUl[bi_[0] % 2], psDl[bi_[0] % 2]
            uk, dk = ("psU", bi_[0] % 2), ("psD", bi_[0] % 2)
            bi_[0] += 1

            def fin():
                if br == 0:
                    P.op("dve", lambda e: e.tensor_scalar(out=rden[:], in0=psD[:], scalar1=1e-18, scalar2=None, op0=ALU.max),
                         reads=[dk], writes=["rden"])
                    P.op("act", lambda e: e.activation(out=rden[:], in_=rden[:], func=AF.Ln), reads=["rden"], writes=["rden"])
                else:
                    P.op("act", lambda e: e.activation(out=rden[:], in_=psD[:], func=AF.Ln), reads=[dk], writes=["rden"])
                P.op("act", lambda e: e.activation(out=rden[:], in_=rden[:], func=AF.Exp, scale=-1.0), reads=["rden"], writes=["rden"])
                r = br * 16 + hl
                psG, gk = PR.next()
                P.op("pe", lambda e: e.matmul(psG[:], SelR[:, r * 128:(r + 1) * 128], GT[:, qsl], start=True, stop=True),
                     reads=["SelR", "GT"], writes=[gk])
                P.op("dve", lambda e: e.tensor_tensor(out=Ff[:], in0=rden[:], in1=psG[:], op=ALU.mult), reads=["rden", gk], writes=["Ff"])
                if mode == "ret":
                    P.op("dve", lambda e: e.tensor_tensor(out=Tt[:], in0=psU[:], in1=Ff[:], op=ALU.mult), reads=[uk, "Ff"], writes=["Tt"])
                elif mode == "set":
                    P.op("dve", lambda e: e.tensor_tensor(out=acc[:], in0=psU[:], in1=Ff[:], op=ALU.mult), reads=[uk, "Ff"], writes=["acc"])
                else:
                    P.op("dve", lambda e: e.tensor_tensor(out=Tt[:], in0=psU[:], in1=Ff[:], op=ALU.mult), reads=[uk, "Ff"], writes=["Tt"])
                    P.op("pool", lambda e: e.tensor_tensor(out=acc[:], in0=acc[:], in1=Tt[:], op=ALU.add), reads=["acc", "Tt"], writes=["acc"])
            pend.append(fin)

        def cidx_of(hl, m):
            return hl * NREL + (m + 15)

        for gl in range(2):
            for n in range(4):
                P.dma("sp", kvT[:, n, :], kvT_s[gl, n], reads=["scratch"], writes=["kvT"])
            for n in range(2):
                P.dma("sp", vtm[:, n], vtm_s[gl, n].rearrange("(t p) d -> p t d", p=128), reads=["scratch"], writes=["vtm"])
            P.dma("sp", selbT[:, 0:1024], selb0_d, writes=["selbT"])
            for which in range(2):
                for c0 in range(0, 32, 8):
                    P.dma("pool", w1sb[:, c0:c0 + 8, :], w1_d[which].rearrange("(l d) e -> d l e", d=128)[:, c0:c0 + 8, :], writes=["w1sb"])
                P.dma("pool", w2sb[:], w2_d[which], writes=["w2sb"])
                P.dma("sp", peT[:], peT_d[which], writes=["peT"])
                P.op("act", lambda e: e.copy(out=peTb[:], in_=peT[:]), reads=["peT"], writes=["peTb"])
                ps, pk = PR.next()
                for l in range(32):
                    P.op("pe", lambda e, l=l, ps=ps: e.matmul(ps[:, 0:127], w1sb[:, l, :], kvT[:, which, l:l + 16 * 126 + 1:16],
                                                             start=(l == 0), stop=(l == 31)), reads=["w1sb", "kvT"], writes=[pk])
                ps2, pk2 = PR.next()
                for l in range(32):
                    P.op("pe", lambda e, l=l, ps2=ps2: e.matmul(ps2[:, 0:1], w1sb[:, l, :], peTb[:, l:l + 1],
                                                               start=(l == 0), stop=(l == 31)), reads=["w1sb", "peTb"], writes=[pk2])
                P.op("act", lambda e, ps2=ps2: e.copy(out=cb_[:], in_=ps2[:, 0:1]), reads=[pk2], writes=["cmpb"])
                P.op("act", lambda e, ps=ps: e.activation(out=xg[:, 0:127], in_=ps[:, 0:127], func=AF.Identity, bias=cb_[:, 0:1]),
                     reads=[pk, "cmpb"], writes=["xg"])
                P.op("dve", lambda e: e.tensor_tensor(out=xg2[:, 0:127], in0=xg[:, 0:127], in1=xg[:, 0:127], op=ALU.mult), reads=["xg"], writes=["xg2"])
                P.op("dve", lambda e: e.tensor_scalar(out=xg2[:, 0:127], in0=xg2[:, 0:127], scalar1=0.044715, scalar2=1.0,
                                                       op0=ALU.mult, op1=ALU.add), reads=["xg2"], writes=["xg2"])
                P.op("dve", lambda e: e.tensor_tensor(out=xg2[:, 0:127], in0=xg2[:, 0:127], in1=xg[:, 0:127], op=ALU.mult),
                     reads=["xg", "xg2"], writes=["xg2"])
                P.op("act", lambda e: e.activation(out=xg2[:, 0:127], in_=xg2[:, 0:127], func=AF.Sigmoid, scale=1.5957691216),
                     reads=["xg2"], writes=["xg2"])
                P.op("dve", lambda e: e.tensor_tensor(out=hidT[:, 0:127], in0=xg[:, 0:127], in1=xg2[:, 0:127], op=ALU.mult),
                     reads=["xg", "xg2"], writes=["hidT"])
                ps3, pk3 = PR.next()
                if which == 0:
                    P.op("pe", lambda e, ps3=ps3: e.matmul(ps3[:, 0:127], w2sb[:], hidT[:, 0:127], start=True, stop=True),
                         reads=["w2sb", "hidT"], writes=[pk3])
                    P.op("act", lambda e, ps3=ps3: e.copy(out=KcT[:, 0:127], in_=ps3[:, 0:127]), reads=[pk3], writes=["KcT"])
                else:
                    P.op("pe", lambda e, ps3=ps3: e.matmul(ps3[0:127, 0:128], hidT[:, 0:127], w2sb[:], start=True, stop=True),
                         reads=["w2sb", "hidT"], writes=[pk3])
                    P.op("act", lambda e, ps3=ps3: e.copy(out=Vc[0:127, :], in_=ps3[0:127, 0:128]), reads=[pk3], writes=["Vc"])
            for j in range(8):
                hl = gl * 8 + j
                qc[0] = j % 2
                if j == 0:
                    P.dma("sp", qTbs[0][:], qT_s[hl], reads=["scratch"], writes=[("qTb", 0)])
                if j + 1 < 8:
                    P.dma("sp", qTbs[(j + 1) % 2][:], qT_s[hl + 1], reads=["scratch"], writes=[("qTb", (j + 1) % 2)])
                for qb in range(4):
                    qsl = slice(qb * 512, (qb + 1) * 512)
                    branch_step(KcT[:, 0:127], "KcT", Vc[0:127, :], "Vc", qsl, hl, 9 + qb, None, 127, first=True, last=True,
                                want_f32=(qb >= 2))
                    drain(1)
                    finish_branch(0, hl, qsl, "ret")

                    def tail(j=j, qb=qb, qsl=qsl, Pf=Pfs[pfi[0]], pfk=("Pf", pfi[0])):
                        P.dma("sp", tcmp_s[j, :, qsl], Tt[:], reads=["Tt"], writes=["tcmp"])
                        if qb >= 2:
                            P.op("dve", lambda e: e.tensor_tensor(out=pn[0:127, :], in0=Pf[0:127, :], in1=rden[0:127, :], op=ALU.mult),
                                 reads=[pfk, "rden"], writes=["pn"])
                            ps, pk = PR.next()
                            for k in range(4):
                                P.op("pe", lambda e, k=k, ps=ps: e.matmul(ps[:, k * 32:(k + 1) * 32], pn[0:127, k * 128:(k + 1) * 128],
                                                                         Mmat[0:127, :], start=True, stop=True),
                                     reads=["pn", "Mmat"], writes=[pk])
                            dst = pslc[:, (qb - 2) * 128:(qb - 1) * 128]
                            if j == 0:
                                P.op("dve", lambda e, ps=ps, dst=dst: e.tensor_copy(out=dst, in_=ps[:, 0:128]), reads=[pk], writes=["pslc"])
                            else:
                                P.op("dve", lambda e, ps=ps, dst=dst: e.tensor_tensor(out=dst, in0=dst, in1=ps[:, 0:128], op=ALU.add),
                                     reads=[pk, "pslc"], writes=["pslc"])
                    pend.append(tail)
                    if qb >= 2:
                        pfi[0] ^= 1
            drain(0)
            for ti in range(8):
                csl = slice(ti * 32, (ti + 1) * 32)
                P.op("dve", lambda e, csl=csl: e.tensor_tensor(out=sc[:], in0=pslc[:, csl], in1=At[:, csl], op=ALU.mult),
                     reads=["pslc", "At"], writes=["sc"])
                P.op("dve", lambda e, csl=csl: e.tensor_tensor(out=sc[:], in0=sc[:], in1=Bt[:, csl], op=ALU.add), reads=["sc", "Bt"], writes=["sc"])
                P.op("dve", lambda e: e.max(out=m8[:, 0:8], in_=sc[:]), reads=["sc"], writes=["m8"])
                P.op("dve", lambda e: e.match_replace(out=sc2[:], in_to_replace=m8[:, 0:8], in_values=sc[:], imm_value=-1e30),
                     reads=["sc", "m8"], writes=["sc2"])
                P.op("dve", lambda e: e.max(out=m8[:, 8:16], in_=sc2[:]), reads=["sc2"], writes=["m8"])
                P.op("dve", lambda e: e.tensor_scalar(out=selb[:], in0=sc[:], scalar1=m8[:, 15:16], scalar2=None, op0=ALU.is_ge),
                     reads=["sc", "m8"], writes=["selb"])
                P.op("dve", lambda e: e.tensor_scalar(out=selb[:], in0=selb[:], scalar1=30000.0, scalar2=-30000.0, op0=ALU.mult, op1=ALU.add),
                     reads=["selb"], writes=["selb"])
                ps, pk = PR.next()
                P.op("pe", lambda e, ps=ps: e.transpose(out=ps[0:32, 0:128], in_=selb[:], identity=ident[:]), reads=["selb", "ident"], writes=[pk])
                P.op("act", lambda e, ps=ps, ti=ti: e.copy(out=selbT[:, 1024 + ti * 128:1024 + (ti + 1) * 128], in_=ps[0:32, 0:128]),
                     reads=[pk], writes=["selbT"])
            for j in range(8):
                hl = gl * 8 + j
                drain(0)
                qc[0] = j % 2
                if j == 0:
                    P.dma("sp", qTbs[0][:], qT_s[hl], reads=["scratch"], writes=[("qTb", 0)])
                    P.dma("sp", zTbs[0][:], zT_s[hl], reads=["scratch"], writes=[("zTb", 0)])
                if j + 1 < 8:
                    P.dma("sp", qTbs[(j + 1) % 2][:], qT_s[hl + 1], reads=["scratch"], writes=[("qTb", (j + 1) % 2)])
                    P.dma("sp", zTbs[(j + 1) % 2][:], zT_s[hl + 1], reads=["scratch"], writes=[("zTb", (j + 1) % 2)])
                zTb, zkey = zTbs[j % 2], ("zTb", j % 2)
                for qb in range(4):
                    qsl = slice(qb * 512, (qb + 1) * 512)
                    drain(0)
                    P.dma("sp", pn[:], tcmp_s[j, :, qsl], reads=["tcmp"], writes=["pn"])
                    kts = list(range(0, 4 * qb + 4))
                    for kt in kts:
                        m = 4 * qb - kt
                        rvar, ci = (0, cidx_of(hl, -m)) if m >= 1 else (1 + (-m), cidx_of(hl, -m))
                        branch_step(kvT[:, 2, kt * 128:(kt + 1) * 128], "kvT", vtm[:, 0, kt, :], "vtm", qsl, hl, rvar, ci, 128,
                                    extra=(E[:, kt * 128:(kt + 1) * 128] if qb >= 2 else None), first=(kt == kts[0]),
                                    last=(kt == kts[-1]))
                        drain(2)
                    finish_branch(1, hl, qsl, "set")
                    kts = list(range(max(0, 4 * qb - 4), 4 * qb + 4))
                    for kt in kts:
                        m = 4 * qb - kt
                        if m >= 1:
                            rvar = 5 + (4 - m)
                        else:
                            rvar = 1 + (-m)
                        branch_step(kvT[:, 3, kt * 128:(kt + 1) * 128], "kvT", vtm[:, 1, kt, :], "vtm", qsl, hl, rvar, cidx_of(hl, -m), 128,
                                    first=(kt == kts[0]), last=(kt == kts[-1]))
                        drain(2)
                    finish_branch(2, hl, qsl, "add")
                    drain(0)
                    P.op("pool", lambda e: e.tensor_tensor(out=acc[:], in0=acc[:], in1=pn[:], op=ALU.add), reads=["acc", "pn"], writes=["acc"])
                    P.op("pool", lambda e, qsl=qsl, zTb=zTb: e.tensor_tensor(out=ys[:], in0=acc[:], in1=zTb[:, qsl], op=ALU.mult),
                         reads=["acc", zkey], writes=["nys"])
                    P.dma("sp", yT[hl * 128:(hl + 1) * 128, qsl], ys[:], reads=["nys"], writes=[("ysrc", hl, qb)])
                io["ydone"](hl)


IDENT = np.eye(128, dtype=np.float32)


def gainT(g):
    return np.ascontiguousarray(g.reshape(32, 128).T).astype(np.float32)


def colT(v, h, n=16):
    return np.ascontiguousarray(v[h * n * 128:(h + 1) * n * 128].reshape(n, 128).T).astype(np.float32)

def nsa_consts(h):
    heads = np.arange(16) + 16 * h
    sl = (2.0 ** (-8.0 * (heads + 1) / 32)).astype(np.float64)
    slope = np.broadcast_to(sl[None, :], (128, 16)).astype(np.float32)
    cst = np.zeros((128, 16, NREL), np.float64)
    for m in range(-15, 4):
        cst[:, :, m + 15] = sl[None, :] * 128.0 * m
    kj = np.arange(128)[:, None].astype(np.float64)
    qi = np.arange(512)[None, :].astype(np.float64)
    base = kj - qi
    Rt = np.zeros((13, 128, 512), np.float64)
    Rt[0] = base
    for v in range(4):
        Rt[1 + v] = np.where(-128 * v + qi - kj >= 0, base, NEGM)
    for n, rel in enumerate((512, 384, 256, 128)):
        Rt[5 + n] = np.where(rel + qi - kj < 512, base, NEGM)
    for qb in range(4):
        dist = 512 * qb + qi - 16 * kj - 31
        Rt[9 + qb] = np.where(dist >= 0, -dist, NEGM)
    key = np.arange(2048)
    E = (key[None, :] // 64 == np.arange(32)[:, None]).astype(np.float32)
    t = np.arange(1024)
    selb0 = np.where(np.arange(32)[:, None] <= (t[None, :] // 64), 0.0, -30000.0)
    Mmat = np.zeros((128, 32), np.float32)
    for i in range(32):
        for off, wgt in ((-1, 1), (0, 2), (1, 2), (2, 2), (3, 1)):
            n = 4 * i + off
            if 0 <= n < 127:
                Mmat[n, i] += wgt
    At = np.zeros((128, 8, 32), np.float32)
    Bt = np.zeros((128, 8, 32), np.float32)
    for ti in range(8):
        tt = 1024 + ti * 128 + np.arange(128)
        cur = (tt // 64)[:, None]
        blk = np.arange(32)[None, :]
        forced = (blk == 0) | (blk == cur) | (blk == cur - 1)
        future = blk > cur
        At[:, ti] = (~forced & ~future)
        Bt[:, ti] = np.where(forced, 1e6, np.where(future, -1.0, 0.0))
    SelR = np.zeros((48, 48, 128), np.float32)
    for r in range(48):
        SelR[r, r, :] = 1.0
    return {"slope": slope, "cst": np.ascontiguousarray(cst.reshape(128, 16 * NREL)).astype(np.float32),
            "Rt": Rt.astype(np.float32), "E": E.astype(ml_dtypes.bfloat16), "selb0": selb0.astype(ml_dtypes.bfloat16),
            "Mmat": Mmat, "At": np.ascontiguousarray(At.reshape(128, 256)), "Bt": np.ascontiguousarray(Bt.reshape(128, 256)),
            "SelR": np.ascontiguousarray(SelR.reshape(48, 48 * 128))}


PAIRS = [[0, 1], [2, 3], [4, 5], [6, 7]]
NSA_IN = ("wq", [D, 2048]), ("wz", [D, 2048]), ("wkv", [D, 1536]), ("wgl", [D, 48]), ("peT", [2, 128, 32]), \
    ("w1", [2, 4096, 128]), ("w2", [2, 128, 128])
NSA_CONST = ("slope", [128, 16], F32), ("cst", [128, 16 * NREL], F32), ("Rt", [13, 128, 512], F32), ("E", [32, 2048], BF16), \
    ("selb0", [32, 1024], BF16), ("Mmat", [128, 32], F32), ("At", [128, 256], F32), ("Bt", [128, 256], F32), \
    ("SelR", [48, 48 * 128], F32)
RG_IN = ("wx", [D, 2048]), ("wz", [D, 2048]), ("cw", [128, 64]), ("cb", [128, 16]), ("lam", [128, 16]), ("gb", [128, 32]), \
    ("gw", [2, 8, 256, 256])
HG_IN = ("wq", [D, 2048]), ("wf", [D, 2048]), ("wv", [D, 2048]), ("wg", [D, 2048]), ("lbl", [128, 64]), ("ng", [128, 1]), \
    ("ones", [128, 128]), ("rmask", [128, 512]), ("bdmask", [128, 128])


def build_fused():
    nc = bass.Bass("TRN2", target_bir_lowering=False)
    I = lambda n, s, dt=F32: nc.dram_tensor(n, list(s), dt, kind="ExternalInput").ap()
    T = lambda n, s, dt: nc.dram_tensor(n, list(s), dt, kind="Internal").ap()
    x = I("x", [S, D])
    x_my = I("x_my", [S, 2048])
    gT_d = I("gT", [128, 128])
    gR_d = I("gR", [4, 128, D])
    post_d = I("post", [4, 128, 2048])
    ident_d = I("ident", [128, 128])
    consts = {n: I(n, s, dt) for n, s, dt in NSA_CONST}
    lio = {}
    for i in range(4):
        spec = (NSA_IN, RG_IN, HG_IN)[i % 3]
        lio[i] = {n: I("L%d_%s" % (i, n), s) for n, s in spec}
        lio[i]["wo"] = I("L%d_wo" % i, [D, 2048])
    out = nc.dram_tensor("out", [S, 2048], F32, kind="ExternalOutput").ap()
    ysrc = T("ysrc", [2048, S], BF16)
    yT_g = T("yT_g", [D, S], BF16)
    xsrc = [T("xsrc0", [S, 2048], F32), T("xsrc1", [S, 2048], F32)]
    x_g = T("x_g", [2 * S, 2048], F32)
    ss_src = T("ss_src", [128, 8], F32)
    ss_g = T("ss_g", [256, 8], F32)
    scratch = {"qT_s": T("qT_s", [16, 128, S], BF16), "zT_s": T("zT_s", [16, 128, S], BF16),
               "kvT_s": T("kvT_s", [2, 4, 128, S], BF16), "vtm_s": T("vtm_s", [2, 2, S, 128], BF16),
               "tcmp_s": T("tcmp_s", [8, 128, S], F32)}
    with contextlib.ExitStack() as es:
        P = Prog(nc, es)
        PR = PsumRing(P, 4)
        psUl = [P.ps("psU%d" % i, [128, 512]) for i in range(2)]
        psDl = [P.ps("psD%d" % i, [128, 512]) for i in range(2)]
        gT_sb = P.sb("gT_sb", [128, 128], F32)
        ident = P.sb("ident_sb", [128, 128], F32)
        P.dma("sp", gT_sb[:], gT_d, writes=["gT"])
        P.dma("sp", ident[:], ident_d, writes=["ident"])
        for i in range(4):
            kind = i % 3
            io = dict(lio[i])
            io.update(gT=None, ident=None, x=(x if i == 0 else x_g), yT=ysrc, xsplit=(i > 0), gR=gR_d[i])

            def ydone(ch):
                if ch % 4 == 3:
                    k = ch // 4
                    P.cc("AllGather", PAIRS, ysrc[k * 512:(k + 1) * 512, :], yT_g[k * 1024:(k + 1) * 1024, :],
                         reads=[("ysrc", c_, t_) for c_ in range(4 * k, 4 * k + 4) for t_ in range(4)], writes=["yT_g"])
            io["ydone"] = ydone
            with contextlib.ExitStack() as esl:
                P.es_cur = esl
                P.pre = "L%dm_" % i
                if kind == 0:
                    io.update(consts)
                    io.update(scratch)
                    emit_nsa(P, nc, PR, psUl, psDl, gT_sb, i * 32, ident, io)
                elif kind == 1:
                    emit_rg(P, nc, PR, gT_sb, i * 32, ident, io)
                else:
                    emit_hg(P, nc, PR, gT_sb, i * 32, ident, io, layer=i)
                barrier(P)
            xo = out if i == 3 else xsrc[i % 2]

            def gather_x(k0, k1, xo=xo):
                for k in range(k0, k1):
                    P.cc("AllGather", PAIRS, xo[k * 256:(k + 1) * 256, :], x_g[k * 512:(k + 1) * 512, :],
                         reads=[("x_out", 2 * k), ("x_out", 2 * k + 1)], writes=["x_g"])
            with contextlib.ExitStack() as esl:
                P.es_cur = esl
                P.pre = "L%do_" % i
                x_res = x_my if i == 0 else xsrc[(i - 1) % 2]
                emit_outproj(P, nc, PR, yT_g, x_res, io["wo"], post_d[i], xo, ss_src, ss_g,
                             mid_cc=((lambda: gather_x(0, 4)) if i < 3 else None))
                barrier(P)
            P.es_cur = None
            if i < 3:
                gather_x(4, 8)
        P.finish()
    return nc


def kernel(x, pre_norm_gain, post_norm_gain, nsa_w_in, nsa_cmp_pe, nsa_cmp_w1, nsa_cmp_w2, nsa_w_out,
           rg_w_in, rg_conv_w, rg_conv_b, rg_gate_w, rg_gate_b, rg_lambda, rg_w_out,
           hg_w_in, hg_lb_logits, hg_norm_gain, hg_w_out):
    f = lambda a: np.asarray(a, dtype=np.float32)
    ca = np.ascontiguousarray
    x = f(x)
    pre, post = f(pre_norm_gain), f(post_norm_gain)
    common = {
        "gT": ca(np.concatenate([gainT(pre[i]) for i in range(4)], axis=1)),
        "ident": IDENT,
        "gR": ca(np.broadcast_to(pre[:, None, :], (4, 128, D))).astype(np.float32),
    }
    rmask = np.ones((128, 512), np.float32)
    rmask[:, ::64] = 0.0
    si = np.arange(128)[:, None]
    ti = np.arange(128)[None, :]
    bdmask = ((si // 64 == ti // 64) & (ti >= si)).astype(np.float32)
    in_maps = []
    for c in range(NCORES):
        b, h = c // 2, c % 2
        m = dict(common)
        m["x"] = ca(x[b])
        m["x_my"] = ca(x[b][:, h * 2048:(h + 1) * 2048])
        m["post"] = ca(np.broadcast_to(post[:, None, h * 2048:(h + 1) * 2048], (4, 128, 2048))).astype(np.float32)
        m.update(nsa_consts(h))
        for i, j in ((0, 0), (3, 1)):
            w_in = f(nsa_w_in[j])
            pfx = "L%d_" % i
            m[pfx + "wq"] = ca(w_in[:, h * 2048:(h + 1) * 2048])
            m[pfx + "wz"] = ca(w_in[:, 7264 + h * 2048:7264 + (h + 1) * 2048])
            m[pfx + "wkv"] = ca(np.concatenate(
                [w_in[:, 4096 + k * 512 + h * 256:4096 + k * 512 + (h + 1) * 256] for k in range(6)], axis=1))
            m[pfx + "wgl"] = ca(np.concatenate(
                [w_in[:, 7168 + br * 32 + h * 16:7168 + br * 32 + (h + 1) * 16] for br in range(3)], axis=1))
            m[pfx + "peT"] = ca(f(nsa_cmp_pe[j]).transpose(0, 2, 1))
            m[pfx + "w1"] = ca(f(nsa_cmp_w1[j]))
            m[pfx + "w2"] = ca(f(nsa_cmp_w2[j]))
            m[pfx + "wo"] = ca(f(nsa_w_out[j])[:, h * 2048:(h + 1) * 2048])
        w_in = f(rg_w_in[0])
        m["L1_wx"] = ca(w_in[:, h * 2048:(h + 1) * 2048])
        m["L1_wz"] = ca(w_in[:, 4096 + h * 2048:4096 + (h + 1) * 2048])
        cw = f(rg_conv_w[0])
        m["L1_cw"] = ca(np.stack([colT(cw[k], h) for k in range(4)], axis=-1).reshape(128, 64))
        m["L1_cb"] = colT(f(rg_conv_b[0]), h)
        m["L1_lam"] = colT(f(rg_lambda[0]), h)
        gbv = f(rg_gate_b[0])
        m["L1_gb"] = ca(np.stack([colT(gbv[k].reshape(-1), h) for k in range(2)], axis=1).reshape(128, 32))
        m["L1_gw"] = ca(f(rg_gate_w[0])[:, h * 8:(h + 1) * 8])
        m["L1_wo"] = ca(f(rg_w_out[0])[:, h * 2048:(h + 1) * 2048])
        w_in = f(hg_w_in[0])
        for k, n in enumerate(("wq", "wf", "wv", "wg")):
            m["L2_" + n] = ca(w_in[:, k * 4096 + h * 2048:k * 4096 + (h + 1) * 2048])
        lbl = f(hg_lb_logits)
        m["L2_lbl"] = ca(np.concatenate([colT(lbl[l], h) for l in range(4)], axis=1))
        m["L2_ng"] = ca(f(hg_norm_gain[0]).reshape(128, 1))
        m["L2_ones"] = np.ones((128, 128), np.float32)
        m["L2_rmask"] = rmask
        m["L2_bdmask"] = bdmask
        m["L2_wo"] = ca(f(hg_w_out[0])[:, h * 2048:(h + 1) * 2048])
        in_maps.append(m)
    nc = build_fused()
    res = run_bass_kernel_spmd(nc, in_maps, core_ids=list(range(NCORES)))
    out = np.empty((B, S, D), np.float32)
    for c in range(NCORES):
        b, h = c // 2, c % 2
        out[b][:, h * 2048:(h + 1) * 2048] = res.results[c]["out"]
    return out
```
